# Optimizing a Trainium2 kernel written in Bass

```python
import math
import jax, jax.numpy as jnp
from jax import lax
import numpy as np


D_MODEL = 1024
BATCH = 8
SEQ = 4096
DEPTH = 2

GRID_W = 64
CTX_LEN = 256
N_ADA = 9
FFN_DIM = 2816
RMS_EPS = 1e-6
ROPE_THETA = 10000.0
Q_BLOCK = 128
MLA_HEADS = 8
MLA_NOPE = 64
MLA_ROPE = 32
MLA_V = 64
MLA_Q_LORA = 256
MLA_KV_LORA = 128
MLA_SCALE = (MLA_NOPE + MLA_ROPE) ** -0.5
GLA_HEADS = 4
GLA_DK = 64
GLA_DV = 128
GLA_GATE_RANK = 16
GLA_GATE_TEMP = 16.0
GLA_CHUNK = 64
GQA_HEADS = 8
GQA_KV_HEADS = 2
GQA_HEAD_DIM = 64
GQA_SCALE = GQA_HEAD_DIM ** -0.5
HY_WIDTH = 512
HY_ORDER = 2
HY_SHORT = 3
HY_BANDS = 16
HY_EMB = 2 * HY_BANDS + 1
HY_HIDDEN = 64
HY_SIN_FREQ = 1.0
HY_DECAY_TARGET = 1e-2
HY_SHORT_DECAY_PCT = 0.3
HY_LONG_DECAY_PCT = 1.5
HY_FILTER_OUT_STD = 0.01
N_BRANCH = 4
BRANCH_WIDTH = 512

IN_SIZES = (MLA_Q_LORA, MLA_KV_LORA, MLA_ROPE,
            GLA_HEADS * GLA_DK, GLA_HEADS * GLA_DK, GLA_HEADS * GLA_DV, 2 * GLA_GATE_RANK, GLA_HEADS * GLA_DV,
            GQA_HEADS * GQA_HEAD_DIM, GQA_KV_HEADS * GQA_HEAD_DIM, GQA_KV_HEADS * GQA_HEAD_DIM,
            (HY_ORDER + 1) * HY_WIDTH,
            N_BRANCH * D_MODEL)
P_IN = sum(IN_SIZES)

kernel_name = "hybrid_parallel_mla_gla_gqa_hyena_dit_block"


def rms_norm(x, g):
    xf = x.astype(jnp.float32)
    y = xf * lax.rsqrt(jnp.mean(xf * xf, axis=-1, keepdims=True) + RMS_EPS)
    return y.astype(x.dtype) * g


def modulate(x, shift, scale):
    return x * (1.0 + scale) + shift


def swiglu(x, w_gate, w_up, w_down):
    return (jax.nn.silu(x @ w_gate) * (x @ w_up)) @ w_down


def split_cols(p):
    offsets, acc = [], 0
    for s in IN_SIZES[:-1]:
        acc += s
        offsets.append(acc)
    return jnp.split(p, offsets, axis=-1)


def grid_positions(rows):
    r = jnp.repeat(jnp.arange(rows, dtype=jnp.float32), GRID_W)
    col = jnp.tile(jnp.arange(GRID_W, dtype=jnp.float32), rows)
    return r, col


def rope_1d(t, pos):
    half = t.shape[-1] // 2
    freqs = ROPE_THETA ** (-jnp.arange(half, dtype=jnp.float32) / half)
    ang = pos[:, None] * freqs[None, :]
    cos, sin = jnp.cos(ang).astype(t.dtype), jnp.sin(ang).astype(t.dtype)
    t1, t2 = t[..., :half], t[..., half:]
    return jnp.concatenate([t1 * cos - t2 * sin, t1 * sin + t2 * cos], axis=-1)


def axial_rope(t, pos):
    rows, cols = pos
    r = t.shape[-1] // 2
    return jnp.concatenate([rope_1d(t[..., :r], rows), rope_1d(t[..., r:], cols)], axis=-1)


def attend(q, k, v, scale):
    B, G, R, Lq, d = q.shape
    nb = Lq // Q_BLOCK
    qb = jnp.moveaxis(q.reshape(B, G, R, nb, Q_BLOCK, d), 3, 0)

    def one_block(qblk):
        s = jnp.einsum('bgrqd,bgkd->bgrqk', qblk, k).astype(jnp.float32) * scale
        p = jax.nn.softmax(s, axis=-1).astype(v.dtype)
        return jnp.einsum('bgrqk,bgkv->bgrqv', p, v)

    o = lax.map(one_block, qb)
    return jnp.moveaxis(o, 0, 3).reshape(B, G, R, Lq, v.shape[-1])


def heads_out(o):
    B, G, R, L, dv = o.shape
    return jnp.transpose(o, (0, 3, 1, 2, 4)).reshape(B, L, G * R * dv)


def mla_qkv(c_q, c_kv, k_rope, q_norm_g, w_uq, kv_norm_g, w_ukv, pos):
    B, L, _ = c_q.shape
    q = (rms_norm(c_q, q_norm_g) @ w_uq).reshape(B, L, MLA_HEADS, MLA_NOPE + MLA_ROPE).transpose(0, 2, 1, 3)
    kv = (rms_norm(c_kv, kv_norm_g) @ w_ukv).reshape(B, L, MLA_HEADS, MLA_NOPE + MLA_V).transpose(0, 2, 1, 3)
    q_nope, q_rope = q[..., :MLA_NOPE], q[..., MLA_NOPE:]
    if pos is not None:
        q_rope = axial_rope(q_rope, pos)
        k_rope = axial_rope(k_rope, pos)
    q = jnp.concatenate([q_nope, q_rope], axis=-1)
    k = jnp.concatenate([kv[..., :MLA_NOPE], jnp.broadcast_to(k_rope[:, None], (B, MLA_HEADS, L, MLA_ROPE))], axis=-1)
    return q[:, :, None], k, kv[..., MLA_NOPE:]


def gqa_qkv(q, k, v, q_norm_g, k_norm_g, pos):
    B, L, _ = q.shape
    q = rms_norm(q.reshape(B, L, GQA_HEADS, GQA_HEAD_DIM), q_norm_g).transpose(0, 2, 1, 3)
    k = rms_norm(k.reshape(B, L, GQA_KV_HEADS, GQA_HEAD_DIM), k_norm_g).transpose(0, 2, 1, 3)
    v = v.reshape(B, L, GQA_KV_HEADS, GQA_HEAD_DIM).transpose(0, 2, 1, 3)
    if pos is not None:
        q = axial_rope(q, pos)
        k = axial_rope(k, pos)
    return q.reshape(B, GQA_KV_HEADS, GQA_HEADS // GQA_KV_HEADS, L, GQA_HEAD_DIM), k, v


def gla_inputs(q, k, v, g_lr, w_gate, b_gate):
    B, L, _ = q.shape

    def heads(t, d):
        return t.reshape(B, L, GLA_HEADS, d).transpose(0, 2, 1, 3)

    lr = g_lr.reshape(B, L, 2, GLA_GATE_RANK)
    pre = jnp.einsum('blnr,nrk->blnk', lr, w_gate) + b_gate
    log_a = jax.nn.log_sigmoid(pre.astype(jnp.float32)) / GLA_GATE_TEMP
    return (heads(q, GLA_DK) * GLA_DK ** -0.5, heads(k, GLA_DK), heads(v, GLA_DV),
            heads(log_a[:, :, 0], GLA_DK), heads(log_a[:, :, 1], GLA_DK))


def gla_chunk_scan(q, k, v, log_a, s0):
    B, H, L, DK = q.shape
    DV = v.shape[-1]
    n = L // GLA_CHUNK

    def chunks(t):
        return jnp.moveaxis(t.reshape(B, H, n, GLA_CHUNK, t.shape[-1]), 2, 0)

    lower = jnp.tril(jnp.ones((GLA_CHUNK, GLA_CHUNK), dtype=bool))

    def step(s, blk):
        qb, kb, vb, ab = blk
        bcum = jnp.cumsum(ab, axis=-2)
        b_last = bcum[:, :, -1:, :]
        q_dec = qb * jnp.exp(bcum)
        k_inv = kb * jnp.exp(-bcum)
        a = jnp.where(lower, jnp.einsum('bhid,bhjd->bhij', q_dec, k_inv), 0.0)
        o = jnp.einsum('bhij,bhjv->bhiv', a, vb) + jnp.einsum('bhid,bhdv->bhiv', q_dec, s)
        k_end = kb * jnp.exp(b_last - bcum)
        s_new = jnp.exp(b_last[:, :, 0, :, None]) * s + jnp.einsum('bhjd,bhjv->bhdv', k_end, vb)
        return s_new, o

    s_fin, o = lax.scan(step, s0, (chunks(q), chunks(k), chunks(v), chunks(log_a)))
    o = jnp.moveaxis(o, 0, 2).reshape(B, H, L, DV)
    return o.astype(v.dtype), s_fin


def gla_bidir(q, k, v, la_f, la_b, s_f0, s_b0):
    o_f, s_f = gla_chunk_scan(q, k, v, la_f, s_f0)
    flip = lambda t: jnp.flip(t, axis=2)
    o_b, s_b = gla_chunk_scan(flip(q), flip(k), flip(v), flip(la_b), s_b0)
    return o_f + flip(o_b), s_f, s_b


def gla_output(o, out_gate, norm_g):
    B, H, L, DV = o.shape
    o = rms_norm(o, norm_g).transpose(0, 2, 1, 3).reshape(B, L, H * DV)
    return o * jax.nn.silu(out_gate)


def hyena_filters(L, w1, b1, w2, b2, w3):
    t = jnp.linspace(0.0, 1.0, L, dtype=jnp.float32)
    w = 2.0 * math.pi * jnp.arange(L, dtype=jnp.float32) / L
    f = jnp.linspace(1e-4, HY_BANDS - 1, HY_BANDS, dtype=jnp.float32)
    fw = w[:, None] * f[None, :]
    feat = jnp.concatenate([t[:, None], jnp.cos(fw), -jnp.sin(fw)], axis=-1)
    h = jnp.sin(HY_SIN_FREQ * (feat @ w1 + b1))
    h = jnp.sin(HY_SIN_FREQ * (h @ w2 + b2))
    h = (h @ w3).reshape(L, HY_ORDER, 2, HY_WIDTH)
    deltas = jnp.abs(jnp.linspace(math.log(HY_DECAY_TARGET) / HY_LONG_DECAY_PCT,
                                  math.log(HY_DECAY_TARGET) / HY_SHORT_DECAY_PCT, HY_WIDTH, dtype=jnp.float32))
    h = h * jnp.exp(-t[:, None] * deltas[None, :])[:, None, None, :]
    full = jnp.concatenate([h[:, :, 0], jnp.zeros((1, HY_ORDER, HY_WIDTH), h.dtype), jnp.flip(h[1:, :, 1], axis=0)], axis=0)
    return jnp.fft.rfft(full.astype(jnp.float32), axis=0)


def fft_long_conv(u, filt_f, bias):
    L = u.shape[1]
    U = jnp.fft.rfft(u.astype(jnp.float32), n=2 * L, axis=1)
    y = jnp.fft.irfft(U * filt_f[None], n=2 * L, axis=1)[:, :L]
    return (y + u * bias).astype(u.dtype)


def hyena_mixer(p, hy):
    sconv_w, sconv_b, w1, b1, w2, b2, w3, fbias = hy
    C = p.shape[-1]
    z = lax.conv_general_dilated(p, sconv_w[:, None, :], window_strides=(1,), padding='SAME',
                                 dimension_numbers=('NWC', 'WIO', 'NWC'), feature_group_count=C) + sconv_b
    x1, x2, v = jnp.split(z, 3, axis=-1)
    filt = hyena_filters(p.shape[1], w1, b1, w2, b2, w3)
    v = x1 * fft_long_conv(v, filt[:, 0], fbias[0])
    v = x2 * fft_long_conv(v, filt[:, 1], fbias[1])
    return v


def merge(outs, gate_cols, w_branch):
    B, L, _ = gate_cols.shape
    g = jax.nn.sigmoid(gate_cols.reshape(B, L, N_BRANCH, D_MODEL))
    y = g[:, :, 0] * (outs[0] @ w_branch[0])
    for i in range(1, N_BRANCH):
        y = y + g[:, :, i] * (outs[i] @ w_branch[i])
    return y


def token_mixers(u, uc, pos, need_ctx, w_in, mla_q_norm_g, mla_w_uq, mla_kv_norm_g, mla_w_ukv,
                 gla_w_gate, gla_b_gate, gla_norm_g, gqa_q_norm_g, gqa_k_norm_g, hy, w_branch, w_out):
    B = u.shape[0]
    (a_cq, a_ckv, a_kr, b_q, b_k, b_v, b_lr, b_og, c_q, c_k, c_v, d_p, gates) = split_cols(u @ w_in)
    (a_cq_c, a_ckv_c, a_kr_c, b_q_c, b_k_c, b_v_c, b_lr_c, b_og_c, c_q_c, c_k_c, c_v_c, d_p_c, gates_c) = split_cols(uc @ w_in)

    qa, ka, va = mla_qkv(a_cq, a_ckv, a_kr, mla_q_norm_g, mla_w_uq, mla_kv_norm_g, mla_w_ukv, pos)
    qa_c, ka_c, va_c = mla_qkv(a_cq_c, a_ckv_c, a_kr_c, mla_q_norm_g, mla_w_uq, mla_kv_norm_g, mla_w_ukv, None)
    o_a = heads_out(attend(qa, jnp.concatenate([ka, ka_c], axis=2), jnp.concatenate([va, va_c], axis=2), MLA_SCALE))

    s0 = jnp.zeros((B, GLA_HEADS, GLA_DK, GLA_DV), jnp.float32)
    ob_c, s_f, s_b = gla_bidir(*gla_inputs(b_q_c, b_k_c, b_v_c, b_lr_c, gla_w_gate, gla_b_gate), s0, s0)
    ob_l, _, _ = gla_bidir(*gla_inputs(b_q, b_k, b_v, b_lr, gla_w_gate, gla_b_gate), s_f, s_b)
    o_b = gla_output(ob_l, b_og, gla_norm_g)

    qc, kc, vc = gqa_qkv(c_q, c_k, c_v, gqa_q_norm_g, gqa_k_norm_g, pos)
    qc_c, kc_c, vc_c = gqa_qkv(c_q_c, c_k_c, c_v_c, gqa_q_norm_g, gqa_k_norm_g, None)
    o_c = heads_out(attend(qc, jnp.concatenate([kc, kc_c], axis=2), jnp.concatenate([vc, vc_c], axis=2), GQA_SCALE))

    o_d = hyena_mixer(d_p, hy)

    y = merge((o_a, o_b, o_c, o_d), gates, w_branch) @ w_out
    if not need_ctx:
        return y, None

    o_ac = heads_out(attend(qa_c, ka_c, va_c, MLA_SCALE))
    o_bc = gla_output(ob_c, b_og_c, gla_norm_g)
    o_cc = heads_out(attend(qc_c, kc_c, vc_c, GQA_SCALE))
    o_dc = hyena_mixer(d_p_c, hy)
    yc = merge((o_ac, o_bc, o_cc, o_dc), gates_c, w_branch) @ w_out
    return y, yc


def setup_inputs(seed: int = 0) -> dict:
    key = jax.random.key(seed)
    ks = jax.random.split(key, 31)
    f32 = jnp.float32

    def nrm(k, shape, std):
        return std * jax.random.normal(k, shape, f32)

    def gain(k, shape):
        return 1.0 + 0.05 * jax.random.normal(k, shape, f32)

    return {
        "x": nrm(ks[0], (BATCH, SEQ, D_MODEL), 1.0),
        "c": nrm(ks[1], (BATCH, D_MODEL), 1.0),
        "ctx": nrm(ks[2], (BATCH, CTX_LEN, D_MODEL), 1.0),
        "c_ctx": nrm(ks[3], (D_MODEL,), 1.0),
        "ada_w": nrm(ks[4], (DEPTH, D_MODEL, N_ADA * D_MODEL), 0.02),
        "ada_b": nrm(ks[5], (DEPTH, N_ADA * D_MODEL), 0.01),
        "norm_g": gain(ks[6], (DEPTH, 3, D_MODEL)),
        "ffn_w_gate": nrm(ks[7], (DEPTH, 2, D_MODEL, FFN_DIM), D_MODEL ** -0.5),
        "ffn_w_up": nrm(ks[8], (DEPTH, 2, D_MODEL, FFN_DIM), D_MODEL ** -0.5),
        "ffn_w_down": nrm(ks[9], (DEPTH, 2, FFN_DIM, D_MODEL), FFN_DIM ** -0.5),
        "w_in": nrm(ks[10], (DEPTH, D_MODEL, P_IN), D_MODEL ** -0.5),
        "mla_q_norm_g": gain(ks[11], (DEPTH, MLA_Q_LORA)),
        "mla_w_uq": nrm(ks[12], (DEPTH, MLA_Q_LORA, MLA_HEADS * (MLA_NOPE + MLA_ROPE)), MLA_Q_LORA ** -0.5),
        "mla_kv_norm_g": gain(ks[13], (DEPTH, MLA_KV_LORA)),
        "mla_w_ukv": nrm(ks[14], (DEPTH, MLA_KV_LORA, MLA_HEADS * (MLA_NOPE + MLA_V)), MLA_KV_LORA ** -0.5),
        "gla_w_gate": nrm(ks[15], (DEPTH, 2, GLA_GATE_RANK, GLA_HEADS * GLA_DK), GLA_GATE_RANK ** -0.5),
        "gla_b_gate": nrm(ks[16], (DEPTH, 2, GLA_HEADS * GLA_DK), 0.1),
        "gla_norm_g": gain(ks[17], (DEPTH, GLA_DV)),
        "gqa_q_norm_g": gain(ks[18], (DEPTH, GQA_HEAD_DIM)),
        "gqa_k_norm_g": gain(ks[19], (DEPTH, GQA_HEAD_DIM)),
        "hy_sconv_w": nrm(ks[20], (DEPTH, HY_SHORT, (HY_ORDER + 1) * HY_WIDTH), HY_SHORT ** -0.5),
        "hy_sconv_b": nrm(ks[21], (DEPTH, (HY_ORDER + 1) * HY_WIDTH), 0.01),
        "hy_filt_w1": nrm(ks[22], (DEPTH, HY_EMB, HY_HIDDEN), HY_EMB ** -0.5),
        "hy_filt_b1": nrm(ks[23], (DEPTH, HY_HIDDEN), 0.1),
        "hy_filt_w2": nrm(ks[24], (DEPTH, HY_HIDDEN, HY_HIDDEN), HY_HIDDEN ** -0.5),
        "hy_filt_b2": nrm(ks[25], (DEPTH, HY_HIDDEN), 0.1),
        "hy_filt_w3": nrm(ks[26], (DEPTH, HY_HIDDEN, HY_ORDER * 2 * HY_WIDTH), HY_FILTER_OUT_STD),
        "hy_filt_bias": nrm(ks[27], (DEPTH, HY_ORDER, HY_WIDTH), 0.1),
        "w_branch": nrm(ks[28], (DEPTH, N_BRANCH, BRANCH_WIDTH, D_MODEL), BRANCH_WIDTH ** -0.5),
        "w_out": nrm(ks[29], (DEPTH, D_MODEL, D_MODEL), D_MODEL ** -0.5),
        "final_g": gain(ks[30], (D_MODEL,)),
    }


def reference(x, c, ctx, c_ctx, ada_w, ada_b, norm_g, ffn_w_gate, ffn_w_up, ffn_w_down, w_in,
              mla_q_norm_g, mla_w_uq, mla_kv_norm_g, mla_w_ukv, gla_w_gate, gla_b_gate, gla_norm_g,
              gqa_q_norm_g, gqa_k_norm_g, hy_sconv_w, hy_sconv_b, hy_filt_w1, hy_filt_b1, hy_filt_w2,
              hy_filt_b2, hy_filt_w3, hy_filt_bias, w_branch, w_out, final_g):
    B, L, _ = x.shape
    rows = L // GRID_W
    pos = grid_positions(rows)
    silu_c = jax.nn.silu(c)
    silu_cc = jax.nn.silu(c_ctx)
    xc = ctx
    for l in range(DEPTH):
        need_ctx = l < DEPTH - 1
        mod = (silu_c @ ada_w[l] + ada_b[l]).reshape(B, 1, N_ADA, D_MODEL)
        modc = (silu_cc @ ada_w[l] + ada_b[l]).reshape(N_ADA, D_MODEL)
        sh1, sc1, g1, sh2, sc2, g2, sh3, sc3, g3 = [mod[:, :, i] for i in range(N_ADA)]
        sh1c, sc1c, g1c, sh2c, sc2c, g2c, sh3c, sc3c, g3c = [modc[i] for i in range(N_ADA)]

        x = x + 0.5 * g1 * swiglu(modulate(rms_norm(x, norm_g[l, 0]), sh1, sc1), ffn_w_gate[l, 0], ffn_w_up[l, 0], ffn_w_down[l, 0])
        xc = xc + 0.5 * g1c * swiglu(modulate(rms_norm(xc, norm_g[l, 0]), sh1c, sc1c), ffn_w_gate[l, 0], ffn_w_up[l, 0], ffn_w_down[l, 0])

        hy = (hy_sconv_w[l], hy_sconv_b[l], hy_filt_w1[l], hy_filt_b1[l], hy_filt_w2[l], hy_filt_b2[l], hy_filt_w3[l], hy_filt_bias[l])
        y, yc = token_mixers(modulate(rms_norm(x, norm_g[l, 1]), sh2, sc2),
                             modulate(rms_norm(xc, norm_g[l, 1]), sh2c, sc2c),
                             pos, need_ctx, w_in[l], mla_q_norm_g[l], mla_w_uq[l], mla_kv_norm_g[l], mla_w_ukv[l],
                             gla_w_gate[l], gla_b_gate[l], gla_norm_g[l], gqa_q_norm_g[l], gqa_k_norm_g[l],
                             hy, w_branch[l], w_out[l])
        x = x + g2 * y

        x = x + 0.5 * g3 * swiglu(modulate(rms_norm(x, norm_g[l, 2]), sh3, sc3), ffn_w_gate[l, 1], ffn_w_up[l, 1], ffn_w_down[l, 1])
        if need_ctx:
            xc = xc + g2c * yc
            xc = xc + 0.5 * g3c * swiglu(modulate(rms_norm(xc, norm_g[l, 2]), sh3c, sc3c), ffn_w_gate[l, 1], ffn_w_up[l, 1], ffn_w_down[l, 1])
    return rms_norm(x, final_g)
```

```python
import math
from contextlib import ExitStack
import numpy as np
import ml_dtypes
import concourse.bass as bass
import concourse.mybir as mybir
from concourse.bass_utils import run_bass_kernel_spmd

F32 = mybir.dt.float32
BF16 = mybir.dt.bfloat16
AF = mybir.ActivationFunctionType
ALU = mybir.AluOpType
AX = mybir.AxisListType

D = 1024
L = 4096
CT = 256
NT = L + CT
DEPTH = 2
FF = 2816
NFC = FF // 128
P_IN = 8384
EPS = 1e-6
TILES = [(i * 512, 512) for i in range(8)] + [(L, CT)]
O_CQ, O_CKV, O_KR, O_BQ, O_BK, O_BV, O_LR, O_OG, O_CQQ, O_CK, O_CV, O_DP, O_GT = (
    0, 256, 384, 416, 672, 928, 1440, 1472, 1984, 2496, 2624, 2752, 4288)

DBG_STOP = None
PROJ_SECTIONS = {'mla', 'gqa', 'gla', 'hy'}
PROJ_TILES = 9
SKIP = ''
MLA_STEP = 99
NHEADS_DBG = 8
RING_F32, RING_BF, RING_OUT = 10, 16, 12


class Buf:
    __slots__ = ("name", "last_w", "readers")

    def __init__(self, name="b"):
        self.name = name
        self.last_w = None
        self.readers = []


class Op:
    __slots__ = ("eng", "fn", "deps", "signal", "sig_val", "sem", "is_dma", "bar", "snap")

    def __init__(self, eng, fn, is_dma):
        self.eng = eng
        self.fn = fn
        self.deps = []
        self.signal = False
        self.sig_val = None
        self.sem = None
        self.is_dma = is_dma
        self.bar = None
        self.snap = None


ENGS = ("pe", "act", "dve", "pool", "sp")
NWAIT = {}
SAME_ENGINE_SKIP = ("pe",)
SIGNAL_ALL = True
SKIP_ATTN = False
SKIP_GLA = False
ROPE_COPY_ENG = "act"
DMA_SLOTS = 8


class Prog:
    def __init__(self, nc):
        self.nc = nc
        self.ops = {e: [] for e in ENGS}
        self.nbar = 0

    def op(self, eng, fn, reads=(), writes=(), dma=False):
        o = Op(eng, fn, dma)
        if SIGNAL_ALL and not dma:
            o.signal = True
        deps = []
        for b in reads:
            if b.last_w is not None:
                deps.append(b.last_w)
        for b in writes:
            if b.last_w is not None:
                deps.append(b.last_w)
            deps.extend(b.readers)
        seen = set()
        for d in deps:
            if id(d) in seen or d is o:
                continue
            seen.add(id(d))
            if d.eng == eng and eng in SAME_ENGINE_SKIP and not d.is_dma:
                continue
            o.deps.append(d)
            d.signal = True
        for b in reads:
            b.readers.append(o)
        for b in writes:
            b.last_w = o
            b.readers = []
        self.ops[eng].append(o)
        return o

    def dma(self, out, in_, reads=(), writes=(), eng="sp", **kw):
        return self.op(eng, lambda e: e.dma_start(out=out, in_=in_, **kw), reads, writes, dma=True)

    def barrier(self):
        self.nbar += 1
        for e in ENGS:
            last = None
            for o in reversed(self.ops[e]):
                if o.bar is None:
                    last = o
                    break
            if last is not None and not last.is_dma:
                last.signal = True
            o = Op(e, None, False)
            o.bar = self.nbar
            o.snap = last
            self.ops[e].append(o)

    def emit(self):
        nc = self.nc
        with ExitStack() as st:
            csem = {e: st.enter_context(nc.semaphore("c_" + e)) for e in ENGS}
            bsem = {e: st.enter_context(nc.semaphore("b_" + e)) for e in ENGS}
            dsem = {e: [st.enter_context(nc.semaphore("d_%s_%d" % (e, i))) for i in range(DMA_SLOTS)] for e in ENGS}
            ccount = {e: 0 for e in ENGS}
            dcount = {e: [0] * DMA_SLOTS for e in ENGS}
            dnext = {e: 0 for e in ENGS}
            for e in ENGS:
                for o in self.ops[e]:
                    if o.bar is not None:
                        o.sig_val = list(dcount[e])
                        continue
                    if o.is_dma:
                        s = dnext[e]
                        dnext[e] = (s + 1) % DMA_SLOTS
                        o.sem = dsem[e][s]
                        prev = dcount[e][s]
                        dcount[e][s] = prev + 16
                        o.sig_val = prev + 16
                        o.signal = True
                        o.deps.append(("slot", o.sem, prev))
                    elif o.signal:
                        ccount[e] += 1
                        o.sem = csem[e]
                        o.sig_val = ccount[e]
            final_d = {e: list(dcount[e]) for e in ENGS}
            st.enter_context(nc.allow_non_contiguous_dma(reason="small strided vector loads"))
            blk = st.enter_context(nc.Block())

            def make(e):
                def body(eng):
                    waited = {}

                    def wait(sem, val):
                        if val <= 0:
                            return
                        k = id(sem)
                        if waited.get(k, 0) >= val:
                            return
                        waited[k] = val
                        NWAIT[e] = NWAIT.get(e, 0) + 1
                        eng.wait_ge(sem, val)

                    for o in self.ops[e]:
                        if o.bar is not None:
                            for i in range(DMA_SLOTS):
                                wait(dsem[e][i], o.sig_val[i])
                            if o.snap is not None and not o.snap.is_dma:
                                wait(o.snap.sem, o.snap.sig_val)
                            eng.nop().then_inc(bsem[e], 1)
                            for e2 in ENGS:
                                if e2 != e:
                                    wait(bsem[e2], o.bar)
                            continue
                        for d in o.deps:
                            if isinstance(d, tuple):
                                wait(d[1], d[2])
                            else:
                                wait(d.sem, d.sig_val)
                        ins = o.fn(eng)
                        if o.signal:
                            ins.then_inc(o.sem, 16 if o.is_dma else 1)
                    for i in range(DMA_SLOTS):
                        wait(dsem[e][i], final_d[e][i])
                return body

            blk.tensor(make("pe"))
            blk.scalar(make("act"))
            blk.vector(make("dve"))
            blk.gpsimd(make("pool"))
            blk.sync(make("sp"))


class Ring:
    def __init__(self, k, st, name, n, shape, dt):
        self.t = [st.enter_context(k.nc.sbuf_tensor("%s%d_%d" % (name, k.uid(), i), shape, dt)) for i in range(n)]
        self.b = [Buf(name + str(i)) for i in range(n)]
        self.i = 0

    def next(self):
        i = self.i
        self.i = (i + 1) % len(self.t)
        return self.t[i], self.b[i]


class KB:
    def __init__(self, nc, st):
        self.nc = nc
        self.st = st
        self.P = Prog(nc)
        self._uid = 0
        self.ps_t = [st.enter_context(nc.psum_tensor("psb%d" % i, [128, 512], F32)) for i in range(8)]
        self.ps_b = [Buf("ps%d" % i) for i in range(8)]
        self.acc_i = 0
        self.tmp_i = 0
        self.rr = 0

    def uid(self):
        self._uid += 1
        return self._uid

    def acc(self):
        i = self.acc_i
        self.acc_i = (i + 1) % 2
        return self.ps_t[i], self.ps_b[i]

    def tmp(self):
        i = 2 + self.tmp_i
        self.tmp_i = (self.tmp_i + 1) % 6
        return self.ps_t[i], self.ps_b[i]

    def sb(self, st, name, shape, dt):
        return st.enter_context(self.nc.sbuf_tensor("%s_%d" % (name, self.uid()), shape, dt)), Buf(name)

    def dram(self, name, shape, dt):
        return self.nc.dram_tensor(name, shape, dt).ap(), Buf(name)

    def ev(self):
        self.rr ^= 1
        return "dve" if self.rr else "act"


def copy_on(P, eng, out, in_, reads, writes):
    if eng == "act":
        return P.op("act", lambda e: e.copy(out=out, in_=in_), reads, writes)
    return P.op(eng, lambda e: e.tensor_copy(out=out, in_=in_), reads, writes)


def mm_group(P, ps, psb, pairs, reads, n=None):
    def fn(e):
        ins = None
        for i, (a, b) in enumerate(pairs):
            ins = e.matmul(ps, lhsT=a, rhs=b, start=(i == 0), stop=(i == len(pairs) - 1))
        return ins
    return P.op("pe", fn, reads, [psb])


def rope_tables(dim_layout):
    cos = np.ones((128, NT), np.float32)
    sin = np.zeros((128, NT), np.float32)
    RT = np.zeros((128, 128), np.float32)
    t = np.arange(L)
    rows = (t // 64).astype(np.float32)
    cols = (t % 64).astype(np.float32)
    for (r0, rd) in dim_layout:
        r = rd // 2
        half = r // 2
        freqs = (10000.0 ** (-np.arange(half, dtype=np.float32) / half)).astype(np.float32)
        for part, pos in ((0, rows), (1, cols)):
            base = r0 + part * r
            ang = pos[None, :] * freqs[:, None]
            c, s = np.cos(ang), np.sin(ang)
            cos[base:base + half, :L] = c
            cos[base + half:base + r, :L] = c
            sin[base:base + half, :L] = s
            sin[base + half:base + r, :L] = s
            for i in range(half):
                RT[base + half + i, base + i] = -1.0
                RT[base + i, base + half + i] = 1.0
    return cos, sin, RT


def dft_tables(n):
    N = 2 * n
    k = np.arange(n, dtype=np.float64)
    t = np.arange(n, dtype=np.float64)
    ang = 2.0 * np.pi * np.outer(t, k + 0.5) / N
    c, s = np.cos(ang), np.sin(ang)
    bf = ml_dtypes.bfloat16
    return (c.astype(np.float32).astype(bf), s.astype(np.float32).astype(bf),
            ((2.0 / N) * c.T).astype(np.float32).astype(bf), ((2.0 / N) * s.T).astype(np.float32).astype(bf))


def hyena_consts(n):
    t = np.linspace(0.0, 1.0, n, dtype=np.float32)
    w = (2.0 * math.pi * np.arange(n, dtype=np.float32) / n).astype(np.float32)
    f = np.linspace(1e-4, 15, 16, dtype=np.float32)
    fw = w[:, None] * f[None, :]
    feat = np.concatenate([t[:, None], np.cos(fw), -np.sin(fw)], axis=-1).astype(np.float32)
    deltas = np.abs(np.linspace(math.log(1e-2) / 1.5, math.log(1e-2) / 0.3, 512, dtype=np.float32))
    dec = np.exp(-t[:, None] * deltas[None, :]).astype(np.float32)
    decb = dec.copy()
    decb[0, :] = 0.0
    return np.ascontiguousarray(feat.T), dec, decb


def make_consts():
    c = {}
    c["k_ident"] = np.eye(128, dtype=np.float32)
    c["k_ones"] = np.ones((128, 128), np.float32)
    bo = np.zeros((128, 128), np.float32)
    bo[:64, :64] = 1.0
    bo[64:, 64:] = 1.0
    c["k_bones"] = bo
    cm, sm, rm = rope_tables([(64, 32)])
    c["k_cosM"], c["k_sinM"], c["k_rtM"] = cm, sm, rm
    cg, sg, rg = rope_tables([(0, 64), (64, 64)])
    c["k_cosG"], c["k_sinG"], c["k_rtG"] = cg, sg, rg
    for n, tag in ((L, "L"), (CT, "C")):
        a, b, c2, d2 = dft_tables(n)
        c["k_dc1" + tag], c["k_ds1" + tag], c["k_dc2" + tag], c["k_ds2" + tag] = a, b, c2, d2
        ft, dec, decb = hyena_consts(n)
        c["k_feat" + tag], c["k_dec" + tag], c["k_decb" + tag] = ft, dec, decb
    j = np.arange(64)
    c["k_m_le"] = (j[:, None] <= j[None, :]).astype(np.float32)
    c["k_m_ge"] = (j[:, None] >= j[None, :]).astype(np.float32)
    c["k_m_gt"] = (j[:, None] > j[None, :]).astype(np.float32)
    c["k_m_lt"] = (j[:, None] < j[None, :]).astype(np.float32)
    return c


_CONSTS = None


def consts():
    global _CONSTS
    if _CONSTS is None:
        _CONSTS = make_consts()
    return _CONSTS


def build(nc, stop=None, dbg=None):
    st = ExitStack()
    k = KB(nc, st)
    P = k.P
    I = {}

    def din(name, shape, dt=F32):
        I[name] = nc.dram_tensor(name, list(shape), dt, kind="ExternalInput").ap()
        return I[name]

    din("x", [L, D]); din("ctx", [CT, D]); din("cc", [128, 8, 2])
    din("ada_w", [DEPTH, D, 9 * D]); din("ada_b", [128, DEPTH, 72]); din("norm_g", [128, DEPTH, 3, 8])
    din("ffn_w_gate", [DEPTH, 2, D, FF]); din("ffn_w_up", [DEPTH, 2, D, FF]); din("ffn_w_down", [DEPTH, 2, FF, D])
    din("w_in", [DEPTH, D, P_IN])
    din("mla_q_norm_g", [128, DEPTH, 2]); din("mla_w_uq", [DEPTH, 256, 768])
    din("mla_kv_norm_g", [128, DEPTH]); din("mla_w_ukv", [DEPTH, 128, 1024])
    din("gla_w_gate", [DEPTH, 2, 16, 256]); din("gla_b_gate", [DEPTH, 2, 256]); din("gla_norm_g", [128, DEPTH])
    din("gqa_q_norm_g", [128, DEPTH]); din("gqa_k_norm_g", [128, DEPTH])
    din("hy_sconv_w", [128, DEPTH, 3, 12]); din("hy_sconv_b", [128, DEPTH, 12])
    din("hy_filt_w1", [DEPTH, 33, 64]); din("hy_filt_b1", [64, DEPTH]); din("hy_filt_w2", [DEPTH, 64, 64])
    din("hy_filt_b2", [64, DEPTH]); din("hy_filt_w3", [DEPTH, 64, 2048]); din("hy_filt_bias", [128, DEPTH, 2, 4])
    din("w_branch", [DEPTH, 4, 512, D]); din("w_out", [DEPTH, D, D]); din("final_g", [128, 8])
    cs = consts()
    for name, arr in cs.items():
        din(name, arr.shape, BF16 if arr.dtype == ml_dtypes.bfloat16 else F32)
    out_d = nc.dram_tensor("out", [L, D], F32, kind="ExternalOutput").ap()
    dbg_d = None
    if dbg is not None:
        dbg_d = nc.dram_tensor("dbg", list(dbg), F32, kind="ExternalOutput").ap()

    G = ExitStack()
    st.enter_context(G)
    ident_f, b_identf = k.sb(G, "identf", [128, 128], F32)
    ident_b, b_identb = k.sb(G, "identb", [128, 128], BF16)
    ones_b, b_ones = k.sb(G, "onesb", [128, 128], BF16)
    bones_b, b_bones = k.sb(G, "bonesb", [128, 128], BF16)
    P.dma(ident_f[:], I["k_ident"], writes=[b_identf])
    P.dma(ident_b[:], I["k_ident"], writes=[b_identb], eng="pool")
    P.dma(ones_b[:], I["k_ones"], writes=[b_ones], eng="pool")
    P.dma(bones_b[:], I["k_bones"], writes=[b_bones], eng="pool")
    modv, b_modv = k.sb(G, "modv", [128, 72, 2], F32)
    gsv, b_gsv = k.sb(G, "gsv", [128, 3, 8, 2], F32)
    gtv, b_gtv = k.sb(G, "gtv", [128, 3, 8, 2], F32)
    smalls, b_smalls = k.sb(G, "smalls", [128, 64], F32)
    epsv, b_eps = k.sb(G, "epsv", [128, 1], F32)
    P.op("pool", lambda e: e.memset(epsv[:], EPS), [], [b_eps])

    xT, b_xT = k.dram("xT", [8, 128, NT], F32)
    b_xTt = [Buf("xT%d" % i) for i in range(len(TILES))]
    uT, _ = k.dram("uT", [8, 128, NT], BF16)
    b_uTt = [Buf("uT%d" % i) for i in range(len(TILES))]
    wg_b = [k.dram("wg_b%d" % i, [D, FF], BF16) for i in range(2)]
    wu_b = [k.dram("wu_b%d" % i, [D, FF], BF16) for i in range(2)]
    wd_b = [k.dram("wd_b%d" % i, [FF, D], BF16) for i in range(2)]
    win_b = k.dram("win_b", [D, P_IN], BF16)
    wbr_b = k.dram("wbr_b", [4, 512, D], BF16)
    wout_b = k.dram("wout_b", [D, D], BF16)

    def xT_tile_ap(t0, n):
        return xT[:, :, t0:t0 + n].rearrange("c p t -> p c t")

    def uT_tile_ap(t0, n):
        return uT[:, :, t0:t0 + n].rearrange("c p t -> p c t")

    with ExitStack() as S:
        xin = Ring(k, S, "xin", 8, [128, D], F32)
        xo = Ring(k, S, "xo", 2, [128, 8, 512], F32)
        for ti, (t0, n) in enumerate(TILES):
            ot, ob = xo.next()
            nb = n // 128
            blks = []
            for j in range(nb):
                tt, tb = xin.next()
                r0 = t0 + j * 128
                src = I["x"][r0:r0 + 128, :] if r0 < L else I["ctx"][r0 - L:r0 - L + 128, :]
                P.dma(tt[:], src, writes=[tb])
                blks.append((tt, tb))
                for fc in range(8):
                    pass
            for fc in range(8):
                ps, pb = k.tmp()
                for j in range(nb):
                    tt, tb = blks[j]
                    P.op("pe", lambda e, ps=ps, tt=tt, j=j, fc=fc: e.transpose(ps[:, j * 128:(j + 1) * 128], tt[:, fc * 128:(fc + 1) * 128], ident_f[:]),
                         [tb, b_identf], [pb])
                copy_on(P, k.ev(), ot[:, fc, :n], ps[:, :n], [pb], [ob])
            P.dma(xT_tile_ap(t0, n), ot[:, :, :n], reads=[ob], writes=[b_xTt[ti]], eng="pool")
        P.barrier()

    if stop == "s0":
        return finish(k, st, I, out_d, dbg_d, xT, b_xTt)

    def cast_dram(dst, src, rows, cols, bufs):
        step = 256
        for r in range(0, rows, step):
            rr = min(step, rows - r)
            P.dma(dst[r:r + rr, :], src[r:r + rr, :], writes=bufs, eng="pool")

    def norm_mod(S, rings, xt, xb, n, s, j, out, outb):
        sq_r, rstd_r, tmp_r = rings
        ps, pb = k.acc()
        sqs = []
        for fc in range(8):
            q, qb = sq_r.next()
            P.op("act", lambda e, q=q, fc=fc: e.activation(out=q[:, :n], in_=xt[:, fc, :n], func=AF.Square), [xb], [qb])
            sqs.append((q, qb))
            P.op("pe", lambda e, q=q, fc=fc, ps=ps: e.matmul(ps[:, :n], lhsT=ones_b[:], rhs=q[:, :n], start=(fc == 0), stop=(fc == 7)),
                 [qb, b_ones], [pb])
        rstd, rb = rstd_r.next()
        P.op("act", lambda e: e.activation(out=rstd[:, :n], in_=ps[:, :n], func=AF.Sqrt, bias=epsv[:, 0:1], scale=1.0 / D), [pb, b_eps], [rb])
        P.op("dve", lambda e: e.reciprocal(out=rstd[:, :n], in_=rstd[:, :n]), [rb], [rb])
        for fc in range(8):
            tp, tb = tmp_r.next()
            P.op("dve", lambda e, tp=tp, fc=fc: e.scalar_tensor_tensor(out=tp[:, :n], in0=xt[:, fc, :n], scalar=gsv[:, s, fc, j:j + 1],
                                                                         in1=rstd[:, :n], op0=ALU.mult, op1=ALU.mult),
                 [xb, rb, b_gsv], [tb])
            P.op("act", lambda e, tp=tp, fc=fc: e.activation(out=out[:, fc, :n], in_=tp[:, :n], func=AF.Identity,
                                                              bias=modv[:, 3 * s * 8 + fc, j:j + 1], scale=1.0),
                 [tb, b_modv], [outb])

    qA = k.dram("qA", [8, 128, NT], BF16)[0]; kA = k.dram("kA", [8, 128, NT], BF16)[0]
    vA = k.dram("vA", [34, 128, 8, 128], BF16)[0]
    qC = k.dram("qC", [4, 128, NT], BF16)[0]; kC = k.dram("kC", [4, 128, NT], BF16)[0]
    vC = k.dram("vC", [34, 128, 2, 128], BF16)[0]
    bqT = k.dram("bqT", [2, 128, NT], BF16)[0]; bkT = k.dram("bkT", [2, 128, NT], BF16)[0]
    bk_tok = k.dram("bk_tok", [NT, 256], BF16)[0]; bv_tok = k.dram("bv_tok", [NT, 512], BF16)[0]
    la_d = k.dram("la_d", [2, NT, 256], BF16)[0]
    ogT = k.dram("ogT", [4, 128, NT], BF16)[0]; dpT = k.dram("dpT", [12, 128, NT], BF16)[0]
    oT = k.dram("oT", [4, 4, 128, NT], BF16)[0]
    ofT = k.dram("ofT", [4, 128, NT], F32)[0]
    zT = k.dram("zT", [12, 128, NT], BF16)[0]
    hpm = k.dram("hpm", [2, 2, L, 512], BF16)[0]
    PQ = k.dram("PQ", [2, 2, L, 512], F32)[0]

    def do_layer(l):
        need_ctx = l < DEPTH - 1
        with ExitStack() as S:
            cct, b_cct = k.sb(S, "cct", [128, 8, 2], F32)
            P.dma(cct[:], I["cc"], writes=[b_cct])
            P.op("act", lambda e: e.activation(out=cct[:], in_=cct[:], func=AF.Silu), [b_cct], [b_cct])
            adab, b_adab = k.sb(S, "adab", [128, 72], F32)
            P.dma(adab[:], I["ada_b"][:, l, :], writes=[b_adab])
            ng, b_ng = k.sb(S, "ng", [128, 3, 8], F32)
            P.dma(ng[:], I["norm_g"][:, l, :, :], writes=[b_ng])
            awr = Ring(k, S, "adaw", 2, [128, 8, 1024], F32)
            ps, pb = k.acc()
            for i in range(9):
                wt, wb = awr.next()
                P.dma(wt[:], I["ada_w"][l, :, i * 1024:(i + 1) * 1024].rearrange("(kc p) n -> p kc n", p=128), writes=[wb])
                for fc in range(8):
                    ch = i * 8 + fc

                    def fn(e, wt=wt, fc=fc, ch=ch, ps=ps):
                        ins = None
                        for kc in range(8):
                            ins = e.matmul(ps[:, ch * 2:ch * 2 + 2], lhsT=wt[:, kc, fc * 128:(fc + 1) * 128], rhs=cct[:, kc, :],
                                           start=(kc == 0), stop=(kc == 7))
                        return ins
                    P.op("pe", fn, [wb, b_cct], [pb])
            for j in range(2):
                P.op("dve", lambda e, j=j, ps=ps: e.tensor_tensor(out=modv[:, :, j], in0=ps[:, j:144:2], in1=adab[:], op=ALU.add),
                     [pb, b_adab], [b_modv])
            for s in range(3):
                for j in range(2):
                    P.op("dve", lambda e, s=s, j=j: e.scalar_tensor_tensor(out=gsv[:, s, :, j], in0=modv[:, (3 * s + 1) * 8:(3 * s + 2) * 8, j], scalar=1.0,
                                                                             in1=ng[:, s, :], op0=ALU.add, op1=ALU.mult),
                         [b_modv, b_ng], [b_gsv])
                    P.op("dve", lambda e, s=s, j=j: e.tensor_scalar(out=gtv[:, s, :, j], in0=modv[:, (3 * s + 2) * 8:(3 * s + 3) * 8, j],
                                                                      scalar1=(1.0 if s == 1 else 0.5), scalar2=None, op0=ALU.mult),
                         [b_modv], [b_gtv])
            P.barrier()

        if stop == "ada" and l == 0:
            return finish_dbg(k, st, I, out_d, dbg_d, [(modv, b_modv, [128, 144])])

        for i in range(2):
            cast_dram(wg_b[i][0], I["ffn_w_gate"][l, i], D, FF, [wg_b[i][1]])
            cast_dram(wu_b[i][0], I["ffn_w_up"][l, i], D, FF, [wu_b[i][1]])
            cast_dram(wd_b[i][0], I["ffn_w_down"][l, i], FF, D, [wd_b[i][1]])
        cast_dram(win_b[0], I["w_in"][l], D, P_IN, [win_b[1]])
        cast_dram(wbr_b[0].rearrange("a k n -> (a k) n"), I["w_branch"][l].rearrange("a k n -> (a k) n"), 2048, D, [wbr_b[1]])
        cast_dram(wout_b[0], I["w_out"][l], D, D, [wout_b[1]])

        def ffn_tile(S, rg, xt, xb, xn, xnb, n, fi, s, j):
            wgr, wur, wdr, sgr, hT, hb = rg
            for fp in range(6):
                c0 = fp * 512
                pc = min(512, FF - c0)
                wgt, wgb = wgr.next()
                wut, wub = wur.next()
                P.dma(wgt[:, :, :pc], wg_b[fi][0].rearrange("(kc p) f -> p kc f", p=128)[:, :, c0:c0 + pc], reads=[wg_b[fi][1]], writes=[wgb])
                P.dma(wut[:, :, :pc], wu_b[fi][0].rearrange("(kc p) f -> p kc f", p=128)[:, :, c0:c0 + pc], reads=[wu_b[fi][1]], writes=[wub])
                for jj in range(pc // 128):
                    f = fp * 4 + jj
                    psg, pgb = k.tmp()
                    psu, pub = k.tmp()
                    mm_group(P, psg[:, :n], pgb, [(wgt[:, kc, jj * 128:(jj + 1) * 128], xn[:, kc, :n]) for kc in range(8)], [wgb, xnb])
                    mm_group(P, psu[:, :n], pub, [(wut[:, kc, jj * 128:(jj + 1) * 128], xn[:, kc, :n]) for kc in range(8)], [wub, xnb])
                    sg, sgb = sgr.next()
                    P.op("act", lambda e, sg=sg, psg=psg: e.activation(out=sg[:, :n], in_=psg[:, :n], func=AF.Silu), [pgb], [sgb])
                    P.op("dve", lambda e, sg=sg, psu=psu, f=f: e.tensor_tensor(out=hT[:, f, :n], in0=sg[:, :n], in1=psu[:, :n], op=ALU.mult),
                         [sgb, pub], [hb])
            for mp in range(2):
                wdt, wdb = wdr.next()
                P.dma(wdt[:], wd_b[fi][0].rearrange("(kc p) m -> p kc m", p=128)[:, :, mp * 512:(mp + 1) * 512], reads=[wd_b[fi][1]], writes=[wdb])
                for jj in range(4):
                    m = mp * 4 + jj
                    ps, pb = k.tmp()
                    mm_group(P, ps[:, :n], pb, [(wdt[:, kc, jj * 128:(jj + 1) * 128], hT[:, kc, :n]) for kc in range(NFC)], [wdb, hb])
                    P.op("dve", lambda e, ps=ps, m=m: e.scalar_tensor_tensor(out=xt[:, m, :n], in0=ps[:, :n], scalar=gtv[:, s, m, j:j + 1],
                                                                               in1=xt[:, m, :n], op0=ALU.mult, op1=ALU.add),
                         [pb, b_gtv, xb], [xb])

        def ffn_rings(S):
            wgr = Ring(k, S, "wg", 2, [128, 8, 512], BF16)
            wur = Ring(k, S, "wu", 2, [128, 8, 512], BF16)
            wdr = Ring(k, S, "wd", 2, [128, NFC, 512], BF16)
            sgr = Ring(k, S, "sg", 3, [128, 512], F32)
            hT, hb = k.sb(S, "hT", [128, NFC, 512], BF16)
            return (wgr, wur, wdr, sgr, hT, hb)

        with ExitStack() as S:
            rg = ffn_rings(S)
            nmr = (Ring(k, S, "sq", 3, [128, 512], BF16), Ring(k, S, "rstd", 2, [128, 512], F32), Ring(k, S, "nmtmp", 2, [128, 512], F32))
            xr = Ring(k, S, "xt", 2, [128, 8, 512], F32)
            xnr = Ring(k, S, "xn", 2, [128, 8, 512], BF16)
            for ti, (t0, n) in enumerate(TILES):
                j = 0 if t0 < L else 1
                xt, xb = xr.next()
                xn, xnb = xnr.next()
                P.dma(xt[:, :, :n], xT_tile_ap(t0, n), reads=[b_xTt[ti]], writes=[xb])
                norm_mod(S, nmr, xt, xb, n, 0, j, xn, xnb)
                ffn_tile(S, rg, xt, xb, xn, xnb, n, 0, 0, j)
                P.dma(xT_tile_ap(t0, n), xt[:, :, :n], reads=[xb], writes=[b_xTt[ti]], eng="pool")
                un, unb = xnr.next()
                norm_mod(S, nmr, xt, xb, n, 1, j, un, unb)
                P.dma(uT_tile_ap(t0, n), un[:, :, :n], reads=[unb], writes=[b_uTt[ti]], eng="pool")
            P.barrier()

        if stop == "ffn1" and l == 0:
            return finish(k, st, I, out_d, dbg_d, xT, b_xTt)

        winv = win_b[0].rearrange("(kc p) n -> p kc n", p=128)
        with ExitStack() as S:
            NB = lambda: Buf("x")
            WuqP, b_WuqP = k.sb(S, "WuqP", [128, 2, 8, 128], BF16)
            WukP, b_WukP = k.sb(S, "WukP", [128, 8, 128], BF16)
            WukV, b_WukV = k.sb(S, "WukV", [128, 8, 64], BF16)
            WkrP, b_WkrP = k.sb(S, "WkrP", [128, 8, 128], BF16)
            WckP, b_WckP = k.sb(S, "WckP", [128, 8, 4, 128], BF16)
            wgate, b_wgate = k.sb(S, "wgate", [16, 2, 256], BF16)
            biasb, b_biasb = k.sb(S, "biasb", [128, 512], F32)
            pv, b_pv = k.sb(S, "pv", [128, 8], F32)
            rtM, b_rtM = k.sb(S, "rtM", [128, 128], BF16)
            rtG, b_rtG = k.sb(S, "rtG", [128, 128], BF16)
            P.dma(rtM[:], I["k_rtM"], writes=[b_rtM], eng="pool")
            P.dma(rtG[:], I["k_rtG"], writes=[b_rtG], eng="pool")
            P.op("dve", lambda e: e.memset(WuqP[:], 0.0), [], [b_WuqP])
            P.op("dve", lambda e: e.memset(WukP[:], 0.0), [], [b_WukP])
            P.op("dve", lambda e: e.memset(WkrP[:], 0.0), [], [b_WkrP])
            P.op("dve", lambda e: e.memset(WckP[:], 0.0), [], [b_WckP])
            for kc in range(2):
                if 'B' not in SKIP:
                    P.dma(WuqP[:, kc, :, 0:96], I["mla_w_uq"][l, kc * 128:(kc + 1) * 128, :].rearrange("p (h c) -> p h c", c=96), writes=[b_WuqP], eng="pool")
            ukv = I["mla_w_ukv"][l].rearrange("p (h c) -> p h c", c=128)
            if 'B' not in SKIP:
                P.dma(WukP[:, :, 0:64], ukv[:, :, 0:64], writes=[b_WukP], eng="pool")
            if 'B' not in SKIP:
                P.dma(WukV[:], ukv[:, :, 64:128], writes=[b_WukV], eng="pool")
            if 'D' not in SKIP:
                P.dma(WkrP[:, :, 64:96], winv[:, :, O_KR:O_KR + 32], reads=[win_b[1]], writes=[b_WkrP])
            for g in range(2):
                for par in range(2):
                    if 'D' not in SKIP:
                        P.dma(WckP[:, :, g * 2 + par, par * 64:par * 64 + 64], winv[:, :, O_CK + g * 64:O_CK + g * 64 + 64], reads=[win_b[1]], writes=[b_WckP])
            if 'C' not in SKIP:
                P.dma(wgate[:], I["gla_w_gate"][l].rearrange("d r c -> r d c"), writes=[b_wgate], eng="pool")
            if 'A' not in SKIP:
                P.dma(biasb[:], I["gla_b_gate"][l].rearrange("d c -> (d c)").partition_broadcast(128), writes=[b_biasb])
            if 'E' not in SKIP:
                P.dma(pv[:, 0:2], I["mla_q_norm_g"][:, l, :], writes=[b_pv])
            if 'E' not in SKIP:
                P.dma(pv[:, 2:3], I["mla_kv_norm_g"][:, l:l + 1], writes=[b_pv])
            if 'E' not in SKIP:
                P.dma(pv[:, 3:4], I["gqa_q_norm_g"][:, l:l + 1], writes=[b_pv])
            if 'E' not in SKIP:
                P.dma(pv[:, 4:5], I["gqa_k_norm_g"][:, l:l + 1], writes=[b_pv])
            ur = Ring(k, S, "u", 2, [128, 8, 512], BF16)
            wr = Ring(k, S, "wp", 3, [128, 8, 512], BF16)
            tabr = Ring(k, S, "tab", 2, [128, 4, 512], F32)
            f32r = Ring(k, S, "f32r", RING_F32, [128, 512], F32)
            bfr = Ring(k, S, "bfr", RING_BF, [128, 512], BF16)
            cqr = Ring(k, S, "cq", 2, [128, 2, 512], F32)
            cqnr = Ring(k, S, "cqn", 2, [128, 2, 512], BF16)
            outr = Ring(k, S, "outr", RING_OUT, [128, 512], BF16)
            krr_r = Ring(k, S, "krr", 2, [128, 512], BF16)
            ckvn_r = Ring(k, S, "ckvn", 2, [128, 512], BF16)
            vaA = Ring(k, S, "vaA", 2, [128, 8, 128], BF16)
            vaC = Ring(k, S, "vaC", 2, [128, 2, 128], BF16)
            for (t_, b_) in zip(vaA.t + vaC.t, vaA.b + vaC.b):
                P.op("dve", lambda e, t_=t_: e.memset(t_[:], 1.0), [], [b_])
            lrr = Ring(k, S, "lrT", 4, [16, 512], BF16)

            def wcols(c0, width):
                wt, wb = wr.next()
                P.dma(wt[:, :, :width], winv[:, :, c0:c0 + width], reads=[win_b[1]], writes=[wb])
                return wt, wb

            def store(dst, src, sb, eng="pool"):
                P.dma(dst, src, reads=[sb], writes=[NB()], eng=eng)

            def do_tile(ti, t0, n):
                u, ub = ur.next()
                P.dma(u[:, :, :n], uT_tile_ap(t0, n), reads=[b_uTt[ti]], writes=[ub])
                tab, tabb = tabr.next()
                for i_, nm in enumerate(("k_cosM", "k_sinM", "k_cosG", "k_sinG")):
                    P.dma(tab[:, i_, :n], I[nm][:, t0:t0 + n], writes=[tabb])
                nchunk = n // 128

                def proj(wt, wb, c0, M):
                    ps, pb = k.tmp()
                    mm_group(P, ps[:M, :n], pb, [(wt[:, kc, c0:c0 + M], u[:, kc, :n]) for kc in range(8)], [wb, ub])
                    return ps, pb

                def rms_rstd(srcs, nfeat, lhsT, lb):
                    pss, pssb = k.acc()
                    for i_, (a, ab) in enumerate(srcs):
                        sq, sqb = bfr.next()
                        P.op("act", lambda e, sq=sq, a=a: e.activation(out=sq[:, :n], in_=a, func=AF.Square), [ab], [sqb])
                        P.op("pe", lambda e, sq=sq, i_=i_, pss=pss: e.matmul(pss[:, :n], lhsT=lhsT[:], rhs=sq[:, :n], start=(i_ == 0), stop=(i_ == len(srcs) - 1)),
                             [sqb, lb], [pssb])
                    r, rb = f32r.next()
                    P.op("act", lambda e: e.activation(out=r[:, :n], in_=pss[:, :n], func=AF.Sqrt, bias=epsv[:, 0:1], scale=1.0 / nfeat), [pssb, b_eps], [rb])
                    P.op("dve", lambda e: e.reciprocal(out=r[:, :n], in_=r[:, :n]), [rb], [rb])
                    return r, rb

                def rope(src, srcb, ci, rt, rtb, dst):
                    if dst is not None or True:
                        pc_, pcb_ = f32r.next()
                        copy_on(P, "dve", pc_[:, :n], src, [srcb], [pcb_])
                        src, srcb = pc_[:, :n], pcb_
                    qb_, qbb = bfr.next()
                    copy_on(P, ROPE_COPY_ENG, qb_[:, :n], src, [srcb], [qbb])
                    psr, psrb = k.tmp()
                    P.op("pe", lambda e: e.matmul(psr[:, :n], lhsT=rt[:], rhs=qb_[:, :n], start=True, stop=True), [qbb, rtb], [psrb])
                    t1, t1b = f32r.next()
                    t2, t2b = f32r.next()
                    P.op("dve", lambda e: e.tensor_tensor(out=t1[:, :n], in0=src, in1=tab[:, ci, :n], op=ALU.mult), [srcb, tabb], [t1b])
                    P.op("dve", lambda e: e.tensor_tensor(out=t2[:, :n], in0=psr[:, :n], in1=tab[:, ci + 1, :n], op=ALU.mult), [psrb, tabb], [t2b])
                    o_, ob_ = (dst or outr).next()
                    P.op("dve", lambda e: e.tensor_tensor(out=o_[:, :n], in0=t1[:, :n], in1=t2[:, :n], op=ALU.add), [t1b, t2b], [ob_])
                    return o_, ob_

                if 'mla' not in PROJ_SECTIONS:
                    return
                W, Wb = wcols(0, 416)
                cq, cqb = cqr.next()
                for c in range(2):
                    ps, pb = proj(W, Wb, c * 128, 128)
                    copy_on(P, "dve", cq[:, c, :n], ps[:, :n], [pb], [cqb])
                if MLA_STEP <= 1:
                    return
                r, rb = rms_rstd([(cq[:, 0, :n], cqb), (cq[:, 1, :n], cqb)], 256, ones_b, b_ones)
                cqn, cqnb = cqnr.next()
                for c in range(2):
                    P.op("dve", lambda e, c=c: e.scalar_tensor_tensor(out=cqn[:, c, :n], in0=cq[:, c, :n], scalar=pv[:, c:c + 1], in1=r[:, :n], op0=ALU.mult, op1=ALU.mult),
                         [cqb, rb, b_pv], [cqnb])
                if MLA_STEP <= 2:
                    return
                ps, pb = proj(W, Wb, 256, 128)
                ckv, ckvb = f32r.next()
                copy_on(P, "dve", ckv[:, :n], ps[:, :n], [pb], [ckvb])
                r2, r2b = rms_rstd([(ckv[:, :n], ckvb)], 128, ones_b, b_ones)
                ckvn, ckvnb = ckvn_r.next()
                P.op("dve", lambda e: e.scalar_tensor_tensor(out=ckvn[:, :n], in0=ckv[:, :n], scalar=pv[:, 2:3], in1=r2[:, :n], op0=ALU.mult, op1=ALU.mult),
                     [ckvb, r2b, b_pv], [ckvnb])
                if MLA_STEP <= 3:
                    return
                pskr, pskrb = k.tmp()
                mm_group(P, pskr[:, :n], pskrb, [(WkrP[:, kc, :], u[:, kc, :n]) for kc in range(8)], [b_WkrP, ub])
                krr, krrb = rope(pskr[:, :n], pskrb, 0, rtM, b_rtM, krr_r)
                if MLA_STEP <= 4:
                    return
                for h in range(NHEADS_DBG):
                    if 'Q' not in SKIP:
                        psq, psqb = k.tmp()
                        mm_group(P, psq[:, :n], psqb, [(WuqP[:, kc, h, :], cqn[:, kc, :n]) for kc in range(2)], [b_WuqP, cqnb])
                        if 'R' in SKIP:
                            qo, qob = outr.next()
                            copy_on(P, "act", qo[:, :n], psq[:, :n], [psqb], [qob])
                        else:
                            qo, qob = rope(psq[:, :n], psqb, 0, rtM, b_rtM, None)
                        if 'S' not in SKIP:
                            store(qA[h, :, t0:t0 + n], qo[:, :n], qob)
                    if 'K' not in SKIP:
                        psk, pskb = k.tmp()
                        mm_group(P, psk[:, :n], pskb, [(WukP[:, h, :], ckvn[:, :n]), (ident_b[:], krr[:, :n])], [b_WukP, ckvnb, b_identb, krrb])
                        ko, kob = outr.next()
                        copy_on(P, "act", ko[:, :n], psk[:, :n], [pskb], [kob])
                        if 'S' not in SKIP:
                            store(kA[h, :, t0:t0 + n], ko[:, :n], kob)
                if MLA_STEP <= 5:
                    return
                for jc in range(nchunk):
                    psv, psvb = k.tmp()
                    P.op("pe", lambda e, jc=jc, psv=psv: e.matmul(psv[:, :], lhsT=ckvn[:, jc * 128:(jc + 1) * 128], rhs=WukV[:].rearrange("p h d -> p (h d)"), start=True, stop=True),
                         [ckvnb, b_WukV], [psvb])
                    va, vab = vaA.next()
                    copy_on(P, "dve", va[:, :, 0:64], psv[:, :].rearrange("p (h d) -> p h d", d=64), [psvb], [vab])
                    store(vA[t0 // 128 + jc], va[:], vab)

                if 'gqa' not in PROJ_SECTIONS:
                    return
                def normrope(ps, pb, gcol):
                    pc, pcb = f32r.next()
                    copy_on(P, "dve", pc[:, :n], ps[:, :n], [pb], [pcb])
                    r_, rb_ = rms_rstd([(pc[:, :n], pcb)], 64, bones_b, b_bones)
                    P.op("dve", lambda e: e.scalar_tensor_tensor(out=pc[:, :n], in0=pc[:, :n], scalar=pv[:, gcol:gcol + 1], in1=r_[:, :n], op0=ALU.mult, op1=ALU.mult),
                         [pcb, rb_, b_pv], [pcb])
                    return rope(pc[:, :n], pcb, 2, rtG, b_rtG, None)

                W, Wb = wcols(O_CQQ, 512)
                for c in range(4):
                    ps, pb = proj(W, Wb, c * 128, 128)
                    qo, qob = normrope(ps, pb, 3)
                    store(qC[c, :, t0:t0 + n], qo[:, :n], qob)
                for kt in range(4):
                    ps, pb = k.tmp()
                    mm_group(P, ps[:, :n], pb, [(WckP[:, kc, kt, :], u[:, kc, :n]) for kc in range(8)], [b_WckP, ub])
                    ko, kob = normrope(ps, pb, 4)
                    store(kC[kt, :, t0:t0 + n], ko[:, :n], kob)
                W, Wb = wcols(O_CV, 128)
                for jc in range(nchunk):
                    psv, psvb = k.tmp()
                    mm_group(P, psv[:, :128], psvb, [(u[:, kc, jc * 128:(jc + 1) * 128], W[:, kc, 0:128]) for kc in range(8)], [ub, Wb])
                    va, vab = vaC.next()
                    copy_on(P, "dve", va[:, :, 0:64], psv[:, 0:128].rearrange("p (h d) -> p h d", d=64), [psvb], [vab])
                    store(vC[t0 // 128 + jc], va[:], vab)

                if 'gla' not in PROJ_SECTIONS:
                    return
                W, Wb = wcols(O_BQ, 512)
                for c in range(2):
                    ps, pb = proj(W, Wb, c * 128, 128)
                    o_, ob_ = outr.next()
                    P.op("act", lambda e, o_=o_, ps=ps: e.activation(out=o_[:, :n], in_=ps[:, :n], func=AF.Copy, scale=0.125), [pb], [ob_])
                    store(bqT[c, :, t0:t0 + n], o_[:, :n], ob_)
                    ps, pb = proj(W, Wb, 256 + c * 128, 128)
                    o_, ob_ = outr.next()
                    copy_on(P, "dve", o_[:, :n], ps[:, :n], [pb], [ob_])
                    store(bkT[c, :, t0:t0 + n], o_[:, :n], ob_)
                for jc in range(nchunk):
                    ps, pb = k.tmp()
                    mm_group(P, ps[:, :256], pb, [(u[:, kc, jc * 128:(jc + 1) * 128], W[:, kc, 256:512]) for kc in range(8)], [ub, Wb])
                    o_, ob_ = outr.next()
                    copy_on(P, "act", o_[:, :256], ps[:, :256], [pb], [ob_])
                    store(bk_tok[t0 + jc * 128:t0 + (jc + 1) * 128, :], o_[:, :256], ob_)
                W, Wb = wcols(O_BV, 512)
                for jc in range(nchunk):
                    ps, pb = k.tmp()
                    mm_group(P, ps[:, :], pb, [(u[:, kc, jc * 128:(jc + 1) * 128], W[:, kc, 0:512]) for kc in range(8)], [ub, Wb])
                    o_, ob_ = outr.next()
                    copy_on(P, "dve", o_[:, :], ps[:, :], [pb], [ob_])
                    store(bv_tok[t0 + jc * 128:t0 + (jc + 1) * 128, :], o_[:, :], ob_)
                W, Wb = wcols(O_LR, 32)
                for d in range(2):
                    ps, pb = k.tmp()
                    mm_group(P, ps[:16, :n], pb, [(W[:, kc, d * 16:(d + 1) * 16], u[:, kc, :n]) for kc in range(8)], [Wb, ub])
                    lr, lrb = lrr.next()
                    copy_on(P, "act", lr[:, :n], ps[:16, :n], [pb], [lrb])
                    for jc in range(nchunk):
                        ps2, pb2 = k.tmp()
                        P.op("pe", lambda e, ps2=ps2, lr=lr, jc=jc, d=d: e.matmul(ps2[:, :256], lhsT=lr[:, jc * 128:(jc + 1) * 128], rhs=wgate[:, d, :], start=True, stop=True),
                             [lrb, b_wgate], [pb2])
                        t1, t1b = f32r.next()
                        P.op("dve", lambda e, t1=t1, ps2=ps2, d=d: e.tensor_tensor(out=t1[:, :256], in0=ps2[:, :256], in1=biasb[:, d * 256:(d + 1) * 256], op=ALU.add), [pb2, b_biasb], [t1b])
                        P.op("act", lambda e, t1=t1: e.activation(out=t1[:, :256], in_=t1[:, :256], func=AF.Exp, scale=-1.0), [t1b], [t1b])
                        P.op("act", lambda e, t1=t1: e.activation(out=t1[:, :256], in_=t1[:, :256], func=AF.Ln, bias=1.0, scale=1.0), [t1b], [t1b])
                        o_, ob_ = outr.next()
                        P.op("dve", lambda e, t1=t1, o_=o_: e.tensor_scalar(out=o_[:, :256], in0=t1[:, :256], scalar1=-1.0 / 16.0, scalar2=None, op0=ALU.mult), [t1b], [ob_])
                        store(la_d[d, t0 + jc * 128:t0 + (jc + 1) * 128, :], o_[:, :256], ob_)
                W, Wb = wcols(O_OG, 512)
                for c in range(4):
                    ps, pb = proj(W, Wb, c * 128, 128)
                    o_, ob_ = outr.next()
                    P.op("act", lambda e, o_=o_, ps=ps: e.activation(out=o_[:, :n], in_=ps[:, :n], func=AF.Silu), [pb], [ob_])
                    store(ogT[c, :, t0:t0 + n], o_[:, :n], ob_)
                if 'hy' not in PROJ_SECTIONS:
                    return
                for pp in range(3):
                    W, Wb = wcols(O_DP + pp * 512, 512)
                    for c in range(4):
                        ps, pb = proj(W, Wb, c * 128, 128)
                        o_, ob_ = outr.next()
                        copy_on(P, k.ev(), o_[:, :n], ps[:, :n], [pb], [ob_])
                        store(dpT[pp * 4 + c, :, t0:t0 + n], o_[:, :n], ob_)

            for ti, (t0, n) in enumerate(TILES[:PROJ_TILES]):
                do_tile(ti, t0, n)
                P.barrier()
            P.barrier()

        if stop == "proj" and l == 0:
            return finish_list(k, st, dbg_d, dbg_items(locals()))

        with ExitStack() as S:
            NB = lambda: Buf("x")
            Kr = Ring(k, S, "Kr", 2, [128, NT], BF16)
            Vr = Ring(k, S, "Vr", 2, [128, 34, 128], BF16)
            qr = Ring(k, S, "qr", 3, [128, 512], BF16)
            pr = Ring(k, S, "pr", 4, [128, 512], BF16)
            rcr = Ring(k, S, "rcr", 2, [128, 512], F32)
            orr = Ring(k, S, "orr", 3, [128, 512], BF16)
            for br in (() if SKIP_ATTN else (0, 2)):
                scale = (96.0 ** -0.5) if br == 0 else 0.125
                for h in range(8):
                    Kt, Kb = Kr.next()
                    Vt, Vb = Vr.next()
                    if br == 0:
                        ksrc = kA[h]; vsrc = vA[:, :, h, :]; qsrc = qA[h]
                    else:
                        g = h // 4
                        ksrc = kC[g * 2 + (h % 2)]; vsrc = vC[:, :, g, :]; qsrc = qC[h // 2]
                    P.dma(Kt[:], ksrc, writes=[Kb])
                    P.dma(Vt[:], vsrc.rearrange("c p d -> p c d"), writes=[Vb])
                    for ti, (t0, n) in enumerate(TILES):
                        if t0 >= L:
                            if not need_ctx:
                                continue
                            kcs = list(range(32, 34))
                        else:
                            kcs = list(range(34))
                        qt, qb_ = qr.next()
                        P.dma(qt[:, :n], qsrc[:, t0:t0 + n], writes=[qb_])
                        psO, psOb = k.acc()
                        for i_, kc in enumerate(kcs):
                            psS, psSb = k.tmp()
                            P.op("pe", lambda e, psS=psS, Kt=Kt, kc=kc, qt=qt, n=n: e.matmul(psS[:, :n], lhsT=Kt[:, kc * 128:(kc + 1) * 128], rhs=qt[:, :n], start=True, stop=True),
                                 [Kb, qb_], [psSb])
                            pt, ptb = pr.next()
                            P.op("act", lambda e, pt=pt, psS=psS, n=n, scale=scale: e.activation(out=pt[:, :n], in_=psS[:, :n], func=AF.Exp, scale=scale), [psSb], [ptb])
                            P.op("pe", lambda e, psO=psO, Vt=Vt, kc=kc, pt=pt, n=n, i_=i_, kcs=kcs: e.matmul(psO[:, :n], lhsT=Vt[:, kc, :], rhs=pt[:, :n], start=(i_ == 0), stop=(i_ == len(kcs) - 1)),
                                 [Vb, ptb], [psOb])
                        rc, rcb = rcr.next()
                        P.op("dve", lambda e, rc=rc, psO=psO, n=n: e.reciprocal(out=rc[64:128, :n], in_=psO[64:128, :n]), [psOb], [rcb])
                        ot, otb = orr.next()
                        P.op("dve", lambda e, ot=ot, psO=psO, rc=rc, n=n: e.tensor_tensor(out=ot[0:64, :n], in0=psO[0:64, :n], in1=rc[64:128, :n], op=ALU.mult), [psOb, rcb], [otb])
                        P.dma(oT[br, h // 2, (h % 2) * 64:(h % 2) * 64 + 64, t0:t0 + n], ot[0:64, :n], reads=[otb], writes=[NB()], eng="pool")
            P.barrier()

        if stop == "attn" and l == 0:
            return finish_list(k, st, dbg_d, dbg_items(locals()))

        with ExitStack() as S:
            NB = lambda: Buf("x")
            msk, b_msk = k.sb(S, "msk", [64, 4, 64], F32)
            msk4, b_msk4 = k.sb(S, "msk4", [64, 2, 4, 64], F32)
            hmask, b_hmask = k.sb(S, "hmask", [128, 2], F32)
            P.op("dve", lambda e: e.memset(hmask[:], 0.0), [], [b_hmask])
            P.op("dve", lambda e: e.memset(hmask[0:64, 0:1], 1.0), [b_hmask], [b_hmask])
            P.op("dve", lambda e: e.memset(hmask[64:128, 1:2], 1.0), [b_hmask], [b_hmask])
            for m_ in range(2):
                for h_ in range(4):
                    P.dma(msk4[:, m_, h_, :], I["k_m_le" if m_ == 0 else "k_m_ge"], writes=[b_msk4])
            mskb, b_mskb = k.sb(S, "mskb", [64, 4, 64], BF16)
            for i_, nm in enumerate(("k_m_le", "k_m_ge", "k_m_gt", "k_m_lt")):
                P.dma(msk[:, i_, :], I[nm], writes=[b_msk])
                P.dma(mskb[:, i_, :], I[nm], writes=[b_mskb], eng="pool")
            gng, b_gng = k.sb(S, "gng", [128, 1], F32)
            P.dma(gng[:], I["gla_norm_g"][:, l:l + 1], writes=[b_gng])
            Sst = [k.sb(S, "Sst%d" % i, [128, 128], F32) for i in range(2)]
            Sbf = [k.sb(S, "Sbf%d" % i, [128, 128], BF16) for i in range(2)]
            lar = Ring(k, S, "la", 2, [64, 8, 256], BF16)
            bktr = Ring(k, S, "bkt", 2, [64, 8, 256], BF16)
            bvr = Ring(k, S, "bv", 2, [64, 8, 512], BF16)
            bqr = Ring(k, S, "bq", 4, [128, 512], BF16)
            bkr = Ring(k, S, "bk", 4, [128, 512], BF16)
            Er = Ring(k, S, "E", 8, [128, 64], F32)
            qdr = Ring(k, S, "qd", 12, [128, 64], BF16)
            Abr = Ring(k, S, "Ab", 3, [64, 256], BF16)
            ker = Ring(k, S, "ke", 3, [64, 256], F32)
            kebr = Ring(k, S, "keb", 3, [64, 256], BF16)
            ogr = Ring(k, S, "og", 2, [128, 4, 512], F32)
            sqr = Ring(k, S, "gsq", 3, [128, 512], BF16)
            rsr = Ring(k, S, "grs", 2, [128, 512], F32)
            ogtr = Ring(k, S, "ogt", 2, [128, 512], BF16)
            outr2 = Ring(k, S, "gout", 3, [128, 512], BF16)
            tmpf = Ring(k, S, "gtmp", 2, [128, 512], F32)

            def gla_chunk(dirn, la, lab, bkt, bktb, bv, bvb, bq, bk, ci, og, ogb):
                mi = 0 if dirn == 0 else 1
                mk = 2 if dirn == 0 else 3
                last = 63 if dirn == 0 else 0
                qd = []
                decs = []
                for pr in range(2):
                    psb, psbb = k.tmp()
                    P.op("pe", lambda e, psb=psb, pr=pr: e.matmul(psb[:, :64], lhsT=la[:, ci, pr * 128:(pr + 1) * 128], rhs=mskb[:, mi, :], start=True, stop=True),
                         [lab, b_mskb], [psbb])
                    E, Eb = Er.next()
                    Ei, Eib = Er.next()
                    P.op("act", lambda e, E=E, psb=psb: e.activation(out=E[:, :], in_=psb[:, :64], func=AF.Exp), [psbb], [Eb])
                    P.op("act", lambda e, Ei=Ei, psb=psb: e.activation(out=Ei[:, :], in_=psb[:, :64], func=AF.Exp, scale=-1.0), [psbb], [Eib])
                    qt0, qt0b = qdr.next()
                    qt1, qt1b = qdr.next()
                    kt, ktb = qdr.next()
                    bqt, bqb = bq[pr]
                    bkt_, bkb_ = bk[pr]
                    for hh_, (qt_, qtb_) in enumerate(((qt0, qt0b), (qt1, qt1b))):
                        P.op("dve", lambda e, qt_=qt_, bqt=bqt, E=E, hh_=hh_: e.scalar_tensor_tensor(out=qt_[:, :], in0=bqt[:, ci * 64:(ci + 1) * 64], scalar=hmask[:, hh_:hh_ + 1], in1=E[:, :],
                                                                                                     op0=ALU.mult, op1=ALU.mult), [bqb, Eb, b_hmask], [qtb_])
                    P.op("dve", lambda e, kt=kt, bkt_=bkt_, Ei=Ei: e.tensor_tensor(out=kt[:, :], in0=bkt_[:, ci * 64:(ci + 1) * 64], in1=Ei[:, :], op=ALU.mult), [bkb_, Eib], [ktb])
                    qd.append(((qt0, qt0b), (qt1, qt1b), kt, ktb))
                    decs.append((E, Eb))
                psA, psAb = k.tmp()
                for h in range(4):
                    q0_, q1_, kt, ktb = qd[h // 2]
                    qt, qtb = (q0_, q1_)[h % 2]
                    P.op("pe", lambda e, h=h, qt=qt, kt=kt: e.matmul(psA[:64, h * 64:(h + 1) * 64], lhsT=kt[:, :], rhs=qt[:, :], start=True, stop=True),
                         [qtb, ktb], [psAb])
                Ab, Abb = Abr.next()
                P.op("dve", lambda e: e.tensor_tensor(out=Ab[:, :], in0=psA[:64, :256], in1=msk4[:, mi, :, :].rearrange("p h i -> p (h i)"), op=ALU.mult), [psAb, b_msk4], [Abb])
                pso, psob = k.tmp()
                for h in range(4):
                    q0_, q1_, kt, ktb = qd[h // 2]
                    qt, qtb = (q0_, q1_)[h % 2]
                    Sb_, Sbb_ = Sbf[h // 2]

                    def fn(e, h=h, qt=qt, Sb_=Sb_):
                        e.matmul(pso[:, h * 64:(h + 1) * 64], lhsT=bv[:, ci, h * 128:(h + 1) * 128], rhs=Ab[:, h * 64:(h + 1) * 64], start=True, stop=False)
                        return e.matmul(pso[:, h * 64:(h + 1) * 64], lhsT=Sb_[:, :], rhs=qt[:, :], start=False, stop=True)
                    P.op("pe", fn, [bvb, Abb, Sbb_, qtb], [psob])
                ogv = og[:, :, ci * 64:(ci + 1) * 64]
                if dirn == 0:
                    P.op("act", lambda e: e.copy(out=ogv, in_=pso[:, :256].rearrange("p (h i) -> p h i", h=4)), [psob], [ogb])
                else:
                    P.op("dve", lambda e: e.tensor_tensor(out=ogv, in0=ogv, in1=pso[:, :256].rearrange("p (h i) -> p h i", h=4), op=ALU.add), [psob, ogb], [ogb])
                psR, psRb = k.tmp()
                P.op("pe", lambda e: e.matmul(psR[:64, :256], lhsT=mskb[:, mk, :], rhs=la[:, ci, :], start=True, stop=True), [lab, b_mskb], [psRb])
                ke, keb = ker.next()
                P.op("act", lambda e: e.activation(out=ke[:, :], in_=psR[:64, :256], func=AF.Exp), [psRb], [keb])
                kb_, kbb_ = kebr.next()
                P.op("dve", lambda e: e.tensor_tensor(out=kb_[:, :], in0=ke[:, :], in1=bkt[:, ci, :], op=ALU.mult), [keb, bktb], [kbb_])
                for pr in range(2):
                    E, Eb = decs[pr]
                    St, Stb = Sst[pr]
                    Sb_, Sbb_ = Sbf[pr]
                    for hh in range(2):
                        h = pr * 2 + hh
                        psU, psUb = k.tmp()
                        P.op("pe", lambda e, psU=psU, pr=pr, h=h: e.matmul(psU[:, :128], lhsT=kb_[:, pr * 128:(pr + 1) * 128], rhs=bv[:, ci, h * 128:(h + 1) * 128], start=True, stop=True),
                             [kbb_, bvb], [psUb])
                        r0 = hh * 64
                        P.op("dve", lambda e, psU=psU, r0=r0, St=St, E=E: e.scalar_tensor_tensor(out=St[r0:r0 + 64, :], in0=St[r0:r0 + 64, :], scalar=E[r0:r0 + 64, last:last + 1],
                                                                                                   in1=psU[r0:r0 + 64, :128], op0=ALU.mult, op1=ALU.add),
                             [psUb, Stb, Eb], [Stb])
                    P.op("act", lambda e, St=St, Sb_=Sb_: e.copy(out=Sb_[:, :], in_=St[:, :]), [Stb], [Sbb_])

            def finalize(og, ogb, t0, n):
                for h in range(4):
                    sq, sqb = sqr.next()
                    P.op("act", lambda e, sq=sq, h=h: e.activation(out=sq[:, :n], in_=og[:, h, :n], func=AF.Square), [ogb], [sqb])
                    pss, pssb = k.acc()
                    P.op("pe", lambda e, sq=sq, pss=pss: e.matmul(pss[:, :n], lhsT=ones_b[:], rhs=sq[:, :n], start=True, stop=True), [sqb, b_ones], [pssb])
                    rs, rsb = rsr.next()
                    P.op("act", lambda e, rs=rs, pss=pss: e.activation(out=rs[:, :n], in_=pss[:, :n], func=AF.Sqrt, bias=epsv[:, 0:1], scale=1.0 / 128), [pssb, b_eps], [rsb])
                    P.op("dve", lambda e, rs=rs: e.reciprocal(out=rs[:, :n], in_=rs[:, :n]), [rsb], [rsb])
                    ogt, ogtb = ogtr.next()
                    P.dma(ogt[:, :n], ogT[h, :, t0:t0 + n], writes=[ogtb])
                    tf, tfb = tmpf.next()
                    P.op("dve", lambda e, tf=tf, h=h, rs=rs: e.scalar_tensor_tensor(out=tf[:, :n], in0=og[:, h, :n], scalar=gng[:, 0:1], in1=rs[:, :n], op0=ALU.mult, op1=ALU.mult),
                         [ogb, rsb, b_gng], [tfb])
                    ot_, otb_ = outr2.next()
                    P.op("dve", lambda e, tf=tf, ot_=ot_, ogt=ogt: e.tensor_tensor(out=ot_[:, :n], in0=tf[:, :n], in1=ogt[:, :n], op=ALU.mult), [tfb, ogtb], [otb_])
                    P.dma(oT[1, h, :, t0:t0 + n], ot_[:, :n], reads=[otb_], writes=[NB()], eng="pool")

            def gla_group(dirn, t0, n, want_out):
                nch = n // 64
                la, lab = lar.next()
                bkt, bktb = bktr.next()
                bv, bvb = bvr.next()
                P.dma(la[:, :nch, :], la_d[dirn, t0:t0 + n, :].rearrange("(c j) d -> j c d", j=64), writes=[lab])
                P.dma(bkt[:, :nch, :], bk_tok[t0:t0 + n, :].rearrange("(c j) d -> j c d", j=64), writes=[bktb])
                P.dma(bv[:, :nch, :], bv_tok[t0:t0 + n, :].rearrange("(c j) d -> j c d", j=64), writes=[bvb])
                bq, bk = [], []
                for pr in range(2):
                    t_, b_ = bqr.next()
                    P.dma(t_[:, :n], bqT[pr, :, t0:t0 + n], writes=[b_])
                    bq.append((t_, b_))
                    t_, b_ = bkr.next()
                    P.dma(t_[:, :n], bkT[pr, :, t0:t0 + n], writes=[b_])
                    bk.append((t_, b_))
                og, ogb = ogr.next()
                if dirn == 1 and want_out:
                    P.dma(og[:, :, :n], ofT[:, :, t0:t0 + n].rearrange("h p t -> p h t"), writes=[ogb])
                order = range(nch) if dirn == 0 else range(nch - 1, -1, -1)
                for ci in order:
                    gla_chunk(dirn, la, lab, bkt, bktb, bv, bvb, bq, bk, ci, og, ogb)
                if want_out:
                    if dirn == 0:
                        P.dma(ofT[:, :, t0:t0 + n].rearrange("h p t -> p h t"), og[:, :, :n], reads=[ogb], writes=[NB()], eng="pool")
                    else:
                        finalize(og, ogb, t0, n)

            for dirn in (() if SKIP_GLA else range(2)):
                for (St, Stb), (Sb_, Sbb_) in zip(Sst, Sbf):
                    P.op("dve", lambda e, St=St: e.memset(St[:], 0.0), [], [Stb])
                    P.op("dve", lambda e, Sb_=Sb_: e.memset(Sb_[:], 0.0), [], [Sbb_])
                gla_group(dirn, L, CT, need_ctx)
                groups = [(i * 512, 512) for i in range(8)]
                if dirn == 1:
                    groups = groups[::-1]
                if dirn == 1:
                    P.barrier()
                for (t0, n) in groups:
                    gla_group(dirn, t0, n, True)
            P.barrier()

        if stop == "gla" and l == 0:
            return finish_list(k, st, dbg_d, dbg_items(locals()))

        TWO_PI = 2.0 * math.pi
        segs = [(0, L, "L")] + ([(L, CT, "C")] if need_ctx else [])
        def hy_segment(toff, ns, tag):
            nblk = ns // 128
            tw = min(512, ns)
            ntt = ns // tw
            with ExitStack() as S:
                w1s, b_w1s = k.sb(S, "hw1", [33, 64], F32)
                w2s, b_w2s = k.sb(S, "hw2", [64, 64], F32)
                w3s, b_w3s = k.sb(S, "hw3", [64, 2048], F32)
                bb, b_bb = k.sb(S, "hbb", [64, 4], F32)
                P.dma(w1s[:], I["hy_filt_w1"][l], writes=[b_w1s])
                P.dma(w2s[:], I["hy_filt_w2"][l], writes=[b_w2s])
                P.dma(w3s[:], I["hy_filt_w3"][l], writes=[b_w3s])
                P.dma(bb[:, 0:1], I["hy_filt_b1"][:, l:l + 1], writes=[b_bb])
                P.dma(bb[:, 1:2], I["hy_filt_b2"][:, l:l + 1], writes=[b_bb])
                P.op("dve", lambda e: e.memset(bb[:, 2:3], -math.pi), [b_bb], [b_bb])
                ftr = Ring(k, S, "feat", 2, [33, 512], F32)
                h1r = Ring(k, S, "h1", 2, [64, 512], F32)
                h2r = Ring(k, S, "h2", 2, [64, 512], F32)
                decr = Ring(k, S, "dec", 2, [128, 2, 512], F32)
                hfr = Ring(k, S, "hf", 4, [128, 512], F32)
                hor = Ring(k, S, "ho", 4, [128, 512], BF16)

                kir = Ring(k, S, "ki", 2, [64, 512], mybir.dt.int32)
                kfr = Ring(k, S, "kf", 4, [64, 512], F32)

                def sin_layer(ps, pb, bcol, out, outb, w):
                    u_, ub_ = kfr.next()
                    P.op("dve", lambda e: e.tensor_scalar(out=u_[:, :w], in0=ps[:64, :w], scalar1=bb[:, bcol:bcol + 1], scalar2=1.0 / TWO_PI, op0=ALU.add, op1=ALU.mult), [pb, b_bb], [ub_])
                    ki, kib = kir.next()
                    P.op("dve", lambda e: e.tensor_copy(out=ki[:, :w], in_=u_[:, :w]), [ub_], [kib])
                    kf, kfb = kfr.next()
                    P.op("dve", lambda e: e.tensor_copy(out=kf[:, :w], in_=ki[:, :w]), [kib], [kfb])
                    P.op("dve", lambda e: e.tensor_tensor(out=u_[:, :w], in0=u_[:, :w], in1=kf[:, :w], op=ALU.subtract), [ub_, kfb], [ub_])
                    P.op("dve", lambda e: e.tensor_scalar(out=kf[:, :w], in0=u_[:, :w], scalar1=0.5, scalar2=None, op0=ALU.is_gt), [ub_], [kfb])
                    P.op("dve", lambda e: e.tensor_tensor(out=u_[:, :w], in0=u_[:, :w], in1=kf[:, :w], op=ALU.subtract), [ub_, kfb], [ub_])
                    P.op("dve", lambda e: e.tensor_scalar(out=kf[:, :w], in0=u_[:, :w], scalar1=-0.5, scalar2=None, op0=ALU.is_lt), [ub_], [kfb])
                    P.op("dve", lambda e: e.tensor_tensor(out=u_[:, :w], in0=u_[:, :w], in1=kf[:, :w], op=ALU.add), [ub_, kfb], [ub_])
                    P.op("act", lambda e: e.activation(out=out[:, :w], in_=u_[:, :w], func=AF.Sin, scale=TWO_PI), [ub_], [outb])

                def filt_tile(t0):
                    ft, ftb = ftr.next()
                    P.dma(ft[:, :tw], I["k_feat" + tag][:, t0:t0 + tw], writes=[ftb])
                    ps, pb = k.tmp()
                    P.op("pe", lambda e: e.matmul(ps[:64, :tw], lhsT=w1s[:], rhs=ft[:, :tw], start=True, stop=True), [b_w1s, ftb], [pb])
                    h1, h1b = h1r.next()
                    sin_layer(ps, pb, 0, h1, h1b, tw)
                    ps2, pb2 = k.tmp()
                    P.op("pe", lambda e: e.matmul(ps2[:64, :tw], lhsT=w2s[:], rhs=h1[:, :tw], start=True, stop=True), [b_w2s, h1b], [pb2])
                    h2, h2b = h2r.next()
                    sin_layer(ps2, pb2, 1, h2, h2b, tw)
                    for jb in range(tw // 128):
                        r0 = t0 + jb * 128
                        dc, dcb = decr.next()
                        P.dma(dc[:, 0, :], I["k_dec" + tag][r0:r0 + 128, :], writes=[dcb])
                        P.dma(dc[:, 1, :], I["k_decb" + tag][r0:r0 + 128, :], writes=[dcb])
                        for o in range(2):
                            hf_, hb_ = [], []
                            for dr in range(2):
                                ps3, pb3 = k.tmp()
                                c0 = o * 1024 + dr * 512
                                P.op("pe", lambda e, ps3=ps3, c0=c0, jb=jb: e.matmul(ps3[:, :], lhsT=h2[:, jb * 128:(jb + 1) * 128], rhs=w3s[:, c0:c0 + 512], start=True, stop=True),
                                     [h2b, b_w3s], [pb3])
                                hf, hfb = hfr.next()
                                P.op("dve", lambda e, hf=hf, ps3=ps3, dr=dr, dc=dc: e.tensor_tensor(out=hf[:, :], in0=ps3[:, :], in1=dc[:, dr, :], op=ALU.mult), [pb3, dcb], [hfb])
                                hf_.append((hf, hfb))
                            for pm, op_ in ((0, ALU.add), (1, ALU.subtract)):
                                ho, hob = hor.next()
                                P.op("pool", lambda e, ho=ho, op_=op_, a=hf_[0][0], b=hf_[1][0]: e.tensor_tensor(out=ho[:, :], in0=a[:, :], in1=b[:, :], op=op_),
                                     [hf_[0][1], hf_[1][1]], [hob])
                                P.dma(hpm[o, pm, r0:r0 + 128, :], ho[:, :], reads=[hob], writes=[Buf("x")], eng="pool")

                for tt in range(ntt):
                    filt_tile(tt * tw)
                P.barrier()
            with ExitStack() as S:
                hres, b_hres = k.sb(S, "hres", [128, 2, 32, 512], BF16)
                tbr = Ring(k, S, "tb1", 2, [128, 2, 32, 128], BF16)
                pqr = Ring(k, S, "pq", 4, [128, 512], F32)
                for o in range(2):
                    for pm in range(2):
                        P.dma(hres[:, pm, :nblk, :], hpm[o, pm, 0:ns, :].rearrange("(c p) d -> p c d", p=128), writes=[b_hres])
                    for fc in range(nblk):
                        tb, tbb = tbr.next()
                        P.dma(tb[:, 0, :nblk, :], I["k_dc1" + tag][:, fc * 128:(fc + 1) * 128].rearrange("(c p) f -> p c f", p=128), writes=[tbb])
                        P.dma(tb[:, 1, :nblk, :], I["k_ds1" + tag][:, fc * 128:(fc + 1) * 128].rearrange("(c p) f -> p c f", p=128), writes=[tbb])
                        for pm in range(2):
                            ps, pb = k.tmp()
                            mm_group(P, ps[:, :], pb, [(tb[:, pm, tc, :], hres[:, pm, tc, :]) for tc in range(nblk)], [tbb, b_hres])
                            pq, pqb = pqr.next()
                            copy_on(P, k.ev(), pq[:, :], ps[:, :], [pb], [pqb])
                            P.dma(PQ[o, pm, fc * 128:(fc + 1) * 128, :], pq[:, :], reads=[pqb], writes=[Buf("x")], eng="pool")
                P.barrier()
            with ExitStack() as S:
                scw, b_scw = k.sb(S, "scw", [128, 3, 12], F32)
                scb, b_scb = k.sb(S, "scb", [128, 12], F32)
                fbs, b_fbs = k.sb(S, "fbs", [128, 2, 4], F32)
                P.dma(scw[:], I["hy_sconv_w"][:, l, :, :], writes=[b_scw])
                P.dma(scb[:], I["hy_sconv_b"][:, l, :], writes=[b_scb])
                P.dma(fbs[:], I["hy_filt_bias"][:, l, :, :], writes=[b_fbs])
                vtok, b_vtok = k.sb(S, "vtok", [128, 32, 512], BF16)
                Yt, b_Yt = k.sb(S, "Yt", [128, 2, 32, 512], BF16)
                pr_ = Ring(k, S, "pin", 2, [128, 514], BF16)
                zr = Ring(k, S, "zf", 3, [128, 512], F32)
                zbr = Ring(k, S, "zb", 3, [128, 512], BF16)
                tb1 = Ring(k, S, "tb1b", 2, [128, 2, 32, 128], BF16)
                tb2 = Ring(k, S, "tb2", 2, [128, 2, 4, 512], BF16)
                pqr = Ring(k, S, "pq2", 2, [128, 2, 512], F32)
                ewr = Ring(k, S, "ew", 4, [128, 512], F32)
                xr_ = Ring(k, S, "x1t", 2, [128, 512], BF16)
                vr_ = Ring(k, S, "vtt", 2, [128, 512], BF16)

                def to_vtok(zf, zfb, cc, t0, w):
                    pst, ptb = k.tmp()
                    for jb in range(w // 128):
                        P.op("pe", lambda e, jb=jb: e.transpose(pst[:, jb * 128:(jb + 1) * 128], zf[:, jb * 128:(jb + 1) * 128], ident_f[:]), [zfb, b_identf], [ptb])
                    b0 = t0 // 128
                    nb_ = w // 128
                    copy_on(P, k.ev(), vtok[:, b0:b0 + nb_, cc * 128:(cc + 1) * 128], pst[:, :w].rearrange("p (b c) -> p b c", c=128), [ptb], [b_vtok])

                def sconv_tile(c, t0):
                    pin, pinb = pr_.next()
                    lo = max(t0 - 1, 0)
                    hi = min(t0 + tw + 1, ns)
                    off = lo - (t0 - 1)
                    P.dma(pin[:, off:off + (hi - lo)], dpT[c, :, toff + lo:toff + hi], writes=[pinb])
                    zf, zfb = zr.next()
                    P.op("act", lambda e: e.activation(out=zf[:, :tw], in_=pin[:, 1:tw + 1], func=AF.Identity, scale=scw[:, 1, c:c + 1], bias=scb[:, c:c + 1]), [pinb, b_scw, b_scb], [zfb])
                    a0 = 1 if t0 == 0 else 0
                    P.op("dve", lambda e: e.scalar_tensor_tensor(out=zf[:, a0:tw], in0=pin[:, a0:tw], scalar=scw[:, 0, c:c + 1], in1=zf[:, a0:tw], op0=ALU.mult, op1=ALU.add), [pinb, b_scw, zfb], [zfb])
                    a1 = tw - 1 if t0 + tw >= ns else tw
                    P.op("dve", lambda e: e.scalar_tensor_tensor(out=zf[:, 0:a1], in0=pin[:, 2:a1 + 2], scalar=scw[:, 2, c:c + 1], in1=zf[:, 0:a1], op0=ALU.mult, op1=ALU.add), [pinb, b_scw, zfb], [zfb])
                    zb, zbb = zbr.next()
                    copy_on(P, "act", zb[:, :tw], zf[:, :tw], [zfb], [zbb])
                    P.dma(zT[c, :, toff + t0:toff + t0 + tw], zb[:, :tw], reads=[zbb], writes=[Buf("x")], eng="pool")
                    if c >= 8:
                        to_vtok(zf, zfb, c - 8, t0, tw)

                for c in range(12):
                    for tt in range(ntt):
                        sconv_tile(c, tt * tw)
                P.barrier()

                bank = [(k.ps_t[i], k.ps_b[i]) for i in range(8)]
                for o in range(2):
                    for fc in range(nblk):
                        tb, tbb = tb1.next()
                        P.dma(tb[:, 0, :nblk, :], I["k_dc1" + tag][:, fc * 128:(fc + 1) * 128].rearrange("(c p) f -> p c f", p=128), writes=[tbb])
                        P.dma(tb[:, 1, :nblk, :], I["k_ds1" + tag][:, fc * 128:(fc + 1) * 128].rearrange("(c p) f -> p c f", p=128), writes=[tbb])
                        pq, pqb = pqr.next()
                        P.dma(pq[:, 0, :], PQ[o, 0, fc * 128:(fc + 1) * 128, :], writes=[pqb])
                        P.dma(pq[:, 1, :], PQ[o, 1, fc * 128:(fc + 1) * 128, :], writes=[pqb])
                        psA, pAb = k.tmp()
                        psB, pBb = k.tmp()
                        mm_group(P, psA[:, :], pAb, [(tb[:, 0, tc, :], vtok[:, tc, :]) for tc in range(nblk)], [tbb, b_vtok])
                        mm_group(P, psB[:, :], pBb, [(tb[:, 1, tc, :], vtok[:, tc, :]) for tc in range(nblk)], [tbb, b_vtok])
                        e1, e1b = ewr.next(); e2, e2b = ewr.next(); e3, e3b = ewr.next(); e4, e4b = ewr.next()
                        P.op("dve", lambda e, e1=e1, pq=pq, psA=psA: e.tensor_tensor(out=e1[:, :], in0=psA[:, :], in1=pq[:, 0, :], op=ALU.mult), [pAb, pqb], [e1b])
                        P.op("dve", lambda e, e2=e2, pq=pq, psB=psB: e.tensor_tensor(out=e2[:, :], in0=psB[:, :], in1=pq[:, 1, :], op=ALU.mult), [pBb, pqb], [e2b])
                        P.op("dve", lambda e, e3=e3, pq=pq, psB=psB: e.tensor_tensor(out=e3[:, :], in0=psB[:, :], in1=pq[:, 0, :], op=ALU.mult), [pBb, pqb], [e3b])
                        P.op("dve", lambda e, e4=e4, pq=pq, psA=psA: e.tensor_tensor(out=e4[:, :], in0=psA[:, :], in1=pq[:, 1, :], op=ALU.mult), [pAb, pqb], [e4b])
                        P.op("pool", lambda e, e1=e1, e2=e2, fc=fc: e.tensor_tensor(out=Yt[:, 0, fc, :], in0=e1[:, :], in1=e2[:, :], op=ALU.subtract), [e1b, e2b], [b_Yt])
                        P.op("pool", lambda e, e3=e3, e4=e4, fc=fc: e.tensor_tensor(out=Yt[:, 1, fc, :], in0=e3[:, :], in1=e4[:, :], op=ALU.add), [e3b, e4b], [b_Yt])
                    for tt in range(ntt):
                        t0 = tt * tw
                        grp = bank[0:4] if tt % 2 == 0 else bank[4:8]
                        for f4 in range(0, nblk, 4):
                            nf = min(4, nblk - f4)
                            t2, t2b = tb2.next()
                            P.dma(t2[:, 0, :nf, :tw], I["k_dc2" + tag][f4 * 128:(f4 + nf) * 128, t0:t0 + tw].rearrange("(c p) t -> p c t", p=128), writes=[t2b])
                            P.dma(t2[:, 1, :nf, :tw], I["k_ds2" + tag][f4 * 128:(f4 + nf) * 128, t0:t0 + tw].rearrange("(c p) t -> p c t", p=128), writes=[t2b])
                            for cc in range(4):
                                ps, pb = grp[cc]

                                def fn(e, ps=ps, cc=cc, f4=f4, nf=nf, t2=t2):
                                    ins = None
                                    for fi in range(nf):
                                        fc = f4 + fi
                                        for cs in range(2):
                                            ins = e.matmul(ps[:, :tw], lhsT=Yt[:, cs, fc, cc * 128:(cc + 1) * 128], rhs=t2[:, cs, fi, :tw],
                                                           start=(fc == 0 and cs == 0), stop=(fc == nblk - 1 and cs == 1))
                                    return ins
                                P.op("pe", fn, [b_Yt, t2b], [pb])
                        for cc in range(4):
                            ps, pb = grp[cc]
                            xsrc = zT[(0 if o == 0 else 4) + cc, :, toff + t0:toff + t0 + tw]
                            vsrc = zT[8 + cc, :, toff + t0:toff + t0 + tw]
                            x1, x1b = xr_.next()
                            vv, vvb = vr_.next()
                            P.dma(x1[:, :tw], xsrc, writes=[x1b])
                            P.dma(vv[:, :tw], vsrc, writes=[vvb])
                            zf, zfb = zr.next()
                            P.op("dve", lambda e, zf=zf, vv=vv, ps=ps, cc=cc, o=o: e.scalar_tensor_tensor(out=zf[:, :tw], in0=vv[:, :tw], scalar=fbs[:, o, cc:cc + 1], in1=ps[:, :tw], op0=ALU.mult, op1=ALU.add),
                                 [vvb, b_fbs, pb], [zfb])
                            P.op("dve", lambda e, zf=zf, x1=x1: e.tensor_tensor(out=zf[:, :tw], in0=zf[:, :tw], in1=x1[:, :tw], op=ALU.mult), [zfb, x1b], [zfb])
                            zb, zbb = zbr.next()
                            copy_on(P, "act", zb[:, :tw], zf[:, :tw], [zfb], [zbb])
                            if o == 0:
                                P.dma(zT[8 + cc, :, toff + t0:toff + t0 + tw], zb[:, :tw], reads=[zbb, vvb], writes=[Buf("x")], eng="pool")
                            else:
                                P.dma(oT[3, cc, :, toff + t0:toff + t0 + tw], zb[:, :tw], reads=[zbb], writes=[Buf("x")], eng="pool")
                            if o == 0:
                                pass
                        if o == 0:
                            pass
                    if o == 0:
                        P.barrier()
                        for cc in range(4):
                            for tt in range(ntt):
                                t0 = tt * tw
                                vv, vvb = vr_.next()
                                P.dma(vv[:, :tw], zT[8 + cc, :, toff + t0:toff + t0 + tw], writes=[vvb])
                                zf, zfb = zr.next()
                                copy_on(P, "dve", zf[:, :tw], vv[:, :tw], [vvb], [zfb])
                                to_vtok(zf, zfb, cc, t0, tw)
                P.barrier()

        for (toff_, ns_, tag_) in segs:
            hy_segment(toff_, ns_, tag_)

        if stop == "hyena" and l == 0:
            return finish_list(k, st, dbg_d, dbg_items(locals()))

        with ExitStack() as S:
            wbr_sb, b_wbr = k.sb(S, "wbr", [128, 16, 1024], BF16)
            wout_sb, b_wout = k.sb(S, "wout", [128, 8, 1024], BF16)
            P.dma(wbr_sb[:], wbr_b[0].rearrange("a (kc p) n -> p (a kc) n", p=128), reads=[wbr_b[1]], writes=[b_wbr])
            P.dma(wout_sb[:], wout_b[0].rearrange("(kc p) n -> p kc n", p=128), reads=[wout_b[1]], writes=[b_wout])
            xr = Ring(k, S, "xt", 1, [128, 8, 512], F32)
            xnr = Ring(k, S, "xn", 1, [128, 8, 512], BF16)
            ur = Ring(k, S, "u", 1, [128, 8, 512], BF16)
            otr = Ring(k, S, "ot", 1, [128, 16, 512], BF16)
            wgr_ = Ring(k, S, "wgt", 2, [128, 8, 1024], BF16)
            maccr = Ring(k, S, "macc", 1, [128, 8, 512], F32)
            sigr = Ring(k, S, "sig", 3, [128, 512], F32)

            def merge_tile(ti, t0, n):
                j = 0 if t0 < L else 1
                xt, xb = xr.next()
                u, ub = ur.next()
                ot, otb = otr.next()
                P.dma(xt[:, :, :n], xT_tile_ap(t0, n), reads=[b_xTt[ti]], writes=[xb])
                P.dma(u[:, :, :n], uT_tile_ap(t0, n), reads=[b_uTt[ti]], writes=[ub])
                P.dma(ot[:, :, :n], oT[:, :, :, t0:t0 + n].rearrange("a c p t -> p (a c) t"), writes=[otb])
                macc, mb = maccr.next()
                for br in range(4):
                    wg_, wgb_ = wgr_.next()
                    P.dma(wg_[:], winv[:, :, O_GT + br * 1024:O_GT + (br + 1) * 1024], reads=[win_b[1]], writes=[wgb_])
                    for m in range(8):
                        psg, pgb = k.tmp()
                        mm_group(P, psg[:, :n], pgb, [(wg_[:, kc, m * 128:(m + 1) * 128], u[:, kc, :n]) for kc in range(8)], [wgb_, ub])
                        psz, pzb = k.tmp()
                        mm_group(P, psz[:, :n], pzb, [(wbr_sb[:, br * 4 + kc, m * 128:(m + 1) * 128], ot[:, br * 4 + kc, :n]) for kc in range(4)], [b_wbr, otb])
                        sg, sgb = sigr.next()
                        P.op("act", lambda e, sg=sg, psg=psg: e.activation(out=sg[:, :n], in_=psg[:, :n], func=AF.Sigmoid), [pgb], [sgb])
                        if br == 0:
                            P.op("dve", lambda e, sg=sg, psz=psz, m=m: e.tensor_tensor(out=macc[:, m, :n], in0=sg[:, :n], in1=psz[:, :n], op=ALU.mult), [sgb, pzb], [mb])
                        else:
                            P.op("dve", lambda e, sg=sg, psz=psz: e.tensor_tensor(out=sg[:, :n], in0=sg[:, :n], in1=psz[:, :n], op=ALU.mult), [sgb, pzb], [sgb])
                            P.op("dve", lambda e, sg=sg, m=m: e.tensor_tensor(out=macc[:, m, :n], in0=macc[:, m, :n], in1=sg[:, :n], op=ALU.add), [sgb, mb], [mb])
                mT, mTb = xnr.next()
                for m in range(8):
                    copy_on(P, "act", mT[:, m, :n], macc[:, m, :n], [mb], [mTb])
                for mo in range(8):
                    psy, pyb = k.tmp()
                    mm_group(P, psy[:, :n], pyb, [(wout_sb[:, kc, mo * 128:(mo + 1) * 128], mT[:, kc, :n]) for kc in range(8)], [b_wout, mTb])
                    P.op("dve", lambda e, psy=psy, mo=mo: e.scalar_tensor_tensor(out=xt[:, mo, :n], in0=psy[:, :n], scalar=gtv[:, 1, mo, j:j + 1],
                                                                                   in1=xt[:, mo, :n], op0=ALU.mult, op1=ALU.add), [pyb, b_gtv, xb], [xb])
                P.dma(xT_tile_ap(t0, n), xt[:, :, :n], reads=[xb], writes=[b_xTt[ti]], eng="pool")

            for ti, (t0, n) in enumerate(TILES):
                if t0 >= L and not need_ctx:
                    continue
                merge_tile(ti, t0, n)
            P.barrier()

        with ExitStack() as S:
            rg = ffn_rings(S)
            nmr = (Ring(k, S, "sq", 3, [128, 512], BF16), Ring(k, S, "rstd", 2, [128, 512], F32), Ring(k, S, "nmtmp", 2, [128, 512], F32))
            xr = Ring(k, S, "xt", 2, [128, 8, 512], F32)
            xnr = Ring(k, S, "xn", 2, [128, 8, 512], BF16)

            def ffn2_tile(ti, t0, n):
                j = 0 if t0 < L else 1
                xt, xb = xr.next()
                xn, xnb = xnr.next()
                P.dma(xt[:, :, :n], xT_tile_ap(t0, n), reads=[b_xTt[ti]], writes=[xb])
                norm_mod(S, nmr, xt, xb, n, 2, j, xn, xnb)
                ffn_tile(S, rg, xt, xb, xn, xnb, n, 1, 2, j)
                P.dma(xT_tile_ap(t0, n), xt[:, :, :n], reads=[xb], writes=[b_xTt[ti]], eng="pool")

            for ti, (t0, n) in enumerate(TILES):
                if t0 >= L and not need_ctx:
                    continue
                ffn2_tile(ti, t0, n)
            P.barrier()

        if stop == "merge" and l == 0:
            return finish(k, st, I, out_d, dbg_d, xT, b_xTt)

    for l_ in range(DEPTH):
        r_ = do_layer(l_)
        if r_ is not None:
            return r_

    with ExitStack() as S:
        fg, b_fg = k.sb(S, "fg", [128, 8], F32)
        P.dma(fg[:], I["final_g"], writes=[b_fg])
        xr = Ring(k, S, "xt", 2, [128, 8, 512], F32)
        sq_r = Ring(k, S, "sq", 3, [128, 512], BF16)
        rstd_r = Ring(k, S, "rstd", 2, [128, 512], F32)
        yr = Ring(k, S, "yt", 2, [128, 8, 512], F32)
        tokr = Ring(k, S, "tok", 3, [128, 1024], F32)

        def final_tile(ti, t0, n):
            xt, xb = xr.next()
            P.dma(xt[:, :, :n], xT_tile_ap(t0, n), reads=[b_xTt[ti]], writes=[xb])
            ps, pb = k.acc()
            for fc in range(8):
                q, qb = sq_r.next()
                P.op("act", lambda e, q=q, fc=fc: e.activation(out=q[:, :n], in_=xt[:, fc, :n], func=AF.Square), [xb], [qb])
                P.op("pe", lambda e, q=q, fc=fc: e.matmul(ps[:, :n], lhsT=ones_b[:], rhs=q[:, :n], start=(fc == 0), stop=(fc == 7)), [qb, b_ones], [pb])
            rstd, rb = rstd_r.next()
            P.op("act", lambda e: e.activation(out=rstd[:, :n], in_=ps[:, :n], func=AF.Sqrt, bias=epsv[:, 0:1], scale=1.0 / D), [pb, b_eps], [rb])
            P.op("dve", lambda e: e.reciprocal(out=rstd[:, :n], in_=rstd[:, :n]), [rb], [rb])
            yt, yb = yr.next()
            for fc in range(8):
                P.op("dve", lambda e, fc=fc: e.scalar_tensor_tensor(out=yt[:, fc, :n], in0=xt[:, fc, :n], scalar=fg[:, fc:fc + 1], in1=rstd[:, :n],
                                                                     op0=ALU.mult, op1=ALU.mult), [xb, rb, b_fg], [yb])
            for jb in range(n // 128):
                tk, tkb = tokr.next()
                for half in range(2):
                    pst, ptb = k.tmp()
                    for f4 in range(4):
                        fc = half * 4 + f4
                        P.op("pe", lambda e, pst=pst, f4=f4, fc=fc, jb=jb: e.transpose(pst[:, f4 * 128:(f4 + 1) * 128], yt[:, fc, jb * 128:(jb + 1) * 128], ident_f[:]),
                             [yb, b_identf], [ptb])
                    copy_on(P, k.ev(), tk[:, half * 512:(half + 1) * 512], pst[:, :], [ptb], [tkb])
                P.dma(out_d[t0 + jb * 128:t0 + (jb + 1) * 128, :], tk[:], reads=[tkb], eng="pool")

        for ti, (t0, n) in enumerate(TILES):
            if t0 < L:
                final_tile(ti, t0, n)
        P.barrier()
    P.emit()
    st.close()
    return nc


DBG_WANT = []


def dbg_items(loc):
    return [f(loc) for f in DBG_WANT]


def finish_list(k, st, dbg_d, aps):
    P = k.P
    with ExitStack() as S:
        r = Ring(k, S, "fl", 2, [128, 4352], F32)
        row = 0
        for ap in aps:
            rows, cols = ap.shape
            for r0 in range(0, rows, 128):
                rr = min(128, rows - r0)
                t, b = r.next()
                P.dma(t[:rr, :cols], ap[r0:r0 + rr, :], writes=[b], eng="pool")
                P.dma(dbg_d[row:row + rr, :cols], t[:rr, :cols], reads=[b], eng="sp")
                row += rr
    P.emit()
    st.close()
    return k.nc


def finish(k, st, I, out_d, dbg_d, xT, b_xTt):
    P = k.P
    if dbg_d is not None:
        with ExitStack() as S:
            r = Ring(k, S, "fin", 2, [128, 8, 512], F32)
            for ti, (t0, n) in enumerate(TILES):
                t, b = r.next()
                P.dma(t[:, :, :n], xT[:, :, t0:t0 + n].rearrange("c p t -> p c t"), reads=[b_xTt[ti]], writes=[b])
                P.dma(dbg_d[:, :, t0:t0 + n].rearrange("c p t -> p c t"), t[:, :, :n], reads=[b], eng="pool")
    P.emit()
    st.close()
    return k.nc


def finish_dbg(k, st, I, out_d, dbg_d, items):
    P = k.P
    for (t, b, shape) in items:
        P.dma(dbg_d, t[:].rearrange("p a b -> p (a b)") if len(t.shape) == 3 else t[:], reads=[b], eng="pool")
    P.emit()
    st.close()
    return k.nc


def pcol(v, nchunk):
    return np.ascontiguousarray(np.asarray(v, np.float32).reshape(nchunk, 128).T)


def host_inputs(inp, b):
    f = np.float32
    m = {}
    m["x"] = np.ascontiguousarray(inp["x"][b], f)
    m["ctx"] = np.ascontiguousarray(inp["ctx"][b], f)
    cc = np.stack([np.asarray(inp["c"][b], f), np.asarray(inp["c_ctx"], f)], axis=-1)
    m["cc"] = np.ascontiguousarray(cc.reshape(8, 128, 2).transpose(1, 0, 2))
    m["ada_w"] = np.asarray(inp["ada_w"], f)
    m["ada_b"] = np.ascontiguousarray(np.asarray(inp["ada_b"], f).reshape(DEPTH, 72, 128).transpose(2, 0, 1))
    m["norm_g"] = np.ascontiguousarray(np.asarray(inp["norm_g"], f).reshape(DEPTH, 3, 8, 128).transpose(3, 0, 1, 2))
    for n in ("ffn_w_gate", "ffn_w_up", "ffn_w_down", "w_in", "mla_w_uq", "mla_w_ukv", "gla_w_gate", "gla_b_gate",
              "hy_filt_w1", "hy_filt_w2", "hy_filt_w3", "w_branch", "w_out"):
        m[n] = np.asarray(inp[n], f)
    m["mla_q_norm_g"] = np.ascontiguousarray(np.asarray(inp["mla_q_norm_g"], f).reshape(DEPTH, 2, 128).transpose(2, 0, 1))
    m["mla_kv_norm_g"] = np.ascontiguousarray(np.asarray(inp["mla_kv_norm_g"], f).T)
    m["gla_norm_g"] = np.ascontiguousarray(np.asarray(inp["gla_norm_g"], f).T)
    m["gqa_q_norm_g"] = np.ascontiguousarray(np.tile(np.asarray(inp["gqa_q_norm_g"], f), (1, 2)).T)
    m["gqa_k_norm_g"] = np.ascontiguousarray(np.tile(np.asarray(inp["gqa_k_norm_g"], f), (1, 2)).T)
    m["hy_sconv_w"] = np.ascontiguousarray(np.asarray(inp["hy_sconv_w"], f).reshape(DEPTH, 3, 12, 128).transpose(3, 0, 1, 2))
    m["hy_sconv_b"] = np.ascontiguousarray(np.asarray(inp["hy_sconv_b"], f).reshape(DEPTH, 12, 128).transpose(2, 0, 1))
    m["hy_filt_b1"] = np.ascontiguousarray(np.asarray(inp["hy_filt_b1"], f).T)
    m["hy_filt_b2"] = np.ascontiguousarray(np.asarray(inp["hy_filt_b2"], f).T)
    m["hy_filt_bias"] = np.ascontiguousarray(np.asarray(inp["hy_filt_bias"], f).reshape(DEPTH, 2, 4, 128).transpose(3, 0, 1, 2))
    m["final_g"] = pcol(inp["final_g"], 8)
    m.update(consts())
    return m


def kernel(**inputs):
    nc = bass.Bass("TRN2", target_bir_lowering=False)
    build(nc)
    in_maps = [host_inputs(inputs, b) for b in range(8)]
    res = run_bass_kernel_spmd(nc, in_maps, core_ids=list(range(8)))
    return np.stack([np.asarray(r["out"], np.float32) for r in res.results], axis=0)
```

```python
import math
from contextlib import ExitStack
import numpy as np
import ml_dtypes
import concourse.bass as bass
import concourse.mybir as mybir
from concourse.bass_utils import run_bass_kernel_spmd

F32 = mybir.dt.float32
BF16 = mybir.dt.bfloat16
AF = mybir.ActivationFunctionType
ALU = mybir.AluOpType
AX = mybir.AxisListType

D = 1024
L = 4096
CT = 256
NT = L + CT
DEPTH = 2
FF = 2816
NFC = FF // 128
P_IN = 8384
EPS = 1e-6
TILES = [(i * 512, 512) for i in range(8)] + [(L, CT)]
O_CQ, O_CKV, O_KR, O_BQ, O_BK, O_BV, O_LR, O_OG, O_CQQ, O_CK, O_CV, O_DP, O_GT = (
    0, 256, 384, 416, 672, 928, 1440, 1472, 1984, 2496, 2624, 2752, 4288)

DBG_STOP = None
PROJ_SECTIONS = {'mla', 'gqa', 'gla', 'hy'}
PROJ_TILES = 9
SKIP = ''
MLA_STEP = 99
NHEADS_DBG = 8
RING_F32, RING_BF, RING_OUT = 10, 16, 12


class Buf:
    __slots__ = ("name", "last_w", "readers")

    def __init__(self, name="b"):
        self.name = name
        self.last_w = None
        self.readers = []


class Op:
    __slots__ = ("eng", "fn", "deps", "signal", "sig_val", "sem", "is_dma", "bar", "snap")

    def __init__(self, eng, fn, is_dma):
        self.eng = eng
        self.fn = fn
        self.deps = []
        self.signal = False
        self.sig_val = None
        self.sem = None
        self.is_dma = is_dma
        self.bar = None
        self.snap = None


ENGS = ("pe", "act", "dve", "pool", "sp")
NWAIT = {}
SAME_ENGINE_SKIP = ("pe",)
SIGNAL_ALL = True
SKIP_ATTN = False
SKIP_GLA = False
ROPE_COPY_ENG = "act"
DMA_SLOTS = 8


class Prog:
    def __init__(self, nc):
        self.nc = nc
        self.ops = {e: [] for e in ENGS}
        self.nbar = 0

    def op(self, eng, fn, reads=(), writes=(), dma=False):
        o = Op(eng, fn, dma)
        if SIGNAL_ALL and not dma:
            o.signal = True
        deps = []
        for b in reads:
            if b.last_w is not None:
                deps.append(b.last_w)
        for b in writes:
            if b.last_w is not None:
                deps.append(b.last_w)
            deps.extend(b.readers)
        seen = set()
        for d in deps:
            if id(d) in seen or d is o:
                continue
            seen.add(id(d))
            if d.eng == eng and eng in SAME_ENGINE_SKIP and not d.is_dma:
                continue
            o.deps.append(d)
            d.signal = True
        for b in reads:
            b.readers.append(o)
        for b in writes:
            b.last_w = o
            b.readers = []
        self.ops[eng].append(o)
        return o

    def dma(self, out, in_, reads=(), writes=(), eng="sp", **kw):
        return self.op(eng, lambda e: e.dma_start(out=out, in_=in_, **kw), reads, writes, dma=True)

    def barrier(self):
        self.nbar += 1
        for e in ENGS:
            last = None
            for o in reversed(self.ops[e]):
                if o.bar is None:
                    last = o
                    break
            if last is not None and not last.is_dma:
                last.signal = True
            o = Op(e, None, False)
            o.bar = self.nbar
            o.snap = last
            self.ops[e].append(o)

    def emit(self):
        nc = self.nc
        with ExitStack() as st:
            csem = {e: st.enter_context(nc.semaphore("c_" + e)) for e in ENGS}
            bsem = {e: st.enter_context(nc.semaphore("b_" + e)) for e in ENGS}
            dsem = {e: [st.enter_context(nc.semaphore("d_%s_%d" % (e, i))) for i in range(DMA_SLOTS)] for e in ENGS}
            ccount = {e: 0 for e in ENGS}
            dcount = {e: [0] * DMA_SLOTS for e in ENGS}
            dnext = {e: 0 for e in ENGS}
            for e in ENGS:
                for o in self.ops[e]:
                    if o.bar is not None:
                        o.sig_val = list(dcount[e])
                        continue
                    if o.is_dma:
                        s = dnext[e]
                        dnext[e] = (s + 1) % DMA_SLOTS
                        o.sem = dsem[e][s]
                        prev = dcount[e][s]
                        dcount[e][s] = prev + 16
                        o.sig_val = prev + 16
                        o.signal = True
                        o.deps.append(("slot", o.sem, prev))
                    elif o.signal:
                        ccount[e] += 1
                        o.sem = csem[e]
                        o.sig_val = ccount[e]
            final_d = {e: list(dcount[e]) for e in ENGS}
            st.enter_context(nc.allow_non_contiguous_dma(reason="small strided vector loads"))
            blk = st.enter_context(nc.Block())

            def make(e):
                def body(eng):
                    waited = {}

                    def wait(sem, val):
                        if val <= 0:
                            return
                        k = id(sem)
                        if waited.get(k, 0) >= val:
                            return
                        waited[k] = val
                        NWAIT[e] = NWAIT.get(e, 0) + 1
                        eng.wait_ge(sem, val)

                    for o in self.ops[e]:
                        if o.bar is not None:
                            for i in range(DMA_SLOTS):
                                wait(dsem[e][i], o.sig_val[i])
                            if o.snap is not None and not o.snap.is_dma:
                                wait(o.snap.sem, o.snap.sig_val)
                            eng.nop().then_inc(bsem[e], 1)
                            for e2 in ENGS:
                                if e2 != e:
                                    wait(bsem[e2], o.bar)
                            continue
                        for d in o.deps:
                            if isinstance(d, tuple):
                                wait(d[1], d[2])
                            else:
                                wait(d.sem, d.sig_val)
                        ins = o.fn(eng)
                        if o.signal:
                            ins.then_inc(o.sem, 16 if o.is_dma else 1)
                    for i in range(DMA_SLOTS):
                        wait(dsem[e][i], final_d[e][i])
                return body

            blk.tensor(make("pe"))
            blk.scalar(make("act"))
            blk.vector(make("dve"))
            blk.gpsimd(make("pool"))
            blk.sync(make("sp"))


class Ring:
    def __init__(self, k, st, name, n, shape, dt):
        self.t = [st.enter_context(k.nc.sbuf_tensor("%s%d_%d" % (name, k.uid(), i), shape, dt)) for i in range(n)]
        self.b = [Buf(name + str(i)) for i in range(n)]
        self.i = 0

    def next(self):
        i = self.i
        self.i = (i + 1) % len(self.t)
        return self.t[i], self.b[i]


class KB:
    def __init__(self, nc, st):
        self.nc = nc
        self.st = st
        self.P = Prog(nc)
        self._uid = 0
        self.ps_t = [st.enter_context(nc.psum_tensor("psb%d" % i, [128, 512], F32)) for i in range(8)]
        self.ps_b = [Buf("ps%d" % i) for i in range(8)]
        self.acc_i = 0
        self.tmp_i = 0
        self.rr = 0

    def uid(self):
        self._uid += 1
        return self._uid

    def acc(self):
        i = self.acc_i
        self.acc_i = (i + 1) % 2
        return self.ps_t[i], self.ps_b[i]

    def tmp(self):
        i = 2 + self.tmp_i
        self.tmp_i = (self.tmp_i + 1) % 6
        return self.ps_t[i], self.ps_b[i]

    def sb(self, st, name, shape, dt):
        return st.enter_context(self.nc.sbuf_tensor("%s_%d" % (name, self.uid()), shape, dt)), Buf(name)

    def dram(self, name, shape, dt):
        return self.nc.dram_tensor(name, shape, dt).ap(), Buf(name)

    def ev(self):
        self.rr ^= 1
        return "dve" if self.rr else "act"


def copy_on(P, eng, out, in_, reads, writes):
    if eng == "act":
        return P.op("act", lambda e: e.copy(out=out, in_=in_), reads, writes)
    return P.op(eng, lambda e: e.tensor_copy(out=out, in_=in_), reads, writes)


def mm_group(P, ps, psb, pairs, reads, n=None):
    def fn(e):
        ins = None
        for i, (a, b) in enumerate(pairs):
            ins = e.matmul(ps, lhsT=a, rhs=b, start=(i == 0), stop=(i == len(pairs) - 1))
        return ins
    return P.op("pe", fn, reads, [psb])


def rope_tables(dim_layout):
    cos = np.ones((128, NT), np.float32)
    sin = np.zeros((128, NT), np.float32)
    RT = np.zeros((128, 128), np.float32)
    t = np.arange(L)
    rows = (t // 64).astype(np.float32)
    cols = (t % 64).astype(np.float32)
    for (r0, rd) in dim_layout:
        r = rd // 2
        half = r // 2
        freqs = (10000.0 ** (-np.arange(half, dtype=np.float32) / half)).astype(np.float32)
        for part, pos in ((0, rows), (1, cols)):
            base = r0 + part * r
            ang = pos[None, :] * freqs[:, None]
            c, s = np.cos(ang), np.sin(ang)
            cos[base:base + half, :L] = c
            cos[base + half:base + r, :L] = c
            sin[base:base + half, :L] = s
            sin[base + half:base + r, :L] = s
            for i in range(half):
                RT[base + half + i, base + i] = -1.0
                RT[base + i, base + half + i] = 1.0
    return cos, sin, RT


def dft_tables(n):
    N = 2 * n
    k = np.arange(n, dtype=np.float64)
    t = np.arange(n, dtype=np.float64)
    ang = 2.0 * np.pi * np.outer(t, k + 0.5) / N
    c, s = np.cos(ang), np.sin(ang)
    bf = ml_dtypes.bfloat16
    return (c.astype(np.float32).astype(bf), s.astype(np.float32).astype(bf),
            ((2.0 / N) * c.T).astype(np.float32).astype(bf), ((2.0 / N) * s.T).astype(np.float32).astype(bf))


def hyena_consts(n):
    t = np.linspace(0.0, 1.0, n, dtype=np.float32)
    w = (2.0 * math.pi * np.arange(n, dtype=np.float32) / n).astype(np.float32)
    f = np.linspace(1e-4, 15, 16, dtype=np.float32)
    fw = w[:, None] * f[None, :]
    feat = np.concatenate([t[:, None], np.cos(fw), -np.sin(fw)], axis=-1).astype(np.float32)
    deltas = np.abs(np.linspace(math.log(1e-2) / 1.5, math.log(1e-2) / 0.3, 512, dtype=np.float32))
    dec = np.exp(-t[:, None] * deltas[None, :]).astype(np.float32)
    decb = dec.copy()
    decb[0, :] = 0.0
    return np.ascontiguousarray(feat.T), dec, decb


def make_consts():
    c = {}
    c["k_ident"] = np.eye(128, dtype=np.float32)
    c["k_ones"] = np.ones((128, 128), np.float32)
    bo = np.zeros((128, 128), np.float32)
    bo[:64, :64] = 1.0
    bo[64:, 64:] = 1.0
    c["k_bones"] = bo
    cm, sm, rm = rope_tables([(64, 32)])
    c["k_cosM"], c["k_sinM"], c["k_rtM"] = cm, sm, rm
    cg, sg, rg = rope_tables([(0, 64), (64, 64)])
    c["k_cosG"], c["k_sinG"], c["k_rtG"] = cg, sg, rg
    for n, tag in ((L, "L"), (CT, "C")):
        a, b, c2, d2 = dft_tables(n)
        c["k_dc1" + tag], c["k_ds1" + tag], c["k_dc2" + tag], c["k_ds2" + tag] = a, b, c2, d2
        ft, dec, decb = hyena_consts(n)
        c["k_feat" + tag], c["k_dec" + tag], c["k_decb" + tag] = ft, dec, decb
    j = np.arange(64)
    c["k_m_le"] = (j[:, None] <= j[None, :]).astype(np.float32)
    c["k_m_ge"] = (j[:, None] >= j[None, :]).astype(np.float32)
    c["k_m_gt"] = (j[:, None] > j[None, :]).astype(np.float32)
    c["k_m_lt"] = (j[:, None] < j[None, :]).astype(np.float32)
    return c


_CONSTS = None


def consts():
    global _CONSTS
    if _CONSTS is None:
        _CONSTS = make_consts()
    return _CONSTS


def build(nc, stop=None, dbg=None):
    st = ExitStack()
    k = KB(nc, st)
    P = k.P
    I = {}

    def din(name, shape, dt=F32):
        I[name] = nc.dram_tensor(name, list(shape), dt, kind="ExternalInput").ap()
        return I[name]

    din("x", [L, D]); din("ctx", [CT, D]); din("cc", [128, 8, 2])
    din("ada_w", [DEPTH, D, 9 * D]); din("ada_b", [128, DEPTH, 72]); din("norm_g", [128, DEPTH, 3, 8])
    din("ffn_w_gate", [DEPTH, 2, D, FF]); din("ffn_w_up", [DEPTH, 2, D, FF]); din("ffn_w_down", [DEPTH, 2, FF, D])
    din("w_in", [DEPTH, D, P_IN])
    din("mla_q_norm_g", [128, DEPTH, 2]); din("mla_w_uq", [DEPTH, 256, 768])
    din("mla_kv_norm_g", [128, DEPTH]); din("mla_w_ukv", [DEPTH, 128, 1024])
    din("gla_w_gate", [DEPTH, 2, 16, 256]); din("gla_b_gate", [DEPTH, 2, 256]); din("gla_norm_g", [128, DEPTH])
    din("gqa_q_norm_g", [128, DEPTH]); din("gqa_k_norm_g", [128, DEPTH])
    din("hy_sconv_w", [128, DEPTH, 3, 12]); din("hy_sconv_b", [128, DEPTH, 12])
    din("hy_filt_w1", [DEPTH, 33, 64]); din("hy_filt_b1", [64, DEPTH]); din("hy_filt_w2", [DEPTH, 64, 64])
    din("hy_filt_b2", [64, DEPTH]); din("hy_filt_w3", [DEPTH, 64, 2048]); din("hy_filt_bias", [128, DEPTH, 2, 4])
    din("w_branch", [DEPTH, 4, 512, D]); din("w_out", [DEPTH, D, D]); din("final_g", [128, 8])
    cs = consts()
    for name, arr in cs.items():
        din(name, arr.shape, BF16 if arr.dtype == ml_dtypes.bfloat16 else F32)
    out_d = nc.dram_tensor("out", [L, D], F32, kind="ExternalOutput").ap()
    dbg_d = None
    if dbg is not None:
        dbg_d = nc.dram_tensor("dbg", list(dbg), F32, kind="ExternalOutput").ap()

    G = ExitStack()
    st.enter_context(G)
    ident_f, b_identf = k.sb(G, "identf", [128, 128], F32)
    ident_b, b_identb = k.sb(G, "identb", [128, 128], BF16)
    ones_b, b_ones = k.sb(G, "onesb", [128, 128], BF16)
    bones_b, b_bones = k.sb(G, "bonesb", [128, 128], BF16)
    P.dma(ident_f[:], I["k_ident"], writes=[b_identf])
    P.dma(ident_b[:], I["k_ident"], writes=[b_identb], eng="pool")
    P.dma(ones_b[:], I["k_ones"], writes=[b_ones], eng="pool")
    P.dma(bones_b[:], I["k_bones"], writes=[b_bones], eng="pool")
    modv, b_modv = k.sb(G, "modv", [128, 72, 2], F32)
    gsv, b_gsv = k.sb(G, "gsv", [128, 3, 8, 2], F32)
    gtv, b_gtv = k.sb(G, "gtv", [128, 3, 8, 2], F32)
    smalls, b_smalls = k.sb(G, "smalls", [128, 64], F32)
    epsv, b_eps = k.sb(G, "epsv", [128, 1], F32)
    P.op("pool", lambda e: e.memset(epsv[:], EPS), [], [b_eps])

    xT, b_xT = k.dram("xT", [8, 128, NT], F32)
    b_xTt = [Buf("xT%d" % i) for i in range(len(TILES))]
    uT, _ = k.dram("uT", [8, 128, NT], BF16)
    b_uTt = [Buf("uT%d" % i) for i in range(len(TILES))]
    wg_b = [k.dram("wg_b%d" % i, [D, FF], BF16) for i in range(2)]
    wu_b = [k.dram("wu_b%d" % i, [D, FF], BF16) for i in range(2)]
    wd_b = [k.dram("wd_b%d" % i, [FF, D], BF16) for i in range(2)]
    win_b = k.dram("win_b", [D, P_IN], BF16)
    wbr_b = k.dram("wbr_b", [4, 512, D], BF16)
    wout_b = k.dram("wout_b", [D, D], BF16)

    def xT_tile_ap(t0, n):
        return xT[:, :, t0:t0 + n].rearrange("c p t -> p c t")

    def uT_tile_ap(t0, n):
        return uT[:, :, t0:t0 + n].rearrange("c p t -> p c t")

    with ExitStack() as S:
        xin = Ring(k, S, "xin", 8, [128, D], F32)
        xo = Ring(k, S, "xo", 2, [128, 8, 512], F32)
        for ti, (t0, n) in enumerate(TILES):
            ot, ob = xo.next()
            nb = n // 128
            blks = []
            for j in range(nb):
                tt, tb = xin.next()
                r0 = t0 + j * 128
                src = I["x"][r0:r0 + 128, :] if r0 < L else I["ctx"][r0 - L:r0 - L + 128, :]
                P.dma(tt[:], src, writes=[tb])
                blks.append((tt, tb))
                for fc in range(8):
                    pass
            for fc in range(8):
                ps, pb = k.tmp()
                for j in range(nb):
                    tt, tb = blks[j]
                    P.op("pe", lambda e, ps=ps, tt=tt, j=j, fc=fc: e.transpose(ps[:, j * 128:(j + 1) * 128], tt[:, fc * 128:(fc + 1) * 128], ident_f[:]),
                         [tb, b_identf], [pb])
                copy_on(P, k.ev(), ot[:, fc, :n], ps[:, :n], [pb], [ob])
            P.dma(xT_tile_ap(t0, n), ot[:, :, :n], reads=[ob], writes=[b_xTt[ti]], eng="pool")
        P.barrier()

    if stop == "s0":
        return finish(k, st, I, out_d, dbg_d, xT, b_xTt)

    def cast_dram(dst, src, rows, cols, bufs):
        step = 256
        for r in range(0, rows, step):
            rr = min(step, rows - r)
            P.dma(dst[r:r + rr, :], src[r:r + rr, :], writes=bufs, eng="pool")

    def norm_mod(S, rings, xt, xb, n, s, j, out, outb):
        sq_r, rstd_r, tmp_r = rings
        ps, pb = k.acc()
        sqs = []
        for fc in range(8):
            q, qb = sq_r.next()
            P.op("act", lambda e, q=q, fc=fc: e.activation(out=q[:, :n], in_=xt[:, fc, :n], func=AF.Square), [xb], [qb])
            sqs.append((q, qb))
            P.op("pe", lambda e, q=q, fc=fc, ps=ps: e.matmul(ps[:, :n], lhsT=ones_b[:], rhs=q[:, :n], start=(fc == 0), stop=(fc == 7)),
                 [qb, b_ones], [pb])
        rstd, rb = rstd_r.next()
        P.op("act", lambda e: e.activation(out=rstd[:, :n], in_=ps[:, :n], func=AF.Sqrt, bias=epsv[:, 0:1], scale=1.0 / D), [pb, b_eps], [rb])
        P.op("dve", lambda e: e.reciprocal(out=rstd[:, :n], in_=rstd[:, :n]), [rb], [rb])
        for fc in range(8):
            tp, tb = tmp_r.next()
            P.op("dve", lambda e, tp=tp, fc=fc: e.scalar_tensor_tensor(out=tp[:, :n], in0=xt[:, fc, :n], scalar=gsv[:, s, fc, j:j + 1],
                                                                         in1=rstd[:, :n], op0=ALU.mult, op1=ALU.mult),
                 [xb, rb, b_gsv], [tb])
            P.op("act", lambda e, tp=tp, fc=fc: e.activation(out=out[:, fc, :n], in_=tp[:, :n], func=AF.Identity,
                                                              bias=modv[:, 3 * s * 8 + fc, j:j + 1], scale=1.0),
                 [tb, b_modv], [outb])

    qA = k.dram("qA", [8, 128, NT], BF16)[0]; kA = k.dram("kA", [8, 128, NT], BF16)[0]
    vA = k.dram("vA", [34, 128, 8, 128], BF16)[0]
    qC = k.dram("qC", [4, 128, NT], BF16)[0]; kC = k.dram("kC", [4, 128, NT], BF16)[0]
    vC = k.dram("vC", [34, 128, 2, 128], BF16)[0]
    bqT = k.dram("bqT", [2, 128, NT], BF16)[0]; bkT = k.dram("bkT", [2, 128, NT], BF16)[0]
    bk_tok = k.dram("bk_tok", [NT, 256], BF16)[0]; bv_tok = k.dram("bv_tok", [NT, 512], BF16)[0]
    la_d = k.dram("la_d", [2, NT, 256], BF16)[0]
    ogT = k.dram("ogT", [4, 128, NT], BF16)[0]; dpT = k.dram("dpT", [12, 128, NT], BF16)[0]
    oT = k.dram("oT", [4, 4, 128, NT], BF16)[0]
    ofT = k.dram("ofT", [4, 128, NT], F32)[0]
    zT = k.dram("zT", [12, 128, NT], BF16)[0]
    hpm = k.dram("hpm", [2, 2, L, 512], BF16)[0]
    PQ = k.dram("PQ", [2, 2, L, 512], F32)[0]

    def do_layer(l):
        need_ctx = l < DEPTH - 1
        with ExitStack() as S:
            cct, b_cct = k.sb(S, "cct", [128, 8, 2], F32)
            P.dma(cct[:], I["cc"], writes=[b_cct])
            P.op("act", lambda e: e.activation(out=cct[:], in_=cct[:], func=AF.Silu), [b_cct], [b_cct])
            adab, b_adab = k.sb(S, "adab", [128, 72], F32)
            P.dma(adab[:], I["ada_b"][:, l, :], writes=[b_adab])
            ng, b_ng = k.sb(S, "ng", [128, 3, 8], F32)
            P.dma(ng[:], I["norm_g"][:, l, :, :], writes=[b_ng])
            awr = Ring(k, S, "adaw", 2, [128, 8, 1024], F32)
            ps, pb = k.acc()
            for i in range(9):
                wt, wb = awr.next()
                P.dma(wt[:], I["ada_w"][l, :, i * 1024:(i + 1) * 1024].rearrange("(kc p) n -> p kc n", p=128), writes=[wb])
                for fc in range(8):
                    ch = i * 8 + fc

                    def fn(e, wt=wt, fc=fc, ch=ch, ps=ps):
                        ins = None
                        for kc in range(8):
                            ins = e.matmul(ps[:, ch * 2:ch * 2 + 2], lhsT=wt[:, kc, fc * 128:(fc + 1) * 128], rhs=cct[:, kc, :],
                                           start=(kc == 0), stop=(kc == 7))
                        return ins
                    P.op("pe", fn, [wb, b_cct], [pb])
            for j in range(2):
                P.op("dve", lambda e, j=j, ps=ps: e.tensor_tensor(out=modv[:, :, j], in0=ps[:, j:144:2], in1=adab[:], op=ALU.add),
                     [pb, b_adab], [b_modv])
            for s in range(3):
                for j in range(2):
                    P.op("dve", lambda e, s=s, j=j: e.scalar_tensor_tensor(out=gsv[:, s, :, j], in0=modv[:, (3 * s + 1) * 8:(3 * s + 2) * 8, j], scalar=1.0,
                                                                             in1=ng[:, s, :], op0=ALU.add, op1=ALU.mult),
                         [b_modv, b_ng], [b_gsv])
                    P.op("dve", lambda e, s=s, j=j: e.tensor_scalar(out=gtv[:, s, :, j], in0=modv[:, (3 * s + 2) * 8:(3 * s + 3) * 8, j],
                                                                      scalar1=(1.0 if s == 1 else 0.5), scalar2=None, op0=ALU.mult),
                         [b_modv], [b_gtv])
            P.barrier()

        if stop == "ada" and l == 0:
            return finish_dbg(k, st, I, out_d, dbg_d, [(modv, b_modv, [128, 144])])

        for i in range(2):
            cast_dram(wg_b[i][0], I["ffn_w_gate"][l, i], D, FF, [wg_b[i][1]])
            cast_dram(wu_b[i][0], I["ffn_w_up"][l, i], D, FF, [wu_b[i][1]])
            cast_dram(wd_b[i][0], I["ffn_w_down"][l, i], FF, D, [wd_b[i][1]])
        cast_dram(win_b[0], I["w_in"][l], D, P_IN, [win_b[1]])
        cast_dram(wbr_b[0].rearrange("a k n -> (a k) n"), I["w_branch"][l].rearrange("a k n -> (a k) n"), 2048, D, [wbr_b[1]])
        cast_dram(wout_b[0], I["w_out"][l], D, D, [wout_b[1]])

        def ffn_tile(S, rg, xt, xb, xn, xnb, n, fi, s, j):
            wgr, wur, wdr, sgr, hT, hb = rg
            for fp in range(6):
                c0 = fp * 512
                pc = min(512, FF - c0)
                wgt, wgb = wgr.next()
                wut, wub = wur.next()
                P.dma(wgt[:, :, :pc], wg_b[fi][0].rearrange("(kc p) f -> p kc f", p=128)[:, :, c0:c0 + pc], reads=[wg_b[fi][1]], writes=[wgb])
                P.dma(wut[:, :, :pc], wu_b[fi][0].rearrange("(kc p) f -> p kc f", p=128)[:, :, c0:c0 + pc], reads=[wu_b[fi][1]], writes=[wub])
                for jj in range(pc // 128):
                    f = fp * 4 + jj
                    psg, pgb = k.tmp()
                    psu, pub = k.tmp()
                    mm_group(P, psg[:, :n], pgb, [(wgt[:, kc, jj * 128:(jj + 1) * 128], xn[:, kc, :n]) for kc in range(8)], [wgb, xnb])
                    mm_group(P, psu[:, :n], pub, [(wut[:, kc, jj * 128:(jj + 1) * 128], xn[:, kc, :n]) for kc in range(8)], [wub, xnb])
                    sg, sgb = sgr.next()
                    P.op("act", lambda e, sg=sg, psg=psg: e.activation(out=sg[:, :n], in_=psg[:, :n], func=AF.Silu), [pgb], [sgb])
                    P.op("dve", lambda e, sg=sg, psu=psu, f=f: e.tensor_tensor(out=hT[:, f, :n], in0=sg[:, :n], in1=psu[:, :n], op=ALU.mult),
                         [sgb, pub], [hb])
            for mp in range(2):
                wdt, wdb = wdr.next()
                P.dma(wdt[:], wd_b[fi][0].rearrange("(kc p) m -> p kc m", p=128)[:, :, mp * 512:(mp + 1) * 512], reads=[wd_b[fi][1]], writes=[wdb])
                for jj in range(4):
                    m = mp * 4 + jj
                    ps, pb = k.tmp()
                    mm_group(P, ps[:, :n], pb, [(wdt[:, kc, jj * 128:(jj + 1) * 128], hT[:, kc, :n]) for kc in range(NFC)], [wdb, hb])
                    P.op("dve", lambda e, ps=ps, m=m: e.scalar_tensor_tensor(out=xt[:, m, :n], in0=ps[:, :n], scalar=gtv[:, s, m, j:j + 1],
                                                                               in1=xt[:, m, :n], op0=ALU.mult, op1=ALU.add),
                         [pb, b_gtv, xb], [xb])

        def ffn_rings(S):
            wgr = Ring(k, S, "wg", 2, [128, 8, 512], BF16)
            wur = Ring(k, S, "wu", 2, [128, 8, 512], BF16)
            wdr = Ring(k, S, "wd", 2, [128, NFC, 512], BF16)
            sgr = Ring(k, S, "sg", 3, [128, 512], F32)
            hT, hb = k.sb(S, "hT", [128, NFC, 512], BF16)
            return (wgr, wur, wdr, sgr, hT, hb)

        with ExitStack() as S:
            rg = ffn_rings(S)
            nmr = (Ring(k, S, "sq", 3, [128, 512], BF16), Ring(k, S, "rstd", 2, [128, 512], F32), Ring(k, S, "nmtmp", 2, [128, 512], F32))
            xr = Ring(k, S, "xt", 2, [128, 8, 512], F32)
            xnr = Ring(k, S, "xn", 2, [128, 8, 512], BF16)
            for ti, (t0, n) in enumerate(TILES):
                j = 0 if t0 < L else 1
                xt, xb = xr.next()
                xn, xnb = xnr.next()
                P.dma(xt[:, :, :n], xT_tile_ap(t0, n), reads=[b_xTt[ti]], writes=[xb])
                norm_mod(S, nmr, xt, xb, n, 0, j, xn, xnb)
                ffn_tile(S, rg, xt, xb, xn, xnb, n, 0, 0, j)
                P.dma(xT_tile_ap(t0, n), xt[:, :, :n], reads=[xb], writes=[b_xTt[ti]], eng="pool")
                un, unb = xnr.next()
                norm_mod(S, nmr, xt, xb, n, 1, j, un, unb)
                P.dma(uT_tile_ap(t0, n), un[:, :, :n], reads=[unb], writes=[b_uTt[ti]], eng="pool")
            P.barrier()

        if stop == "ffn1" and l == 0:
            return finish(k, st, I, out_d, dbg_d, xT, b_xTt)

        winv = win_b[0].rearrange("(kc p) n -> p kc n", p=128)
        with ExitStack() as S:
            NB = lambda: Buf("x")
            WuqP, b_WuqP = k.sb(S, "WuqP", [128, 2, 8, 128], BF16)
            WukP, b_WukP = k.sb(S, "WukP", [128, 8, 128], BF16)
            WukV, b_WukV = k.sb(S, "WukV", [128, 8, 64], BF16)
            WkrP, b_WkrP = k.sb(S, "WkrP", [128, 8, 128], BF16)
            WckP, b_WckP = k.sb(S, "WckP", [128, 8, 4, 128], BF16)
            wgate, b_wgate = k.sb(S, "wgate", [16, 2, 256], BF16)
            biasb, b_biasb = k.sb(S, "biasb", [128, 512], F32)
            pv, b_pv = k.sb(S, "pv", [128, 8], F32)
            rtM, b_rtM = k.sb(S, "rtM", [128, 128], BF16)
            rtG, b_rtG = k.sb(S, "rtG", [128, 128], BF16)
            P.dma(rtM[:], I["k_rtM"], writes=[b_rtM], eng="pool")
            P.dma(rtG[:], I["k_rtG"], writes=[b_rtG], eng="pool")
            P.op("dve", lambda e: e.memset(WuqP[:], 0.0), [], [b_WuqP])
            P.op("dve", lambda e: e.memset(WukP[:], 0.0), [], [b_WukP])
            P.op("dve", lambda e: e.memset(WkrP[:], 0.0), [], [b_WkrP])
            P.op("dve", lambda e: e.memset(WckP[:], 0.0), [], [b_WckP])
            for kc in range(2):
                if 'B' not in SKIP:
                    P.dma(WuqP[:, kc, :, 0:96], I["mla_w_uq"][l, kc * 128:(kc + 1) * 128, :].rearrange("p (h c) -> p h c", c=96), writes=[b_WuqP], eng="pool")
            ukv = I["mla_w_ukv"][l].rearrange("p (h c) -> p h c", c=128)
            if 'B' not in SKIP:
                P.dma(WukP[:, :, 0:64], ukv[:, :, 0:64], writes=[b_WukP], eng="pool")
            if 'B' not in SKIP:
                P.dma(WukV[:], ukv[:, :, 64:128], writes=[b_WukV], eng="pool")
            if 'D' not in SKIP:
                P.dma(WkrP[:, :, 64:96], winv[:, :, O_KR:O_KR + 32], reads=[win_b[1]], writes=[b_WkrP])
            for g in range(2):
                for par in range(2):
                    if 'D' not in SKIP:
                        P.dma(WckP[:, :, g * 2 + par, par * 64:par * 64 + 64], winv[:, :, O_CK + g * 64:O_CK + g * 64 + 64], reads=[win_b[1]], writes=[b_WckP])
            if 'C' not in SKIP:
                P.dma(wgate[:], I["gla_w_gate"][l].rearrange("d r c -> r d c"), writes=[b_wgate], eng="pool")
            if 'A' not in SKIP:
                P.dma(biasb[:], I["gla_b_gate"][l].rearrange("d c -> (d c)").partition_broadcast(128), writes=[b_biasb])
            if 'E' not in SKIP:
                P.dma(pv[:, 0:2], I["mla_q_norm_g"][:, l, :], writes=[b_pv])
            if 'E' not in SKIP:
                P.dma(pv[:, 2:3], I["mla_kv_norm_g"][:, l:l + 1], writes=[b_pv])
            if 'E' not in SKIP:
                P.dma(pv[:, 3:4], I["gqa_q_norm_g"][:, l:l + 1], writes=[b_pv])
            if 'E' not in SKIP:
                P.dma(pv[:, 4:5], I["gqa_k_norm_g"][:, l:l + 1], writes=[b_pv])
            ur = Ring(k, S, "u", 2, [128, 8, 512], BF16)
            wr = Ring(k, S, "wp", 3, [128, 8, 512], BF16)
            tabr = Ring(k, S, "tab", 2, [128, 4, 512], F32)
            f32r = Ring(k, S, "f32r", RING_F32, [128, 512], F32)
            bfr = Ring(k, S, "bfr", RING_BF, [128, 512], BF16)
            cqr = Ring(k, S, "cq", 2, [128, 2, 512], F32)
            cqnr = Ring(k, S, "cqn", 2, [128, 2, 512], BF16)
            outr = Ring(k, S, "outr", RING_OUT, [128, 512], BF16)
            krr_r = Ring(k, S, "krr", 2, [128, 512], BF16)
            ckvn_r = Ring(k, S, "ckvn", 2, [128, 512], BF16)
            vaA = Ring(k, S, "vaA", 2, [128, 8, 128], BF16)
            vaC = Ring(k, S, "vaC", 2, [128, 2, 128], BF16)
            for (t_, b_) in zip(vaA.t + vaC.t, vaA.b + vaC.b):
                P.op("dve", lambda e, t_=t_: e.memset(t_[:], 1.0), [], [b_])
            lrr = Ring(k, S, "lrT", 4, [16, 512], BF16)

            def wcols(c0, width):
                wt, wb = wr.next()
                P.dma(wt[:, :, :width], winv[:, :, c0:c0 + width], reads=[win_b[1]], writes=[wb])
                return wt, wb

            def store(dst, src, sb, eng="pool"):
                P.dma(dst, src, reads=[sb], writes=[NB()], eng=eng)

            def do_tile(ti, t0, n):
                u, ub = ur.next()
                P.dma(u[:, :, :n], uT_tile_ap(t0, n), reads=[b_uTt[ti]], writes=[ub])
                tab, tabb = tabr.next()
                for i_, nm in enumerate(("k_cosM", "k_sinM", "k_cosG", "k_sinG")):
                    P.dma(tab[:, i_, :n], I[nm][:, t0:t0 + n], writes=[tabb])
                nchunk = n // 128

                def proj(wt, wb, c0, M):
                    ps, pb = k.tmp()
                    mm_group(P, ps[:M, :n], pb, [(wt[:, kc, c0:c0 + M], u[:, kc, :n]) for kc in range(8)], [wb, ub])
                    return ps, pb

                def rms_rstd(srcs, nfeat, lhsT, lb):
                    pss, pssb = k.acc()
                    for i_, (a, ab) in enumerate(srcs):
                        sq, sqb = bfr.next()
                        P.op("act", lambda e, sq=sq, a=a: e.activation(out=sq[:, :n], in_=a, func=AF.Square), [ab], [sqb])
                        P.op("pe", lambda e, sq=sq, i_=i_, pss=pss: e.matmul(pss[:, :n], lhsT=lhsT[:], rhs=sq[:, :n], start=(i_ == 0), stop=(i_ == len(srcs) - 1)),
                             [sqb, lb], [pssb])
                    r, rb = f32r.next()
                    P.op("act", lambda e: e.activation(out=r[:, :n], in_=pss[:, :n], func=AF.Sqrt, bias=epsv[:, 0:1], scale=1.0 / nfeat), [pssb, b_eps], [rb])
                    P.op("dve", lambda e: e.reciprocal(out=r[:, :n], in_=r[:, :n]), [rb], [rb])
                    return r, rb

                def rope(src, srcb, ci, rt, rtb, dst):
                    if dst is not None or True:
                        pc_, pcb_ = f32r.next()
                        copy_on(P, "dve", pc_[:, :n], src, [srcb], [pcb_])
                        src, srcb = pc_[:, :n], pcb_
                    qb_, qbb = bfr.next()
                    copy_on(P, ROPE_COPY_ENG, qb_[:, :n], src, [srcb], [qbb])
                    psr, psrb = k.tmp()
                    P.op("pe", lambda e: e.matmul(psr[:, :n], lhsT=rt[:], rhs=qb_[:, :n], start=True, stop=True), [qbb, rtb], [psrb])
                    t1, t1b = f32r.next()
                    t2, t2b = f32r.next()
                    P.op("dve", lambda e: e.tensor_tensor(out=t1[:, :n], in0=src, in1=tab[:, ci, :n], op=ALU.mult), [srcb, tabb], [t1b])
                    P.op("dve", lambda e: e.tensor_tensor(out=t2[:, :n], in0=psr[:, :n], in1=tab[:, ci + 1, :n], op=ALU.mult), [psrb, tabb], [t2b])
                    o_, ob_ = (dst or outr).next()
                    P.op("dve", lambda e: e.tensor_tensor(out=o_[:, :n], in0=t1[:, :n], in1=t2[:, :n], op=ALU.add), [t1b, t2b], [ob_])
                    return o_, ob_

                if 'mla' not in PROJ_SECTIONS:
                    return
                W, Wb = wcols(0, 416)
                cq, cqb = cqr.next()
                for c in range(2):
                    ps, pb = proj(W, Wb, c * 128, 128)
                    copy_on(P, "dve", cq[:, c, :n], ps[:, :n], [pb], [cqb])
                if MLA_STEP <= 1:
                    return
                r, rb = rms_rstd([(cq[:, 0, :n], cqb), (cq[:, 1, :n], cqb)], 256, ones_b, b_ones)
                cqn, cqnb = cqnr.next()
                for c in range(2):
                    P.op("dve", lambda e, c=c: e.scalar_tensor_tensor(out=cqn[:, c, :n], in0=cq[:, c, :n], scalar=pv[:, c:c + 1], in1=r[:, :n], op0=ALU.mult, op1=ALU.mult),
                         [cqb, rb, b_pv], [cqnb])
                if MLA_STEP <= 2:
                    return
                ps, pb = proj(W, Wb, 256, 128)
                ckv, ckvb = f32r.next()
                copy_on(P, "dve", ckv[:, :n], ps[:, :n], [pb], [ckvb])
                r2, r2b = rms_rstd([(ckv[:, :n], ckvb)], 128, ones_b, b_ones)
                ckvn, ckvnb = ckvn_r.next()
                P.op("dve", lambda e: e.scalar_tensor_tensor(out=ckvn[:, :n], in0=ckv[:, :n], scalar=pv[:, 2:3], in1=r2[:, :n], op0=ALU.mult, op1=ALU.mult),
                     [ckvb, r2b, b_pv], [ckvnb])
                if MLA_STEP <= 3:
                    return
                pskr, pskrb = k.tmp()
                mm_group(P, pskr[:, :n], pskrb, [(WkrP[:, kc, :], u[:, kc, :n]) for kc in range(8)], [b_WkrP, ub])
                krr, krrb = rope(pskr[:, :n], pskrb, 0, rtM, b_rtM, krr_r)
                if MLA_STEP <= 4:
                    return
                for h in range(NHEADS_DBG):
                    if 'Q' not in SKIP:
                        psq, psqb = k.tmp()
                        mm_group(P, psq[:, :n], psqb, [(WuqP[:, kc, h, :], cqn[:, kc, :n]) for kc in range(2)], [b_WuqP, cqnb])
                        if 'R' in SKIP:
                            qo, qob = outr.next()
                            copy_on(P, "act", qo[:, :n], psq[:, :n], [psqb], [qob])
                        else:
                            qo, qob = rope(psq[:, :n], psqb, 0, rtM, b_rtM, None)
                        if 'S' not in SKIP:
                            store(qA[h, :, t0:t0 + n], qo[:, :n], qob)
                    if 'K' not in SKIP:
                        psk, pskb = k.tmp()
                        mm_group(P, psk[:, :n], pskb, [(WukP[:, h, :], ckvn[:, :n]), (ident_b[:], krr[:, :n])], [b_WukP, ckvnb, b_identb, krrb])
                        ko, kob = outr.next()
                        copy_on(P, "act", ko[:, :n], psk[:, :n], [pskb], [kob])
                        if 'S' not in SKIP:
                            store(kA[h, :, t0:t0 + n], ko[:, :n], kob)
                if MLA_STEP <= 5:
                    return
                for jc in range(nchunk):
                    psv, psvb = k.tmp()
                    P.op("pe", lambda e, jc=jc, psv=psv: e.matmul(psv[:, :], lhsT=ckvn[:, jc * 128:(jc + 1) * 128], rhs=WukV[:].rearrange("p h d -> p (h d)"), start=True, stop=True),
                         [ckvnb, b_WukV], [psvb])
                    va, vab = vaA.next()
                    copy_on(P, "dve", va[:, :, 0:64], psv[:, :].rearrange("p (h d) -> p h d", d=64), [psvb], [vab])
                    store(vA[t0 // 128 + jc], va[:], vab)

                if 'gqa' not in PROJ_SECTIONS:
                    return
                def normrope(ps, pb, gcol):
                    pc, pcb = f32r.next()
                    copy_on(P, "dve", pc[:, :n], ps[:, :n], [pb], [pcb])
                    r_, rb_ = rms_rstd([(pc[:, :n], pcb)], 64, bones_b, b_bones)
                    P.op("dve", lambda e: e.scalar_tensor_tensor(out=pc[:, :n], in0=pc[:, :n], scalar=pv[:, gcol:gcol + 1], in1=r_[:, :n], op0=ALU.mult, op1=ALU.mult),
                         [pcb, rb_, b_pv], [pcb])
                    return rope(pc[:, :n], pcb, 2, rtG, b_rtG, None)

                W, Wb = wcols(O_CQQ, 512)
                for c in range(4):
                    ps, pb = proj(W, Wb, c * 128, 128)
                    qo, qob = normrope(ps, pb, 3)
                    store(qC[c, :, t0:t0 + n], qo[:, :n], qob)
                for kt in range(4):
                    ps, pb = k.tmp()
                    mm_group(P, ps[:, :n], pb, [(WckP[:, kc, kt, :], u[:, kc, :n]) for kc in range(8)], [b_WckP, ub])
                    ko, kob = normrope(ps, pb, 4)
                    store(kC[kt, :, t0:t0 + n], ko[:, :n], kob)
                W, Wb = wcols(O_CV, 128)
                for jc in range(nchunk):
                    psv, psvb = k.tmp()
                    mm_group(P, psv[:, :128], psvb, [(u[:, kc, jc * 128:(jc + 1) * 128], W[:, kc, 0:128]) for kc in range(8)], [ub, Wb])
                    va, vab = vaC.next()
                    copy_on(P, "dve", va[:, :, 0:64], psv[:, 0:128].rearrange("p (h d) -> p h d", d=64), [psvb], [vab])
                    store(vC[t0 // 128 + jc], va[:], vab)

                if 'gla' not in PROJ_SECTIONS:
                    return
                W, Wb = wcols(O_BQ, 512)
                for c in range(2):
                    ps, pb = proj(W, Wb, c * 128, 128)
                    o_, ob_ = outr.next()
                    P.op("act", lambda e, o_=o_, ps=ps: e.activation(out=o_[:, :n], in_=ps[:, :n], func=AF.Copy, scale=0.125), [pb], [ob_])
                    store(bqT[c, :, t0:t0 + n], o_[:, :n], ob_)
                    ps, pb = proj(W, Wb, 256 + c * 128, 128)
                    o_, ob_ = outr.next()
                    copy_on(P, "dve", o_[:, :n], ps[:, :n], [pb], [ob_])
                    store(bkT[c, :, t0:t0 + n], o_[:, :n], ob_)
                for jc in range(nchunk):
                    ps, pb = k.tmp()
                    mm_group(P, ps[:, :256], pb, [(u[:, kc, jc * 128:(jc + 1) * 128], W[:, kc, 256:512]) for kc in range(8)], [ub, Wb])
                    o_, ob_ = outr.next()
                    copy_on(P, "act", o_[:, :256], ps[:, :256], [pb], [ob_])
                    store(bk_tok[t0 + jc * 128:t0 + (jc + 1) * 128, :], o_[:, :256], ob_)
                W, Wb = wcols(O_BV, 512)
                for jc in range(nchunk):
                    ps, pb = k.tmp()
                    mm_group(P, ps[:, :], pb, [(u[:, kc, jc * 128:(jc + 1) * 128], W[:, kc, 0:512]) for kc in range(8)], [ub, Wb])
                    o_, ob_ = outr.next()
                    copy_on(P, "dve", o_[:, :], ps[:, :], [pb], [ob_])
                    store(bv_tok[t0 + jc * 128:t0 + (jc + 1) * 128, :], o_[:, :], ob_)
                W, Wb = wcols(O_LR, 32)
                for d in range(2):
                    ps, pb = k.tmp()
                    mm_group(P, ps[:16, :n], pb, [(W[:, kc, d * 16:(d + 1) * 16], u[:, kc, :n]) for kc in range(8)], [Wb, ub])
                    lr, lrb = lrr.next()
                    copy_on(P, "act", lr[:, :n], ps[:16, :n], [pb], [lrb])
                    for jc in range(nchunk):
                        ps2, pb2 = k.tmp()
                        P.op("pe", lambda e, ps2=ps2, lr=lr, jc=jc, d=d: e.matmul(ps2[:, :256], lhsT=lr[:, jc * 128:(jc + 1) * 128], rhs=wgate[:, d, :], start=True, stop=True),
                             [lrb, b_wgate], [pb2])
                        t1, t1b = f32r.next()
                        P.op("dve", lambda e, t1=t1, ps2=ps2, d=d: e.tensor_tensor(out=t1[:, :256], in0=ps2[:, :256], in1=biasb[:, d * 256:(d + 1) * 256], op=ALU.add), [pb2, b_biasb], [t1b])
                        P.op("act", lambda e, t1=t1: e.activation(out=t1[:, :256], in_=t1[:, :256], func=AF.Exp, scale=-1.0), [t1b], [t1b])
                        P.op("act", lambda e, t1=t1: e.activation(out=t1[:, :256], in_=t1[:, :256], func=AF.Ln, bias=1.0, scale=1.0), [t1b], [t1b])
                        o_, ob_ = outr.next()
                        P.op("dve", lambda e, t1=t1, o_=o_: e.tensor_scalar(out=o_[:, :256], in0=t1[:, :256], scalar1=-1.0 / 16.0, scalar2=None, op0=ALU.mult), [t1b], [ob_])
                        store(la_d[d, t0 + jc * 128:t0 + (jc + 1) * 128, :], o_[:, :256], ob_)
                W, Wb = wcols(O_OG, 512)
                for c in range(4):
                    ps, pb = proj(W, Wb, c * 128, 128)
                    o_, ob_ = outr.next()
                    P.op("act", lambda e, o_=o_, ps=ps: e.activation(out=o_[:, :n], in_=ps[:, :n], func=AF.Silu), [pb], [ob_])
                    store(ogT[c, :, t0:t0 + n], o_[:, :n], ob_)
                if 'hy' not in PROJ_SECTIONS:
                    return
                for pp in range(3):
                    W, Wb = wcols(O_DP + pp * 512, 512)
                    for c in range(4):
                        ps, pb = proj(W, Wb, c * 128, 128)
                        o_, ob_ = outr.next()
                        copy_on(P, k.ev(), o_[:, :n], ps[:, :n], [pb], [ob_])
                        store(dpT[pp * 4 + c, :, t0:t0 + n], o_[:, :n], ob_)

            for ti, (t0, n) in enumerate(TILES[:PROJ_TILES]):
                do_tile(ti, t0, n)
                P.barrier()
            P.barrier()

        if stop == "proj" and l == 0:
            return finish_list(k, st, dbg_d, dbg_items(locals()))

        with ExitStack() as S:
            NB = lambda: Buf("x")
            Kr = Ring(k, S, "Kr", 2, [128, NT], BF16)
            Vr = Ring(k, S, "Vr", 2, [128, 34, 128], BF16)
            qr = Ring(k, S, "qr", 3, [128, 512], BF16)
            pr = Ring(k, S, "pr", 7, [128, 512], BF16)
            rcr = Ring(k, S, "rcr", 2, [128, 512], F32)
            orr = Ring(k, S, "orr", 3, [128, 512], BF16)
            for br in (() if SKIP_ATTN else (0, 2)):
                scale = (96.0 ** -0.5) if br == 0 else 0.125
                for h in range(8):
                    Kt, Kb = Kr.next()
                    Vt, Vb = Vr.next()
                    if br == 0:
                        ksrc = kA[h]; vsrc = vA[:, :, h, :]; qsrc = qA[h]
                    else:
                        g = h // 4
                        ksrc = kC[g * 2 + (h % 2)]; vsrc = vC[:, :, g, :]; qsrc = qC[h // 2]
                    P.dma(Kt[:], ksrc, writes=[Kb])
                    P.dma(Vt[:], vsrc.rearrange("c p d -> p c d"), writes=[Vb])
                    for ti, (t0, n) in enumerate(TILES):
                        if t0 >= L:
                            if not need_ctx:
                                continue
                            kcs = list(range(32, 34))
                        else:
                            kcs = list(range(34))
                        qt, qb_ = qr.next()
                        P.dma(qt[:, :n], qsrc[:, t0:t0 + n], writes=[qb_])
                        psO, psOb = k.acc()
                        pend = []

                        def emit_pv(item, psO=psO, psOb=psOb, Vt=Vt, Vb=Vb, n=n, kcs=kcs):
                            i_, kc, pt, ptb = item
                            P.op("pe", lambda e: e.matmul(psO[:, :n], lhsT=Vt[:, kc, :], rhs=pt[:, :n], start=(i_ == 0), stop=(i_ == len(kcs) - 1)),
                                 [Vb, ptb], [psOb])
                        for i_, kc in enumerate(kcs):
                            psS, psSb = k.tmp()
                            P.op("pe", lambda e, psS=psS, Kt=Kt, kc=kc, qt=qt, n=n: e.matmul(psS[:, :n], lhsT=Kt[:, kc * 128:(kc + 1) * 128], rhs=qt[:, :n], start=True, stop=True),
                                 [Kb, qb_], [psSb])
                            pt, ptb = pr.next()
                            P.op("act", lambda e, pt=pt, psS=psS, n=n, scale=scale: e.activation(out=pt[:, :n], in_=psS[:, :n], func=AF.Exp, scale=scale), [psSb], [ptb])
                            pend.append((i_, kc, pt, ptb))
                            if len(pend) > 3:
                                emit_pv(pend.pop(0))
                        while pend:
                            emit_pv(pend.pop(0))
                        rc, rcb = rcr.next()
                        P.op("dve", lambda e, rc=rc, psO=psO, n=n: e.reciprocal(out=rc[64:128, :n], in_=psO[64:128, :n]), [psOb], [rcb])
                        ot, otb = orr.next()
                        P.op("dve", lambda e, ot=ot, psO=psO, rc=rc, n=n: e.tensor_tensor(out=ot[0:64, :n], in0=psO[0:64, :n], in1=rc[64:128, :n], op=ALU.mult), [psOb, rcb], [otb])
                        P.dma(oT[br, h // 2, (h % 2) * 64:(h % 2) * 64 + 64, t0:t0 + n], ot[0:64, :n], reads=[otb], writes=[NB()], eng="pool")
            P.barrier()

        if stop == "attn" and l == 0:
            return finish_list(k, st, dbg_d, dbg_items(locals()))

        with ExitStack() as S:
            NB = lambda: Buf("x")
            msk, b_msk = k.sb(S, "msk", [64, 4, 64], F32)
            msk4, b_msk4 = k.sb(S, "msk4", [64, 2, 4, 64], F32)
            hmask, b_hmask = k.sb(S, "hmask", [128, 2], F32)
            P.op("dve", lambda e: e.memset(hmask[:], 0.0), [], [b_hmask])
            P.op("dve", lambda e: e.memset(hmask[0:64, 0:1], 1.0), [b_hmask], [b_hmask])
            P.op("dve", lambda e: e.memset(hmask[64:128, 1:2], 1.0), [b_hmask], [b_hmask])
            for m_ in range(2):
                for h_ in range(4):
                    P.dma(msk4[:, m_, h_, :], I["k_m_le" if m_ == 0 else "k_m_ge"], writes=[b_msk4])
            mskb, b_mskb = k.sb(S, "mskb", [64, 4, 64], BF16)
            for i_, nm in enumerate(("k_m_le", "k_m_ge", "k_m_gt", "k_m_lt")):
                P.dma(msk[:, i_, :], I[nm], writes=[b_msk])
                P.dma(mskb[:, i_, :], I[nm], writes=[b_mskb], eng="pool")
            gng, b_gng = k.sb(S, "gng", [128, 1], F32)
            P.dma(gng[:], I["gla_norm_g"][:, l:l + 1], writes=[b_gng])
            Sst = [k.sb(S, "Sst%d" % i, [128, 128], F32) for i in range(2)]
            Sbf = [k.sb(S, "Sbf%d" % i, [128, 128], BF16) for i in range(2)]
            lar = Ring(k, S, "la", 2, [64, 8, 256], BF16)
            bktr = Ring(k, S, "bkt", 2, [64, 8, 256], BF16)
            bvr = Ring(k, S, "bv", 2, [64, 8, 512], BF16)
            bqr = Ring(k, S, "bq", 4, [128, 512], BF16)
            bkr = Ring(k, S, "bk", 4, [128, 512], BF16)
            Er = Ring(k, S, "E", 8, [128, 64], F32)
            qdr = Ring(k, S, "qd", 12, [128, 64], BF16)
            Abr = Ring(k, S, "Ab", 3, [64, 256], BF16)
            ker = Ring(k, S, "ke", 3, [64, 256], F32)
            kebr = Ring(k, S, "keb", 3, [64, 256], BF16)
            ogr = Ring(k, S, "og", 2, [128, 4, 512], F32)
            sqr = Ring(k, S, "gsq", 3, [128, 512], BF16)
            rsr = Ring(k, S, "grs", 2, [128, 512], F32)
            ogtr = Ring(k, S, "ogt", 2, [128, 512], BF16)
            outr2 = Ring(k, S, "gout", 3, [128, 512], BF16)
            tmpf = Ring(k, S, "gtmp", 2, [128, 512], F32)

            def gla_chunk(dirn, la, lab, bkt, bktb, bv, bvb, bq, bk, ci, og, ogb):
                mi = 0 if dirn == 0 else 1
                mk = 2 if dirn == 0 else 3
                last = 63 if dirn == 0 else 0
                qd = []
                decs = []
                for pr in range(2):
                    psb, psbb = k.tmp()
                    P.op("pe", lambda e, psb=psb, pr=pr: e.matmul(psb[:, :64], lhsT=la[:, ci, pr * 128:(pr + 1) * 128], rhs=mskb[:, mi, :], start=True, stop=True),
                         [lab, b_mskb], [psbb])
                    E, Eb = Er.next()
                    Ei, Eib = Er.next()
                    P.op("act", lambda e, E=E, psb=psb: e.activation(out=E[:, :], in_=psb[:, :64], func=AF.Exp), [psbb], [Eb])
                    P.op("act", lambda e, Ei=Ei, psb=psb: e.activation(out=Ei[:, :], in_=psb[:, :64], func=AF.Exp, scale=-1.0), [psbb], [Eib])
                    qt0, qt0b = qdr.next()
                    qt1, qt1b = qdr.next()
                    kt, ktb = qdr.next()
                    bqt, bqb = bq[pr]
                    bkt_, bkb_ = bk[pr]
                    for hh_, (qt_, qtb_) in enumerate(((qt0, qt0b), (qt1, qt1b))):
                        P.op("dve", lambda e, qt_=qt_, bqt=bqt, E=E, hh_=hh_: e.scalar_tensor_tensor(out=qt_[:, :], in0=bqt[:, ci * 64:(ci + 1) * 64], scalar=hmask[:, hh_:hh_ + 1], in1=E[:, :],
                                                                                                     op0=ALU.mult, op1=ALU.mult), [bqb, Eb, b_hmask], [qtb_])
                    P.op("dve", lambda e, kt=kt, bkt_=bkt_, Ei=Ei: e.tensor_tensor(out=kt[:, :], in0=bkt_[:, ci * 64:(ci + 1) * 64], in1=Ei[:, :], op=ALU.mult), [bkb_, Eib], [ktb])
                    qd.append(((qt0, qt0b), (qt1, qt1b), kt, ktb))
                    decs.append((E, Eb))
                psA, psAb = k.tmp()
                for h in range(4):
                    q0_, q1_, kt, ktb = qd[h // 2]
                    qt, qtb = (q0_, q1_)[h % 2]
                    P.op("pe", lambda e, h=h, qt=qt, kt=kt: e.matmul(psA[:64, h * 64:(h + 1) * 64], lhsT=kt[:, :], rhs=qt[:, :], start=True, stop=True),
                         [qtb, ktb], [psAb])
                Ab, Abb = Abr.next()
                P.op("dve", lambda e: e.tensor_tensor(out=Ab[:, :], in0=psA[:64, :256], in1=msk4[:, mi, :, :].rearrange("p h i -> p (h i)"), op=ALU.mult), [psAb, b_msk4], [Abb])
                pso, psob = k.tmp()
                for h in range(4):
                    q0_, q1_, kt, ktb = qd[h // 2]
                    qt, qtb = (q0_, q1_)[h % 2]
                    Sb_, Sbb_ = Sbf[h // 2]

                    def fn(e, h=h, qt=qt, Sb_=Sb_):
                        e.matmul(pso[:, h * 64:(h + 1) * 64], lhsT=bv[:, ci, h * 128:(h + 1) * 128], rhs=Ab[:, h * 64:(h + 1) * 64], start=True, stop=False)
                        return e.matmul(pso[:, h * 64:(h + 1) * 64], lhsT=Sb_[:, :], rhs=qt[:, :], start=False, stop=True)
                    P.op("pe", fn, [bvb, Abb, Sbb_, qtb], [psob])
                ogv = og[:, :, ci * 64:(ci + 1) * 64]
                if dirn == 0:
                    P.op("act", lambda e: e.copy(out=ogv, in_=pso[:, :256].rearrange("p (h i) -> p h i", h=4)), [psob], [ogb])
                else:
                    P.op("dve", lambda e: e.tensor_tensor(out=ogv, in0=ogv, in1=pso[:, :256].rearrange("p (h i) -> p h i", h=4), op=ALU.add), [psob, ogb], [ogb])
                psR, psRb = k.tmp()
                P.op("pe", lambda e: e.matmul(psR[:64, :256], lhsT=mskb[:, mk, :], rhs=la[:, ci, :], start=True, stop=True), [lab, b_mskb], [psRb])
                ke, keb = ker.next()
                P.op("act", lambda e: e.activation(out=ke[:, :], in_=psR[:64, :256], func=AF.Exp), [psRb], [keb])
                kb_, kbb_ = kebr.next()
                P.op("dve", lambda e: e.tensor_tensor(out=kb_[:, :], in0=ke[:, :], in1=bkt[:, ci, :], op=ALU.mult), [keb, bktb], [kbb_])
                for pr in range(2):
                    E, Eb = decs[pr]
                    St, Stb = Sst[pr]
                    Sb_, Sbb_ = Sbf[pr]
                    for hh in range(2):
                        h = pr * 2 + hh
                        psU, psUb = k.tmp()
                        P.op("pe", lambda e, psU=psU, pr=pr, h=h: e.matmul(psU[:, :128], lhsT=kb_[:, pr * 128:(pr + 1) * 128], rhs=bv[:, ci, h * 128:(h + 1) * 128], start=True, stop=True),
                             [kbb_, bvb], [psUb])
                        r0 = hh * 64
                        P.op("dve", lambda e, psU=psU, r0=r0, St=St, E=E: e.scalar_tensor_tensor(out=St[r0:r0 + 64, :], in0=St[r0:r0 + 64, :], scalar=E[r0:r0 + 64, last:last + 1],
                                                                                                   in1=psU[r0:r0 + 64, :128], op0=ALU.mult, op1=ALU.add),
                             [psUb, Stb, Eb], [Stb])
                    P.op("act", lambda e, St=St, Sb_=Sb_: e.copy(out=Sb_[:, :], in_=St[:, :]), [Stb], [Sbb_])

            def finalize(og, ogb, t0, n):
                for h in range(4):
                    sq, sqb = sqr.next()
                    P.op("act", lambda e, sq=sq, h=h: e.activation(out=sq[:, :n], in_=og[:, h, :n], func=AF.Square), [ogb], [sqb])
                    pss, pssb = k.acc()
                    P.op("pe", lambda e, sq=sq, pss=pss: e.matmul(pss[:, :n], lhsT=ones_b[:], rhs=sq[:, :n], start=True, stop=True), [sqb, b_ones], [pssb])
                    rs, rsb = rsr.next()
                    P.op("act", lambda e, rs=rs, pss=pss: e.activation(out=rs[:, :n], in_=pss[:, :n], func=AF.Sqrt, bias=epsv[:, 0:1], scale=1.0 / 128), [pssb, b_eps], [rsb])
                    P.op("dve", lambda e, rs=rs: e.reciprocal(out=rs[:, :n], in_=rs[:, :n]), [rsb], [rsb])
                    ogt, ogtb = ogtr.next()
                    P.dma(ogt[:, :n], ogT[h, :, t0:t0 + n], writes=[ogtb])
                    tf, tfb = tmpf.next()
                    P.op("dve", lambda e, tf=tf, h=h, rs=rs: e.scalar_tensor_tensor(out=tf[:, :n], in0=og[:, h, :n], scalar=gng[:, 0:1], in1=rs[:, :n], op0=ALU.mult, op1=ALU.mult),
                         [ogb, rsb, b_gng], [tfb])
                    ot_, otb_ = outr2.next()
                    P.op("dve", lambda e, tf=tf, ot_=ot_, ogt=ogt: e.tensor_tensor(out=ot_[:, :n], in0=tf[:, :n], in1=ogt[:, :n], op=ALU.mult), [tfb, ogtb], [otb_])
                    P.dma(oT[1, h, :, t0:t0 + n], ot_[:, :n], reads=[otb_], writes=[NB()], eng="pool")

            def gla_group(dirn, t0, n, want_out):
                nch = n // 64
                la, lab = lar.next()
                bkt, bktb = bktr.next()
                bv, bvb = bvr.next()
                P.dma(la[:, :nch, :], la_d[dirn, t0:t0 + n, :].rearrange("(c j) d -> j c d", j=64), writes=[lab])
                P.dma(bkt[:, :nch, :], bk_tok[t0:t0 + n, :].rearrange("(c j) d -> j c d", j=64), writes=[bktb])
                P.dma(bv[:, :nch, :], bv_tok[t0:t0 + n, :].rearrange("(c j) d -> j c d", j=64), writes=[bvb])
                bq, bk = [], []
                for pr in range(2):
                    t_, b_ = bqr.next()
                    P.dma(t_[:, :n], bqT[pr, :, t0:t0 + n], writes=[b_])
                    bq.append((t_, b_))
                    t_, b_ = bkr.next()
                    P.dma(t_[:, :n], bkT[pr, :, t0:t0 + n], writes=[b_])
                    bk.append((t_, b_))
                og, ogb = ogr.next()
                if dirn == 1 and want_out:
                    P.dma(og[:, :, :n], ofT[:, :, t0:t0 + n].rearrange("h p t -> p h t"), writes=[ogb])
                order = range(nch) if dirn == 0 else range(nch - 1, -1, -1)
                for ci in order:
                    gla_chunk(dirn, la, lab, bkt, bktb, bv, bvb, bq, bk, ci, og, ogb)
                if want_out:
                    if dirn == 0:
                        P.dma(ofT[:, :, t0:t0 + n].rearrange("h p t -> p h t"), og[:, :, :n], reads=[ogb], writes=[NB()], eng="pool")
                    else:
                        finalize(og, ogb, t0, n)

            for dirn in (() if SKIP_GLA else range(2)):
                for (St, Stb), (Sb_, Sbb_) in zip(Sst, Sbf):
                    P.op("dve", lambda e, St=St: e.memset(St[:], 0.0), [], [Stb])
                    P.op("dve", lambda e, Sb_=Sb_: e.memset(Sb_[:], 0.0), [], [Sbb_])
                gla_group(dirn, L, CT, need_ctx)
                groups = [(i * 512, 512) for i in range(8)]
                if dirn == 1:
                    groups = groups[::-1]
                if dirn == 1:
                    P.barrier()
                for (t0, n) in groups:
                    gla_group(dirn, t0, n, True)
            P.barrier()

        if stop == "gla" and l == 0:
            return finish_list(k, st, dbg_d, dbg_items(locals()))

        TWO_PI = 2.0 * math.pi
        segs = [(0, L, "L")] + ([(L, CT, "C")] if need_ctx else [])
        def hy_segment(toff, ns, tag):
            nblk = ns // 128
            tw = min(512, ns)
            ntt = ns // tw
            with ExitStack() as S:
                w1s, b_w1s = k.sb(S, "hw1", [33, 64], F32)
                w2s, b_w2s = k.sb(S, "hw2", [64, 64], F32)
                w3s, b_w3s = k.sb(S, "hw3", [64, 2048], F32)
                bb, b_bb = k.sb(S, "hbb", [64, 4], F32)
                P.dma(w1s[:], I["hy_filt_w1"][l], writes=[b_w1s])
                P.dma(w2s[:], I["hy_filt_w2"][l], writes=[b_w2s])
                P.dma(w3s[:], I["hy_filt_w3"][l], writes=[b_w3s])
                P.dma(bb[:, 0:1], I["hy_filt_b1"][:, l:l + 1], writes=[b_bb])
                P.dma(bb[:, 1:2], I["hy_filt_b2"][:, l:l + 1], writes=[b_bb])
                P.op("dve", lambda e: e.memset(bb[:, 2:3], -math.pi), [b_bb], [b_bb])
                ftr = Ring(k, S, "feat", 2, [33, 512], F32)
                h1r = Ring(k, S, "h1", 2, [64, 512], F32)
                h2r = Ring(k, S, "h2", 2, [64, 512], F32)
                decr = Ring(k, S, "dec", 2, [128, 2, 512], F32)
                hfr = Ring(k, S, "hf", 4, [128, 512], F32)
                hor = Ring(k, S, "ho", 4, [128, 512], BF16)

                kir = Ring(k, S, "ki", 2, [64, 512], mybir.dt.int32)
                kfr = Ring(k, S, "kf", 4, [64, 512], F32)

                def sin_layer(ps, pb, bcol, out, outb, w):
                    u_, ub_ = kfr.next()
                    P.op("dve", lambda e: e.tensor_scalar(out=u_[:, :w], in0=ps[:64, :w], scalar1=bb[:, bcol:bcol + 1], scalar2=1.0 / TWO_PI, op0=ALU.add, op1=ALU.mult), [pb, b_bb], [ub_])
                    ki, kib = kir.next()
                    P.op("dve", lambda e: e.tensor_copy(out=ki[:, :w], in_=u_[:, :w]), [ub_], [kib])
                    kf, kfb = kfr.next()
                    P.op("dve", lambda e: e.tensor_copy(out=kf[:, :w], in_=ki[:, :w]), [kib], [kfb])
                    P.op("dve", lambda e: e.tensor_tensor(out=u_[:, :w], in0=u_[:, :w], in1=kf[:, :w], op=ALU.subtract), [ub_, kfb], [ub_])
                    P.op("dve", lambda e: e.tensor_scalar(out=kf[:, :w], in0=u_[:, :w], scalar1=0.5, scalar2=None, op0=ALU.is_gt), [ub_], [kfb])
                    P.op("dve", lambda e: e.tensor_tensor(out=u_[:, :w], in0=u_[:, :w], in1=kf[:, :w], op=ALU.subtract), [ub_, kfb], [ub_])
                    P.op("dve", lambda e: e.tensor_scalar(out=kf[:, :w], in0=u_[:, :w], scalar1=-0.5, scalar2=None, op0=ALU.is_lt), [ub_], [kfb])
                    P.op("dve", lambda e: e.tensor_tensor(out=u_[:, :w], in0=u_[:, :w], in1=kf[:, :w], op=ALU.add), [ub_, kfb], [ub_])
                    P.op("act", lambda e: e.activation(out=out[:, :w], in_=u_[:, :w], func=AF.Sin, scale=TWO_PI), [ub_], [outb])

                def filt_tile(t0):
                    ft, ftb = ftr.next()
                    P.dma(ft[:, :tw], I["k_feat" + tag][:, t0:t0 + tw], writes=[ftb])
                    ps, pb = k.tmp()
                    P.op("pe", lambda e: e.matmul(ps[:64, :tw], lhsT=w1s[:], rhs=ft[:, :tw], start=True, stop=True), [b_w1s, ftb], [pb])
                    h1, h1b = h1r.next()
                    sin_layer(ps, pb, 0, h1, h1b, tw)
                    ps2, pb2 = k.tmp()
                    P.op("pe", lambda e: e.matmul(ps2[:64, :tw], lhsT=w2s[:], rhs=h1[:, :tw], start=True, stop=True), [b_w2s, h1b], [pb2])
                    h2, h2b = h2r.next()
                    sin_layer(ps2, pb2, 1, h2, h2b, tw)
                    for jb in range(tw // 128):
                        r0 = t0 + jb * 128
                        dc, dcb = decr.next()
                        P.dma(dc[:, 0, :], I["k_dec" + tag][r0:r0 + 128, :], writes=[dcb])
                        P.dma(dc[:, 1, :], I["k_decb" + tag][r0:r0 + 128, :], writes=[dcb])
                        for o in range(2):
                            hf_, hb_ = [], []
                            for dr in range(2):
                                ps3, pb3 = k.tmp()
                                c0 = o * 1024 + dr * 512
                                P.op("pe", lambda e, ps3=ps3, c0=c0, jb=jb: e.matmul(ps3[:, :], lhsT=h2[:, jb * 128:(jb + 1) * 128], rhs=w3s[:, c0:c0 + 512], start=True, stop=True),
                                     [h2b, b_w3s], [pb3])
                                hf, hfb = hfr.next()
                                P.op("dve", lambda e, hf=hf, ps3=ps3, dr=dr, dc=dc: e.tensor_tensor(out=hf[:, :], in0=ps3[:, :], in1=dc[:, dr, :], op=ALU.mult), [pb3, dcb], [hfb])
                                hf_.append((hf, hfb))
                            for pm, op_ in ((0, ALU.add), (1, ALU.subtract)):
                                ho, hob = hor.next()
                                P.op("pool", lambda e, ho=ho, op_=op_, a=hf_[0][0], b=hf_[1][0]: e.tensor_tensor(out=ho[:, :], in0=a[:, :], in1=b[:, :], op=op_),
                                     [hf_[0][1], hf_[1][1]], [hob])
                                P.dma(hpm[o, pm, r0:r0 + 128, :], ho[:, :], reads=[hob], writes=[Buf("x")], eng="pool")

                for tt in range(ntt):
                    filt_tile(tt * tw)
                P.barrier()
            with ExitStack() as S:
                hres, b_hres = k.sb(S, "hres", [128, 2, 32, 512], BF16)
                tbr = Ring(k, S, "tb1", 2, [128, 2, 32, 128], BF16)
                pqr = Ring(k, S, "pq", 4, [128, 512], F32)
                for o in range(2):
                    for pm in range(2):
                        P.dma(hres[:, pm, :nblk, :], hpm[o, pm, 0:ns, :].rearrange("(c p) d -> p c d", p=128), writes=[b_hres])
                    for fc in range(nblk):
                        tb, tbb = tbr.next()
                        P.dma(tb[:, 0, :nblk, :], I["k_dc1" + tag][:, fc * 128:(fc + 1) * 128].rearrange("(c p) f -> p c f", p=128), writes=[tbb])
                        P.dma(tb[:, 1, :nblk, :], I["k_ds1" + tag][:, fc * 128:(fc + 1) * 128].rearrange("(c p) f -> p c f", p=128), writes=[tbb])
                        for pm in range(2):
                            ps, pb = k.tmp()
                            mm_group(P, ps[:, :], pb, [(tb[:, pm, tc, :], hres[:, pm, tc, :]) for tc in range(nblk)], [tbb, b_hres])
                            pq, pqb = pqr.next()
                            copy_on(P, k.ev(), pq[:, :], ps[:, :], [pb], [pqb])
                            P.dma(PQ[o, pm, fc * 128:(fc + 1) * 128, :], pq[:, :], reads=[pqb], writes=[Buf("x")], eng="pool")
                P.barrier()
            with ExitStack() as S:
                scw, b_scw = k.sb(S, "scw", [128, 3, 12], F32)
                scb, b_scb = k.sb(S, "scb", [128, 12], F32)
                fbs, b_fbs = k.sb(S, "fbs", [128, 2, 4], F32)
                P.dma(scw[:], I["hy_sconv_w"][:, l, :, :], writes=[b_scw])
                P.dma(scb[:], I["hy_sconv_b"][:, l, :], writes=[b_scb])
                P.dma(fbs[:], I["hy_filt_bias"][:, l, :, :], writes=[b_fbs])
                vtok, b_vtok = k.sb(S, "vtok", [128, 32, 512], BF16)
                Yt, b_Yt = k.sb(S, "Yt", [128, 2, 32, 512], BF16)
                pr_ = Ring(k, S, "pin", 2, [128, 514], BF16)
                zr = Ring(k, S, "zf", 3, [128, 512], F32)
                zbr = Ring(k, S, "zb", 3, [128, 512], BF16)
                tb1 = Ring(k, S, "tb1b", 2, [128, 2, 32, 128], BF16)
                tb2 = Ring(k, S, "tb2", 2, [128, 2, 4, 512], BF16)
                pqr = Ring(k, S, "pq2", 2, [128, 2, 512], F32)
                ewr = Ring(k, S, "ew", 4, [128, 512], F32)
                xr_ = Ring(k, S, "x1t", 2, [128, 512], BF16)
                vr_ = Ring(k, S, "vtt", 2, [128, 512], BF16)

                def to_vtok(zf, zfb, cc, t0, w):
                    pst, ptb = k.tmp()
                    for jb in range(w // 128):
                        P.op("pe", lambda e, jb=jb: e.transpose(pst[:, jb * 128:(jb + 1) * 128], zf[:, jb * 128:(jb + 1) * 128], ident_f[:]), [zfb, b_identf], [ptb])
                    b0 = t0 // 128
                    nb_ = w // 128
                    copy_on(P, k.ev(), vtok[:, b0:b0 + nb_, cc * 128:(cc + 1) * 128], pst[:, :w].rearrange("p (b c) -> p b c", c=128), [ptb], [b_vtok])

                def sconv_tile(c, t0):
                    pin, pinb = pr_.next()
                    lo = max(t0 - 1, 0)
                    hi = min(t0 + tw + 1, ns)
                    off = lo - (t0 - 1)
                    P.dma(pin[:, off:off + (hi - lo)], dpT[c, :, toff + lo:toff + hi], writes=[pinb])
                    zf, zfb = zr.next()
                    P.op("act", lambda e: e.activation(out=zf[:, :tw], in_=pin[:, 1:tw + 1], func=AF.Identity, scale=scw[:, 1, c:c + 1], bias=scb[:, c:c + 1]), [pinb, b_scw, b_scb], [zfb])
                    a0 = 1 if t0 == 0 else 0
                    P.op("dve", lambda e: e.scalar_tensor_tensor(out=zf[:, a0:tw], in0=pin[:, a0:tw], scalar=scw[:, 0, c:c + 1], in1=zf[:, a0:tw], op0=ALU.mult, op1=ALU.add), [pinb, b_scw, zfb], [zfb])
                    a1 = tw - 1 if t0 + tw >= ns else tw
                    P.op("dve", lambda e: e.scalar_tensor_tensor(out=zf[:, 0:a1], in0=pin[:, 2:a1 + 2], scalar=scw[:, 2, c:c + 1], in1=zf[:, 0:a1], op0=ALU.mult, op1=ALU.add), [pinb, b_scw, zfb], [zfb])
                    zb, zbb = zbr.next()
                    copy_on(P, "act", zb[:, :tw], zf[:, :tw], [zfb], [zbb])
                    P.dma(zT[c, :, toff + t0:toff + t0 + tw], zb[:, :tw], reads=[zbb], writes=[Buf("x")], eng="pool")
                    if c >= 8:
                        to_vtok(zf, zfb, c - 8, t0, tw)

                for c in range(12):
                    for tt in range(ntt):
                        sconv_tile(c, tt * tw)
                P.barrier()

                bank = [(k.ps_t[i], k.ps_b[i]) for i in range(8)]
                for o in range(2):
                    for fc in range(nblk):
                        tb, tbb = tb1.next()
                        P.dma(tb[:, 0, :nblk, :], I["k_dc1" + tag][:, fc * 128:(fc + 1) * 128].rearrange("(c p) f -> p c f", p=128), writes=[tbb])
                        P.dma(tb[:, 1, :nblk, :], I["k_ds1" + tag][:, fc * 128:(fc + 1) * 128].rearrange("(c p) f -> p c f", p=128), writes=[tbb])
                        pq, pqb = pqr.next()
                        P.dma(pq[:, 0, :], PQ[o, 0, fc * 128:(fc + 1) * 128, :], writes=[pqb])
                        P.dma(pq[:, 1, :], PQ[o, 1, fc * 128:(fc + 1) * 128, :], writes=[pqb])
                        psA, pAb = k.tmp()
                        psB, pBb = k.tmp()
                        mm_group(P, psA[:, :], pAb, [(tb[:, 0, tc, :], vtok[:, tc, :]) for tc in range(nblk)], [tbb, b_vtok])
                        mm_group(P, psB[:, :], pBb, [(tb[:, 1, tc, :], vtok[:, tc, :]) for tc in range(nblk)], [tbb, b_vtok])
                        e1, e1b = ewr.next(); e2, e2b = ewr.next(); e3, e3b = ewr.next(); e4, e4b = ewr.next()
                        P.op("dve", lambda e, e1=e1, pq=pq, psA=psA: e.tensor_tensor(out=e1[:, :], in0=psA[:, :], in1=pq[:, 0, :], op=ALU.mult), [pAb, pqb], [e1b])
                        P.op("dve", lambda e, e2=e2, pq=pq, psB=psB: e.tensor_tensor(out=e2[:, :], in0=psB[:, :], in1=pq[:, 1, :], op=ALU.mult), [pBb, pqb], [e2b])
                        P.op("dve", lambda e, e3=e3, pq=pq, psB=psB: e.tensor_tensor(out=e3[:, :], in0=psB[:, :], in1=pq[:, 0, :], op=ALU.mult), [pBb, pqb], [e3b])
                        P.op("dve", lambda e, e4=e4, pq=pq, psA=psA: e.tensor_tensor(out=e4[:, :], in0=psA[:, :], in1=pq[:, 1, :], op=ALU.mult), [pAb, pqb], [e4b])
                        P.op("pool", lambda e, e1=e1, e2=e2, fc=fc: e.tensor_tensor(out=Yt[:, 0, fc, :], in0=e1[:, :], in1=e2[:, :], op=ALU.subtract), [e1b, e2b], [b_Yt])
                        P.op("pool", lambda e, e3=e3, e4=e4, fc=fc: e.tensor_tensor(out=Yt[:, 1, fc, :], in0=e3[:, :], in1=e4[:, :], op=ALU.add), [e3b, e4b], [b_Yt])
                    for tt in range(ntt):
                        t0 = tt * tw
                        grp = bank[0:4] if tt % 2 == 0 else bank[4:8]
                        for f4 in range(0, nblk, 4):
                            nf = min(4, nblk - f4)
                            t2, t2b = tb2.next()
                            P.dma(t2[:, 0, :nf, :tw], I["k_dc2" + tag][f4 * 128:(f4 + nf) * 128, t0:t0 + tw].rearrange("(c p) t -> p c t", p=128), writes=[t2b])
                            P.dma(t2[:, 1, :nf, :tw], I["k_ds2" + tag][f4 * 128:(f4 + nf) * 128, t0:t0 + tw].rearrange("(c p) t -> p c t", p=128), writes=[t2b])
                            for cc in range(4):
                                ps, pb = grp[cc]

                                def fn(e, ps=ps, cc=cc, f4=f4, nf=nf, t2=t2):
                                    ins = None
                                    for fi in range(nf):
                                        fc = f4 + fi
                                        for cs in range(2):
                                            ins = e.matmul(ps[:, :tw], lhsT=Yt[:, cs, fc, cc * 128:(cc + 1) * 128], rhs=t2[:, cs, fi, :tw],
                                                           start=(fc == 0 and cs == 0), stop=(fc == nblk - 1 and cs == 1))
                                    return ins
                                P.op("pe", fn, [b_Yt, t2b], [pb])
                        for cc in range(4):
                            ps, pb = grp[cc]
                            xsrc = zT[(0 if o == 0 else 4) + cc, :, toff + t0:toff + t0 + tw]
                            vsrc = zT[8 + cc, :, toff + t0:toff + t0 + tw]
                            x1, x1b = xr_.next()
                            vv, vvb = vr_.next()
                            P.dma(x1[:, :tw], xsrc, writes=[x1b])
                            P.dma(vv[:, :tw], vsrc, writes=[vvb])
                            zf, zfb = zr.next()
                            P.op("dve", lambda e, zf=zf, vv=vv, ps=ps, cc=cc, o=o: e.scalar_tensor_tensor(out=zf[:, :tw], in0=vv[:, :tw], scalar=fbs[:, o, cc:cc + 1], in1=ps[:, :tw], op0=ALU.mult, op1=ALU.add),
                                 [vvb, b_fbs, pb], [zfb])
                            P.op("dve", lambda e, zf=zf, x1=x1: e.tensor_tensor(out=zf[:, :tw], in0=zf[:, :tw], in1=x1[:, :tw], op=ALU.mult), [zfb, x1b], [zfb])
                            zb, zbb = zbr.next()
                            copy_on(P, "act", zb[:, :tw], zf[:, :tw], [zfb], [zbb])
                            if o == 0:
                                P.dma(zT[8 + cc, :, toff + t0:toff + t0 + tw], zb[:, :tw], reads=[zbb, vvb], writes=[Buf("x")], eng="pool")
                            else:
                                P.dma(oT[3, cc, :, toff + t0:toff + t0 + tw], zb[:, :tw], reads=[zbb], writes=[Buf("x")], eng="pool")
                            if o == 0:
                                pass
                        if o == 0:
                            pass
                    if o == 0:
                        P.barrier()
                        for cc in range(4):
                            for tt in range(ntt):
                                t0 = tt * tw
                                vv, vvb = vr_.next()
                                P.dma(vv[:, :tw], zT[8 + cc, :, toff + t0:toff + t0 + tw], writes=[vvb])
                                zf, zfb = zr.next()
                                copy_on(P, "dve", zf[:, :tw], vv[:, :tw], [vvb], [zfb])
                                to_vtok(zf, zfb, cc, t0, tw)
                P.barrier()

        for (toff_, ns_, tag_) in segs:
            hy_segment(toff_, ns_, tag_)

        if stop == "hyena" and l == 0:
            return finish_list(k, st, dbg_d, dbg_items(locals()))

        with ExitStack() as S:
            wbr_sb, b_wbr = k.sb(S, "wbr", [128, 16, 1024], BF16)
            wout_sb, b_wout = k.sb(S, "wout", [128, 8, 1024], BF16)
            P.dma(wbr_sb[:], wbr_b[0].rearrange("a (kc p) n -> p (a kc) n", p=128), reads=[wbr_b[1]], writes=[b_wbr])
            P.dma(wout_sb[:], wout_b[0].rearrange("(kc p) n -> p kc n", p=128), reads=[wout_b[1]], writes=[b_wout])
            xr = Ring(k, S, "xt", 1, [128, 8, 512], F32)
            xnr = Ring(k, S, "xn", 1, [128, 8, 512], BF16)
            ur = Ring(k, S, "u", 1, [128, 8, 512], BF16)
            otr = Ring(k, S, "ot", 1, [128, 16, 512], BF16)
            wgr_ = Ring(k, S, "wgt", 2, [128, 8, 1024], BF16)
            maccr = Ring(k, S, "macc", 1, [128, 8, 512], F32)
            sigr = Ring(k, S, "sig", 3, [128, 512], F32)

            def merge_tile(ti, t0, n):
                j = 0 if t0 < L else 1
                xt, xb = xr.next()
                u, ub = ur.next()
                ot, otb = otr.next()
                P.dma(xt[:, :, :n], xT_tile_ap(t0, n), reads=[b_xTt[ti]], writes=[xb])
                P.dma(u[:, :, :n], uT_tile_ap(t0, n), reads=[b_uTt[ti]], writes=[ub])
                P.dma(ot[:, :, :n], oT[:, :, :, t0:t0 + n].rearrange("a c p t -> p (a c) t"), writes=[otb])
                macc, mb = maccr.next()
                for br in range(4):
                    wg_, wgb_ = wgr_.next()
                    P.dma(wg_[:], winv[:, :, O_GT + br * 1024:O_GT + (br + 1) * 1024], reads=[win_b[1]], writes=[wgb_])
                    for m in range(8):
                        psg, pgb = k.tmp()
                        mm_group(P, psg[:, :n], pgb, [(wg_[:, kc, m * 128:(m + 1) * 128], u[:, kc, :n]) for kc in range(8)], [wgb_, ub])
                        psz, pzb = k.tmp()
                        mm_group(P, psz[:, :n], pzb, [(wbr_sb[:, br * 4 + kc, m * 128:(m + 1) * 128], ot[:, br * 4 + kc, :n]) for kc in range(4)], [b_wbr, otb])
                        sg, sgb = sigr.next()
                        P.op("act", lambda e, sg=sg, psg=psg: e.activation(out=sg[:, :n], in_=psg[:, :n], func=AF.Sigmoid), [pgb], [sgb])
                        if br == 0:
                            P.op("dve", lambda e, sg=sg, psz=psz, m=m: e.tensor_tensor(out=macc[:, m, :n], in0=sg[:, :n], in1=psz[:, :n], op=ALU.mult), [sgb, pzb], [mb])
                        else:
                            P.op("dve", lambda e, sg=sg, psz=psz: e.tensor_tensor(out=sg[:, :n], in0=sg[:, :n], in1=psz[:, :n], op=ALU.mult), [sgb, pzb], [sgb])
                            P.op("dve", lambda e, sg=sg, m=m: e.tensor_tensor(out=macc[:, m, :n], in0=macc[:, m, :n], in1=sg[:, :n], op=ALU.add), [sgb, mb], [mb])
                mT, mTb = xnr.next()
                for m in range(8):
                    copy_on(P, "act", mT[:, m, :n], macc[:, m, :n], [mb], [mTb])
                for mo in range(8):
                    psy, pyb = k.tmp()
                    mm_group(P, psy[:, :n], pyb, [(wout_sb[:, kc, mo * 128:(mo + 1) * 128], mT[:, kc, :n]) for kc in range(8)], [b_wout, mTb])
                    P.op("dve", lambda e, psy=psy, mo=mo: e.scalar_tensor_tensor(out=xt[:, mo, :n], in0=psy[:, :n], scalar=gtv[:, 1, mo, j:j + 1],
                                                                                   in1=xt[:, mo, :n], op0=ALU.mult, op1=ALU.add), [pyb, b_gtv, xb], [xb])
                P.dma(xT_tile_ap(t0, n), xt[:, :, :n], reads=[xb], writes=[b_xTt[ti]], eng="pool")

            for ti, (t0, n) in enumerate(TILES):
                if t0 >= L and not need_ctx:
                    continue
                merge_tile(ti, t0, n)
            P.barrier()

        with ExitStack() as S:
            rg = ffn_rings(S)
            nmr = (Ring(k, S, "sq", 3, [128, 512], BF16), Ring(k, S, "rstd", 2, [128, 512], F32), Ring(k, S, "nmtmp", 2, [128, 512], F32))
            xr = Ring(k, S, "xt", 2, [128, 8, 512], F32)
            xnr = Ring(k, S, "xn", 2, [128, 8, 512], BF16)

            def ffn2_tile(ti, t0, n):
                j = 0 if t0 < L else 1
                xt, xb = xr.next()
                xn, xnb = xnr.next()
                P.dma(xt[:, :, :n], xT_tile_ap(t0, n), reads=[b_xTt[ti]], writes=[xb])
                norm_mod(S, nmr, xt, xb, n, 2, j, xn, xnb)
                ffn_tile(S, rg, xt, xb, xn, xnb, n, 1, 2, j)
                P.dma(xT_tile_ap(t0, n), xt[:, :, :n], reads=[xb], writes=[b_xTt[ti]], eng="pool")

            for ti, (t0, n) in enumerate(TILES):
                if t0 >= L and not need_ctx:
                    continue
                ffn2_tile(ti, t0, n)
            P.barrier()

        if stop == "merge" and l == 0:
            return finish(k, st, I, out_d, dbg_d, xT, b_xTt)

    for l_ in range(DEPTH):
        r_ = do_layer(l_)
        if r_ is not None:
            return r_

    with ExitStack() as S:
        fg, b_fg = k.sb(S, "fg", [128, 8], F32)
        P.dma(fg[:], I["final_g"], writes=[b_fg])
        xr = Ring(k, S, "xt", 2, [128, 8, 512], F32)
        sq_r = Ring(k, S, "sq", 3, [128, 512], BF16)
        rstd_r = Ring(k, S, "rstd", 2, [128, 512], F32)
        yr = Ring(k, S, "yt", 2, [128, 8, 512], F32)
        tokr = Ring(k, S, "tok", 3, [128, 1024], F32)

        def final_tile(ti, t0, n):
            xt, xb = xr.next()
            P.dma(xt[:, :, :n], xT_tile_ap(t0, n), reads=[b_xTt[ti]], writes=[xb])
            ps, pb = k.acc()
            for fc in range(8):
                q, qb = sq_r.next()
                P.op("act", lambda e, q=q, fc=fc: e.activation(out=q[:, :n], in_=xt[:, fc, :n], func=AF.Square), [xb], [qb])
                P.op("pe", lambda e, q=q, fc=fc: e.matmul(ps[:, :n], lhsT=ones_b[:], rhs=q[:, :n], start=(fc == 0), stop=(fc == 7)), [qb, b_ones], [pb])
            rstd, rb = rstd_r.next()
            P.op("act", lambda e: e.activation(out=rstd[:, :n], in_=ps[:, :n], func=AF.Sqrt, bias=epsv[:, 0:1], scale=1.0 / D), [pb, b_eps], [rb])
            P.op("dve", lambda e: e.reciprocal(out=rstd[:, :n], in_=rstd[:, :n]), [rb], [rb])
            yt, yb = yr.next()
            for fc in range(8):
                P.op("dve", lambda e, fc=fc: e.scalar_tensor_tensor(out=yt[:, fc, :n], in0=xt[:, fc, :n], scalar=fg[:, fc:fc + 1], in1=rstd[:, :n],
                                                                     op0=ALU.mult, op1=ALU.mult), [xb, rb, b_fg], [yb])
            for jb in range(n // 128):
                tk, tkb = tokr.next()
                for half in range(2):
                    pst, ptb = k.tmp()
                    for f4 in range(4):
                        fc = half * 4 + f4
                        P.op("pe", lambda e, pst=pst, f4=f4, fc=fc, jb=jb: e.transpose(pst[:, f4 * 128:(f4 + 1) * 128], yt[:, fc, jb * 128:(jb + 1) * 128], ident_f[:]),
                             [yb, b_identf], [ptb])
                    copy_on(P, k.ev(), tk[:, half * 512:(half + 1) * 512], pst[:, :], [ptb], [tkb])
                P.dma(out_d[t0 + jb * 128:t0 + (jb + 1) * 128, :], tk[:], reads=[tkb], eng="pool")

        for ti, (t0, n) in enumerate(TILES):
            if t0 < L:
                final_tile(ti, t0, n)
        P.barrier()
    P.emit()
    st.close()
    return nc


DBG_WANT = []


def dbg_items(loc):
    return [f(loc) for f in DBG_WANT]


def finish_list(k, st, dbg_d, aps):
    P = k.P
    with ExitStack() as S:
        r = Ring(k, S, "fl", 2, [128, 4352], F32)
        row = 0
        for ap in aps:
            rows, cols = ap.shape
            for r0 in range(0, rows, 128):
                rr = min(128, rows - r0)
                t, b = r.next()
                P.dma(t[:rr, :cols], ap[r0:r0 + rr, :], writes=[b], eng="pool")
                P.dma(dbg_d[row:row + rr, :cols], t[:rr, :cols], reads=[b], eng="sp")
                row += rr
    P.emit()
    st.close()
    return k.nc


def finish(k, st, I, out_d, dbg_d, xT, b_xTt):
    P = k.P
    if dbg_d is not None:
        with ExitStack() as S:
            r = Ring(k, S, "fin", 2, [128, 8, 512], F32)
            for ti, (t0, n) in enumerate(TILES):
                t, b = r.next()
                P.dma(t[:, :, :n], xT[:, :, t0:t0 + n].rearrange("c p t -> p c t"), reads=[b_xTt[ti]], writes=[b])
                P.dma(dbg_d[:, :, t0:t0 + n].rearrange("c p t -> p c t"), t[:, :, :n], reads=[b], eng="pool")
    P.emit()
    st.close()
    return k.nc


def finish_dbg(k, st, I, out_d, dbg_d, items):
    P = k.P
    for (t, b, shape) in items:
        P.dma(dbg_d, t[:].rearrange("p a b -> p (a b)") if len(t.shape) == 3 else t[:], reads=[b], eng="pool")
    P.emit()
    st.close()
    return k.nc


def pcol(v, nchunk):
    return np.ascontiguousarray(np.asarray(v, np.float32).reshape(nchunk, 128).T)


def host_inputs(inp, b):
    f = np.float32
    m = {}
    m["x"] = np.ascontiguousarray(inp["x"][b], f)
    m["ctx"] = np.ascontiguousarray(inp["ctx"][b], f)
    cc = np.stack([np.asarray(inp["c"][b], f), np.asarray(inp["c_ctx"], f)], axis=-1)
    m["cc"] = np.ascontiguousarray(cc.reshape(8, 128, 2).transpose(1, 0, 2))
    m["ada_w"] = np.asarray(inp["ada_w"], f)
    m["ada_b"] = np.ascontiguousarray(np.asarray(inp["ada_b"], f).reshape(DEPTH, 72, 128).transpose(2, 0, 1))
    m["norm_g"] = np.ascontiguousarray(np.asarray(inp["norm_g"], f).reshape(DEPTH, 3, 8, 128).transpose(3, 0, 1, 2))
    for n in ("ffn_w_gate", "ffn_w_up", "ffn_w_down", "w_in", "mla_w_uq", "mla_w_ukv", "gla_w_gate", "gla_b_gate",
              "hy_filt_w1", "hy_filt_w2", "hy_filt_w3", "w_branch", "w_out"):
        m[n] = np.asarray(inp[n], f)
    m["mla_q_norm_g"] = np.ascontiguousarray(np.asarray(inp["mla_q_norm_g"], f).reshape(DEPTH, 2, 128).transpose(2, 0, 1))
    m["mla_kv_norm_g"] = np.ascontiguousarray(np.asarray(inp["mla_kv_norm_g"], f).T)
    m["gla_norm_g"] = np.ascontiguousarray(np.asarray(inp["gla_norm_g"], f).T)
    m["gqa_q_norm_g"] = np.ascontiguousarray(np.tile(np.asarray(inp["gqa_q_norm_g"], f), (1, 2)).T)
    m["gqa_k_norm_g"] = np.ascontiguousarray(np.tile(np.asarray(inp["gqa_k_norm_g"], f), (1, 2)).T)
    m["hy_sconv_w"] = np.ascontiguousarray(np.asarray(inp["hy_sconv_w"], f).reshape(DEPTH, 3, 12, 128).transpose(3, 0, 1, 2))
    m["hy_sconv_b"] = np.ascontiguousarray(np.asarray(inp["hy_sconv_b"], f).reshape(DEPTH, 12, 128).transpose(2, 0, 1))
    m["hy_filt_b1"] = np.ascontiguousarray(np.asarray(inp["hy_filt_b1"], f).T)
    m["hy_filt_b2"] = np.ascontiguousarray(np.asarray(inp["hy_filt_b2"], f).T)
    m["hy_filt_bias"] = np.ascontiguousarray(np.asarray(inp["hy_filt_bias"], f).reshape(DEPTH, 2, 4, 128).transpose(3, 0, 1, 2))
    m["final_g"] = pcol(inp["final_g"], 8)
    m.update(consts())
    return m


def kernel(**inputs):
    nc = bass.Bass("TRN2", target_bir_lowering=False)
    build(nc)
    in_maps = [host_inputs(inputs, b) for b in range(8)]
    res = run_bass_kernel_spmd(nc, in_maps, core_ids=list(range(8)))
    return np.stack([np.asarray(r["out"], np.float32) for r in res.results], axis=0)
```

```python
import math
from contextlib import ExitStack
import numpy as np
import ml_dtypes
import concourse.bass as bass
import concourse.mybir as mybir
from concourse.bass_utils import run_bass_kernel_spmd

F32 = mybir.dt.float32
BF16 = mybir.dt.bfloat16
AF = mybir.ActivationFunctionType
ALU = mybir.AluOpType
AX = mybir.AxisListType

D = 1024
L = 4096
CT = 256
NT = L + CT
DEPTH = 2
FF = 2816
NFC = FF // 128
P_IN = 8384
EPS = 1e-6
TILES = [(i * 512, 512) for i in range(8)] + [(L, CT)]
O_CQ, O_CKV, O_KR, O_BQ, O_BK, O_BV, O_LR, O_OG, O_CQQ, O_CK, O_CV, O_DP, O_GT = (
    0, 256, 384, 416, 672, 928, 1440, 1472, 1984, 2496, 2624, 2752, 4288)

DBG_STOP = None
PROJ_SECTIONS = {'mla', 'gqa', 'gla', 'hy'}
PROJ_TILES = 9
SKIP = ''
MLA_STEP = 99
NHEADS_DBG = 8
RING_F32, RING_BF, RING_OUT = 10, 16, 12


class Buf:
    __slots__ = ("name", "last_w", "readers")

    def __init__(self, name="b"):
        self.name = name
        self.last_w = None
        self.readers = []


class Op:
    __slots__ = ("eng", "fn", "deps", "signal", "sig_val", "sem", "is_dma", "bar", "snap")

    def __init__(self, eng, fn, is_dma):
        self.eng = eng
        self.fn = fn
        self.deps = []
        self.signal = False
        self.sig_val = None
        self.sem = None
        self.is_dma = is_dma
        self.bar = None
        self.snap = None


ENGS = ("pe", "act", "dve", "pool", "sp")
NWAIT = {}
SAME_ENGINE_SKIP = ("pe", "act", "dve")
SIGNAL_ALL = True
SKIP_ATTN = False
SKIP_GLA = False
ROPE_COPY_ENG = "act"
DMA_SLOTS = 8


class Prog:
    def __init__(self, nc):
        self.nc = nc
        self.ops = {e: [] for e in ENGS}
        self.nbar = 0

    def op(self, eng, fn, reads=(), writes=(), dma=False):
        o = Op(eng, fn, dma)
        if SIGNAL_ALL and not dma:
            o.signal = True
        deps = []
        for b in reads:
            if b.last_w is not None:
                deps.append(b.last_w)
        for b in writes:
            if b.last_w is not None:
                deps.append(b.last_w)
            deps.extend(b.readers)
        seen = set()
        for d in deps:
            if id(d) in seen or d is o:
                continue
            seen.add(id(d))
            if d.eng == eng and eng in SAME_ENGINE_SKIP and not d.is_dma:
                continue
            o.deps.append(d)
            d.signal = True
        for b in reads:
            b.readers.append(o)
        for b in writes:
            b.last_w = o
            b.readers = []
        self.ops[eng].append(o)
        return o

    def dma(self, out, in_, reads=(), writes=(), eng="sp", **kw):
        return self.op(eng, lambda e: e.dma_start(out=out, in_=in_, **kw), reads, writes, dma=True)

    def barrier(self):
        self.nbar += 1
        for e in ENGS:
            last = None
            for o in reversed(self.ops[e]):
                if o.bar is None:
                    last = o
                    break
            if last is not None and not last.is_dma:
                last.signal = True
            o = Op(e, None, False)
            o.bar = self.nbar
            o.snap = last
            self.ops[e].append(o)

    def emit(self):
        nc = self.nc
        with ExitStack() as st:
            csem = {e: st.enter_context(nc.semaphore("c_" + e)) for e in ENGS}
            bsem = {e: st.enter_context(nc.semaphore("b_" + e)) for e in ENGS}
            dsem = {e: [st.enter_context(nc.semaphore("d_%s_%d" % (e, i))) for i in range(DMA_SLOTS)] for e in ENGS}
            ccount = {e: 0 for e in ENGS}
            dcount = {e: [0] * DMA_SLOTS for e in ENGS}
            dnext = {e: 0 for e in ENGS}
            for e in ENGS:
                for o in self.ops[e]:
                    if o.bar is not None:
                        o.sig_val = list(dcount[e])
                        continue
                    if o.is_dma:
                        s = dnext[e]
                        dnext[e] = (s + 1) % DMA_SLOTS
                        o.sem = dsem[e][s]
                        prev = dcount[e][s]
                        dcount[e][s] = prev + 16
                        o.sig_val = prev + 16
                        o.signal = True
                        o.deps.append(("slot", o.sem, prev))
                    elif o.signal:
                        ccount[e] += 1
                        o.sem = csem[e]
                        o.sig_val = ccount[e]
            final_d = {e: list(dcount[e]) for e in ENGS}
            st.enter_context(nc.allow_non_contiguous_dma(reason="small strided vector loads"))
            blk = st.enter_context(nc.Block())

            def make(e):
                def body(eng):
                    waited = {}

                    def wait(sem, val):
                        if val <= 0:
                            return
                        k = id(sem)
                        if waited.get(k, 0) >= val:
                            return
                        waited[k] = val
                        NWAIT[e] = NWAIT.get(e, 0) + 1
                        eng.wait_ge(sem, val)

                    for o in self.ops[e]:
                        if o.bar is not None:
                            for i in range(DMA_SLOTS):
                                wait(dsem[e][i], o.sig_val[i])
                            if o.snap is not None and not o.snap.is_dma:
                                wait(o.snap.sem, o.snap.sig_val)
                            eng.nop().then_inc(bsem[e], 1)
                            for e2 in ENGS:
                                if e2 != e:
                                    wait(bsem[e2], o.bar)
                            continue
                        for d in o.deps:
                            if isinstance(d, tuple):
                                wait(d[1], d[2])
                            else:
                                wait(d.sem, d.sig_val)
                        ins = o.fn(eng)
                        if o.signal:
                            ins.then_inc(o.sem, 16 if o.is_dma else 1)
                    for i in range(DMA_SLOTS):
                        wait(dsem[e][i], final_d[e][i])
                return body

            blk.tensor(make("pe"))
            blk.scalar(make("act"))
            blk.vector(make("dve"))
            blk.gpsimd(make("pool"))
            blk.sync(make("sp"))


class Ring:
    def __init__(self, k, st, name, n, shape, dt):
        self.t = [st.enter_context(k.nc.sbuf_tensor("%s%d_%d" % (name, k.uid(), i), shape, dt)) for i in range(n)]
        self.b = [Buf(name + str(i)) for i in range(n)]
        self.i = 0

    def next(self):
        i = self.i
        self.i = (i + 1) % len(self.t)
        return self.t[i], self.b[i]


class KB:
    def __init__(self, nc, st):
        self.nc = nc
        self.st = st
        self.P = Prog(nc)
        self._uid = 0
        self.ps_t = [st.enter_context(nc.psum_tensor("psb%d" % i, [128, 512], F32)) for i in range(8)]
        self.ps_b = [Buf("ps%d" % i) for i in range(8)]
        self.acc_i = 0
        self.tmp_i = 0
        self.rr = 0

    def uid(self):
        self._uid += 1
        return self._uid

    def acc(self):
        i = self.acc_i
        self.acc_i = (i + 1) % 2
        return self.ps_t[i], self.ps_b[i]

    def tmp(self):
        i = 2 + self.tmp_i
        self.tmp_i = (self.tmp_i + 1) % 6
        return self.ps_t[i], self.ps_b[i]

    def sb(self, st, name, shape, dt):
        return st.enter_context(self.nc.sbuf_tensor("%s_%d" % (name, self.uid()), shape, dt)), Buf(name)

    def dram(self, name, shape, dt):
        return self.nc.dram_tensor(name, shape, dt).ap(), Buf(name)

    def ev(self):
        self.rr ^= 1
        return "dve" if self.rr else "act"


def copy_on(P, eng, out, in_, reads, writes):
    if eng == "act":
        return P.op("act", lambda e: e.copy(out=out, in_=in_), reads, writes)
    return P.op(eng, lambda e: e.tensor_copy(out=out, in_=in_), reads, writes)


def mm_group(P, ps, psb, pairs, reads, n=None):
    def fn(e):
        ins = None
        for i, (a, b) in enumerate(pairs):
            ins = e.matmul(ps, lhsT=a, rhs=b, start=(i == 0), stop=(i == len(pairs) - 1))
        return ins
    return P.op("pe", fn, reads, [psb])


def rope_tables(dim_layout):
    cos = np.ones((128, NT), np.float32)
    sin = np.zeros((128, NT), np.float32)
    RT = np.zeros((128, 128), np.float32)
    t = np.arange(L)
    rows = (t // 64).astype(np.float32)
    cols = (t % 64).astype(np.float32)
    for (r0, rd) in dim_layout:
        r = rd // 2
        half = r // 2
        freqs = (10000.0 ** (-np.arange(half, dtype=np.float32) / half)).astype(np.float32)
        for part, pos in ((0, rows), (1, cols)):
            base = r0 + part * r
            ang = pos[None, :] * freqs[:, None]
            c, s = np.cos(ang), np.sin(ang)
            cos[base:base + half, :L] = c
            cos[base + half:base + r, :L] = c
            sin[base:base + half, :L] = s
            sin[base + half:base + r, :L] = s
            for i in range(half):
                RT[base + half + i, base + i] = -1.0
                RT[base + i, base + half + i] = 1.0
    return cos, sin, RT


def dft_tables(n):
    N = 2 * n
    k = np.arange(n, dtype=np.float64)
    t = np.arange(n, dtype=np.float64)
    ang = 2.0 * np.pi * np.outer(t, k + 0.5) / N
    c, s = np.cos(ang), np.sin(ang)
    bf = ml_dtypes.bfloat16
    return (c.astype(np.float32).astype(bf), s.astype(np.float32).astype(bf),
            ((2.0 / N) * c.T).astype(np.float32).astype(bf), ((2.0 / N) * s.T).astype(np.float32).astype(bf))


def hyena_consts(n):
    t = np.linspace(0.0, 1.0, n, dtype=np.float32)
    w = (2.0 * math.pi * np.arange(n, dtype=np.float32) / n).astype(np.float32)
    f = np.linspace(1e-4, 15, 16, dtype=np.float32)
    fw = w[:, None] * f[None, :]
    feat = np.concatenate([t[:, None], np.cos(fw), -np.sin(fw)], axis=-1).astype(np.float32)
    deltas = np.abs(np.linspace(math.log(1e-2) / 1.5, math.log(1e-2) / 0.3, 512, dtype=np.float32))
    dec = np.exp(-t[:, None] * deltas[None, :]).astype(np.float32)
    decb = dec.copy()
    decb[0, :] = 0.0
    return np.ascontiguousarray(feat.T), dec, decb


def make_consts():
    c = {}
    c["k_ident"] = np.eye(128, dtype=np.float32)
    c["k_ones"] = np.ones((128, 128), np.float32)
    bo = np.zeros((128, 128), np.float32)
    bo[:64, :64] = 1.0
    bo[64:, 64:] = 1.0
    c["k_bones"] = bo
    cm, sm, rm = rope_tables([(64, 32)])
    c["k_cosM"], c["k_sinM"], c["k_rtM"] = cm, sm, rm
    cg, sg, rg = rope_tables([(0, 64), (64, 64)])
    c["k_cosG"], c["k_sinG"], c["k_rtG"] = cg, sg, rg
    for n, tag in ((L, "L"), (CT, "C")):
        a, b, c2, d2 = dft_tables(n)
        c["k_dc1" + tag], c["k_ds1" + tag], c["k_dc2" + tag], c["k_ds2" + tag] = a, b, c2, d2
        ft, dec, decb = hyena_consts(n)
        c["k_feat" + tag], c["k_dec" + tag], c["k_decb" + tag] = ft, dec, decb
    j = np.arange(64)
    c["k_m_le"] = (j[:, None] <= j[None, :]).astype(np.float32)
    c["k_m_ge"] = (j[:, None] >= j[None, :]).astype(np.float32)
    c["k_m_gt"] = (j[:, None] > j[None, :]).astype(np.float32)
    c["k_m_lt"] = (j[:, None] < j[None, :]).astype(np.float32)
    return c


_CONSTS = None


def consts():
    global _CONSTS
    if _CONSTS is None:
        _CONSTS = make_consts()
    return _CONSTS


def build(nc, stop=None, dbg=None):
    st = ExitStack()
    k = KB(nc, st)
    P = k.P
    I = {}

    def din(name, shape, dt=F32):
        I[name] = nc.dram_tensor(name, list(shape), dt, kind="ExternalInput").ap()
        return I[name]

    din("x", [L, D]); din("ctx", [CT, D]); din("cc", [128, 8, 2])
    din("ada_w", [DEPTH, D, 9 * D]); din("ada_b", [128, DEPTH, 72]); din("norm_g", [128, DEPTH, 3, 8])
    din("ffn_w_gate", [DEPTH, 2, D, FF]); din("ffn_w_up", [DEPTH, 2, D, FF]); din("ffn_w_down", [DEPTH, 2, FF, D])
    din("w_in", [DEPTH, D, P_IN])
    din("mla_q_norm_g", [128, DEPTH, 2]); din("mla_w_uq", [DEPTH, 256, 768])
    din("mla_kv_norm_g", [128, DEPTH]); din("mla_w_ukv", [DEPTH, 128, 1024])
    din("gla_w_gate", [DEPTH, 2, 16, 256]); din("gla_b_gate", [DEPTH, 2, 256]); din("gla_norm_g", [128, DEPTH])
    din("gqa_q_norm_g", [128, DEPTH]); din("gqa_k_norm_g", [128, DEPTH])
    din("hy_sconv_w", [128, DEPTH, 3, 12]); din("hy_sconv_b", [128, DEPTH, 12])
    din("hy_filt_w1", [DEPTH, 33, 64]); din("hy_filt_b1", [64, DEPTH]); din("hy_filt_w2", [DEPTH, 64, 64])
    din("hy_filt_b2", [64, DEPTH]); din("hy_filt_w3", [DEPTH, 64, 2048]); din("hy_filt_bias", [128, DEPTH, 2, 4])
    din("w_branch", [DEPTH, 4, 512, D]); din("w_out", [DEPTH, D, D]); din("final_g", [128, 8])
    cs = consts()
    for name, arr in cs.items():
        din(name, arr.shape, BF16 if arr.dtype == ml_dtypes.bfloat16 else F32)
    out_d = nc.dram_tensor("out", [L, D], F32, kind="ExternalOutput").ap()
    dbg_d = None
    if dbg is not None:
        dbg_d = nc.dram_tensor("dbg", list(dbg), F32, kind="ExternalOutput").ap()

    G = ExitStack()
    st.enter_context(G)
    ident_f, b_identf = k.sb(G, "identf", [128, 128], F32)
    ident_b, b_identb = k.sb(G, "identb", [128, 128], BF16)
    ones_b, b_ones = k.sb(G, "onesb", [128, 128], BF16)
    bones_b, b_bones = k.sb(G, "bonesb", [128, 128], BF16)
    P.dma(ident_f[:], I["k_ident"], writes=[b_identf])
    P.dma(ident_b[:], I["k_ident"], writes=[b_identb], eng="pool")
    P.dma(ones_b[:], I["k_ones"], writes=[b_ones], eng="pool")
    P.dma(bones_b[:], I["k_bones"], writes=[b_bones], eng="pool")
    modv, b_modv = k.sb(G, "modv", [128, 72, 2], F32)
    gsv, b_gsv = k.sb(G, "gsv", [128, 3, 8, 2], F32)
    gtv, b_gtv = k.sb(G, "gtv", [128, 3, 8, 2], F32)
    smalls, b_smalls = k.sb(G, "smalls", [128, 64], F32)
    epsv, b_eps = k.sb(G, "epsv", [128, 1], F32)
    P.op("pool", lambda e: e.memset(epsv[:], EPS), [], [b_eps])

    xT, b_xT = k.dram("xT", [8, 128, NT], F32)
    b_xTt = [Buf("xT%d" % i) for i in range(len(TILES))]
    uT, _ = k.dram("uT", [8, 128, NT], BF16)
    b_uTt = [Buf("uT%d" % i) for i in range(len(TILES))]
    wg_b = [k.dram("wg_b%d" % i, [D, FF], BF16) for i in range(2)]
    wu_b = [k.dram("wu_b%d" % i, [D, FF], BF16) for i in range(2)]
    wd_b = [k.dram("wd_b%d" % i, [FF, D], BF16) for i in range(2)]
    win_b = k.dram("win_b", [D, P_IN], BF16)
    wbr_b = k.dram("wbr_b", [4, 512, D], BF16)
    wout_b = k.dram("wout_b", [D, D], BF16)

    def xT_tile_ap(t0, n):
        return xT[:, :, t0:t0 + n].rearrange("c p t -> p c t")

    def uT_tile_ap(t0, n):
        return uT[:, :, t0:t0 + n].rearrange("c p t -> p c t")

    with ExitStack() as S:
        xin = Ring(k, S, "xin", 8, [128, D], F32)
        xo = Ring(k, S, "xo", 2, [128, 8, 512], F32)
        for ti, (t0, n) in enumerate(TILES):
            ot, ob = xo.next()
            nb = n // 128
            blks = []
            for j in range(nb):
                tt, tb = xin.next()
                r0 = t0 + j * 128
                src = I["x"][r0:r0 + 128, :] if r0 < L else I["ctx"][r0 - L:r0 - L + 128, :]
                P.dma(tt[:], src, writes=[tb])
                blks.append((tt, tb))
                for fc in range(8):
                    pass
            for fc in range(8):
                ps, pb = k.tmp()
                for j in range(nb):
                    tt, tb = blks[j]
                    P.op("pe", lambda e, ps=ps, tt=tt, j=j, fc=fc: e.transpose(ps[:, j * 128:(j + 1) * 128], tt[:, fc * 128:(fc + 1) * 128], ident_f[:]),
                         [tb, b_identf], [pb])
                copy_on(P, k.ev(), ot[:, fc, :n], ps[:, :n], [pb], [ob])
            P.dma(xT_tile_ap(t0, n), ot[:, :, :n], reads=[ob], writes=[b_xTt[ti]], eng="pool")
        P.barrier()

    if stop == "s0":
        return finish(k, st, I, out_d, dbg_d, xT, b_xTt)

    def cast_dram(dst, src, rows, cols, bufs):
        step = 256
        for r in range(0, rows, step):
            rr = min(step, rows - r)
            P.dma(dst[r:r + rr, :], src[r:r + rr, :], writes=bufs, eng="pool")

    def norm_mod(S, rings, xt, xb, n, s, j, out, outb):
        sq_r, rstd_r, tmp_r = rings
        ps, pb = k.acc()
        sqs = []
        for fc in range(8):
            q, qb = sq_r.next()
            P.op("act", lambda e, q=q, fc=fc: e.activation(out=q[:, :n], in_=xt[:, fc, :n], func=AF.Square), [xb], [qb])
            sqs.append((q, qb))
            P.op("pe", lambda e, q=q, fc=fc, ps=ps: e.matmul(ps[:, :n], lhsT=ones_b[:], rhs=q[:, :n], start=(fc == 0), stop=(fc == 7)),
                 [qb, b_ones], [pb])
        rstd, rb = rstd_r.next()
        P.op("act", lambda e: e.activation(out=rstd[:, :n], in_=ps[:, :n], func=AF.Sqrt, bias=epsv[:, 0:1], scale=1.0 / D), [pb, b_eps], [rb])
        P.op("dve", lambda e: e.reciprocal(out=rstd[:, :n], in_=rstd[:, :n]), [rb], [rb])
        for fc in range(8):
            tp, tb = tmp_r.next()
            P.op("dve", lambda e, tp=tp, fc=fc: e.scalar_tensor_tensor(out=tp[:, :n], in0=xt[:, fc, :n], scalar=gsv[:, s, fc, j:j + 1],
                                                                         in1=rstd[:, :n], op0=ALU.mult, op1=ALU.mult),
                 [xb, rb, b_gsv], [tb])
            P.op("act", lambda e, tp=tp, fc=fc: e.activation(out=out[:, fc, :n], in_=tp[:, :n], func=AF.Identity,
                                                              bias=modv[:, 3 * s * 8 + fc, j:j + 1], scale=1.0),
                 [tb, b_modv], [outb])

    qA = k.dram("qA", [8, 128, NT], BF16)[0]; kA = k.dram("kA", [8, 128, NT], BF16)[0]
    vA = k.dram("vA", [34, 128, 8, 128], BF16)[0]
    qC = k.dram("qC", [4, 128, NT], BF16)[0]; kC = k.dram("kC", [4, 128, NT], BF16)[0]
    vC = k.dram("vC", [34, 128, 2, 128], BF16)[0]
    bqT = k.dram("bqT", [2, 128, NT], BF16)[0]; bkT = k.dram("bkT", [2, 128, NT], BF16)[0]
    bk_tok = k.dram("bk_tok", [NT, 256], BF16)[0]; bv_tok = k.dram("bv_tok", [NT, 512], BF16)[0]
    la_d = k.dram("la_d", [2, NT, 256], BF16)[0]
    ogT = k.dram("ogT", [4, 128, NT], BF16)[0]; dpT = k.dram("dpT", [12, 128, NT], BF16)[0]
    oT = k.dram("oT", [4, 4, 128, NT], BF16)[0]
    ofT = k.dram("ofT", [4, 128, NT], F32)[0]
    zT = k.dram("zT", [12, 128, NT], BF16)[0]
    hpm = k.dram("hpm", [2, 2, L, 512], BF16)[0]
    PQ = k.dram("PQ", [2, 2, L, 512], F32)[0]

    def do_layer(l):
        need_ctx = l < DEPTH - 1
        with ExitStack() as S:
            cct, b_cct = k.sb(S, "cct", [128, 8, 2], F32)
            P.dma(cct[:], I["cc"], writes=[b_cct])
            P.op("act", lambda e: e.activation(out=cct[:], in_=cct[:], func=AF.Silu), [b_cct], [b_cct])
            adab, b_adab = k.sb(S, "adab", [128, 72], F32)
            P.dma(adab[:], I["ada_b"][:, l, :], writes=[b_adab])
            ng, b_ng = k.sb(S, "ng", [128, 3, 8], F32)
            P.dma(ng[:], I["norm_g"][:, l, :, :], writes=[b_ng])
            awr = Ring(k, S, "adaw", 2, [128, 8, 1024], F32)
            ps, pb = k.acc()
            for i in range(9):
                wt, wb = awr.next()
                P.dma(wt[:], I["ada_w"][l, :, i * 1024:(i + 1) * 1024].rearrange("(kc p) n -> p kc n", p=128), writes=[wb])
                for fc in range(8):
                    ch = i * 8 + fc

                    def fn(e, wt=wt, fc=fc, ch=ch, ps=ps):
                        ins = None
                        for kc in range(8):
                            ins = e.matmul(ps[:, ch * 2:ch * 2 + 2], lhsT=wt[:, kc, fc * 128:(fc + 1) * 128], rhs=cct[:, kc, :],
                                           start=(kc == 0), stop=(kc == 7))
                        return ins
                    P.op("pe", fn, [wb, b_cct], [pb])
            for j in range(2):
                P.op("dve", lambda e, j=j, ps=ps: e.tensor_tensor(out=modv[:, :, j], in0=ps[:, j:144:2], in1=adab[:], op=ALU.add),
                     [pb, b_adab], [b_modv])
            for s in range(3):
                for j in range(2):
                    P.op("dve", lambda e, s=s, j=j: e.scalar_tensor_tensor(out=gsv[:, s, :, j], in0=modv[:, (3 * s + 1) * 8:(3 * s + 2) * 8, j], scalar=1.0,
                                                                             in1=ng[:, s, :], op0=ALU.add, op1=ALU.mult),
                         [b_modv, b_ng], [b_gsv])
                    P.op("dve", lambda e, s=s, j=j: e.tensor_scalar(out=gtv[:, s, :, j], in0=modv[:, (3 * s + 2) * 8:(3 * s + 3) * 8, j],
                                                                      scalar1=(1.0 if s == 1 else 0.5), scalar2=None, op0=ALU.mult),
                         [b_modv], [b_gtv])
            P.barrier()

        if stop == "ada" and l == 0:
            return finish_dbg(k, st, I, out_d, dbg_d, [(modv, b_modv, [128, 144])])

        for i in range(2):
            cast_dram(wg_b[i][0], I["ffn_w_gate"][l, i], D, FF, [wg_b[i][1]])
            cast_dram(wu_b[i][0], I["ffn_w_up"][l, i], D, FF, [wu_b[i][1]])
            cast_dram(wd_b[i][0], I["ffn_w_down"][l, i], FF, D, [wd_b[i][1]])
        cast_dram(win_b[0], I["w_in"][l], D, P_IN, [win_b[1]])
        cast_dram(wbr_b[0].rearrange("a k n -> (a k) n"), I["w_branch"][l].rearrange("a k n -> (a k) n"), 2048, D, [wbr_b[1]])
        cast_dram(wout_b[0], I["w_out"][l], D, D, [wout_b[1]])

        def ffn_tile(S, rg, xt, xb, xn, xnb, n, fi, s, j):
            wgr, wur, wdr, sgr, hT, hb = rg
            for fp in range(6):
                c0 = fp * 512
                pc = min(512, FF - c0)
                wgt, wgb = wgr.next()
                wut, wub = wur.next()
                P.dma(wgt[:, :, :pc], wg_b[fi][0].rearrange("(kc p) f -> p kc f", p=128)[:, :, c0:c0 + pc], reads=[wg_b[fi][1]], writes=[wgb])
                P.dma(wut[:, :, :pc], wu_b[fi][0].rearrange("(kc p) f -> p kc f", p=128)[:, :, c0:c0 + pc], reads=[wu_b[fi][1]], writes=[wub])
                for jj in range(pc // 128):
                    f = fp * 4 + jj
                    psg, pgb = k.tmp()
                    psu, pub = k.tmp()
                    mm_group(P, psg[:, :n], pgb, [(wgt[:, kc, jj * 128:(jj + 1) * 128], xn[:, kc, :n]) for kc in range(8)], [wgb, xnb])
                    mm_group(P, psu[:, :n], pub, [(wut[:, kc, jj * 128:(jj + 1) * 128], xn[:, kc, :n]) for kc in range(8)], [wub, xnb])
                    sg, sgb = sgr.next()
                    P.op("act", lambda e, sg=sg, psg=psg: e.activation(out=sg[:, :n], in_=psg[:, :n], func=AF.Silu), [pgb], [sgb])
                    P.op("dve", lambda e, sg=sg, psu=psu, f=f: e.tensor_tensor(out=hT[:, f, :n], in0=sg[:, :n], in1=psu[:, :n], op=ALU.mult),
                         [sgb, pub], [hb])
            for mp in range(2):
                wdt, wdb = wdr.next()
                P.dma(wdt[:], wd_b[fi][0].rearrange("(kc p) m -> p kc m", p=128)[:, :, mp * 512:(mp + 1) * 512], reads=[wd_b[fi][1]], writes=[wdb])
                for jj in range(4):
                    m = mp * 4 + jj
                    ps, pb = k.tmp()
                    mm_group(P, ps[:, :n], pb, [(wdt[:, kc, jj * 128:(jj + 1) * 128], hT[:, kc, :n]) for kc in range(NFC)], [wdb, hb])
                    P.op("dve", lambda e, ps=ps, m=m: e.scalar_tensor_tensor(out=xt[:, m, :n], in0=ps[:, :n], scalar=gtv[:, s, m, j:j + 1],
                                                                               in1=xt[:, m, :n], op0=ALU.mult, op1=ALU.add),
                         [pb, b_gtv, xb], [xb])

        def ffn_rings(S):
            wgr = Ring(k, S, "wg", 2, [128, 8, 512], BF16)
            wur = Ring(k, S, "wu", 2, [128, 8, 512], BF16)
            wdr = Ring(k, S, "wd", 2, [128, NFC, 512], BF16)
            sgr = Ring(k, S, "sg", 3, [128, 512], F32)
            hT, hb = k.sb(S, "hT", [128, NFC, 512], BF16)
            return (wgr, wur, wdr, sgr, hT, hb)

        with ExitStack() as S:
            rg = ffn_rings(S)
            nmr = (Ring(k, S, "sq", 3, [128, 512], BF16), Ring(k, S, "rstd", 2, [128, 512], F32), Ring(k, S, "nmtmp", 2, [128, 512], F32))
            xr = Ring(k, S, "xt", 2, [128, 8, 512], F32)
            xnr = Ring(k, S, "xn", 2, [128, 8, 512], BF16)
            for ti, (t0, n) in enumerate(TILES):
                j = 0 if t0 < L else 1
                xt, xb = xr.next()
                xn, xnb = xnr.next()
                P.dma(xt[:, :, :n], xT_tile_ap(t0, n), reads=[b_xTt[ti]], writes=[xb])
                norm_mod(S, nmr, xt, xb, n, 0, j, xn, xnb)
                ffn_tile(S, rg, xt, xb, xn, xnb, n, 0, 0, j)
                P.dma(xT_tile_ap(t0, n), xt[:, :, :n], reads=[xb], writes=[b_xTt[ti]], eng="pool")
                un, unb = xnr.next()
                norm_mod(S, nmr, xt, xb, n, 1, j, un, unb)
                P.dma(uT_tile_ap(t0, n), un[:, :, :n], reads=[unb], writes=[b_uTt[ti]], eng="pool")
            P.barrier()

        if stop == "ffn1" and l == 0:
            return finish(k, st, I, out_d, dbg_d, xT, b_xTt)

        winv = win_b[0].rearrange("(kc p) n -> p kc n", p=128)
        with ExitStack() as S:
            NB = lambda: Buf("x")
            WuqP, b_WuqP = k.sb(S, "WuqP", [128, 2, 8, 128], BF16)
            WukP, b_WukP = k.sb(S, "WukP", [128, 8, 128], BF16)
            WukV, b_WukV = k.sb(S, "WukV", [128, 8, 64], BF16)
            WkrP, b_WkrP = k.sb(S, "WkrP", [128, 8, 128], BF16)
            WckP, b_WckP = k.sb(S, "WckP", [128, 8, 4, 128], BF16)
            wgate, b_wgate = k.sb(S, "wgate", [16, 2, 256], BF16)
            biasb, b_biasb = k.sb(S, "biasb", [128, 512], F32)
            pv, b_pv = k.sb(S, "pv", [128, 8], F32)
            rtM, b_rtM = k.sb(S, "rtM", [128, 128], BF16)
            rtG, b_rtG = k.sb(S, "rtG", [128, 128], BF16)
            P.dma(rtM[:], I["k_rtM"], writes=[b_rtM], eng="pool")
            P.dma(rtG[:], I["k_rtG"], writes=[b_rtG], eng="pool")
            P.op("dve", lambda e: e.memset(WuqP[:], 0.0), [], [b_WuqP])
            P.op("dve", lambda e: e.memset(WukP[:], 0.0), [], [b_WukP])
            P.op("dve", lambda e: e.memset(WkrP[:], 0.0), [], [b_WkrP])
            P.op("dve", lambda e: e.memset(WckP[:], 0.0), [], [b_WckP])
            for kc in range(2):
                if 'B' not in SKIP:
                    P.dma(WuqP[:, kc, :, 0:96], I["mla_w_uq"][l, kc * 128:(kc + 1) * 128, :].rearrange("p (h c) -> p h c", c=96), writes=[b_WuqP], eng="pool")
            ukv = I["mla_w_ukv"][l].rearrange("p (h c) -> p h c", c=128)
            if 'B' not in SKIP:
                P.dma(WukP[:, :, 0:64], ukv[:, :, 0:64], writes=[b_WukP], eng="pool")
            if 'B' not in SKIP:
                P.dma(WukV[:], ukv[:, :, 64:128], writes=[b_WukV], eng="pool")
            if 'D' not in SKIP:
                P.dma(WkrP[:, :, 64:96], winv[:, :, O_KR:O_KR + 32], reads=[win_b[1]], writes=[b_WkrP])
            for g in range(2):
                for par in range(2):
                    if 'D' not in SKIP:
                        P.dma(WckP[:, :, g * 2 + par, par * 64:par * 64 + 64], winv[:, :, O_CK + g * 64:O_CK + g * 64 + 64], reads=[win_b[1]], writes=[b_WckP])
            if 'C' not in SKIP:
                P.dma(wgate[:], I["gla_w_gate"][l].rearrange("d r c -> r d c"), writes=[b_wgate], eng="pool")
            if 'A' not in SKIP:
                P.dma(biasb[:], I["gla_b_gate"][l].rearrange("d c -> (d c)").partition_broadcast(128), writes=[b_biasb])
            if 'E' not in SKIP:
                P.dma(pv[:, 0:2], I["mla_q_norm_g"][:, l, :], writes=[b_pv])
            if 'E' not in SKIP:
                P.dma(pv[:, 2:3], I["mla_kv_norm_g"][:, l:l + 1], writes=[b_pv])
            if 'E' not in SKIP:
                P.dma(pv[:, 3:4], I["gqa_q_norm_g"][:, l:l + 1], writes=[b_pv])
            if 'E' not in SKIP:
                P.dma(pv[:, 4:5], I["gqa_k_norm_g"][:, l:l + 1], writes=[b_pv])
            ur = Ring(k, S, "u", 2, [128, 8, 512], BF16)
            wr = Ring(k, S, "wp", 3, [128, 8, 512], BF16)
            tabr = Ring(k, S, "tab", 2, [128, 4, 512], F32)
            f32r = Ring(k, S, "f32r", RING_F32, [128, 512], F32)
            bfr = Ring(k, S, "bfr", RING_BF, [128, 512], BF16)
            cqr = Ring(k, S, "cq", 2, [128, 2, 512], F32)
            cqnr = Ring(k, S, "cqn", 2, [128, 2, 512], BF16)
            outr = Ring(k, S, "outr", RING_OUT, [128, 512], BF16)
            krr_r = Ring(k, S, "krr", 2, [128, 512], BF16)
            ckvn_r = Ring(k, S, "ckvn", 2, [128, 512], BF16)
            vaA = Ring(k, S, "vaA", 2, [128, 8, 128], BF16)
            vaC = Ring(k, S, "vaC", 2, [128, 2, 128], BF16)
            for (t_, b_) in zip(vaA.t + vaC.t, vaA.b + vaC.b):
                P.op("dve", lambda e, t_=t_: e.memset(t_[:], 1.0), [], [b_])
            lrr = Ring(k, S, "lrT", 4, [16, 512], BF16)

            def wcols(c0, width):
                wt, wb = wr.next()
                P.dma(wt[:, :, :width], winv[:, :, c0:c0 + width], reads=[win_b[1]], writes=[wb])
                return wt, wb

            def store(dst, src, sb, eng="pool"):
                P.dma(dst, src, reads=[sb], writes=[NB()], eng=eng)

            def do_tile(ti, t0, n):
                u, ub = ur.next()
                P.dma(u[:, :, :n], uT_tile_ap(t0, n), reads=[b_uTt[ti]], writes=[ub])
                tab, tabb = tabr.next()
                for i_, nm in enumerate(("k_cosM", "k_sinM", "k_cosG", "k_sinG")):
                    P.dma(tab[:, i_, :n], I[nm][:, t0:t0 + n], writes=[tabb])
                nchunk = n // 128

                def proj(wt, wb, c0, M):
                    ps, pb = k.tmp()
                    mm_group(P, ps[:M, :n], pb, [(wt[:, kc, c0:c0 + M], u[:, kc, :n]) for kc in range(8)], [wb, ub])
                    return ps, pb

                def rms_rstd(srcs, nfeat, lhsT, lb):
                    pss, pssb = k.acc()
                    for i_, (a, ab) in enumerate(srcs):
                        sq, sqb = bfr.next()
                        P.op("act", lambda e, sq=sq, a=a: e.activation(out=sq[:, :n], in_=a, func=AF.Square), [ab], [sqb])
                        P.op("pe", lambda e, sq=sq, i_=i_, pss=pss: e.matmul(pss[:, :n], lhsT=lhsT[:], rhs=sq[:, :n], start=(i_ == 0), stop=(i_ == len(srcs) - 1)),
                             [sqb, lb], [pssb])
                    r, rb = f32r.next()
                    P.op("act", lambda e: e.activation(out=r[:, :n], in_=pss[:, :n], func=AF.Sqrt, bias=epsv[:, 0:1], scale=1.0 / nfeat), [pssb, b_eps], [rb])
                    P.op("dve", lambda e: e.reciprocal(out=r[:, :n], in_=r[:, :n]), [rb], [rb])
                    return r, rb

                def rope(src, srcb, ci, rt, rtb, dst):
                    if dst is not None or True:
                        pc_, pcb_ = f32r.next()
                        copy_on(P, "dve", pc_[:, :n], src, [srcb], [pcb_])
                        src, srcb = pc_[:, :n], pcb_
                    qb_, qbb = bfr.next()
                    copy_on(P, ROPE_COPY_ENG, qb_[:, :n], src, [srcb], [qbb])
                    psr, psrb = k.tmp()
                    P.op("pe", lambda e: e.matmul(psr[:, :n], lhsT=rt[:], rhs=qb_[:, :n], start=True, stop=True), [qbb, rtb], [psrb])
                    t1, t1b = f32r.next()
                    t2, t2b = f32r.next()
                    P.op("dve", lambda e: e.tensor_tensor(out=t1[:, :n], in0=src, in1=tab[:, ci, :n], op=ALU.mult), [srcb, tabb], [t1b])
                    P.op("dve", lambda e: e.tensor_tensor(out=t2[:, :n], in0=psr[:, :n], in1=tab[:, ci + 1, :n], op=ALU.mult), [psrb, tabb], [t2b])
                    o_, ob_ = (dst or outr).next()
                    P.op("dve", lambda e: e.tensor_tensor(out=o_[:, :n], in0=t1[:, :n], in1=t2[:, :n], op=ALU.add), [t1b, t2b], [ob_])
                    return o_, ob_

                if 'mla' not in PROJ_SECTIONS:
                    return
                W, Wb = wcols(0, 416)
                cq, cqb = cqr.next()
                for c in range(2):
                    ps, pb = proj(W, Wb, c * 128, 128)
                    copy_on(P, "dve", cq[:, c, :n], ps[:, :n], [pb], [cqb])
                if MLA_STEP <= 1:
                    return
                r, rb = rms_rstd([(cq[:, 0, :n], cqb), (cq[:, 1, :n], cqb)], 256, ones_b, b_ones)
                cqn, cqnb = cqnr.next()
                for c in range(2):
                    P.op("dve", lambda e, c=c: e.scalar_tensor_tensor(out=cqn[:, c, :n], in0=cq[:, c, :n], scalar=pv[:, c:c + 1], in1=r[:, :n], op0=ALU.mult, op1=ALU.mult),
                         [cqb, rb, b_pv], [cqnb])
                if MLA_STEP <= 2:
                    return
                ps, pb = proj(W, Wb, 256, 128)
                ckv, ckvb = f32r.next()
                copy_on(P, "dve", ckv[:, :n], ps[:, :n], [pb], [ckvb])
                r2, r2b = rms_rstd([(ckv[:, :n], ckvb)], 128, ones_b, b_ones)
                ckvn, ckvnb = ckvn_r.next()
                P.op("dve", lambda e: e.scalar_tensor_tensor(out=ckvn[:, :n], in0=ckv[:, :n], scalar=pv[:, 2:3], in1=r2[:, :n], op0=ALU.mult, op1=ALU.mult),
                     [ckvb, r2b, b_pv], [ckvnb])
                if MLA_STEP <= 3:
                    return
                pskr, pskrb = k.tmp()
                mm_group(P, pskr[:, :n], pskrb, [(WkrP[:, kc, :], u[:, kc, :n]) for kc in range(8)], [b_WkrP, ub])
                krr, krrb = rope(pskr[:, :n], pskrb, 0, rtM, b_rtM, krr_r)
                if MLA_STEP <= 4:
                    return
                for h in range(NHEADS_DBG):
                    if 'Q' not in SKIP:
                        psq, psqb = k.tmp()
                        mm_group(P, psq[:, :n], psqb, [(WuqP[:, kc, h, :], cqn[:, kc, :n]) for kc in range(2)], [b_WuqP, cqnb])
                        if 'R' in SKIP:
                            qo, qob = outr.next()
                            copy_on(P, "act", qo[:, :n], psq[:, :n], [psqb], [qob])
                        else:
                            qo, qob = rope(psq[:, :n], psqb, 0, rtM, b_rtM, None)
                        if 'S' not in SKIP:
                            store(qA[h, :, t0:t0 + n], qo[:, :n], qob)
                    if 'K' not in SKIP:
                        psk, pskb = k.tmp()
                        mm_group(P, psk[:, :n], pskb, [(WukP[:, h, :], ckvn[:, :n]), (ident_b[:], krr[:, :n])], [b_WukP, ckvnb, b_identb, krrb])
                        ko, kob = outr.next()
                        copy_on(P, "act", ko[:, :n], psk[:, :n], [pskb], [kob])
                        if 'S' not in SKIP:
                            store(kA[h, :, t0:t0 + n], ko[:, :n], kob)
                if MLA_STEP <= 5:
                    return
                for jc in range(nchunk):
                    psv, psvb = k.tmp()
                    P.op("pe", lambda e, jc=jc, psv=psv: e.matmul(psv[:, :], lhsT=ckvn[:, jc * 128:(jc + 1) * 128], rhs=WukV[:].rearrange("p h d -> p (h d)"), start=True, stop=True),
                         [ckvnb, b_WukV], [psvb])
                    va, vab = vaA.next()
                    copy_on(P, "dve", va[:, :, 0:64], psv[:, :].rearrange("p (h d) -> p h d", d=64), [psvb], [vab])
                    store(vA[t0 // 128 + jc], va[:], vab)

                if 'gqa' not in PROJ_SECTIONS:
                    return
                def normrope(ps, pb, gcol):
                    pc, pcb = f32r.next()
                    copy_on(P, "dve", pc[:, :n], ps[:, :n], [pb], [pcb])
                    r_, rb_ = rms_rstd([(pc[:, :n], pcb)], 64, bones_b, b_bones)
                    P.op("dve", lambda e: e.scalar_tensor_tensor(out=pc[:, :n], in0=pc[:, :n], scalar=pv[:, gcol:gcol + 1], in1=r_[:, :n], op0=ALU.mult, op1=ALU.mult),
                         [pcb, rb_, b_pv], [pcb])
                    return rope(pc[:, :n], pcb, 2, rtG, b_rtG, None)

                W, Wb = wcols(O_CQQ, 512)
                for c in range(4):
                    ps, pb = proj(W, Wb, c * 128, 128)
                    qo, qob = normrope(ps, pb, 3)
                    store(qC[c, :, t0:t0 + n], qo[:, :n], qob)
                for kt in range(4):
                    ps, pb = k.tmp()
                    mm_group(P, ps[:, :n], pb, [(WckP[:, kc, kt, :], u[:, kc, :n]) for kc in range(8)], [b_WckP, ub])
                    ko, kob = normrope(ps, pb, 4)
                    store(kC[kt, :, t0:t0 + n], ko[:, :n], kob)
                W, Wb = wcols(O_CV, 128)
                for jc in range(nchunk):
                    psv, psvb = k.tmp()
                    mm_group(P, psv[:, :128], psvb, [(u[:, kc, jc * 128:(jc + 1) * 128], W[:, kc, 0:128]) for kc in range(8)], [ub, Wb])
                    va, vab = vaC.next()
                    copy_on(P, "dve", va[:, :, 0:64], psv[:, 0:128].rearrange("p (h d) -> p h d", d=64), [psvb], [vab])
                    store(vC[t0 // 128 + jc], va[:], vab)

                if 'gla' not in PROJ_SECTIONS:
                    return
                W, Wb = wcols(O_BQ, 512)
                for c in range(2):
                    ps, pb = proj(W, Wb, c * 128, 128)
                    o_, ob_ = outr.next()
                    P.op("act", lambda e, o_=o_, ps=ps: e.activation(out=o_[:, :n], in_=ps[:, :n], func=AF.Copy, scale=0.125), [pb], [ob_])
                    store(bqT[c, :, t0:t0 + n], o_[:, :n], ob_)
                    ps, pb = proj(W, Wb, 256 + c * 128, 128)
                    o_, ob_ = outr.next()
                    copy_on(P, "dve", o_[:, :n], ps[:, :n], [pb], [ob_])
                    store(bkT[c, :, t0:t0 + n], o_[:, :n], ob_)
                for jc in range(nchunk):
                    ps, pb = k.tmp()
                    mm_group(P, ps[:, :256], pb, [(u[:, kc, jc * 128:(jc + 1) * 128], W[:, kc, 256:512]) for kc in range(8)], [ub, Wb])
                    o_, ob_ = outr.next()
                    copy_on(P, "act", o_[:, :256], ps[:, :256], [pb], [ob_])
                    store(bk_tok[t0 + jc * 128:t0 + (jc + 1) * 128, :], o_[:, :256], ob_)
                W, Wb = wcols(O_BV, 512)
                for jc in range(nchunk):
                    ps, pb = k.tmp()
                    mm_group(P, ps[:, :], pb, [(u[:, kc, jc * 128:(jc + 1) * 128], W[:, kc, 0:512]) for kc in range(8)], [ub, Wb])
                    o_, ob_ = outr.next()
                    copy_on(P, "dve", o_[:, :], ps[:, :], [pb], [ob_])
                    store(bv_tok[t0 + jc * 128:t0 + (jc + 1) * 128, :], o_[:, :], ob_)
                W, Wb = wcols(O_LR, 32)
                for d in range(2):
                    ps, pb = k.tmp()
                    mm_group(P, ps[:16, :n], pb, [(W[:, kc, d * 16:(d + 1) * 16], u[:, kc, :n]) for kc in range(8)], [Wb, ub])
                    lr, lrb = lrr.next()
                    copy_on(P, "act", lr[:, :n], ps[:16, :n], [pb], [lrb])
                    for jc in range(nchunk):
                        ps2, pb2 = k.tmp()
                        P.op("pe", lambda e, ps2=ps2, lr=lr, jc=jc, d=d: e.matmul(ps2[:, :256], lhsT=lr[:, jc * 128:(jc + 1) * 128], rhs=wgate[:, d, :], start=True, stop=True),
                             [lrb, b_wgate], [pb2])
                        t1, t1b = f32r.next()
                        P.op("dve", lambda e, t1=t1, ps2=ps2, d=d: e.tensor_tensor(out=t1[:, :256], in0=ps2[:, :256], in1=biasb[:, d * 256:(d + 1) * 256], op=ALU.add), [pb2, b_biasb], [t1b])
                        P.op("act", lambda e, t1=t1: e.activation(out=t1[:, :256], in_=t1[:, :256], func=AF.Exp, scale=-1.0), [t1b], [t1b])
                        P.op("act", lambda e, t1=t1: e.activation(out=t1[:, :256], in_=t1[:, :256], func=AF.Ln, bias=1.0, scale=1.0), [t1b], [t1b])
                        o_, ob_ = outr.next()
                        P.op("dve", lambda e, t1=t1, o_=o_: e.tensor_scalar(out=o_[:, :256], in0=t1[:, :256], scalar1=-1.0 / 16.0, scalar2=None, op0=ALU.mult), [t1b], [ob_])
                        store(la_d[d, t0 + jc * 128:t0 + (jc + 1) * 128, :], o_[:, :256], ob_)
                W, Wb = wcols(O_OG, 512)
                for c in range(4):
                    ps, pb = proj(W, Wb, c * 128, 128)
                    o_, ob_ = outr.next()
                    P.op("act", lambda e, o_=o_, ps=ps: e.activation(out=o_[:, :n], in_=ps[:, :n], func=AF.Silu), [pb], [ob_])
                    store(ogT[c, :, t0:t0 + n], o_[:, :n], ob_)
                if 'hy' not in PROJ_SECTIONS:
                    return
                for pp in range(3):
                    W, Wb = wcols(O_DP + pp * 512, 512)
                    for c in range(4):
                        ps, pb = proj(W, Wb, c * 128, 128)
                        o_, ob_ = outr.next()
                        copy_on(P, k.ev(), o_[:, :n], ps[:, :n], [pb], [ob_])
                        store(dpT[pp * 4 + c, :, t0:t0 + n], o_[:, :n], ob_)

            for ti, (t0, n) in enumerate(TILES[:PROJ_TILES]):
                do_tile(ti, t0, n)
                P.barrier()
            P.barrier()

        if stop == "proj" and l == 0:
            return finish_list(k, st, dbg_d, dbg_items(locals()))

        with ExitStack() as S:
            NB = lambda: Buf("x")
            Kr = Ring(k, S, "Kr", 2, [128, NT], BF16)
            Vr = Ring(k, S, "Vr", 2, [128, 34, 128], BF16)
            qr = Ring(k, S, "qr", 3, [128, 512], BF16)
            pr = Ring(k, S, "pr", 7, [128, 512], BF16)
            rcr = Ring(k, S, "rcr", 2, [128, 512], F32)
            orr = Ring(k, S, "orr", 3, [128, 512], BF16)
            for br in (() if SKIP_ATTN else (0, 2)):
                scale = (96.0 ** -0.5) if br == 0 else 0.125
                for h in range(8):
                    Kt, Kb = Kr.next()
                    Vt, Vb = Vr.next()
                    if br == 0:
                        ksrc = kA[h]; vsrc = vA[:, :, h, :]; qsrc = qA[h]
                    else:
                        g = h // 4
                        ksrc = kC[g * 2 + (h % 2)]; vsrc = vC[:, :, g, :]; qsrc = qC[h // 2]
                    P.dma(Kt[:], ksrc, writes=[Kb])
                    P.dma(Vt[:], vsrc.rearrange("c p d -> p c d"), writes=[Vb])
                    for ti, (t0, n) in enumerate(TILES):
                        if t0 >= L:
                            if not need_ctx:
                                continue
                            kcs = list(range(32, 34))
                        else:
                            kcs = list(range(34))
                        qt, qb_ = qr.next()
                        P.dma(qt[:, :n], qsrc[:, t0:t0 + n], writes=[qb_])
                        psO, psOb = k.acc()
                        pend = []

                        def emit_pv(item, psO=psO, psOb=psOb, Vt=Vt, Vb=Vb, n=n, kcs=kcs):
                            i_, kc, pt, ptb = item
                            P.op("pe", lambda e: e.matmul(psO[:, :n], lhsT=Vt[:, kc, :], rhs=pt[:, :n], start=(i_ == 0), stop=(i_ == len(kcs) - 1)),
                                 [Vb, ptb], [psOb])
                        for i_, kc in enumerate(kcs):
                            psS, psSb = k.tmp()
                            P.op("pe", lambda e, psS=psS, Kt=Kt, kc=kc, qt=qt, n=n: e.matmul(psS[:, :n], lhsT=Kt[:, kc * 128:(kc + 1) * 128], rhs=qt[:, :n], start=True, stop=True),
                                 [Kb, qb_], [psSb])
                            pt, ptb = pr.next()
                            P.op("act", lambda e, pt=pt, psS=psS, n=n, scale=scale: e.activation(out=pt[:, :n], in_=psS[:, :n], func=AF.Exp, scale=scale), [psSb], [ptb])
                            pend.append((i_, kc, pt, ptb))
                            if len(pend) > 3:
                                emit_pv(pend.pop(0))
                        while pend:
                            emit_pv(pend.pop(0))
                        rc, rcb = rcr.next()
                        P.op("dve", lambda e, rc=rc, psO=psO, n=n: e.reciprocal(out=rc[64:128, :n], in_=psO[64:128, :n]), [psOb], [rcb])
                        ot, otb = orr.next()
                        P.op("dve", lambda e, ot=ot, psO=psO, rc=rc, n=n: e.tensor_tensor(out=ot[0:64, :n], in0=psO[0:64, :n], in1=rc[64:128, :n], op=ALU.mult), [psOb, rcb], [otb])
                        P.dma(oT[br, h // 2, (h % 2) * 64:(h % 2) * 64 + 64, t0:t0 + n], ot[0:64, :n], reads=[otb], writes=[NB()], eng="pool")
            P.barrier()

        if stop == "attn" and l == 0:
            return finish_list(k, st, dbg_d, dbg_items(locals()))

        with ExitStack() as S:
            NB = lambda: Buf("x")
            msk, b_msk = k.sb(S, "msk", [64, 4, 64], F32)
            msk4, b_msk4 = k.sb(S, "msk4", [64, 2, 4, 64], F32)
            hmask, b_hmask = k.sb(S, "hmask", [128, 2], F32)
            P.op("dve", lambda e: e.memset(hmask[:], 0.0), [], [b_hmask])
            P.op("dve", lambda e: e.memset(hmask[0:64, 0:1], 1.0), [b_hmask], [b_hmask])
            P.op("dve", lambda e: e.memset(hmask[64:128, 1:2], 1.0), [b_hmask], [b_hmask])
            for m_ in range(2):
                for h_ in range(4):
                    P.dma(msk4[:, m_, h_, :], I["k_m_le" if m_ == 0 else "k_m_ge"], writes=[b_msk4])
            mskb, b_mskb = k.sb(S, "mskb", [64, 4, 64], BF16)
            for i_, nm in enumerate(("k_m_le", "k_m_ge", "k_m_gt", "k_m_lt")):
                P.dma(msk[:, i_, :], I[nm], writes=[b_msk])
                P.dma(mskb[:, i_, :], I[nm], writes=[b_mskb], eng="pool")
            gng, b_gng = k.sb(S, "gng", [128, 1], F32)
            P.dma(gng[:], I["gla_norm_g"][:, l:l + 1], writes=[b_gng])
            Sst = [k.sb(S, "Sst%d" % i, [128, 128], F32) for i in range(2)]
            Sbf = [k.sb(S, "Sbf%d" % i, [128, 128], BF16) for i in range(2)]
            lar = Ring(k, S, "la", 2, [64, 8, 256], BF16)
            bktr = Ring(k, S, "bkt", 2, [64, 8, 256], BF16)
            bvr = Ring(k, S, "bv", 2, [64, 8, 512], BF16)
            bqr = Ring(k, S, "bq", 4, [128, 512], BF16)
            bkr = Ring(k, S, "bk", 4, [128, 512], BF16)
            Er = Ring(k, S, "E", 8, [128, 64], F32)
            qdr = Ring(k, S, "qd", 12, [128, 64], BF16)
            Abr = Ring(k, S, "Ab", 3, [64, 256], BF16)
            ker = Ring(k, S, "ke", 3, [64, 256], F32)
            kebr = Ring(k, S, "keb", 3, [64, 256], BF16)
            ogr = Ring(k, S, "og", 2, [128, 4, 512], F32)
            sqr = Ring(k, S, "gsq", 3, [128, 512], BF16)
            rsr = Ring(k, S, "grs", 2, [128, 512], F32)
            ogtr = Ring(k, S, "ogt", 2, [128, 512], BF16)
            outr2 = Ring(k, S, "gout", 3, [128, 512], BF16)
            tmpf = Ring(k, S, "gtmp", 2, [128, 512], F32)

            def gla_chunk(dirn, la, lab, bkt, bktb, bv, bvb, bq, bk, ci, og, ogb):
                mi = 0 if dirn == 0 else 1
                mk = 2 if dirn == 0 else 3
                last = 63 if dirn == 0 else 0
                qd = []
                decs = []
                for pr in range(2):
                    psb, psbb = k.tmp()
                    P.op("pe", lambda e, psb=psb, pr=pr: e.matmul(psb[:, :64], lhsT=la[:, ci, pr * 128:(pr + 1) * 128], rhs=mskb[:, mi, :], start=True, stop=True),
                         [lab, b_mskb], [psbb])
                    E, Eb = Er.next()
                    Ei, Eib = Er.next()
                    P.op("act", lambda e, E=E, psb=psb: e.activation(out=E[:, :], in_=psb[:, :64], func=AF.Exp), [psbb], [Eb])
                    P.op("act", lambda e, Ei=Ei, psb=psb: e.activation(out=Ei[:, :], in_=psb[:, :64], func=AF.Exp, scale=-1.0), [psbb], [Eib])
                    qt0, qt0b = qdr.next()
                    qt1, qt1b = qdr.next()
                    kt, ktb = qdr.next()
                    bqt, bqb = bq[pr]
                    bkt_, bkb_ = bk[pr]
                    for hh_, (qt_, qtb_) in enumerate(((qt0, qt0b), (qt1, qt1b))):
                        P.op("dve", lambda e, qt_=qt_, bqt=bqt, E=E, hh_=hh_: e.scalar_tensor_tensor(out=qt_[:, :], in0=bqt[:, ci * 64:(ci + 1) * 64], scalar=hmask[:, hh_:hh_ + 1], in1=E[:, :],
                                                                                                     op0=ALU.mult, op1=ALU.mult), [bqb, Eb, b_hmask], [qtb_])
                    P.op("dve", lambda e, kt=kt, bkt_=bkt_, Ei=Ei: e.tensor_tensor(out=kt[:, :], in0=bkt_[:, ci * 64:(ci + 1) * 64], in1=Ei[:, :], op=ALU.mult), [bkb_, Eib], [ktb])
                    qd.append(((qt0, qt0b), (qt1, qt1b), kt, ktb))
                    decs.append((E, Eb))
                psA, psAb = k.tmp()
                for h in range(4):
                    q0_, q1_, kt, ktb = qd[h // 2]
                    qt, qtb = (q0_, q1_)[h % 2]
                    P.op("pe", lambda e, h=h, qt=qt, kt=kt: e.matmul(psA[:64, h * 64:(h + 1) * 64], lhsT=kt[:, :], rhs=qt[:, :], start=True, stop=True),
                         [qtb, ktb], [psAb])
                Ab, Abb = Abr.next()
                P.op("dve", lambda e: e.tensor_tensor(out=Ab[:, :], in0=psA[:64, :256], in1=msk4[:, mi, :, :].rearrange("p h i -> p (h i)"), op=ALU.mult), [psAb, b_msk4], [Abb])
                pso, psob = k.tmp()
                for h in range(4):
                    q0_, q1_, kt, ktb = qd[h // 2]
                    qt, qtb = (q0_, q1_)[h % 2]
                    Sb_, Sbb_ = Sbf[h // 2]

                    def fn(e, h=h, qt=qt, Sb_=Sb_):
                        e.matmul(pso[:, h * 64:(h + 1) * 64], lhsT=bv[:, ci, h * 128:(h + 1) * 128], rhs=Ab[:, h * 64:(h + 1) * 64], start=True, stop=False)
                        return e.matmul(pso[:, h * 64:(h + 1) * 64], lhsT=Sb_[:, :], rhs=qt[:, :], start=False, stop=True)
                    P.op("pe", fn, [bvb, Abb, Sbb_, qtb], [psob])
                ogv = og[:, :, ci * 64:(ci + 1) * 64]
                if dirn == 0:
                    P.op("act", lambda e: e.copy(out=ogv, in_=pso[:, :256].rearrange("p (h i) -> p h i", h=4)), [psob], [ogb])
                else:
                    P.op("dve", lambda e: e.tensor_tensor(out=ogv, in0=ogv, in1=pso[:, :256].rearrange("p (h i) -> p h i", h=4), op=ALU.add), [psob, ogb], [ogb])
                psR, psRb = k.tmp()
                P.op("pe", lambda e: e.matmul(psR[:64, :256], lhsT=mskb[:, mk, :], rhs=la[:, ci, :], start=True, stop=True), [lab, b_mskb], [psRb])
                ke, keb = ker.next()
                P.op("act", lambda e: e.activation(out=ke[:, :], in_=psR[:64, :256], func=AF.Exp), [psRb], [keb])
                kb_, kbb_ = kebr.next()
                P.op("dve", lambda e: e.tensor_tensor(out=kb_[:, :], in0=ke[:, :], in1=bkt[:, ci, :], op=ALU.mult), [keb, bktb], [kbb_])
                for pr in range(2):
                    E, Eb = decs[pr]
                    St, Stb = Sst[pr]
                    Sb_, Sbb_ = Sbf[pr]
                    for hh in range(2):
                        h = pr * 2 + hh
                        psU, psUb = k.tmp()
                        P.op("pe", lambda e, psU=psU, pr=pr, h=h: e.matmul(psU[:, :128], lhsT=kb_[:, pr * 128:(pr + 1) * 128], rhs=bv[:, ci, h * 128:(h + 1) * 128], start=True, stop=True),
                             [kbb_, bvb], [psUb])
                        r0 = hh * 64
                        P.op("dve", lambda e, psU=psU, r0=r0, St=St, E=E: e.scalar_tensor_tensor(out=St[r0:r0 + 64, :], in0=St[r0:r0 + 64, :], scalar=E[r0:r0 + 64, last:last + 1],
                                                                                                   in1=psU[r0:r0 + 64, :128], op0=ALU.mult, op1=ALU.add),
                             [psUb, Stb, Eb], [Stb])
                    P.op("act", lambda e, St=St, Sb_=Sb_: e.copy(out=Sb_[:, :], in_=St[:, :]), [Stb], [Sbb_])

            def finalize(og, ogb, t0, n):
                for h in range(4):
                    sq, sqb = sqr.next()
                    P.op("act", lambda e, sq=sq, h=h: e.activation(out=sq[:, :n], in_=og[:, h, :n], func=AF.Square), [ogb], [sqb])
                    pss, pssb = k.acc()
                    P.op("pe", lambda e, sq=sq, pss=pss: e.matmul(pss[:, :n], lhsT=ones_b[:], rhs=sq[:, :n], start=True, stop=True), [sqb, b_ones], [pssb])
                    rs, rsb = rsr.next()
                    P.op("act", lambda e, rs=rs, pss=pss: e.activation(out=rs[:, :n], in_=pss[:, :n], func=AF.Sqrt, bias=epsv[:, 0:1], scale=1.0 / 128), [pssb, b_eps], [rsb])
                    P.op("dve", lambda e, rs=rs: e.reciprocal(out=rs[:, :n], in_=rs[:, :n]), [rsb], [rsb])
                    ogt, ogtb = ogtr.next()
                    P.dma(ogt[:, :n], ogT[h, :, t0:t0 + n], writes=[ogtb])
                    tf, tfb = tmpf.next()
                    P.op("dve", lambda e, tf=tf, h=h, rs=rs: e.scalar_tensor_tensor(out=tf[:, :n], in0=og[:, h, :n], scalar=gng[:, 0:1], in1=rs[:, :n], op0=ALU.mult, op1=ALU.mult),
                         [ogb, rsb, b_gng], [tfb])
                    ot_, otb_ = outr2.next()
                    P.op("dve", lambda e, tf=tf, ot_=ot_, ogt=ogt: e.tensor_tensor(out=ot_[:, :n], in0=tf[:, :n], in1=ogt[:, :n], op=ALU.mult), [tfb, ogtb], [otb_])
                    P.dma(oT[1, h, :, t0:t0 + n], ot_[:, :n], reads=[otb_], writes=[NB()], eng="pool")

            def gla_group(dirn, t0, n, want_out):
                nch = n // 64
                la, lab = lar.next()
                bkt, bktb = bktr.next()
                bv, bvb = bvr.next()
                P.dma(la[:, :nch, :], la_d[dirn, t0:t0 + n, :].rearrange("(c j) d -> j c d", j=64), writes=[lab])
                P.dma(bkt[:, :nch, :], bk_tok[t0:t0 + n, :].rearrange("(c j) d -> j c d", j=64), writes=[bktb])
                P.dma(bv[:, :nch, :], bv_tok[t0:t0 + n, :].rearrange("(c j) d -> j c d", j=64), writes=[bvb])
                bq, bk = [], []
                for pr in range(2):
                    t_, b_ = bqr.next()
                    P.dma(t_[:, :n], bqT[pr, :, t0:t0 + n], writes=[b_])
                    bq.append((t_, b_))
                    t_, b_ = bkr.next()
                    P.dma(t_[:, :n], bkT[pr, :, t0:t0 + n], writes=[b_])
                    bk.append((t_, b_))
                og, ogb = ogr.next()
                if dirn == 1 and want_out:
                    P.dma(og[:, :, :n], ofT[:, :, t0:t0 + n].rearrange("h p t -> p h t"), writes=[ogb])
                order = range(nch) if dirn == 0 else range(nch - 1, -1, -1)
                for ci in order:
                    gla_chunk(dirn, la, lab, bkt, bktb, bv, bvb, bq, bk, ci, og, ogb)
                if want_out:
                    if dirn == 0:
                        P.dma(ofT[:, :, t0:t0 + n].rearrange("h p t -> p h t"), og[:, :, :n], reads=[ogb], writes=[NB()], eng="pool")
                    else:
                        finalize(og, ogb, t0, n)

            for dirn in (() if SKIP_GLA else range(2)):
                for (St, Stb), (Sb_, Sbb_) in zip(Sst, Sbf):
                    P.op("dve", lambda e, St=St: e.memset(St[:], 0.0), [], [Stb])
                    P.op("dve", lambda e, Sb_=Sb_: e.memset(Sb_[:], 0.0), [], [Sbb_])
                gla_group(dirn, L, CT, need_ctx)
                groups = [(i * 512, 512) for i in range(8)]
                if dirn == 1:
                    groups = groups[::-1]
                if dirn == 1:
                    P.barrier()
                for (t0, n) in groups:
                    gla_group(dirn, t0, n, True)
            P.barrier()

        if stop == "gla" and l == 0:
            return finish_list(k, st, dbg_d, dbg_items(locals()))

        TWO_PI = 2.0 * math.pi
        segs = [(0, L, "L")] + ([(L, CT, "C")] if need_ctx else [])
        def hy_segment(toff, ns, tag):
            nblk = ns // 128
            tw = min(512, ns)
            ntt = ns // tw
            with ExitStack() as S:
                w1s, b_w1s = k.sb(S, "hw1", [33, 64], F32)
                w2s, b_w2s = k.sb(S, "hw2", [64, 64], F32)
                w3s, b_w3s = k.sb(S, "hw3", [64, 2048], F32)
                bb, b_bb = k.sb(S, "hbb", [64, 4], F32)
                P.dma(w1s[:], I["hy_filt_w1"][l], writes=[b_w1s])
                P.dma(w2s[:], I["hy_filt_w2"][l], writes=[b_w2s])
                P.dma(w3s[:], I["hy_filt_w3"][l], writes=[b_w3s])
                P.dma(bb[:, 0:1], I["hy_filt_b1"][:, l:l + 1], writes=[b_bb])
                P.dma(bb[:, 1:2], I["hy_filt_b2"][:, l:l + 1], writes=[b_bb])
                P.op("dve", lambda e: e.memset(bb[:, 2:3], -math.pi), [b_bb], [b_bb])
                ftr = Ring(k, S, "feat", 2, [33, 512], F32)
                h1r = Ring(k, S, "h1", 2, [64, 512], F32)
                h2r = Ring(k, S, "h2", 2, [64, 512], F32)
                decr = Ring(k, S, "dec", 2, [128, 2, 512], F32)
                hfr = Ring(k, S, "hf", 4, [128, 512], F32)
                hor = Ring(k, S, "ho", 4, [128, 512], BF16)

                kir = Ring(k, S, "ki", 2, [64, 512], mybir.dt.int32)
                kfr = Ring(k, S, "kf", 4, [64, 512], F32)

                def sin_layer(ps, pb, bcol, out, outb, w):
                    u_, ub_ = kfr.next()
                    P.op("dve", lambda e: e.tensor_scalar(out=u_[:, :w], in0=ps[:64, :w], scalar1=bb[:, bcol:bcol + 1], scalar2=1.0 / TWO_PI, op0=ALU.add, op1=ALU.mult), [pb, b_bb], [ub_])
                    ki, kib = kir.next()
                    P.op("dve", lambda e: e.tensor_copy(out=ki[:, :w], in_=u_[:, :w]), [ub_], [kib])
                    kf, kfb = kfr.next()
                    P.op("dve", lambda e: e.tensor_copy(out=kf[:, :w], in_=ki[:, :w]), [kib], [kfb])
                    P.op("dve", lambda e: e.tensor_tensor(out=u_[:, :w], in0=u_[:, :w], in1=kf[:, :w], op=ALU.subtract), [ub_, kfb], [ub_])
                    P.op("dve", lambda e: e.tensor_scalar(out=kf[:, :w], in0=u_[:, :w], scalar1=0.5, scalar2=None, op0=ALU.is_gt), [ub_], [kfb])
                    P.op("dve", lambda e: e.tensor_tensor(out=u_[:, :w], in0=u_[:, :w], in1=kf[:, :w], op=ALU.subtract), [ub_, kfb], [ub_])
                    P.op("dve", lambda e: e.tensor_scalar(out=kf[:, :w], in0=u_[:, :w], scalar1=-0.5, scalar2=None, op0=ALU.is_lt), [ub_], [kfb])
                    P.op("dve", lambda e: e.tensor_tensor(out=u_[:, :w], in0=u_[:, :w], in1=kf[:, :w], op=ALU.add), [ub_, kfb], [ub_])
                    P.op("act", lambda e: e.activation(out=out[:, :w], in_=u_[:, :w], func=AF.Sin, scale=TWO_PI), [ub_], [outb])

                def filt_tile(t0):
                    ft, ftb = ftr.next()
                    P.dma(ft[:, :tw], I["k_feat" + tag][:, t0:t0 + tw], writes=[ftb])
                    ps, pb = k.tmp()
                    P.op("pe", lambda e: e.matmul(ps[:64, :tw], lhsT=w1s[:], rhs=ft[:, :tw], start=True, stop=True), [b_w1s, ftb], [pb])
                    h1, h1b = h1r.next()
                    sin_layer(ps, pb, 0, h1, h1b, tw)
                    ps2, pb2 = k.tmp()
                    P.op("pe", lambda e: e.matmul(ps2[:64, :tw], lhsT=w2s[:], rhs=h1[:, :tw], start=True, stop=True), [b_w2s, h1b], [pb2])
                    h2, h2b = h2r.next()
                    sin_layer(ps2, pb2, 1, h2, h2b, tw)
                    for jb in range(tw // 128):
                        r0 = t0 + jb * 128
                        dc, dcb = decr.next()
                        P.dma(dc[:, 0, :], I["k_dec" + tag][r0:r0 + 128, :], writes=[dcb])
                        P.dma(dc[:, 1, :], I["k_decb" + tag][r0:r0 + 128, :], writes=[dcb])
                        for o in range(2):
                            hf_, hb_ = [], []
                            for dr in range(2):
                                ps3, pb3 = k.tmp()
                                c0 = o * 1024 + dr * 512
                                P.op("pe", lambda e, ps3=ps3, c0=c0, jb=jb: e.matmul(ps3[:, :], lhsT=h2[:, jb * 128:(jb + 1) * 128], rhs=w3s[:, c0:c0 + 512], start=True, stop=True),
                                     [h2b, b_w3s], [pb3])
                                hf, hfb = hfr.next()
                                P.op("dve", lambda e, hf=hf, ps3=ps3, dr=dr, dc=dc: e.tensor_tensor(out=hf[:, :], in0=ps3[:, :], in1=dc[:, dr, :], op=ALU.mult), [pb3, dcb], [hfb])
                                hf_.append((hf, hfb))
                            for pm, op_ in ((0, ALU.add), (1, ALU.subtract)):
                                ho, hob = hor.next()
                                P.op("pool", lambda e, ho=ho, op_=op_, a=hf_[0][0], b=hf_[1][0]: e.tensor_tensor(out=ho[:, :], in0=a[:, :], in1=b[:, :], op=op_),
                                     [hf_[0][1], hf_[1][1]], [hob])
                                P.dma(hpm[o, pm, r0:r0 + 128, :], ho[:, :], reads=[hob], writes=[Buf("x")], eng="pool")

                for tt in range(ntt):
                    filt_tile(tt * tw)
                P.barrier()
            with ExitStack() as S:
                hres, b_hres = k.sb(S, "hres", [128, 2, 32, 512], BF16)
                tbr = Ring(k, S, "tb1", 2, [128, 2, 32, 128], BF16)
                pqr = Ring(k, S, "pq", 4, [128, 512], F32)
                for o in range(2):
                    for pm in range(2):
                        P.dma(hres[:, pm, :nblk, :], hpm[o, pm, 0:ns, :].rearrange("(c p) d -> p c d", p=128), writes=[b_hres])
                    for fc in range(nblk):
                        tb, tbb = tbr.next()
                        P.dma(tb[:, 0, :nblk, :], I["k_dc1" + tag][:, fc * 128:(fc + 1) * 128].rearrange("(c p) f -> p c f", p=128), writes=[tbb])
                        P.dma(tb[:, 1, :nblk, :], I["k_ds1" + tag][:, fc * 128:(fc + 1) * 128].rearrange("(c p) f -> p c f", p=128), writes=[tbb])
                        for pm in range(2):
                            ps, pb = k.tmp()
                            mm_group(P, ps[:, :], pb, [(tb[:, pm, tc, :], hres[:, pm, tc, :]) for tc in range(nblk)], [tbb, b_hres])
                            pq, pqb = pqr.next()
                            copy_on(P, k.ev(), pq[:, :], ps[:, :], [pb], [pqb])
                            P.dma(PQ[o, pm, fc * 128:(fc + 1) * 128, :], pq[:, :], reads=[pqb], writes=[Buf("x")], eng="pool")
                P.barrier()
            with ExitStack() as S:
                scw, b_scw = k.sb(S, "scw", [128, 3, 12], F32)
                scb, b_scb = k.sb(S, "scb", [128, 12], F32)
                fbs, b_fbs = k.sb(S, "fbs", [128, 2, 4], F32)
                P.dma(scw[:], I["hy_sconv_w"][:, l, :, :], writes=[b_scw])
                P.dma(scb[:], I["hy_sconv_b"][:, l, :], writes=[b_scb])
                P.dma(fbs[:], I["hy_filt_bias"][:, l, :, :], writes=[b_fbs])
                vtok, b_vtok = k.sb(S, "vtok", [128, 32, 512], BF16)
                Yt, b_Yt = k.sb(S, "Yt", [128, 2, 32, 512], BF16)
                pr_ = Ring(k, S, "pin", 2, [128, 514], BF16)
                zr = Ring(k, S, "zf", 3, [128, 512], F32)
                zbr = Ring(k, S, "zb", 3, [128, 512], BF16)
                tb1 = Ring(k, S, "tb1b", 2, [128, 2, 32, 128], BF16)
                tb2 = Ring(k, S, "tb2", 2, [128, 2, 4, 512], BF16)
                pqr = Ring(k, S, "pq2", 2, [128, 2, 512], F32)
                ewr = Ring(k, S, "ew", 4, [128, 512], F32)
                xr_ = Ring(k, S, "x1t", 2, [128, 512], BF16)
                vr_ = Ring(k, S, "vtt", 2, [128, 512], BF16)

                def to_vtok(zf, zfb, cc, t0, w):
                    pst, ptb = k.tmp()
                    for jb in range(w // 128):
                        P.op("pe", lambda e, jb=jb: e.transpose(pst[:, jb * 128:(jb + 1) * 128], zf[:, jb * 128:(jb + 1) * 128], ident_f[:]), [zfb, b_identf], [ptb])
                    b0 = t0 // 128
                    nb_ = w // 128
                    copy_on(P, k.ev(), vtok[:, b0:b0 + nb_, cc * 128:(cc + 1) * 128], pst[:, :w].rearrange("p (b c) -> p b c", c=128), [ptb], [b_vtok])

                def sconv_tile(c, t0):
                    pin, pinb = pr_.next()
                    lo = max(t0 - 1, 0)
                    hi = min(t0 + tw + 1, ns)
                    off = lo - (t0 - 1)
                    P.dma(pin[:, off:off + (hi - lo)], dpT[c, :, toff + lo:toff + hi], writes=[pinb])
                    zf, zfb = zr.next()
                    P.op("act", lambda e: e.activation(out=zf[:, :tw], in_=pin[:, 1:tw + 1], func=AF.Identity, scale=scw[:, 1, c:c + 1], bias=scb[:, c:c + 1]), [pinb, b_scw, b_scb], [zfb])
                    a0 = 1 if t0 == 0 else 0
                    P.op("dve", lambda e: e.scalar_tensor_tensor(out=zf[:, a0:tw], in0=pin[:, a0:tw], scalar=scw[:, 0, c:c + 1], in1=zf[:, a0:tw], op0=ALU.mult, op1=ALU.add), [pinb, b_scw, zfb], [zfb])
                    a1 = tw - 1 if t0 + tw >= ns else tw
                    P.op("dve", lambda e: e.scalar_tensor_tensor(out=zf[:, 0:a1], in0=pin[:, 2:a1 + 2], scalar=scw[:, 2, c:c + 1], in1=zf[:, 0:a1], op0=ALU.mult, op1=ALU.add), [pinb, b_scw, zfb], [zfb])
                    zb, zbb = zbr.next()
                    copy_on(P, "act", zb[:, :tw], zf[:, :tw], [zfb], [zbb])
                    P.dma(zT[c, :, toff + t0:toff + t0 + tw], zb[:, :tw], reads=[zbb], writes=[Buf("x")], eng="pool")
                    if c >= 8:
                        to_vtok(zf, zfb, c - 8, t0, tw)

                for c in range(12):
                    for tt in range(ntt):
                        sconv_tile(c, tt * tw)
                P.barrier()

                bank = [(k.ps_t[i], k.ps_b[i]) for i in range(8)]
                for o in range(2):
                    for fc in range(nblk):
                        tb, tbb = tb1.next()
                        P.dma(tb[:, 0, :nblk, :], I["k_dc1" + tag][:, fc * 128:(fc + 1) * 128].rearrange("(c p) f -> p c f", p=128), writes=[tbb])
                        P.dma(tb[:, 1, :nblk, :], I["k_ds1" + tag][:, fc * 128:(fc + 1) * 128].rearrange("(c p) f -> p c f", p=128), writes=[tbb])
                        pq, pqb = pqr.next()
                        P.dma(pq[:, 0, :], PQ[o, 0, fc * 128:(fc + 1) * 128, :], writes=[pqb])
                        P.dma(pq[:, 1, :], PQ[o, 1, fc * 128:(fc + 1) * 128, :], writes=[pqb])
                        psA, pAb = k.tmp()
                        psB, pBb = k.tmp()
                        mm_group(P, psA[:, :], pAb, [(tb[:, 0, tc, :], vtok[:, tc, :]) for tc in range(nblk)], [tbb, b_vtok])
                        mm_group(P, psB[:, :], pBb, [(tb[:, 1, tc, :], vtok[:, tc, :]) for tc in range(nblk)], [tbb, b_vtok])
                        e1, e1b = ewr.next(); e2, e2b = ewr.next(); e3, e3b = ewr.next(); e4, e4b = ewr.next()
                        P.op("dve", lambda e, e1=e1, pq=pq, psA=psA: e.tensor_tensor(out=e1[:, :], in0=psA[:, :], in1=pq[:, 0, :], op=ALU.mult), [pAb, pqb], [e1b])
                        P.op("dve", lambda e, e2=e2, pq=pq, psB=psB: e.tensor_tensor(out=e2[:, :], in0=psB[:, :], in1=pq[:, 1, :], op=ALU.mult), [pBb, pqb], [e2b])
                        P.op("dve", lambda e, e3=e3, pq=pq, psB=psB: e.tensor_tensor(out=e3[:, :], in0=psB[:, :], in1=pq[:, 0, :], op=ALU.mult), [pBb, pqb], [e3b])
                        P.op("dve", lambda e, e4=e4, pq=pq, psA=psA: e.tensor_tensor(out=e4[:, :], in0=psA[:, :], in1=pq[:, 1, :], op=ALU.mult), [pAb, pqb], [e4b])
                        P.op("pool", lambda e, e1=e1, e2=e2, fc=fc: e.tensor_tensor(out=Yt[:, 0, fc, :], in0=e1[:, :], in1=e2[:, :], op=ALU.subtract), [e1b, e2b], [b_Yt])
                        P.op("pool", lambda e, e3=e3, e4=e4, fc=fc: e.tensor_tensor(out=Yt[:, 1, fc, :], in0=e3[:, :], in1=e4[:, :], op=ALU.add), [e3b, e4b], [b_Yt])
                    for tt in range(ntt):
                        t0 = tt * tw
                        grp = bank[0:4] if tt % 2 == 0 else bank[4:8]
                        for f4 in range(0, nblk, 4):
                            nf = min(4, nblk - f4)
                            t2, t2b = tb2.next()
                            P.dma(t2[:, 0, :nf, :tw], I["k_dc2" + tag][f4 * 128:(f4 + nf) * 128, t0:t0 + tw].rearrange("(c p) t -> p c t", p=128), writes=[t2b])
                            P.dma(t2[:, 1, :nf, :tw], I["k_ds2" + tag][f4 * 128:(f4 + nf) * 128, t0:t0 + tw].rearrange("(c p) t -> p c t", p=128), writes=[t2b])
                            for cc in range(4):
                                ps, pb = grp[cc]

                                def fn(e, ps=ps, cc=cc, f4=f4, nf=nf, t2=t2):
                                    ins = None
                                    for fi in range(nf):
                                        fc = f4 + fi
                                        for cs in range(2):
                                            ins = e.matmul(ps[:, :tw], lhsT=Yt[:, cs, fc, cc * 128:(cc + 1) * 128], rhs=t2[:, cs, fi, :tw],
                                                           start=(fc == 0 and cs == 0), stop=(fc == nblk - 1 and cs == 1))
                                    return ins
                                P.op("pe", fn, [b_Yt, t2b], [pb])
                        for cc in range(4):
                            ps, pb = grp[cc]
                            xsrc = zT[(0 if o == 0 else 4) + cc, :, toff + t0:toff + t0 + tw]
                            vsrc = zT[8 + cc, :, toff + t0:toff + t0 + tw]
                            x1, x1b = xr_.next()
                            vv, vvb = vr_.next()
                            P.dma(x1[:, :tw], xsrc, writes=[x1b])
                            P.dma(vv[:, :tw], vsrc, writes=[vvb])
                            zf, zfb = zr.next()
                            P.op("dve", lambda e, zf=zf, vv=vv, ps=ps, cc=cc, o=o: e.scalar_tensor_tensor(out=zf[:, :tw], in0=vv[:, :tw], scalar=fbs[:, o, cc:cc + 1], in1=ps[:, :tw], op0=ALU.mult, op1=ALU.add),
                                 [vvb, b_fbs, pb], [zfb])
                            P.op("dve", lambda e, zf=zf, x1=x1: e.tensor_tensor(out=zf[:, :tw], in0=zf[:, :tw], in1=x1[:, :tw], op=ALU.mult), [zfb, x1b], [zfb])
                            zb, zbb = zbr.next()
                            copy_on(P, "act", zb[:, :tw], zf[:, :tw], [zfb], [zbb])
                            if o == 0:
                                P.dma(zT[8 + cc, :, toff + t0:toff + t0 + tw], zb[:, :tw], reads=[zbb, vvb], writes=[Buf("x")], eng="pool")
                            else:
                                P.dma(oT[3, cc, :, toff + t0:toff + t0 + tw], zb[:, :tw], reads=[zbb], writes=[Buf("x")], eng="pool")
                            if o == 0:
                                pass
                        if o == 0:
                            pass
                    if o == 0:
                        P.barrier()
                        for cc in range(4):
                            for tt in range(ntt):
                                t0 = tt * tw
                                vv, vvb = vr_.next()
                                P.dma(vv[:, :tw], zT[8 + cc, :, toff + t0:toff + t0 + tw], writes=[vvb])
                                zf, zfb = zr.next()
                                copy_on(P, "dve", zf[:, :tw], vv[:, :tw], [vvb], [zfb])
                                to_vtok(zf, zfb, cc, t0, tw)
                P.barrier()

        for (toff_, ns_, tag_) in segs:
            hy_segment(toff_, ns_, tag_)

        if stop == "hyena" and l == 0:
            return finish_list(k, st, dbg_d, dbg_items(locals()))

        with ExitStack() as S:
            wbr_sb, b_wbr = k.sb(S, "wbr", [128, 16, 1024], BF16)
            wout_sb, b_wout = k.sb(S, "wout", [128, 8, 1024], BF16)
            P.dma(wbr_sb[:], wbr_b[0].rearrange("a (kc p) n -> p (a kc) n", p=128), reads=[wbr_b[1]], writes=[b_wbr])
            P.dma(wout_sb[:], wout_b[0].rearrange("(kc p) n -> p kc n", p=128), reads=[wout_b[1]], writes=[b_wout])
            xr = Ring(k, S, "xt", 1, [128, 8, 512], F32)
            xnr = Ring(k, S, "xn", 1, [128, 8, 512], BF16)
            ur = Ring(k, S, "u", 1, [128, 8, 512], BF16)
            otr = Ring(k, S, "ot", 1, [128, 16, 512], BF16)
            wgr_ = Ring(k, S, "wgt", 2, [128, 8, 1024], BF16)
            maccr = Ring(k, S, "macc", 1, [128, 8, 512], F32)
            sigr = Ring(k, S, "sig", 3, [128, 512], F32)

            def merge_tile(ti, t0, n):
                j = 0 if t0 < L else 1
                xt, xb = xr.next()
                u, ub = ur.next()
                ot, otb = otr.next()
                P.dma(xt[:, :, :n], xT_tile_ap(t0, n), reads=[b_xTt[ti]], writes=[xb])
                P.dma(u[:, :, :n], uT_tile_ap(t0, n), reads=[b_uTt[ti]], writes=[ub])
                P.dma(ot[:, :, :n], oT[:, :, :, t0:t0 + n].rearrange("a c p t -> p (a c) t"), writes=[otb])
                macc, mb = maccr.next()
                for br in range(4):
                    wg_, wgb_ = wgr_.next()
                    P.dma(wg_[:], winv[:, :, O_GT + br * 1024:O_GT + (br + 1) * 1024], reads=[win_b[1]], writes=[wgb_])
                    for m in range(8):
                        psg, pgb = k.tmp()
                        mm_group(P, psg[:, :n], pgb, [(wg_[:, kc, m * 128:(m + 1) * 128], u[:, kc, :n]) for kc in range(8)], [wgb_, ub])
                        psz, pzb = k.tmp()
                        mm_group(P, psz[:, :n], pzb, [(wbr_sb[:, br * 4 + kc, m * 128:(m + 1) * 128], ot[:, br * 4 + kc, :n]) for kc in range(4)], [b_wbr, otb])
                        sg, sgb = sigr.next()
                        P.op("act", lambda e, sg=sg, psg=psg: e.activation(out=sg[:, :n], in_=psg[:, :n], func=AF.Sigmoid), [pgb], [sgb])
                        if br == 0:
                            P.op("dve", lambda e, sg=sg, psz=psz, m=m: e.tensor_tensor(out=macc[:, m, :n], in0=sg[:, :n], in1=psz[:, :n], op=ALU.mult), [sgb, pzb], [mb])
                        else:
                            P.op("dve", lambda e, sg=sg, psz=psz: e.tensor_tensor(out=sg[:, :n], in0=sg[:, :n], in1=psz[:, :n], op=ALU.mult), [sgb, pzb], [sgb])
                            P.op("dve", lambda e, sg=sg, m=m: e.tensor_tensor(out=macc[:, m, :n], in0=macc[:, m, :n], in1=sg[:, :n], op=ALU.add), [sgb, mb], [mb])
                mT, mTb = xnr.next()
                for m in range(8):
                    copy_on(P, "act", mT[:, m, :n], macc[:, m, :n], [mb], [mTb])
                for mo in range(8):
                    psy, pyb = k.tmp()
                    mm_group(P, psy[:, :n], pyb, [(wout_sb[:, kc, mo * 128:(mo + 1) * 128], mT[:, kc, :n]) for kc in range(8)], [b_wout, mTb])
                    P.op("dve", lambda e, psy=psy, mo=mo: e.scalar_tensor_tensor(out=xt[:, mo, :n], in0=psy[:, :n], scalar=gtv[:, 1, mo, j:j + 1],
                                                                                   in1=xt[:, mo, :n], op0=ALU.mult, op1=ALU.add), [pyb, b_gtv, xb], [xb])
                P.dma(xT_tile_ap(t0, n), xt[:, :, :n], reads=[xb], writes=[b_xTt[ti]], eng="pool")

            for ti, (t0, n) in enumerate(TILES):
                if t0 >= L and not need_ctx:
                    continue
                merge_tile(ti, t0, n)
            P.barrier()

        with ExitStack() as S:
            rg = ffn_rings(S)
            nmr = (Ring(k, S, "sq", 3, [128, 512], BF16), Ring(k, S, "rstd", 2, [128, 512], F32), Ring(k, S, "nmtmp", 2, [128, 512], F32))
            xr = Ring(k, S, "xt", 2, [128, 8, 512], F32)
            xnr = Ring(k, S, "xn", 2, [128, 8, 512], BF16)

            def ffn2_tile(ti, t0, n):
                j = 0 if t0 < L else 1
                xt, xb = xr.next()
                xn, xnb = xnr.next()
                P.dma(xt[:, :, :n], xT_tile_ap(t0, n), reads=[b_xTt[ti]], writes=[xb])
                norm_mod(S, nmr, xt, xb, n, 2, j, xn, xnb)
                ffn_tile(S, rg, xt, xb, xn, xnb, n, 1, 2, j)
                P.dma(xT_tile_ap(t0, n), xt[:, :, :n], reads=[xb], writes=[b_xTt[ti]], eng="pool")

            for ti, (t0, n) in enumerate(TILES):
                if t0 >= L and not need_ctx:
                    continue
                ffn2_tile(ti, t0, n)
            P.barrier()

        if stop == "merge" and l == 0:
            return finish(k, st, I, out_d, dbg_d, xT, b_xTt)

    for l_ in range(DEPTH):
        r_ = do_layer(l_)
        if r_ is not None:
            return r_

    with ExitStack() as S:
        fg, b_fg = k.sb(S, "fg", [128, 8], F32)
        P.dma(fg[:], I["final_g"], writes=[b_fg])
        xr = Ring(k, S, "xt", 2, [128, 8, 512], F32)
        sq_r = Ring(k, S, "sq", 3, [128, 512], BF16)
        rstd_r = Ring(k, S, "rstd", 2, [128, 512], F32)
        yr = Ring(k, S, "yt", 2, [128, 8, 512], F32)
        tokr = Ring(k, S, "tok", 3, [128, 1024], F32)

        def final_tile(ti, t0, n):
            xt, xb = xr.next()
            P.dma(xt[:, :, :n], xT_tile_ap(t0, n), reads=[b_xTt[ti]], writes=[xb])
            ps, pb = k.acc()
            for fc in range(8):
                q, qb = sq_r.next()
                P.op("act", lambda e, q=q, fc=fc: e.activation(out=q[:, :n], in_=xt[:, fc, :n], func=AF.Square), [xb], [qb])
                P.op("pe", lambda e, q=q, fc=fc: e.matmul(ps[:, :n], lhsT=ones_b[:], rhs=q[:, :n], start=(fc == 0), stop=(fc == 7)), [qb, b_ones], [pb])
            rstd, rb = rstd_r.next()
            P.op("act", lambda e: e.activation(out=rstd[:, :n], in_=ps[:, :n], func=AF.Sqrt, bias=epsv[:, 0:1], scale=1.0 / D), [pb, b_eps], [rb])
            P.op("dve", lambda e: e.reciprocal(out=rstd[:, :n], in_=rstd[:, :n]), [rb], [rb])
            yt, yb = yr.next()
            for fc in range(8):
                P.op("dve", lambda e, fc=fc: e.scalar_tensor_tensor(out=yt[:, fc, :n], in0=xt[:, fc, :n], scalar=fg[:, fc:fc + 1], in1=rstd[:, :n],
                                                                     op0=ALU.mult, op1=ALU.mult), [xb, rb, b_fg], [yb])
            for jb in range(n // 128):
                tk, tkb = tokr.next()
                for half in range(2):
                    pst, ptb = k.tmp()
                    for f4 in range(4):
                        fc = half * 4 + f4
                        P.op("pe", lambda e, pst=pst, f4=f4, fc=fc, jb=jb: e.transpose(pst[:, f4 * 128:(f4 + 1) * 128], yt[:, fc, jb * 128:(jb + 1) * 128], ident_f[:]),
                             [yb, b_identf], [ptb])
                    copy_on(P, k.ev(), tk[:, half * 512:(half + 1) * 512], pst[:, :], [ptb], [tkb])
                P.dma(out_d[t0 + jb * 128:t0 + (jb + 1) * 128, :], tk[:], reads=[tkb], eng="pool")

        for ti, (t0, n) in enumerate(TILES):
            if t0 < L:
                final_tile(ti, t0, n)
        P.barrier()
    P.emit()
    st.close()
    return nc


DBG_WANT = []


def dbg_items(loc):
    return [f(loc) for f in DBG_WANT]


def finish_list(k, st, dbg_d, aps):
    P = k.P
    with ExitStack() as S:
        r = Ring(k, S, "fl", 2, [128, 4352], F32)
        row = 0
        for ap in aps:
            rows, cols = ap.shape
            for r0 in range(0, rows, 128):
                rr = min(128, rows - r0)
                t, b = r.next()
                P.dma(t[:rr, :cols], ap[r0:r0 + rr, :], writes=[b], eng="pool")
                P.dma(dbg_d[row:row + rr, :cols], t[:rr, :cols], reads=[b], eng="sp")
                row += rr
    P.emit()
    st.close()
    return k.nc


def finish(k, st, I, out_d, dbg_d, xT, b_xTt):
    P = k.P
    if dbg_d is not None:
        with ExitStack() as S:
            r = Ring(k, S, "fin", 2, [128, 8, 512], F32)
            for ti, (t0, n) in enumerate(TILES):
                t, b = r.next()
                P.dma(t[:, :, :n], xT[:, :, t0:t0 + n].rearrange("c p t -> p c t"), reads=[b_xTt[ti]], writes=[b])
                P.dma(dbg_d[:, :, t0:t0 + n].rearrange("c p t -> p c t"), t[:, :, :n], reads=[b], eng="pool")
    P.emit()
    st.close()
    return k.nc


def finish_dbg(k, st, I, out_d, dbg_d, items):
    P = k.P
    for (t, b, shape) in items:
        P.dma(dbg_d, t[:].rearrange("p a b -> p (a b)") if len(t.shape) == 3 else t[:], reads=[b], eng="pool")
    P.emit()
    st.close()
    return k.nc


def pcol(v, nchunk):
    return np.ascontiguousarray(np.asarray(v, np.float32).reshape(nchunk, 128).T)


def host_inputs(inp, b):
    f = np.float32
    m = {}
    m["x"] = np.ascontiguousarray(inp["x"][b], f)
    m["ctx"] = np.ascontiguousarray(inp["ctx"][b], f)
    cc = np.stack([np.asarray(inp["c"][b], f), np.asarray(inp["c_ctx"], f)], axis=-1)
    m["cc"] = np.ascontiguousarray(cc.reshape(8, 128, 2).transpose(1, 0, 2))
    m["ada_w"] = np.asarray(inp["ada_w"], f)
    m["ada_b"] = np.ascontiguousarray(np.asarray(inp["ada_b"], f).reshape(DEPTH, 72, 128).transpose(2, 0, 1))
    m["norm_g"] = np.ascontiguousarray(np.asarray(inp["norm_g"], f).reshape(DEPTH, 3, 8, 128).transpose(3, 0, 1, 2))
    for n in ("ffn_w_gate", "ffn_w_up", "ffn_w_down", "w_in", "mla_w_uq", "mla_w_ukv", "gla_w_gate", "gla_b_gate",
              "hy_filt_w1", "hy_filt_w2", "hy_filt_w3", "w_branch", "w_out"):
        m[n] = np.asarray(inp[n], f)
    m["mla_q_norm_g"] = np.ascontiguousarray(np.asarray(inp["mla_q_norm_g"], f).reshape(DEPTH, 2, 128).transpose(2, 0, 1))
    m["mla_kv_norm_g"] = np.ascontiguousarray(np.asarray(inp["mla_kv_norm_g"], f).T)
    m["gla_norm_g"] = np.ascontiguousarray(np.asarray(inp["gla_norm_g"], f).T)
    m["gqa_q_norm_g"] = np.ascontiguousarray(np.tile(np.asarray(inp["gqa_q_norm_g"], f), (1, 2)).T)
    m["gqa_k_norm_g"] = np.ascontiguousarray(np.tile(np.asarray(inp["gqa_k_norm_g"], f), (1, 2)).T)
    m["hy_sconv_w"] = np.ascontiguousarray(np.asarray(inp["hy_sconv_w"], f).reshape(DEPTH, 3, 12, 128).transpose(3, 0, 1, 2))
    m["hy_sconv_b"] = np.ascontiguousarray(np.asarray(inp["hy_sconv_b"], f).reshape(DEPTH, 12, 128).transpose(2, 0, 1))
    m["hy_filt_b1"] = np.ascontiguousarray(np.asarray(inp["hy_filt_b1"], f).T)
    m["hy_filt_b2"] = np.ascontiguousarray(np.asarray(inp["hy_filt_b2"], f).T)
    m["hy_filt_bias"] = np.ascontiguousarray(np.asarray(inp["hy_filt_bias"], f).reshape(DEPTH, 2, 4, 128).transpose(3, 0, 1, 2))
    m["final_g"] = pcol(inp["final_g"], 8)
    m.update(consts())
    return m


def kernel(**inputs):
    nc = bass.Bass("TRN2", target_bir_lowering=False)
    build(nc)
    in_maps = [host_inputs(inputs, b) for b in range(8)]
    res = run_bass_kernel_spmd(nc, in_maps, core_ids=list(range(8)))
    return np.stack([np.asarray(r["out"], np.float32) for r in res.results], axis=0)
```

```python
import math
from contextlib import ExitStack
import numpy as np
import ml_dtypes
import concourse.bass as bass
import concourse.mybir as mybir
from concourse.bass_utils import run_bass_kernel_spmd

F32 = mybir.dt.float32
BF16 = mybir.dt.bfloat16
AF = mybir.ActivationFunctionType
ALU = mybir.AluOpType
AX = mybir.AxisListType

D = 1024
L = 4096
CT = 256
NT = L + CT
DEPTH = 2
FF = 2816
NFC = FF // 128
P_IN = 8384
EPS = 1e-6
TILES = [(i * 512, 512) for i in range(8)] + [(L, CT)]
O_CQ, O_CKV, O_KR, O_BQ, O_BK, O_BV, O_LR, O_OG, O_CQQ, O_CK, O_CV, O_DP, O_GT = (
    0, 256, 384, 416, 672, 928, 1440, 1472, 1984, 2496, 2624, 2752, 4288)

DBG_STOP = None
PROJ_SECTIONS = {'mla', 'gqa', 'gla', 'hy'}
PROJ_TILES = 9
SKIP = ''
MLA_STEP = 99
NHEADS_DBG = 8
RING_F32, RING_BF, RING_OUT = 10, 16, 12


class Buf:
    __slots__ = ("name", "last_w", "readers")

    def __init__(self, name="b"):
        self.name = name
        self.last_w = None
        self.readers = []


class Op:
    __slots__ = ("eng", "fn", "deps", "signal", "sig_val", "sem", "is_dma", "bar", "snap")

    def __init__(self, eng, fn, is_dma):
        self.eng = eng
        self.fn = fn
        self.deps = []
        self.signal = False
        self.sig_val = None
        self.sem = None
        self.is_dma = is_dma
        self.bar = None
        self.snap = None


ENGS = ("pe", "act", "dve", "pool", "sp")
NWAIT = {}
SAME_ENGINE_SKIP = ("pe", "act", "dve")
SIGNAL_ALL = True
SKIP_ATTN = False
SKIP_GLA = False
ROPE_COPY_ENG = "act"
DMA_SLOTS = 8


class Prog:
    def __init__(self, nc):
        self.nc = nc
        self.ops = {e: [] for e in ENGS}
        self.nbar = 0

    def op(self, eng, fn, reads=(), writes=(), dma=False):
        o = Op(eng, fn, dma)
        if SIGNAL_ALL and not dma:
            o.signal = True
        deps = []
        for b in reads:
            if b.last_w is not None:
                deps.append(b.last_w)
        for b in writes:
            if b.last_w is not None:
                deps.append(b.last_w)
            deps.extend(b.readers)
        seen = set()
        for d in deps:
            if id(d) in seen or d is o:
                continue
            seen.add(id(d))
            if d.eng == eng and eng in SAME_ENGINE_SKIP and not d.is_dma:
                continue
            o.deps.append(d)
            d.signal = True
        for b in reads:
            b.readers.append(o)
        for b in writes:
            b.last_w = o
            b.readers = []
        self.ops[eng].append(o)
        return o

    def dma(self, out, in_, reads=(), writes=(), eng="sp", **kw):
        return self.op(eng, lambda e: e.dma_start(out=out, in_=in_, **kw), reads, writes, dma=True)

    def barrier(self):
        self.nbar += 1
        for e in ENGS:
            last = None
            for o in reversed(self.ops[e]):
                if o.bar is None:
                    last = o
                    break
            if last is not None and not last.is_dma:
                last.signal = True
            o = Op(e, None, False)
            o.bar = self.nbar
            o.snap = last
            self.ops[e].append(o)

    def emit(self):
        nc = self.nc
        with ExitStack() as st:
            csem = {e: st.enter_context(nc.semaphore("c_" + e)) for e in ENGS}
            bsem = {e: st.enter_context(nc.semaphore("b_" + e)) for e in ENGS}
            dsem = {e: [st.enter_context(nc.semaphore("d_%s_%d" % (e, i))) for i in range(DMA_SLOTS)] for e in ENGS}
            ccount = {e: 0 for e in ENGS}
            dcount = {e: [0] * DMA_SLOTS for e in ENGS}
            dnext = {e: 0 for e in ENGS}
            for e in ENGS:
                for o in self.ops[e]:
                    if o.bar is not None:
                        o.sig_val = list(dcount[e])
                        continue
                    if o.is_dma:
                        s = dnext[e]
                        dnext[e] = (s + 1) % DMA_SLOTS
                        o.sem = dsem[e][s]
                        prev = dcount[e][s]
                        dcount[e][s] = prev + 16
                        o.sig_val = prev + 16
                        o.signal = True
                        o.deps.append(("slot", o.sem, prev))
                    elif o.signal:
                        ccount[e] += 1
                        o.sem = csem[e]
                        o.sig_val = ccount[e]
            final_d = {e: list(dcount[e]) for e in ENGS}
            st.enter_context(nc.allow_non_contiguous_dma(reason="small strided vector loads"))
            blk = st.enter_context(nc.Block())

            def make(e):
                def body(eng):
                    waited = {}

                    def wait(sem, val):
                        if val <= 0:
                            return
                        k = id(sem)
                        if waited.get(k, 0) >= val:
                            return
                        waited[k] = val
                        NWAIT[e] = NWAIT.get(e, 0) + 1
                        eng.wait_ge(sem, val)

                    for o in self.ops[e]:
                        if o.bar is not None:
                            for i in range(DMA_SLOTS):
                                wait(dsem[e][i], o.sig_val[i])
                            if o.snap is not None and not o.snap.is_dma:
                                wait(o.snap.sem, o.snap.sig_val)
                            eng.nop().then_inc(bsem[e], 1)
                            for e2 in ENGS:
                                if e2 != e:
                                    wait(bsem[e2], o.bar)
                            continue
                        for d in o.deps:
                            if isinstance(d, tuple):
                                wait(d[1], d[2])
                            else:
                                wait(d.sem, d.sig_val)
                        ins = o.fn(eng)
                        if o.signal:
                            ins.then_inc(o.sem, 16 if o.is_dma else 1)
                    for i in range(DMA_SLOTS):
                        wait(dsem[e][i], final_d[e][i])
                return body

            blk.tensor(make("pe"))
            blk.scalar(make("act"))
            blk.vector(make("dve"))
            blk.gpsimd(make("pool"))
            blk.sync(make("sp"))


class Ring:
    def __init__(self, k, st, name, n, shape, dt):
        self.t = [st.enter_context(k.nc.sbuf_tensor("%s%d_%d" % (name, k.uid(), i), shape, dt)) for i in range(n)]
        self.b = [Buf(name + str(i)) for i in range(n)]
        self.i = 0

    def next(self):
        i = self.i
        self.i = (i + 1) % len(self.t)
        return self.t[i], self.b[i]


class KB:
    def __init__(self, nc, st):
        self.nc = nc
        self.st = st
        self.P = Prog(nc)
        self._uid = 0
        self.ps_t = [st.enter_context(nc.psum_tensor("psb%d" % i, [128, 512], F32)) for i in range(8)]
        self.ps_b = [Buf("ps%d" % i) for i in range(8)]
        self.acc_i = 0
        self.tmp_i = 0
        self.rr = 0

    def uid(self):
        self._uid += 1
        return self._uid

    def acc(self):
        i = self.acc_i
        self.acc_i = (i + 1) % 2
        return self.ps_t[i], self.ps_b[i]

    def tmp(self):
        i = 2 + self.tmp_i
        self.tmp_i = (self.tmp_i + 1) % 6
        return self.ps_t[i], self.ps_b[i]

    def sb(self, st, name, shape, dt):
        return st.enter_context(self.nc.sbuf_tensor("%s_%d" % (name, self.uid()), shape, dt)), Buf(name)

    def dram(self, name, shape, dt):
        return self.nc.dram_tensor(name, shape, dt).ap(), Buf(name)

    def ev(self):
        self.rr ^= 1
        return "dve" if self.rr else "act"


def copy_on(P, eng, out, in_, reads, writes):
    if eng == "act":
        return P.op("act", lambda e: e.copy(out=out, in_=in_), reads, writes)
    return P.op(eng, lambda e: e.tensor_copy(out=out, in_=in_), reads, writes)


def mm_group(P, ps, psb, pairs, reads, n=None):
    def fn(e):
        ins = None
        for i, (a, b) in enumerate(pairs):
            ins = e.matmul(ps, lhsT=a, rhs=b, start=(i == 0), stop=(i == len(pairs) - 1))
        return ins
    return P.op("pe", fn, reads, [psb])


def rope_tables(dim_layout):
    cos = np.ones((128, NT), np.float32)
    sin = np.zeros((128, NT), np.float32)
    RT = np.zeros((128, 128), np.float32)
    t = np.arange(L)
    rows = (t // 64).astype(np.float32)
    cols = (t % 64).astype(np.float32)
    for (r0, rd) in dim_layout:
        r = rd // 2
        half = r // 2
        freqs = (10000.0 ** (-np.arange(half, dtype=np.float32) / half)).astype(np.float32)
        for part, pos in ((0, rows), (1, cols)):
            base = r0 + part * r
            ang = pos[None, :] * freqs[:, None]
            c, s = np.cos(ang), np.sin(ang)
            cos[base:base + half, :L] = c
            cos[base + half:base + r, :L] = c
            sin[base:base + half, :L] = s
            sin[base + half:base + r, :L] = s
            for i in range(half):
                RT[base + half + i, base + i] = -1.0
                RT[base + i, base + half + i] = 1.0
    return cos, sin, RT


def dft_tables(n):
    N = 2 * n
    k = np.arange(n, dtype=np.float64)
    t = np.arange(n, dtype=np.float64)
    ang = 2.0 * np.pi * np.outer(t, k + 0.5) / N
    c, s = np.cos(ang), np.sin(ang)
    bf = ml_dtypes.bfloat16
    return (c.astype(np.float32).astype(bf), s.astype(np.float32).astype(bf),
            ((2.0 / N) * c.T).astype(np.float32).astype(bf), ((2.0 / N) * s.T).astype(np.float32).astype(bf))


def hyena_consts(n):
    t = np.linspace(0.0, 1.0, n, dtype=np.float32)
    w = (2.0 * math.pi * np.arange(n, dtype=np.float32) / n).astype(np.float32)
    f = np.linspace(1e-4, 15, 16, dtype=np.float32)
    fw = w[:, None] * f[None, :]
    feat = np.concatenate([t[:, None], np.cos(fw), -np.sin(fw)], axis=-1).astype(np.float32)
    deltas = np.abs(np.linspace(math.log(1e-2) / 1.5, math.log(1e-2) / 0.3, 512, dtype=np.float32))
    dec = np.exp(-t[:, None] * deltas[None, :]).astype(np.float32)
    decb = dec.copy()
    decb[0, :] = 0.0
    return np.ascontiguousarray(feat.T), dec, decb


def make_consts():
    c = {}
    c["k_ident"] = np.eye(128, dtype=np.float32)
    c["k_ones"] = np.ones((128, 128), np.float32)
    bo = np.zeros((128, 128), np.float32)
    bo[:64, :64] = 1.0
    bo[64:, 64:] = 1.0
    c["k_bones"] = bo
    cm, sm, rm = rope_tables([(64, 32)])
    c["k_cosM"], c["k_sinM"], c["k_rtM"] = cm, sm, rm
    cg, sg, rg = rope_tables([(0, 64), (64, 64)])
    c["k_cosG"], c["k_sinG"], c["k_rtG"] = cg, sg, rg
    for n, tag in ((L, "L"), (CT, "C")):
        a, b, c2, d2 = dft_tables(n)
        nb_ = n // 128
        a = np.ascontiguousarray(a.reshape(nb_, 128, nb_, 128).transpose(2, 1, 0, 3))
        b = np.ascontiguousarray(b.reshape(nb_, 128, nb_, 128).transpose(2, 1, 0, 3))
        c["k_dc1" + tag], c["k_ds1" + tag], c["k_dc2" + tag], c["k_ds2" + tag] = a, b, c2, d2
        ft, dec, decb = hyena_consts(n)
        c["k_feat" + tag], c["k_dec" + tag], c["k_decb" + tag] = ft, dec, decb
    j = np.arange(64)
    c["k_m_le"] = (j[:, None] <= j[None, :]).astype(np.float32)
    c["k_m_ge"] = (j[:, None] >= j[None, :]).astype(np.float32)
    c["k_m_gt"] = (j[:, None] > j[None, :]).astype(np.float32)
    c["k_m_lt"] = (j[:, None] < j[None, :]).astype(np.float32)
    return c


_CONSTS = None


def consts():
    global _CONSTS
    if _CONSTS is None:
        _CONSTS = make_consts()
    return _CONSTS


def build(nc, stop=None, dbg=None):
    st = ExitStack()
    k = KB(nc, st)
    P = k.P
    I = {}

    def din(name, shape, dt=F32):
        I[name] = nc.dram_tensor(name, list(shape), dt, kind="ExternalInput").ap()
        return I[name]

    din("x", [L, D]); din("ctx", [CT, D]); din("cc", [128, 8, 2])
    din("ada_w", [DEPTH, D, 9 * D]); din("ada_b", [128, DEPTH, 72]); din("norm_g", [128, DEPTH, 3, 8])
    din("ffn_w_gate", [DEPTH, 2, D, FF]); din("ffn_w_up", [DEPTH, 2, D, FF]); din("ffn_w_down", [DEPTH, 2, FF, D])
    din("w_in", [DEPTH, D, P_IN])
    din("mla_q_norm_g", [128, DEPTH, 2]); din("mla_w_uq", [DEPTH, 256, 768])
    din("mla_kv_norm_g", [128, DEPTH]); din("mla_w_ukv", [DEPTH, 128, 1024])
    din("gla_w_gate", [DEPTH, 2, 16, 256]); din("gla_b_gate", [DEPTH, 2, 256]); din("gla_norm_g", [128, DEPTH])
    din("gqa_q_norm_g", [128, DEPTH]); din("gqa_k_norm_g", [128, DEPTH])
    din("hy_sconv_w", [128, DEPTH, 3, 12]); din("hy_sconv_b", [128, DEPTH, 12])
    din("hy_filt_w1", [DEPTH, 33, 64]); din("hy_filt_b1", [64, DEPTH]); din("hy_filt_w2", [DEPTH, 64, 64])
    din("hy_filt_b2", [64, DEPTH]); din("hy_filt_w3", [DEPTH, 64, 2048]); din("hy_filt_bias", [128, DEPTH, 2, 4])
    din("w_branch", [DEPTH, 4, 512, D]); din("w_out", [DEPTH, D, D]); din("final_g", [128, 8])
    cs = consts()
    for name, arr in cs.items():
        din(name, arr.shape, BF16 if arr.dtype == ml_dtypes.bfloat16 else F32)
    out_d = nc.dram_tensor("out", [L, D], F32, kind="ExternalOutput").ap()
    dbg_d = None
    if dbg is not None:
        dbg_d = nc.dram_tensor("dbg", list(dbg), F32, kind="ExternalOutput").ap()

    G = ExitStack()
    st.enter_context(G)
    ident_f, b_identf = k.sb(G, "identf", [128, 128], F32)
    ident_b, b_identb = k.sb(G, "identb", [128, 128], BF16)
    ones_b, b_ones = k.sb(G, "onesb", [128, 128], BF16)
    bones_b, b_bones = k.sb(G, "bonesb", [128, 128], BF16)
    P.dma(ident_f[:], I["k_ident"], writes=[b_identf])
    P.dma(ident_b[:], I["k_ident"], writes=[b_identb], eng="pool")
    P.dma(ones_b[:], I["k_ones"], writes=[b_ones], eng="pool")
    P.dma(bones_b[:], I["k_bones"], writes=[b_bones], eng="pool")
    modv, b_modv = k.sb(G, "modv", [128, 72, 2], F32)
    gsv, b_gsv = k.sb(G, "gsv", [128, 3, 8, 2], F32)
    gtv, b_gtv = k.sb(G, "gtv", [128, 3, 8, 2], F32)
    smalls, b_smalls = k.sb(G, "smalls", [128, 64], F32)
    epsv, b_eps = k.sb(G, "epsv", [128, 1], F32)
    P.op("pool", lambda e: e.memset(epsv[:], EPS), [], [b_eps])

    xT, b_xT = k.dram("xT", [8, 128, NT], F32)
    b_xTt = [Buf("xT%d" % i) for i in range(len(TILES))]
    uT, _ = k.dram("uT", [8, 128, NT], BF16)
    b_uTt = [Buf("uT%d" % i) for i in range(len(TILES))]
    wg_b = [k.dram("wg_b%d" % i, [D, FF], BF16) for i in range(2)]
    wu_b = [k.dram("wu_b%d" % i, [D, FF], BF16) for i in range(2)]
    wd_b = [k.dram("wd_b%d" % i, [FF, D], BF16) for i in range(2)]
    win_b = k.dram("win_b", [D, P_IN], BF16)
    wbr_b = k.dram("wbr_b", [4, 512, D], BF16)
    wout_b = k.dram("wout_b", [D, D], BF16)

    def xT_tile_ap(t0, n):
        return xT[:, :, t0:t0 + n].rearrange("c p t -> p c t")

    def uT_tile_ap(t0, n):
        return uT[:, :, t0:t0 + n].rearrange("c p t -> p c t")

    with ExitStack() as S:
        xin = Ring(k, S, "xin", 8, [128, D], F32)
        xo = Ring(k, S, "xo", 2, [128, 8, 512], F32)
        for ti, (t0, n) in enumerate(TILES):
            ot, ob = xo.next()
            nb = n // 128
            blks = []
            for j in range(nb):
                tt, tb = xin.next()
                r0 = t0 + j * 128
                src = I["x"][r0:r0 + 128, :] if r0 < L else I["ctx"][r0 - L:r0 - L + 128, :]
                P.dma(tt[:], src, writes=[tb])
                blks.append((tt, tb))
                for fc in range(8):
                    pass
            for fc in range(8):
                ps, pb = k.tmp()
                for j in range(nb):
                    tt, tb = blks[j]
                    P.op("pe", lambda e, ps=ps, tt=tt, j=j, fc=fc: e.transpose(ps[:, j * 128:(j + 1) * 128], tt[:, fc * 128:(fc + 1) * 128], ident_f[:]),
                         [tb, b_identf], [pb])
                copy_on(P, k.ev(), ot[:, fc, :n], ps[:, :n], [pb], [ob])
            P.dma(xT_tile_ap(t0, n), ot[:, :, :n], reads=[ob], writes=[b_xTt[ti]], eng="pool")
        P.barrier()

    if stop == "s0":
        return finish(k, st, I, out_d, dbg_d, xT, b_xTt)

    def cast_dram(dst, src, rows, cols, bufs):
        step = 256
        for r in range(0, rows, step):
            rr = min(step, rows - r)
            P.dma(dst[r:r + rr, :], src[r:r + rr, :], writes=bufs, eng="pool")

    def norm_mod(S, rings, xt, xb, n, s, j, out, outb):
        sq_r, rstd_r, tmp_r = rings
        ps, pb = k.acc()
        sqs = []
        for fc in range(8):
            q, qb = sq_r.next()
            P.op("act", lambda e, q=q, fc=fc: e.activation(out=q[:, :n], in_=xt[:, fc, :n], func=AF.Square), [xb], [qb])
            sqs.append((q, qb))
            P.op("pe", lambda e, q=q, fc=fc, ps=ps: e.matmul(ps[:, :n], lhsT=ones_b[:], rhs=q[:, :n], start=(fc == 0), stop=(fc == 7)),
                 [qb, b_ones], [pb])
        rstd, rb = rstd_r.next()
        P.op("act", lambda e: e.activation(out=rstd[:, :n], in_=ps[:, :n], func=AF.Sqrt, bias=epsv[:, 0:1], scale=1.0 / D), [pb, b_eps], [rb])
        P.op("dve", lambda e: e.reciprocal(out=rstd[:, :n], in_=rstd[:, :n]), [rb], [rb])
        for fc in range(8):
            tp, tb = tmp_r.next()
            P.op("dve", lambda e, tp=tp, fc=fc: e.scalar_tensor_tensor(out=tp[:, :n], in0=xt[:, fc, :n], scalar=gsv[:, s, fc, j:j + 1],
                                                                         in1=rstd[:, :n], op0=ALU.mult, op1=ALU.mult),
                 [xb, rb, b_gsv], [tb])
            P.op("act", lambda e, tp=tp, fc=fc: e.activation(out=out[:, fc, :n], in_=tp[:, :n], func=AF.Identity,
                                                              bias=modv[:, 3 * s * 8 + fc, j:j + 1], scale=1.0),
                 [tb, b_modv], [outb])

    qA = k.dram("qA", [8, 128, NT], BF16)[0]; kA = k.dram("kA", [8, 128, NT], BF16)[0]
    vA = k.dram("vA", [34, 128, 8, 128], BF16)[0]
    qC = k.dram("qC", [4, 128, NT], BF16)[0]; kC = k.dram("kC", [4, 128, NT], BF16)[0]
    vC = k.dram("vC", [34, 128, 2, 128], BF16)[0]
    bqT = k.dram("bqT", [2, 128, NT], BF16)[0]; bkT = k.dram("bkT", [2, 128, NT], BF16)[0]
    bk_tok = k.dram("bk_tok", [NT, 256], BF16)[0]; bv_tok = k.dram("bv_tok", [NT, 512], BF16)[0]
    la_d = k.dram("la_d", [2, NT, 256], BF16)[0]
    ogT = k.dram("ogT", [4, 128, NT], BF16)[0]; dpT = k.dram("dpT", [12, 128, NT], BF16)[0]
    oT = k.dram("oT", [4, 4, 128, NT], BF16)[0]
    ofT = k.dram("ofT", [4, 128, NT], F32)[0]
    zT = k.dram("zT", [12, 128, NT], BF16)[0]
    hpm = k.dram("hpm", [2, 2, L, 512], BF16)[0]
    PQ = k.dram("PQ", [2, 2, L, 512], F32)[0]

    def do_layer(l):
        need_ctx = l < DEPTH - 1
        with ExitStack() as S:
            cct, b_cct = k.sb(S, "cct", [128, 8, 2], F32)
            P.dma(cct[:], I["cc"], writes=[b_cct])
            P.op("act", lambda e: e.activation(out=cct[:], in_=cct[:], func=AF.Silu), [b_cct], [b_cct])
            adab, b_adab = k.sb(S, "adab", [128, 72], F32)
            P.dma(adab[:], I["ada_b"][:, l, :], writes=[b_adab])
            ng, b_ng = k.sb(S, "ng", [128, 3, 8], F32)
            P.dma(ng[:], I["norm_g"][:, l, :, :], writes=[b_ng])
            awr = Ring(k, S, "adaw", 2, [128, 8, 1024], F32)
            ps, pb = k.acc()
            for i in range(9):
                wt, wb = awr.next()
                P.dma(wt[:], I["ada_w"][l, :, i * 1024:(i + 1) * 1024].rearrange("(kc p) n -> p kc n", p=128), writes=[wb])
                for fc in range(8):
                    ch = i * 8 + fc

                    def fn(e, wt=wt, fc=fc, ch=ch, ps=ps):
                        ins = None
                        for kc in range(8):
                            ins = e.matmul(ps[:, ch * 2:ch * 2 + 2], lhsT=wt[:, kc, fc * 128:(fc + 1) * 128], rhs=cct[:, kc, :],
                                           start=(kc == 0), stop=(kc == 7))
                        return ins
                    P.op("pe", fn, [wb, b_cct], [pb])
            for j in range(2):
                P.op("dve", lambda e, j=j, ps=ps: e.tensor_tensor(out=modv[:, :, j], in0=ps[:, j:144:2], in1=adab[:], op=ALU.add),
                     [pb, b_adab], [b_modv])
            for s in range(3):
                for j in range(2):
                    P.op("dve", lambda e, s=s, j=j: e.scalar_tensor_tensor(out=gsv[:, s, :, j], in0=modv[:, (3 * s + 1) * 8:(3 * s + 2) * 8, j], scalar=1.0,
                                                                             in1=ng[:, s, :], op0=ALU.add, op1=ALU.mult),
                         [b_modv, b_ng], [b_gsv])
                    P.op("dve", lambda e, s=s, j=j: e.tensor_scalar(out=gtv[:, s, :, j], in0=modv[:, (3 * s + 2) * 8:(3 * s + 3) * 8, j],
                                                                      scalar1=(1.0 if s == 1 else 0.5), scalar2=None, op0=ALU.mult),
                         [b_modv], [b_gtv])
            P.barrier()

        if stop == "ada" and l == 0:
            return finish_dbg(k, st, I, out_d, dbg_d, [(modv, b_modv, [128, 144])])

        for i in range(2):
            cast_dram(wg_b[i][0], I["ffn_w_gate"][l, i], D, FF, [wg_b[i][1]])
            cast_dram(wu_b[i][0], I["ffn_w_up"][l, i], D, FF, [wu_b[i][1]])
            cast_dram(wd_b[i][0], I["ffn_w_down"][l, i], FF, D, [wd_b[i][1]])
        cast_dram(win_b[0], I["w_in"][l], D, P_IN, [win_b[1]])
        cast_dram(wbr_b[0].rearrange("a k n -> (a k) n"), I["w_branch"][l].rearrange("a k n -> (a k) n"), 2048, D, [wbr_b[1]])
        cast_dram(wout_b[0], I["w_out"][l], D, D, [wout_b[1]])

        def ffn_tile(S, rg, xt, xb, xn, xnb, n, fi, s, j):
            wgr, wur, wdr, sgr, hT, hb = rg
            for fp in range(6):
                c0 = fp * 512
                pc = min(512, FF - c0)
                wgt, wgb = wgr.next()
                wut, wub = wur.next()
                P.dma(wgt[:, :, :pc], wg_b[fi][0].rearrange("(kc p) f -> p kc f", p=128)[:, :, c0:c0 + pc], reads=[wg_b[fi][1]], writes=[wgb])
                P.dma(wut[:, :, :pc], wu_b[fi][0].rearrange("(kc p) f -> p kc f", p=128)[:, :, c0:c0 + pc], reads=[wu_b[fi][1]], writes=[wub])
                for jj in range(pc // 128):
                    f = fp * 4 + jj
                    psg, pgb = k.tmp()
                    psu, pub = k.tmp()
                    mm_group(P, psg[:, :n], pgb, [(wgt[:, kc, jj * 128:(jj + 1) * 128], xn[:, kc, :n]) for kc in range(8)], [wgb, xnb])
                    mm_group(P, psu[:, :n], pub, [(wut[:, kc, jj * 128:(jj + 1) * 128], xn[:, kc, :n]) for kc in range(8)], [wub, xnb])
                    sg, sgb = sgr.next()
                    P.op("act", lambda e, sg=sg, psg=psg: e.activation(out=sg[:, :n], in_=psg[:, :n], func=AF.Silu), [pgb], [sgb])
                    P.op("dve", lambda e, sg=sg, psu=psu, f=f: e.tensor_tensor(out=hT[:, f, :n], in0=sg[:, :n], in1=psu[:, :n], op=ALU.mult),
                         [sgb, pub], [hb])
            for mp in range(2):
                wdt, wdb = wdr.next()
                P.dma(wdt[:], wd_b[fi][0].rearrange("(kc p) m -> p kc m", p=128)[:, :, mp * 512:(mp + 1) * 512], reads=[wd_b[fi][1]], writes=[wdb])
                for jj in range(4):
                    m = mp * 4 + jj
                    ps, pb = k.tmp()
                    mm_group(P, ps[:, :n], pb, [(wdt[:, kc, jj * 128:(jj + 1) * 128], hT[:, kc, :n]) for kc in range(NFC)], [wdb, hb])
                    P.op("dve", lambda e, ps=ps, m=m: e.scalar_tensor_tensor(out=xt[:, m, :n], in0=ps[:, :n], scalar=gtv[:, s, m, j:j + 1],
                                                                               in1=xt[:, m, :n], op0=ALU.mult, op1=ALU.add),
                         [pb, b_gtv, xb], [xb])

        def ffn_rings(S):
            wgr = Ring(k, S, "wg", 2, [128, 8, 512], BF16)
            wur = Ring(k, S, "wu", 2, [128, 8, 512], BF16)
            wdr = Ring(k, S, "wd", 2, [128, NFC, 512], BF16)
            sgr = Ring(k, S, "sg", 3, [128, 512], F32)
            hT, hb = k.sb(S, "hT", [128, NFC, 512], BF16)
            return (wgr, wur, wdr, sgr, hT, hb)

        with ExitStack() as S:
            rg = ffn_rings(S)
            nmr = (Ring(k, S, "sq", 3, [128, 512], BF16), Ring(k, S, "rstd", 2, [128, 512], F32), Ring(k, S, "nmtmp", 2, [128, 512], F32))
            xr = Ring(k, S, "xt", 2, [128, 8, 512], F32)
            xnr = Ring(k, S, "xn", 2, [128, 8, 512], BF16)
            for ti, (t0, n) in enumerate(TILES):
                j = 0 if t0 < L else 1
                xt, xb = xr.next()
                xn, xnb = xnr.next()
                P.dma(xt[:, :, :n], xT_tile_ap(t0, n), reads=[b_xTt[ti]], writes=[xb])
                norm_mod(S, nmr, xt, xb, n, 0, j, xn, xnb)
                ffn_tile(S, rg, xt, xb, xn, xnb, n, 0, 0, j)
                P.dma(xT_tile_ap(t0, n), xt[:, :, :n], reads=[xb], writes=[b_xTt[ti]], eng="pool")
                un, unb = xnr.next()
                norm_mod(S, nmr, xt, xb, n, 1, j, un, unb)
                P.dma(uT_tile_ap(t0, n), un[:, :, :n], reads=[unb], writes=[b_uTt[ti]], eng="pool")
            P.barrier()

        if stop == "ffn1" and l == 0:
            return finish(k, st, I, out_d, dbg_d, xT, b_xTt)

        winv = win_b[0].rearrange("(kc p) n -> p kc n", p=128)
        with ExitStack() as S:
            NB = lambda: Buf("x")
            WuqP, b_WuqP = k.sb(S, "WuqP", [128, 2, 8, 128], BF16)
            WukP, b_WukP = k.sb(S, "WukP", [128, 8, 128], BF16)
            WukV, b_WukV = k.sb(S, "WukV", [128, 8, 64], BF16)
            WkrP, b_WkrP = k.sb(S, "WkrP", [128, 8, 128], BF16)
            WckP, b_WckP = k.sb(S, "WckP", [128, 8, 4, 128], BF16)
            wgate, b_wgate = k.sb(S, "wgate", [16, 2, 256], BF16)
            biasb, b_biasb = k.sb(S, "biasb", [128, 512], F32)
            pv, b_pv = k.sb(S, "pv", [128, 8], F32)
            rtM, b_rtM = k.sb(S, "rtM", [128, 128], BF16)
            rtG, b_rtG = k.sb(S, "rtG", [128, 128], BF16)
            P.dma(rtM[:], I["k_rtM"], writes=[b_rtM], eng="pool")
            P.dma(rtG[:], I["k_rtG"], writes=[b_rtG], eng="pool")
            P.op("dve", lambda e: e.memset(WuqP[:], 0.0), [], [b_WuqP])
            P.op("dve", lambda e: e.memset(WukP[:], 0.0), [], [b_WukP])
            P.op("dve", lambda e: e.memset(WkrP[:], 0.0), [], [b_WkrP])
            P.op("dve", lambda e: e.memset(WckP[:], 0.0), [], [b_WckP])
            for kc in range(2):
                if 'B' not in SKIP:
                    P.dma(WuqP[:, kc, :, 0:96], I["mla_w_uq"][l, kc * 128:(kc + 1) * 128, :].rearrange("p (h c) -> p h c", c=96), writes=[b_WuqP], eng="pool")
            ukv = I["mla_w_ukv"][l].rearrange("p (h c) -> p h c", c=128)
            if 'B' not in SKIP:
                P.dma(WukP[:, :, 0:64], ukv[:, :, 0:64], writes=[b_WukP], eng="pool")
            if 'B' not in SKIP:
                P.dma(WukV[:], ukv[:, :, 64:128], writes=[b_WukV], eng="pool")
            if 'D' not in SKIP:
                P.dma(WkrP[:, :, 64:96], winv[:, :, O_KR:O_KR + 32], reads=[win_b[1]], writes=[b_WkrP])
            for g in range(2):
                for par in range(2):
                    if 'D' not in SKIP:
                        P.dma(WckP[:, :, g * 2 + par, par * 64:par * 64 + 64], winv[:, :, O_CK + g * 64:O_CK + g * 64 + 64], reads=[win_b[1]], writes=[b_WckP])
            if 'C' not in SKIP:
                P.dma(wgate[:], I["gla_w_gate"][l].rearrange("d r c -> r d c"), writes=[b_wgate], eng="pool")
            if 'A' not in SKIP:
                P.dma(biasb[:], I["gla_b_gate"][l].rearrange("d c -> (d c)").partition_broadcast(128), writes=[b_biasb])
            if 'E' not in SKIP:
                P.dma(pv[:, 0:2], I["mla_q_norm_g"][:, l, :], writes=[b_pv])
            if 'E' not in SKIP:
                P.dma(pv[:, 2:3], I["mla_kv_norm_g"][:, l:l + 1], writes=[b_pv])
            if 'E' not in SKIP:
                P.dma(pv[:, 3:4], I["gqa_q_norm_g"][:, l:l + 1], writes=[b_pv])
            if 'E' not in SKIP:
                P.dma(pv[:, 4:5], I["gqa_k_norm_g"][:, l:l + 1], writes=[b_pv])
            ur = Ring(k, S, "u", 2, [128, 8, 512], BF16)
            wr = Ring(k, S, "wp", 3, [128, 8, 512], BF16)
            tabr = Ring(k, S, "tab", 2, [128, 4, 512], F32)
            f32r = Ring(k, S, "f32r", RING_F32, [128, 512], F32)
            bfr = Ring(k, S, "bfr", RING_BF, [128, 512], BF16)
            cqr = Ring(k, S, "cq", 2, [128, 2, 512], F32)
            cqnr = Ring(k, S, "cqn", 2, [128, 2, 512], BF16)
            outr = Ring(k, S, "outr", RING_OUT, [128, 512], BF16)
            krr_r = Ring(k, S, "krr", 2, [128, 512], BF16)
            ckvn_r = Ring(k, S, "ckvn", 2, [128, 512], BF16)
            vaA = Ring(k, S, "vaA", 2, [128, 8, 128], BF16)
            vaC = Ring(k, S, "vaC", 2, [128, 2, 128], BF16)
            for (t_, b_) in zip(vaA.t + vaC.t, vaA.b + vaC.b):
                P.op("dve", lambda e, t_=t_: e.memset(t_[:], 1.0), [], [b_])
            lrr = Ring(k, S, "lrT", 4, [16, 512], BF16)

            def wcols(c0, width):
                wt, wb = wr.next()
                P.dma(wt[:, :, :width], winv[:, :, c0:c0 + width], reads=[win_b[1]], writes=[wb])
                return wt, wb

            def store(dst, src, sb, eng="pool"):
                P.dma(dst, src, reads=[sb], writes=[NB()], eng=eng)

            def do_tile(ti, t0, n):
                u, ub = ur.next()
                P.dma(u[:, :, :n], uT_tile_ap(t0, n), reads=[b_uTt[ti]], writes=[ub])
                tab, tabb = tabr.next()
                for i_, nm in enumerate(("k_cosM", "k_sinM", "k_cosG", "k_sinG")):
                    P.dma(tab[:, i_, :n], I[nm][:, t0:t0 + n], writes=[tabb])
                nchunk = n // 128

                def proj(wt, wb, c0, M):
                    ps, pb = k.tmp()
                    mm_group(P, ps[:M, :n], pb, [(wt[:, kc, c0:c0 + M], u[:, kc, :n]) for kc in range(8)], [wb, ub])
                    return ps, pb

                def rms_rstd(srcs, nfeat, lhsT, lb):
                    pss, pssb = k.acc()
                    for i_, (a, ab) in enumerate(srcs):
                        sq, sqb = bfr.next()
                        P.op("act", lambda e, sq=sq, a=a: e.activation(out=sq[:, :n], in_=a, func=AF.Square), [ab], [sqb])
                        P.op("pe", lambda e, sq=sq, i_=i_, pss=pss: e.matmul(pss[:, :n], lhsT=lhsT[:], rhs=sq[:, :n], start=(i_ == 0), stop=(i_ == len(srcs) - 1)),
                             [sqb, lb], [pssb])
                    r, rb = f32r.next()
                    P.op("act", lambda e: e.activation(out=r[:, :n], in_=pss[:, :n], func=AF.Sqrt, bias=epsv[:, 0:1], scale=1.0 / nfeat), [pssb, b_eps], [rb])
                    P.op("dve", lambda e: e.reciprocal(out=r[:, :n], in_=r[:, :n]), [rb], [rb])
                    return r, rb

                def rope(src, srcb, ci, rt, rtb, dst):
                    if dst is not None or True:
                        pc_, pcb_ = f32r.next()
                        copy_on(P, "dve", pc_[:, :n], src, [srcb], [pcb_])
                        src, srcb = pc_[:, :n], pcb_
                    qb_, qbb = bfr.next()
                    copy_on(P, ROPE_COPY_ENG, qb_[:, :n], src, [srcb], [qbb])
                    psr, psrb = k.tmp()
                    P.op("pe", lambda e: e.matmul(psr[:, :n], lhsT=rt[:], rhs=qb_[:, :n], start=True, stop=True), [qbb, rtb], [psrb])
                    t1, t1b = f32r.next()
                    t2, t2b = f32r.next()
                    P.op("dve", lambda e: e.tensor_tensor(out=t1[:, :n], in0=src, in1=tab[:, ci, :n], op=ALU.mult), [srcb, tabb], [t1b])
                    P.op("dve", lambda e: e.tensor_tensor(out=t2[:, :n], in0=psr[:, :n], in1=tab[:, ci + 1, :n], op=ALU.mult), [psrb, tabb], [t2b])
                    o_, ob_ = (dst or outr).next()
                    P.op("dve", lambda e: e.tensor_tensor(out=o_[:, :n], in0=t1[:, :n], in1=t2[:, :n], op=ALU.add), [t1b, t2b], [ob_])
                    return o_, ob_

                if 'mla' not in PROJ_SECTIONS:
                    return
                W, Wb = wcols(0, 416)
                cq, cqb = cqr.next()
                for c in range(2):
                    ps, pb = proj(W, Wb, c * 128, 128)
                    copy_on(P, "dve", cq[:, c, :n], ps[:, :n], [pb], [cqb])
                if MLA_STEP <= 1:
                    return
                r, rb = rms_rstd([(cq[:, 0, :n], cqb), (cq[:, 1, :n], cqb)], 256, ones_b, b_ones)
                cqn, cqnb = cqnr.next()
                for c in range(2):
                    P.op("dve", lambda e, c=c: e.scalar_tensor_tensor(out=cqn[:, c, :n], in0=cq[:, c, :n], scalar=pv[:, c:c + 1], in1=r[:, :n], op0=ALU.mult, op1=ALU.mult),
                         [cqb, rb, b_pv], [cqnb])
                if MLA_STEP <= 2:
                    return
                ps, pb = proj(W, Wb, 256, 128)
                ckv, ckvb = f32r.next()
                copy_on(P, "dve", ckv[:, :n], ps[:, :n], [pb], [ckvb])
                r2, r2b = rms_rstd([(ckv[:, :n], ckvb)], 128, ones_b, b_ones)
                ckvn, ckvnb = ckvn_r.next()
                P.op("dve", lambda e: e.scalar_tensor_tensor(out=ckvn[:, :n], in0=ckv[:, :n], scalar=pv[:, 2:3], in1=r2[:, :n], op0=ALU.mult, op1=ALU.mult),
                     [ckvb, r2b, b_pv], [ckvnb])
                if MLA_STEP <= 3:
                    return
                pskr, pskrb = k.tmp()
                mm_group(P, pskr[:, :n], pskrb, [(WkrP[:, kc, :], u[:, kc, :n]) for kc in range(8)], [b_WkrP, ub])
                krr, krrb = rope(pskr[:, :n], pskrb, 0, rtM, b_rtM, krr_r)
                if MLA_STEP <= 4:
                    return
                for h in range(NHEADS_DBG):
                    if 'Q' not in SKIP:
                        psq, psqb = k.tmp()
                        mm_group(P, psq[:, :n], psqb, [(WuqP[:, kc, h, :], cqn[:, kc, :n]) for kc in range(2)], [b_WuqP, cqnb])
                        if 'R' in SKIP:
                            qo, qob = outr.next()
                            copy_on(P, "act", qo[:, :n], psq[:, :n], [psqb], [qob])
                        else:
                            qo, qob = rope(psq[:, :n], psqb, 0, rtM, b_rtM, None)
                        if 'S' not in SKIP:
                            store(qA[h, :, t0:t0 + n], qo[:, :n], qob)
                    if 'K' not in SKIP:
                        psk, pskb = k.tmp()
                        mm_group(P, psk[:, :n], pskb, [(WukP[:, h, :], ckvn[:, :n]), (ident_b[:], krr[:, :n])], [b_WukP, ckvnb, b_identb, krrb])
                        ko, kob = outr.next()
                        copy_on(P, "act", ko[:, :n], psk[:, :n], [pskb], [kob])
                        if 'S' not in SKIP:
                            store(kA[h, :, t0:t0 + n], ko[:, :n], kob)
                if MLA_STEP <= 5:
                    return
                for jc in range(nchunk):
                    psv, psvb = k.tmp()
                    P.op("pe", lambda e, jc=jc, psv=psv: e.matmul(psv[:, :], lhsT=ckvn[:, jc * 128:(jc + 1) * 128], rhs=WukV[:].rearrange("p h d -> p (h d)"), start=True, stop=True),
                         [ckvnb, b_WukV], [psvb])
                    va, vab = vaA.next()
                    copy_on(P, "dve", va[:, :, 0:64], psv[:, :].rearrange("p (h d) -> p h d", d=64), [psvb], [vab])
                    store(vA[t0 // 128 + jc], va[:], vab)

                if 'gqa' not in PROJ_SECTIONS:
                    return
                def normrope(ps, pb, gcol):
                    pc, pcb = f32r.next()
                    copy_on(P, "dve", pc[:, :n], ps[:, :n], [pb], [pcb])
                    r_, rb_ = rms_rstd([(pc[:, :n], pcb)], 64, bones_b, b_bones)
                    P.op("dve", lambda e: e.scalar_tensor_tensor(out=pc[:, :n], in0=pc[:, :n], scalar=pv[:, gcol:gcol + 1], in1=r_[:, :n], op0=ALU.mult, op1=ALU.mult),
                         [pcb, rb_, b_pv], [pcb])
                    return rope(pc[:, :n], pcb, 2, rtG, b_rtG, None)

                W, Wb = wcols(O_CQQ, 512)
                for c in range(4):
                    ps, pb = proj(W, Wb, c * 128, 128)
                    qo, qob = normrope(ps, pb, 3)
                    store(qC[c, :, t0:t0 + n], qo[:, :n], qob)
                for kt in range(4):
                    ps, pb = k.tmp()
                    mm_group(P, ps[:, :n], pb, [(WckP[:, kc, kt, :], u[:, kc, :n]) for kc in range(8)], [b_WckP, ub])
                    ko, kob = normrope(ps, pb, 4)
                    store(kC[kt, :, t0:t0 + n], ko[:, :n], kob)
                W, Wb = wcols(O_CV, 128)
                for jc in range(nchunk):
                    psv, psvb = k.tmp()
                    mm_group(P, psv[:, :128], psvb, [(u[:, kc, jc * 128:(jc + 1) * 128], W[:, kc, 0:128]) for kc in range(8)], [ub, Wb])
                    va, vab = vaC.next()
                    copy_on(P, "dve", va[:, :, 0:64], psv[:, 0:128].rearrange("p (h d) -> p h d", d=64), [psvb], [vab])
                    store(vC[t0 // 128 + jc], va[:], vab)

                if 'gla' not in PROJ_SECTIONS:
                    return
                W, Wb = wcols(O_BQ, 512)
                for c in range(2):
                    ps, pb = proj(W, Wb, c * 128, 128)
                    o_, ob_ = outr.next()
                    P.op("act", lambda e, o_=o_, ps=ps: e.activation(out=o_[:, :n], in_=ps[:, :n], func=AF.Copy, scale=0.125), [pb], [ob_])
                    store(bqT[c, :, t0:t0 + n], o_[:, :n], ob_)
                    ps, pb = proj(W, Wb, 256 + c * 128, 128)
                    o_, ob_ = outr.next()
                    copy_on(P, "dve", o_[:, :n], ps[:, :n], [pb], [ob_])
                    store(bkT[c, :, t0:t0 + n], o_[:, :n], ob_)
                for jc in range(nchunk):
                    ps, pb = k.tmp()
                    mm_group(P, ps[:, :256], pb, [(u[:, kc, jc * 128:(jc + 1) * 128], W[:, kc, 256:512]) for kc in range(8)], [ub, Wb])
                    o_, ob_ = outr.next()
                    copy_on(P, "act", o_[:, :256], ps[:, :256], [pb], [ob_])
                    store(bk_tok[t0 + jc * 128:t0 + (jc + 1) * 128, :], o_[:, :256], ob_)
                W, Wb = wcols(O_BV, 512)
                for jc in range(nchunk):
                    ps, pb = k.tmp()
                    mm_group(P, ps[:, :], pb, [(u[:, kc, jc * 128:(jc + 1) * 128], W[:, kc, 0:512]) for kc in range(8)], [ub, Wb])
                    o_, ob_ = outr.next()
                    copy_on(P, "dve", o_[:, :], ps[:, :], [pb], [ob_])
                    store(bv_tok[t0 + jc * 128:t0 + (jc + 1) * 128, :], o_[:, :], ob_)
                W, Wb = wcols(O_LR, 32)
                for d in range(2):
                    ps, pb = k.tmp()
                    mm_group(P, ps[:16, :n], pb, [(W[:, kc, d * 16:(d + 1) * 16], u[:, kc, :n]) for kc in range(8)], [Wb, ub])
                    lr, lrb = lrr.next()
                    copy_on(P, "act", lr[:, :n], ps[:16, :n], [pb], [lrb])
                    for jc in range(nchunk):
                        ps2, pb2 = k.tmp()
                        P.op("pe", lambda e, ps2=ps2, lr=lr, jc=jc, d=d: e.matmul(ps2[:, :256], lhsT=lr[:, jc * 128:(jc + 1) * 128], rhs=wgate[:, d, :], start=True, stop=True),
                             [lrb, b_wgate], [pb2])
                        t1, t1b = f32r.next()
                        P.op("dve", lambda e, t1=t1, ps2=ps2, d=d: e.tensor_tensor(out=t1[:, :256], in0=ps2[:, :256], in1=biasb[:, d * 256:(d + 1) * 256], op=ALU.add), [pb2, b_biasb], [t1b])
                        P.op("act", lambda e, t1=t1: e.activation(out=t1[:, :256], in_=t1[:, :256], func=AF.Exp, scale=-1.0), [t1b], [t1b])
                        P.op("act", lambda e, t1=t1: e.activation(out=t1[:, :256], in_=t1[:, :256], func=AF.Ln, bias=1.0, scale=1.0), [t1b], [t1b])
                        o_, ob_ = outr.next()
                        P.op("dve", lambda e, t1=t1, o_=o_: e.tensor_scalar(out=o_[:, :256], in0=t1[:, :256], scalar1=-1.0 / 16.0, scalar2=None, op0=ALU.mult), [t1b], [ob_])
                        store(la_d[d, t0 + jc * 128:t0 + (jc + 1) * 128, :], o_[:, :256], ob_)
                W, Wb = wcols(O_OG, 512)
                for c in range(4):
                    ps, pb = proj(W, Wb, c * 128, 128)
                    o_, ob_ = outr.next()
                    P.op("act", lambda e, o_=o_, ps=ps: e.activation(out=o_[:, :n], in_=ps[:, :n], func=AF.Silu), [pb], [ob_])
                    store(ogT[c, :, t0:t0 + n], o_[:, :n], ob_)
                if 'hy' not in PROJ_SECTIONS:
                    return
                for pp in range(3):
                    W, Wb = wcols(O_DP + pp * 512, 512)
                    for c in range(4):
                        ps, pb = proj(W, Wb, c * 128, 128)
                        o_, ob_ = outr.next()
                        copy_on(P, k.ev(), o_[:, :n], ps[:, :n], [pb], [ob_])
                        store(dpT[pp * 4 + c, :, t0:t0 + n], o_[:, :n], ob_)

            for ti, (t0, n) in enumerate(TILES[:PROJ_TILES]):
                do_tile(ti, t0, n)
                P.barrier()
            P.barrier()

        if stop == "proj" and l == 0:
            return finish_list(k, st, dbg_d, dbg_items(locals()))

        with ExitStack() as S:
            NB = lambda: Buf("x")
            Kr = Ring(k, S, "Kr", 2, [128, NT], BF16)
            Vr = Ring(k, S, "Vr", 2, [128, 34, 128], BF16)
            qr = Ring(k, S, "qr", 3, [128, 512], BF16)
            pr = Ring(k, S, "pr", 7, [128, 512], BF16)
            rcr = Ring(k, S, "rcr", 2, [128, 512], F32)
            orr = Ring(k, S, "orr", 3, [128, 512], BF16)
            for br in (() if SKIP_ATTN else (0, 2)):
                scale = (96.0 ** -0.5) if br == 0 else 0.125
                for h in range(8):
                    Kt, Kb = Kr.next()
                    Vt, Vb = Vr.next()
                    if br == 0:
                        ksrc = kA[h]; vsrc = vA[:, :, h, :]; qsrc = qA[h]
                    else:
                        g = h // 4
                        ksrc = kC[g * 2 + (h % 2)]; vsrc = vC[:, :, g, :]; qsrc = qC[h // 2]
                    P.dma(Kt[:], ksrc, writes=[Kb])
                    P.dma(Vt[:], vsrc.rearrange("c p d -> p c d"), writes=[Vb])
                    for ti, (t0, n) in enumerate(TILES):
                        if t0 >= L:
                            if not need_ctx:
                                continue
                            kcs = list(range(32, 34))
                        else:
                            kcs = list(range(34))
                        qt, qb_ = qr.next()
                        P.dma(qt[:, :n], qsrc[:, t0:t0 + n], writes=[qb_])
                        psO, psOb = k.acc()
                        pend = []

                        def emit_pv(item, psO=psO, psOb=psOb, Vt=Vt, Vb=Vb, n=n, kcs=kcs):
                            i_, kc, pt, ptb = item
                            P.op("pe", lambda e: e.matmul(psO[:, :n], lhsT=Vt[:, kc, :], rhs=pt[:, :n], start=(i_ == 0), stop=(i_ == len(kcs) - 1)),
                                 [Vb, ptb], [psOb])
                        for i_, kc in enumerate(kcs):
                            psS, psSb = k.tmp()
                            P.op("pe", lambda e, psS=psS, Kt=Kt, kc=kc, qt=qt, n=n: e.matmul(psS[:, :n], lhsT=Kt[:, kc * 128:(kc + 1) * 128], rhs=qt[:, :n], start=True, stop=True),
                                 [Kb, qb_], [psSb])
                            pt, ptb = pr.next()
                            P.op("act", lambda e, pt=pt, psS=psS, n=n, scale=scale: e.activation(out=pt[:, :n], in_=psS[:, :n], func=AF.Exp, scale=scale), [psSb], [ptb])
                            pend.append((i_, kc, pt, ptb))
                            if len(pend) > 3:
                                emit_pv(pend.pop(0))
                        while pend:
                            emit_pv(pend.pop(0))
                        rc, rcb = rcr.next()
                        P.op("dve", lambda e, rc=rc, psO=psO, n=n: e.reciprocal(out=rc[64:128, :n], in_=psO[64:128, :n]), [psOb], [rcb])
                        ot, otb = orr.next()
                        P.op("dve", lambda e, ot=ot, psO=psO, rc=rc, n=n: e.tensor_tensor(out=ot[0:64, :n], in0=psO[0:64, :n], in1=rc[64:128, :n], op=ALU.mult), [psOb, rcb], [otb])
                        P.dma(oT[br, h // 2, (h % 2) * 64:(h % 2) * 64 + 64, t0:t0 + n], ot[0:64, :n], reads=[otb], writes=[NB()], eng="pool")
            P.barrier()

        if stop == "attn" and l == 0:
            return finish_list(k, st, dbg_d, dbg_items(locals()))

        with ExitStack() as S:
            NB = lambda: Buf("x")
            msk, b_msk = k.sb(S, "msk", [64, 4, 64], F32)
            msk4, b_msk4 = k.sb(S, "msk4", [64, 2, 4, 64], F32)
            hmask, b_hmask = k.sb(S, "hmask", [128, 2], F32)
            P.op("dve", lambda e: e.memset(hmask[:], 0.0), [], [b_hmask])
            P.op("dve", lambda e: e.memset(hmask[0:64, 0:1], 1.0), [b_hmask], [b_hmask])
            P.op("dve", lambda e: e.memset(hmask[64:128, 1:2], 1.0), [b_hmask], [b_hmask])
            for m_ in range(2):
                for h_ in range(4):
                    P.dma(msk4[:, m_, h_, :], I["k_m_le" if m_ == 0 else "k_m_ge"], writes=[b_msk4])
            mskb, b_mskb = k.sb(S, "mskb", [64, 4, 64], BF16)
            for i_, nm in enumerate(("k_m_le", "k_m_ge", "k_m_gt", "k_m_lt")):
                P.dma(msk[:, i_, :], I[nm], writes=[b_msk])
                P.dma(mskb[:, i_, :], I[nm], writes=[b_mskb], eng="pool")
            gng, b_gng = k.sb(S, "gng", [128, 1], F32)
            P.dma(gng[:], I["gla_norm_g"][:, l:l + 1], writes=[b_gng])
            Sst = [k.sb(S, "Sst%d" % i, [128, 128], F32) for i in range(2)]
            Sbf = [k.sb(S, "Sbf%d" % i, [128, 128], BF16) for i in range(2)]
            lar = Ring(k, S, "la", 2, [64, 8, 256], BF16)
            bktr = Ring(k, S, "bkt", 2, [64, 8, 256], BF16)
            bvr = Ring(k, S, "bv", 2, [64, 8, 512], BF16)
            bqr = Ring(k, S, "bq", 4, [128, 512], BF16)
            bkr = Ring(k, S, "bk", 4, [128, 512], BF16)
            Er = Ring(k, S, "E", 8, [128, 64], F32)
            qdr = Ring(k, S, "qd", 12, [128, 64], BF16)
            Abr = Ring(k, S, "Ab", 3, [64, 256], BF16)
            ker = Ring(k, S, "ke", 3, [64, 256], F32)
            kebr = Ring(k, S, "keb", 3, [64, 256], BF16)
            ogr = Ring(k, S, "og", 2, [128, 4, 512], F32)
            sqr = Ring(k, S, "gsq", 3, [128, 512], BF16)
            rsr = Ring(k, S, "grs", 2, [128, 512], F32)
            ogtr = Ring(k, S, "ogt", 2, [128, 512], BF16)
            outr2 = Ring(k, S, "gout", 3, [128, 512], BF16)
            tmpf = Ring(k, S, "gtmp", 2, [128, 512], F32)

            def gla_chunk(dirn, la, lab, bkt, bktb, bv, bvb, bq, bk, ci, og, ogb):
                mi = 0 if dirn == 0 else 1
                mk = 2 if dirn == 0 else 3
                last = 63 if dirn == 0 else 0
                qd = []
                decs = []
                for pr in range(2):
                    psb, psbb = k.tmp()
                    P.op("pe", lambda e, psb=psb, pr=pr: e.matmul(psb[:, :64], lhsT=la[:, ci, pr * 128:(pr + 1) * 128], rhs=mskb[:, mi, :], start=True, stop=True),
                         [lab, b_mskb], [psbb])
                    E, Eb = Er.next()
                    Ei, Eib = Er.next()
                    P.op("act", lambda e, E=E, psb=psb: e.activation(out=E[:, :], in_=psb[:, :64], func=AF.Exp), [psbb], [Eb])
                    P.op("act", lambda e, Ei=Ei, psb=psb: e.activation(out=Ei[:, :], in_=psb[:, :64], func=AF.Exp, scale=-1.0), [psbb], [Eib])
                    qt0, qt0b = qdr.next()
                    qt1, qt1b = qdr.next()
                    kt, ktb = qdr.next()
                    bqt, bqb = bq[pr]
                    bkt_, bkb_ = bk[pr]
                    for hh_, (qt_, qtb_) in enumerate(((qt0, qt0b), (qt1, qt1b))):
                        P.op("dve", lambda e, qt_=qt_, bqt=bqt, E=E, hh_=hh_: e.scalar_tensor_tensor(out=qt_[:, :], in0=bqt[:, ci * 64:(ci + 1) * 64], scalar=hmask[:, hh_:hh_ + 1], in1=E[:, :],
                                                                                                     op0=ALU.mult, op1=ALU.mult), [bqb, Eb, b_hmask], [qtb_])
                    P.op("dve", lambda e, kt=kt, bkt_=bkt_, Ei=Ei: e.tensor_tensor(out=kt[:, :], in0=bkt_[:, ci * 64:(ci + 1) * 64], in1=Ei[:, :], op=ALU.mult), [bkb_, Eib], [ktb])
                    qd.append(((qt0, qt0b), (qt1, qt1b), kt, ktb))
                    decs.append((E, Eb))
                psA, psAb = k.tmp()
                for h in range(4):
                    q0_, q1_, kt, ktb = qd[h // 2]
                    qt, qtb = (q0_, q1_)[h % 2]
                    P.op("pe", lambda e, h=h, qt=qt, kt=kt: e.matmul(psA[:64, h * 64:(h + 1) * 64], lhsT=kt[:, :], rhs=qt[:, :], start=True, stop=True),
                         [qtb, ktb], [psAb])
                Ab, Abb = Abr.next()
                P.op("dve", lambda e: e.tensor_tensor(out=Ab[:, :], in0=psA[:64, :256], in1=msk4[:, mi, :, :].rearrange("p h i -> p (h i)"), op=ALU.mult), [psAb, b_msk4], [Abb])
                pso, psob = k.tmp()
                for h in range(4):
                    q0_, q1_, kt, ktb = qd[h // 2]
                    qt, qtb = (q0_, q1_)[h % 2]
                    Sb_, Sbb_ = Sbf[h // 2]

                    def fn(e, h=h, qt=qt, Sb_=Sb_):
                        e.matmul(pso[:, h * 64:(h + 1) * 64], lhsT=bv[:, ci, h * 128:(h + 1) * 128], rhs=Ab[:, h * 64:(h + 1) * 64], start=True, stop=False)
                        return e.matmul(pso[:, h * 64:(h + 1) * 64], lhsT=Sb_[:, :], rhs=qt[:, :], start=False, stop=True)
                    P.op("pe", fn, [bvb, Abb, Sbb_, qtb], [psob])
                ogv = og[:, :, ci * 64:(ci + 1) * 64]
                if dirn == 0:
                    P.op("act", lambda e: e.copy(out=ogv, in_=pso[:, :256].rearrange("p (h i) -> p h i", h=4)), [psob], [ogb])
                else:
                    P.op("dve", lambda e: e.tensor_tensor(out=ogv, in0=ogv, in1=pso[:, :256].rearrange("p (h i) -> p h i", h=4), op=ALU.add), [psob, ogb], [ogb])
                psR, psRb = k.tmp()
                P.op("pe", lambda e: e.matmul(psR[:64, :256], lhsT=mskb[:, mk, :], rhs=la[:, ci, :], start=True, stop=True), [lab, b_mskb], [psRb])
                ke, keb = ker.next()
                P.op("act", lambda e: e.activation(out=ke[:, :], in_=psR[:64, :256], func=AF.Exp), [psRb], [keb])
                kb_, kbb_ = kebr.next()
                P.op("dve", lambda e: e.tensor_tensor(out=kb_[:, :], in0=ke[:, :], in1=bkt[:, ci, :], op=ALU.mult), [keb, bktb], [kbb_])
                for pr in range(2):
                    E, Eb = decs[pr]
                    St, Stb = Sst[pr]
                    Sb_, Sbb_ = Sbf[pr]
                    for hh in range(2):
                        h = pr * 2 + hh
                        psU, psUb = k.tmp()
                        P.op("pe", lambda e, psU=psU, pr=pr, h=h: e.matmul(psU[:, :128], lhsT=kb_[:, pr * 128:(pr + 1) * 128], rhs=bv[:, ci, h * 128:(h + 1) * 128], start=True, stop=True),
                             [kbb_, bvb], [psUb])
                        r0 = hh * 64
                        P.op("dve", lambda e, psU=psU, r0=r0, St=St, E=E: e.scalar_tensor_tensor(out=St[r0:r0 + 64, :], in0=St[r0:r0 + 64, :], scalar=E[r0:r0 + 64, last:last + 1],
                                                                                                   in1=psU[r0:r0 + 64, :128], op0=ALU.mult, op1=ALU.add),
                             [psUb, Stb, Eb], [Stb])
                    P.op("act", lambda e, St=St, Sb_=Sb_: e.copy(out=Sb_[:, :], in_=St[:, :]), [Stb], [Sbb_])

            def finalize(og, ogb, t0, n):
                for h in range(4):
                    sq, sqb = sqr.next()
                    P.op("act", lambda e, sq=sq, h=h: e.activation(out=sq[:, :n], in_=og[:, h, :n], func=AF.Square), [ogb], [sqb])
                    pss, pssb = k.acc()
                    P.op("pe", lambda e, sq=sq, pss=pss: e.matmul(pss[:, :n], lhsT=ones_b[:], rhs=sq[:, :n], start=True, stop=True), [sqb, b_ones], [pssb])
                    rs, rsb = rsr.next()
                    P.op("act", lambda e, rs=rs, pss=pss: e.activation(out=rs[:, :n], in_=pss[:, :n], func=AF.Sqrt, bias=epsv[:, 0:1], scale=1.0 / 128), [pssb, b_eps], [rsb])
                    P.op("dve", lambda e, rs=rs: e.reciprocal(out=rs[:, :n], in_=rs[:, :n]), [rsb], [rsb])
                    ogt, ogtb = ogtr.next()
                    P.dma(ogt[:, :n], ogT[h, :, t0:t0 + n], writes=[ogtb])
                    tf, tfb = tmpf.next()
                    P.op("dve", lambda e, tf=tf, h=h, rs=rs: e.scalar_tensor_tensor(out=tf[:, :n], in0=og[:, h, :n], scalar=gng[:, 0:1], in1=rs[:, :n], op0=ALU.mult, op1=ALU.mult),
                         [ogb, rsb, b_gng], [tfb])
                    ot_, otb_ = outr2.next()
                    P.op("dve", lambda e, tf=tf, ot_=ot_, ogt=ogt: e.tensor_tensor(out=ot_[:, :n], in0=tf[:, :n], in1=ogt[:, :n], op=ALU.mult), [tfb, ogtb], [otb_])
                    P.dma(oT[1, h, :, t0:t0 + n], ot_[:, :n], reads=[otb_], writes=[NB()], eng="pool")

            def gla_group(dirn, t0, n, want_out):
                nch = n // 64
                la, lab = lar.next()
                bkt, bktb = bktr.next()
                bv, bvb = bvr.next()
                P.dma(la[:, :nch, :], la_d[dirn, t0:t0 + n, :].rearrange("(c j) d -> j c d", j=64), writes=[lab])
                P.dma(bkt[:, :nch, :], bk_tok[t0:t0 + n, :].rearrange("(c j) d -> j c d", j=64), writes=[bktb])
                P.dma(bv[:, :nch, :], bv_tok[t0:t0 + n, :].rearrange("(c j) d -> j c d", j=64), writes=[bvb])
                bq, bk = [], []
                for pr in range(2):
                    t_, b_ = bqr.next()
                    P.dma(t_[:, :n], bqT[pr, :, t0:t0 + n], writes=[b_])
                    bq.append((t_, b_))
                    t_, b_ = bkr.next()
                    P.dma(t_[:, :n], bkT[pr, :, t0:t0 + n], writes=[b_])
                    bk.append((t_, b_))
                og, ogb = ogr.next()
                if dirn == 1 and want_out:
                    P.dma(og[:, :, :n], ofT[:, :, t0:t0 + n].rearrange("h p t -> p h t"), writes=[ogb])
                order = range(nch) if dirn == 0 else range(nch - 1, -1, -1)
                for ci in order:
                    gla_chunk(dirn, la, lab, bkt, bktb, bv, bvb, bq, bk, ci, og, ogb)
                if want_out:
                    if dirn == 0:
                        P.dma(ofT[:, :, t0:t0 + n].rearrange("h p t -> p h t"), og[:, :, :n], reads=[ogb], writes=[NB()], eng="pool")
                    else:
                        finalize(og, ogb, t0, n)

            for dirn in (() if SKIP_GLA else range(2)):
                for (St, Stb), (Sb_, Sbb_) in zip(Sst, Sbf):
                    P.op("dve", lambda e, St=St: e.memset(St[:], 0.0), [], [Stb])
                    P.op("dve", lambda e, Sb_=Sb_: e.memset(Sb_[:], 0.0), [], [Sbb_])
                gla_group(dirn, L, CT, need_ctx)
                groups = [(i * 512, 512) for i in range(8)]
                if dirn == 1:
                    groups = groups[::-1]
                if dirn == 1:
                    P.barrier()
                for (t0, n) in groups:
                    gla_group(dirn, t0, n, True)
            P.barrier()

        if stop == "gla" and l == 0:
            return finish_list(k, st, dbg_d, dbg_items(locals()))

        TWO_PI = 2.0 * math.pi
        segs = [(0, L, "L")] + ([(L, CT, "C")] if need_ctx else [])
        def hy_segment(toff, ns, tag):
            nblk = ns // 128
            tw = min(512, ns)
            ntt = ns // tw
            with ExitStack() as S:
                w1s, b_w1s = k.sb(S, "hw1", [33, 64], F32)
                w2s, b_w2s = k.sb(S, "hw2", [64, 64], F32)
                w3s, b_w3s = k.sb(S, "hw3", [64, 2048], F32)
                bb, b_bb = k.sb(S, "hbb", [64, 4], F32)
                P.dma(w1s[:], I["hy_filt_w1"][l], writes=[b_w1s])
                P.dma(w2s[:], I["hy_filt_w2"][l], writes=[b_w2s])
                P.dma(w3s[:], I["hy_filt_w3"][l], writes=[b_w3s])
                P.dma(bb[:, 0:1], I["hy_filt_b1"][:, l:l + 1], writes=[b_bb])
                P.dma(bb[:, 1:2], I["hy_filt_b2"][:, l:l + 1], writes=[b_bb])
                P.op("dve", lambda e: e.memset(bb[:, 2:3], -math.pi), [b_bb], [b_bb])
                ftr = Ring(k, S, "feat", 2, [33, 512], F32)
                h1r = Ring(k, S, "h1", 2, [64, 512], F32)
                h2r = Ring(k, S, "h2", 2, [64, 512], F32)
                decr = Ring(k, S, "dec", 2, [128, 2, 512], F32)
                hfr = Ring(k, S, "hf", 4, [128, 512], F32)
                hor = Ring(k, S, "ho", 4, [128, 512], BF16)

                kir = Ring(k, S, "ki", 2, [64, 512], mybir.dt.int32)
                kfr = Ring(k, S, "kf", 4, [64, 512], F32)

                def sin_layer(ps, pb, bcol, out, outb, w):
                    u_, ub_ = kfr.next()
                    P.op("dve", lambda e: e.tensor_scalar(out=u_[:, :w], in0=ps[:64, :w], scalar1=bb[:, bcol:bcol + 1], scalar2=1.0 / TWO_PI, op0=ALU.add, op1=ALU.mult), [pb, b_bb], [ub_])
                    ki, kib = kir.next()
                    P.op("dve", lambda e: e.tensor_copy(out=ki[:, :w], in_=u_[:, :w]), [ub_], [kib])
                    kf, kfb = kfr.next()
                    P.op("dve", lambda e: e.tensor_copy(out=kf[:, :w], in_=ki[:, :w]), [kib], [kfb])
                    P.op("dve", lambda e: e.tensor_tensor(out=u_[:, :w], in0=u_[:, :w], in1=kf[:, :w], op=ALU.subtract), [ub_, kfb], [ub_])
                    P.op("dve", lambda e: e.tensor_scalar(out=kf[:, :w], in0=u_[:, :w], scalar1=0.5, scalar2=None, op0=ALU.is_gt), [ub_], [kfb])
                    P.op("dve", lambda e: e.tensor_tensor(out=u_[:, :w], in0=u_[:, :w], in1=kf[:, :w], op=ALU.subtract), [ub_, kfb], [ub_])
                    P.op("dve", lambda e: e.tensor_scalar(out=kf[:, :w], in0=u_[:, :w], scalar1=-0.5, scalar2=None, op0=ALU.is_lt), [ub_], [kfb])
                    P.op("dve", lambda e: e.tensor_tensor(out=u_[:, :w], in0=u_[:, :w], in1=kf[:, :w], op=ALU.add), [ub_, kfb], [ub_])
                    P.op("act", lambda e: e.activation(out=out[:, :w], in_=u_[:, :w], func=AF.Sin, scale=TWO_PI), [ub_], [outb])

                def filt_tile(t0):
                    ft, ftb = ftr.next()
                    P.dma(ft[:, :tw], I["k_feat" + tag][:, t0:t0 + tw], writes=[ftb])
                    ps, pb = k.tmp()
                    P.op("pe", lambda e: e.matmul(ps[:64, :tw], lhsT=w1s[:], rhs=ft[:, :tw], start=True, stop=True), [b_w1s, ftb], [pb])
                    h1, h1b = h1r.next()
                    sin_layer(ps, pb, 0, h1, h1b, tw)
                    ps2, pb2 = k.tmp()
                    P.op("pe", lambda e: e.matmul(ps2[:64, :tw], lhsT=w2s[:], rhs=h1[:, :tw], start=True, stop=True), [b_w2s, h1b], [pb2])
                    h2, h2b = h2r.next()
                    sin_layer(ps2, pb2, 1, h2, h2b, tw)
                    for jb in range(tw // 128):
                        r0 = t0 + jb * 128
                        dc, dcb = decr.next()
                        P.dma(dc[:, 0, :], I["k_dec" + tag][r0:r0 + 128, :], writes=[dcb])
                        P.dma(dc[:, 1, :], I["k_decb" + tag][r0:r0 + 128, :], writes=[dcb])
                        for o in range(2):
                            hf_, hb_ = [], []
                            for dr in range(2):
                                ps3, pb3 = k.tmp()
                                c0 = o * 1024 + dr * 512
                                P.op("pe", lambda e, ps3=ps3, c0=c0, jb=jb: e.matmul(ps3[:, :], lhsT=h2[:, jb * 128:(jb + 1) * 128], rhs=w3s[:, c0:c0 + 512], start=True, stop=True),
                                     [h2b, b_w3s], [pb3])
                                hf, hfb = hfr.next()
                                P.op("dve", lambda e, hf=hf, ps3=ps3, dr=dr, dc=dc: e.tensor_tensor(out=hf[:, :], in0=ps3[:, :], in1=dc[:, dr, :], op=ALU.mult), [pb3, dcb], [hfb])
                                hf_.append((hf, hfb))
                            for pm, op_ in ((0, ALU.add), (1, ALU.subtract)):
                                ho, hob = hor.next()
                                P.op("pool", lambda e, ho=ho, op_=op_, a=hf_[0][0], b=hf_[1][0]: e.tensor_tensor(out=ho[:, :], in0=a[:, :], in1=b[:, :], op=op_),
                                     [hf_[0][1], hf_[1][1]], [hob])
                                P.dma(hpm[o, pm, r0:r0 + 128, :], ho[:, :], reads=[hob], writes=[Buf("x")], eng="pool")

                for tt in range(ntt):
                    filt_tile(tt * tw)
                P.barrier()
            with ExitStack() as S:
                hres, b_hres = k.sb(S, "hres", [128, 2, 32, 512], BF16)
                tbr = Ring(k, S, "tb1", 2, [128, 2, 32, 128], BF16)
                pqr = Ring(k, S, "pq", 4, [128, 512], F32)
                for o in range(2):
                    for pm in range(2):
                        P.dma(hres[:, pm, :nblk, :], hpm[o, pm, 0:ns, :].rearrange("(c p) d -> p c d", p=128), writes=[b_hres])
                    for fc in range(nblk):
                        tb, tbb = tbr.next()
                        P.dma(tb[:, 0, :nblk, :], I["k_dc1" + tag][fc], writes=[tbb])
                        P.dma(tb[:, 1, :nblk, :], I["k_ds1" + tag][fc], writes=[tbb])
                        for pm in range(2):
                            ps, pb = k.tmp()
                            mm_group(P, ps[:, :], pb, [(tb[:, pm, tc, :], hres[:, pm, tc, :]) for tc in range(nblk)], [tbb, b_hres])
                            pq, pqb = pqr.next()
                            copy_on(P, k.ev(), pq[:, :], ps[:, :], [pb], [pqb])
                            P.dma(PQ[o, pm, fc * 128:(fc + 1) * 128, :], pq[:, :], reads=[pqb], writes=[Buf("x")], eng="pool")
                P.barrier()
            with ExitStack() as S:
                scw, b_scw = k.sb(S, "scw", [128, 3, 12], F32)
                scb, b_scb = k.sb(S, "scb", [128, 12], F32)
                fbs, b_fbs = k.sb(S, "fbs", [128, 2, 4], F32)
                P.dma(scw[:], I["hy_sconv_w"][:, l, :, :], writes=[b_scw])
                P.dma(scb[:], I["hy_sconv_b"][:, l, :], writes=[b_scb])
                P.dma(fbs[:], I["hy_filt_bias"][:, l, :, :], writes=[b_fbs])
                vtok, b_vtok = k.sb(S, "vtok", [128, 32, 512], BF16)
                Yt, b_Yt = k.sb(S, "Yt", [128, 2, 32, 512], BF16)
                pr_ = Ring(k, S, "pin", 2, [128, 514], BF16)
                zr = Ring(k, S, "zf", 3, [128, 512], F32)
                zbr = Ring(k, S, "zb", 3, [128, 512], BF16)
                tb1 = Ring(k, S, "tb1b", 2, [128, 2, 32, 128], BF16)
                tb2 = Ring(k, S, "tb2", 2, [128, 2, 4, 512], BF16)
                pqr = Ring(k, S, "pq2", 2, [128, 2, 512], F32)
                ewr = Ring(k, S, "ew", 4, [128, 512], F32)
                xr_ = Ring(k, S, "x1t", 2, [128, 512], BF16)
                vr_ = Ring(k, S, "vtt", 2, [128, 512], BF16)

                def to_vtok(zf, zfb, cc, t0, w):
                    pst, ptb = k.tmp()
                    for jb in range(w // 128):
                        P.op("pe", lambda e, jb=jb: e.transpose(pst[:, jb * 128:(jb + 1) * 128], zf[:, jb * 128:(jb + 1) * 128], ident_f[:]), [zfb, b_identf], [ptb])
                    b0 = t0 // 128
                    nb_ = w // 128
                    copy_on(P, k.ev(), vtok[:, b0:b0 + nb_, cc * 128:(cc + 1) * 128], pst[:, :w].rearrange("p (b c) -> p b c", c=128), [ptb], [b_vtok])

                def sconv_tile(c, t0):
                    pin, pinb = pr_.next()
                    lo = max(t0 - 1, 0)
                    hi = min(t0 + tw + 1, ns)
                    off = lo - (t0 - 1)
                    P.dma(pin[:, off:off + (hi - lo)], dpT[c, :, toff + lo:toff + hi], writes=[pinb])
                    zf, zfb = zr.next()
                    P.op("act", lambda e: e.activation(out=zf[:, :tw], in_=pin[:, 1:tw + 1], func=AF.Identity, scale=scw[:, 1, c:c + 1], bias=scb[:, c:c + 1]), [pinb, b_scw, b_scb], [zfb])
                    a0 = 1 if t0 == 0 else 0
                    P.op("dve", lambda e: e.scalar_tensor_tensor(out=zf[:, a0:tw], in0=pin[:, a0:tw], scalar=scw[:, 0, c:c + 1], in1=zf[:, a0:tw], op0=ALU.mult, op1=ALU.add), [pinb, b_scw, zfb], [zfb])
                    a1 = tw - 1 if t0 + tw >= ns else tw
                    P.op("dve", lambda e: e.scalar_tensor_tensor(out=zf[:, 0:a1], in0=pin[:, 2:a1 + 2], scalar=scw[:, 2, c:c + 1], in1=zf[:, 0:a1], op0=ALU.mult, op1=ALU.add), [pinb, b_scw, zfb], [zfb])
                    zb, zbb = zbr.next()
                    copy_on(P, "act", zb[:, :tw], zf[:, :tw], [zfb], [zbb])
                    P.dma(zT[c, :, toff + t0:toff + t0 + tw], zb[:, :tw], reads=[zbb], writes=[Buf("x")], eng="pool")
                    if c >= 8:
                        to_vtok(zf, zfb, c - 8, t0, tw)

                for c in range(12):
                    for tt in range(ntt):
                        sconv_tile(c, tt * tw)
                P.barrier()

                bank = [(k.ps_t[i], k.ps_b[i]) for i in range(8)]
                for o in range(2):
                    for fc in range(nblk):
                        tb, tbb = tb1.next()
                        P.dma(tb[:, 0, :nblk, :], I["k_dc1" + tag][fc], writes=[tbb])
                        P.dma(tb[:, 1, :nblk, :], I["k_ds1" + tag][fc], writes=[tbb])
                        pq, pqb = pqr.next()
                        P.dma(pq[:, 0, :], PQ[o, 0, fc * 128:(fc + 1) * 128, :], writes=[pqb])
                        P.dma(pq[:, 1, :], PQ[o, 1, fc * 128:(fc + 1) * 128, :], writes=[pqb])
                        psA, pAb = k.tmp()
                        psB, pBb = k.tmp()
                        mm_group(P, psA[:, :], pAb, [(tb[:, 0, tc, :], vtok[:, tc, :]) for tc in range(nblk)], [tbb, b_vtok])
                        mm_group(P, psB[:, :], pBb, [(tb[:, 1, tc, :], vtok[:, tc, :]) for tc in range(nblk)], [tbb, b_vtok])
                        e1, e1b = ewr.next(); e2, e2b = ewr.next(); e3, e3b = ewr.next(); e4, e4b = ewr.next()
                        P.op("dve", lambda e, e1=e1, pq=pq, psA=psA: e.tensor_tensor(out=e1[:, :], in0=psA[:, :], in1=pq[:, 0, :], op=ALU.mult), [pAb, pqb], [e1b])
                        P.op("dve", lambda e, e2=e2, pq=pq, psB=psB: e.tensor_tensor(out=e2[:, :], in0=psB[:, :], in1=pq[:, 1, :], op=ALU.mult), [pBb, pqb], [e2b])
                        P.op("dve", lambda e, e3=e3, pq=pq, psB=psB: e.tensor_tensor(out=e3[:, :], in0=psB[:, :], in1=pq[:, 0, :], op=ALU.mult), [pBb, pqb], [e3b])
                        P.op("dve", lambda e, e4=e4, pq=pq, psA=psA: e.tensor_tensor(out=e4[:, :], in0=psA[:, :], in1=pq[:, 1, :], op=ALU.mult), [pAb, pqb], [e4b])
                        P.op("pool", lambda e, e1=e1, e2=e2, fc=fc: e.tensor_tensor(out=Yt[:, 0, fc, :], in0=e1[:, :], in1=e2[:, :], op=ALU.subtract), [e1b, e2b], [b_Yt])
                        P.op("pool", lambda e, e3=e3, e4=e4, fc=fc: e.tensor_tensor(out=Yt[:, 1, fc, :], in0=e3[:, :], in1=e4[:, :], op=ALU.add), [e3b, e4b], [b_Yt])
                    for tt in range(ntt):
                        t0 = tt * tw
                        grp = bank[0:4] if tt % 2 == 0 else bank[4:8]
                        for f4 in range(0, nblk, 4):
                            nf = min(4, nblk - f4)
                            t2, t2b = tb2.next()
                            P.dma(t2[:, 0, :nf, :tw], I["k_dc2" + tag][f4 * 128:(f4 + nf) * 128, t0:t0 + tw].rearrange("(c p) t -> p c t", p=128), writes=[t2b])
                            P.dma(t2[:, 1, :nf, :tw], I["k_ds2" + tag][f4 * 128:(f4 + nf) * 128, t0:t0 + tw].rearrange("(c p) t -> p c t", p=128), writes=[t2b])
                            for cc in range(4):
                                ps, pb = grp[cc]

                                def fn(e, ps=ps, cc=cc, f4=f4, nf=nf, t2=t2):
                                    ins = None
                                    for fi in range(nf):
                                        fc = f4 + fi
                                        for cs in range(2):
                                            ins = e.matmul(ps[:, :tw], lhsT=Yt[:, cs, fc, cc * 128:(cc + 1) * 128], rhs=t2[:, cs, fi, :tw],
                                                           start=(fc == 0 and cs == 0), stop=(fc == nblk - 1 and cs == 1))
                                    return ins
                                P.op("pe", fn, [b_Yt, t2b], [pb])
                        for cc in range(4):
                            ps, pb = grp[cc]
                            xsrc = zT[(0 if o == 0 else 4) + cc, :, toff + t0:toff + t0 + tw]
                            vsrc = zT[8 + cc, :, toff + t0:toff + t0 + tw]
                            x1, x1b = xr_.next()
                            vv, vvb = vr_.next()
                            P.dma(x1[:, :tw], xsrc, writes=[x1b])
                            P.dma(vv[:, :tw], vsrc, writes=[vvb])
                            zf, zfb = zr.next()
                            P.op("dve", lambda e, zf=zf, vv=vv, ps=ps, cc=cc, o=o: e.scalar_tensor_tensor(out=zf[:, :tw], in0=vv[:, :tw], scalar=fbs[:, o, cc:cc + 1], in1=ps[:, :tw], op0=ALU.mult, op1=ALU.add),
                                 [vvb, b_fbs, pb], [zfb])
                            P.op("dve", lambda e, zf=zf, x1=x1: e.tensor_tensor(out=zf[:, :tw], in0=zf[:, :tw], in1=x1[:, :tw], op=ALU.mult), [zfb, x1b], [zfb])
                            zb, zbb = zbr.next()
                            copy_on(P, "act", zb[:, :tw], zf[:, :tw], [zfb], [zbb])
                            if o == 0:
                                P.dma(zT[8 + cc, :, toff + t0:toff + t0 + tw], zb[:, :tw], reads=[zbb, vvb], writes=[Buf("x")], eng="pool")
                            else:
                                P.dma(oT[3, cc, :, toff + t0:toff + t0 + tw], zb[:, :tw], reads=[zbb], writes=[Buf("x")], eng="pool")
                            if o == 0:
                                pass
                        if o == 0:
                            pass
                    if o == 0:
                        P.barrier()
                        for cc in range(4):
                            for tt in range(ntt):
                                t0 = tt * tw
                                vv, vvb = vr_.next()
                                P.dma(vv[:, :tw], zT[8 + cc, :, toff + t0:toff + t0 + tw], writes=[vvb])
                                zf, zfb = zr.next()
                                copy_on(P, "dve", zf[:, :tw], vv[:, :tw], [vvb], [zfb])
                                to_vtok(zf, zfb, cc, t0, tw)
                P.barrier()

        for (toff_, ns_, tag_) in segs:
            hy_segment(toff_, ns_, tag_)

        if stop == "hyena" and l == 0:
            return finish_list(k, st, dbg_d, dbg_items(locals()))

        with ExitStack() as S:
            wbr_sb, b_wbr = k.sb(S, "wbr", [128, 16, 1024], BF16)
            wout_sb, b_wout = k.sb(S, "wout", [128, 8, 1024], BF16)
            P.dma(wbr_sb[:], wbr_b[0].rearrange("a (kc p) n -> p (a kc) n", p=128), reads=[wbr_b[1]], writes=[b_wbr])
            P.dma(wout_sb[:], wout_b[0].rearrange("(kc p) n -> p kc n", p=128), reads=[wout_b[1]], writes=[b_wout])
            xr = Ring(k, S, "xt", 1, [128, 8, 512], F32)
            xnr = Ring(k, S, "xn", 1, [128, 8, 512], BF16)
            ur = Ring(k, S, "u", 1, [128, 8, 512], BF16)
            otr = Ring(k, S, "ot", 1, [128, 16, 512], BF16)
            wgr_ = Ring(k, S, "wgt", 2, [128, 8, 1024], BF16)
            maccr = Ring(k, S, "macc", 1, [128, 8, 512], F32)
            sigr = Ring(k, S, "sig", 3, [128, 512], F32)

            def merge_tile(ti, t0, n):
                j = 0 if t0 < L else 1
                xt, xb = xr.next()
                u, ub = ur.next()
                ot, otb = otr.next()
                P.dma(xt[:, :, :n], xT_tile_ap(t0, n), reads=[b_xTt[ti]], writes=[xb])
                P.dma(u[:, :, :n], uT_tile_ap(t0, n), reads=[b_uTt[ti]], writes=[ub])
                P.dma(ot[:, :, :n], oT[:, :, :, t0:t0 + n].rearrange("a c p t -> p (a c) t"), writes=[otb])
                macc, mb = maccr.next()
                for br in range(4):
                    wg_, wgb_ = wgr_.next()
                    P.dma(wg_[:], winv[:, :, O_GT + br * 1024:O_GT + (br + 1) * 1024], reads=[win_b[1]], writes=[wgb_])
                    for m in range(8):
                        psg, pgb = k.tmp()
                        mm_group(P, psg[:, :n], pgb, [(wg_[:, kc, m * 128:(m + 1) * 128], u[:, kc, :n]) for kc in range(8)], [wgb_, ub])
                        psz, pzb = k.tmp()
                        mm_group(P, psz[:, :n], pzb, [(wbr_sb[:, br * 4 + kc, m * 128:(m + 1) * 128], ot[:, br * 4 + kc, :n]) for kc in range(4)], [b_wbr, otb])
                        sg, sgb = sigr.next()
                        P.op("act", lambda e, sg=sg, psg=psg: e.activation(out=sg[:, :n], in_=psg[:, :n], func=AF.Sigmoid), [pgb], [sgb])
                        if br == 0:
                            P.op("dve", lambda e, sg=sg, psz=psz, m=m: e.tensor_tensor(out=macc[:, m, :n], in0=sg[:, :n], in1=psz[:, :n], op=ALU.mult), [sgb, pzb], [mb])
                        else:
                            P.op("dve", lambda e, sg=sg, psz=psz: e.tensor_tensor(out=sg[:, :n], in0=sg[:, :n], in1=psz[:, :n], op=ALU.mult), [sgb, pzb], [sgb])
                            P.op("dve", lambda e, sg=sg, m=m: e.tensor_tensor(out=macc[:, m, :n], in0=macc[:, m, :n], in1=sg[:, :n], op=ALU.add), [sgb, mb], [mb])
                mT, mTb = xnr.next()
                for m in range(8):
                    copy_on(P, "act", mT[:, m, :n], macc[:, m, :n], [mb], [mTb])
                for mo in range(8):
                    psy, pyb = k.tmp()
                    mm_group(P, psy[:, :n], pyb, [(wout_sb[:, kc, mo * 128:(mo + 1) * 128], mT[:, kc, :n]) for kc in range(8)], [b_wout, mTb])
                    P.op("dve", lambda e, psy=psy, mo=mo: e.scalar_tensor_tensor(out=xt[:, mo, :n], in0=psy[:, :n], scalar=gtv[:, 1, mo, j:j + 1],
                                                                                   in1=xt[:, mo, :n], op0=ALU.mult, op1=ALU.add), [pyb, b_gtv, xb], [xb])
                P.dma(xT_tile_ap(t0, n), xt[:, :, :n], reads=[xb], writes=[b_xTt[ti]], eng="pool")

            for ti, (t0, n) in enumerate(TILES):
                if t0 >= L and not need_ctx:
                    continue
                merge_tile(ti, t0, n)
            P.barrier()

        with ExitStack() as S:
            rg = ffn_rings(S)
            nmr = (Ring(k, S, "sq", 3, [128, 512], BF16), Ring(k, S, "rstd", 2, [128, 512], F32), Ring(k, S, "nmtmp", 2, [128, 512], F32))
            xr = Ring(k, S, "xt", 2, [128, 8, 512], F32)
            xnr = Ring(k, S, "xn", 2, [128, 8, 512], BF16)

            def ffn2_tile(ti, t0, n):
                j = 0 if t0 < L else 1
                xt, xb = xr.next()
                xn, xnb = xnr.next()
                P.dma(xt[:, :, :n], xT_tile_ap(t0, n), reads=[b_xTt[ti]], writes=[xb])
                norm_mod(S, nmr, xt, xb, n, 2, j, xn, xnb)
                ffn_tile(S, rg, xt, xb, xn, xnb, n, 1, 2, j)
                P.dma(xT_tile_ap(t0, n), xt[:, :, :n], reads=[xb], writes=[b_xTt[ti]], eng="pool")

            for ti, (t0, n) in enumerate(TILES):
                if t0 >= L and not need_ctx:
                    continue
                ffn2_tile(ti, t0, n)
            P.barrier()

        if stop == "merge" and l == 0:
            return finish(k, st, I, out_d, dbg_d, xT, b_xTt)

    for l_ in range(DEPTH):
        r_ = do_layer(l_)
        if r_ is not None:
            return r_

    with ExitStack() as S:
        fg, b_fg = k.sb(S, "fg", [128, 8], F32)
        P.dma(fg[:], I["final_g"], writes=[b_fg])
        xr = Ring(k, S, "xt", 2, [128, 8, 512], F32)
        sq_r = Ring(k, S, "sq", 3, [128, 512], BF16)
        rstd_r = Ring(k, S, "rstd", 2, [128, 512], F32)
        yr = Ring(k, S, "yt", 2, [128, 8, 512], F32)
        tokr = Ring(k, S, "tok", 3, [128, 1024], F32)

        def final_tile(ti, t0, n):
            xt, xb = xr.next()
            P.dma(xt[:, :, :n], xT_tile_ap(t0, n), reads=[b_xTt[ti]], writes=[xb])
            ps, pb = k.acc()
            for fc in range(8):
                q, qb = sq_r.next()
                P.op("act", lambda e, q=q, fc=fc: e.activation(out=q[:, :n], in_=xt[:, fc, :n], func=AF.Square), [xb], [qb])
                P.op("pe", lambda e, q=q, fc=fc: e.matmul(ps[:, :n], lhsT=ones_b[:], rhs=q[:, :n], start=(fc == 0), stop=(fc == 7)), [qb, b_ones], [pb])
            rstd, rb = rstd_r.next()
            P.op("act", lambda e: e.activation(out=rstd[:, :n], in_=ps[:, :n], func=AF.Sqrt, bias=epsv[:, 0:1], scale=1.0 / D), [pb, b_eps], [rb])
            P.op("dve", lambda e: e.reciprocal(out=rstd[:, :n], in_=rstd[:, :n]), [rb], [rb])
            yt, yb = yr.next()
            for fc in range(8):
                P.op("dve", lambda e, fc=fc: e.scalar_tensor_tensor(out=yt[:, fc, :n], in0=xt[:, fc, :n], scalar=fg[:, fc:fc + 1], in1=rstd[:, :n],
                                                                     op0=ALU.mult, op1=ALU.mult), [xb, rb, b_fg], [yb])
            for jb in range(n // 128):
                tk, tkb = tokr.next()
                for half in range(2):
                    pst, ptb = k.tmp()
                    for f4 in range(4):
                        fc = half * 4 + f4
                        P.op("pe", lambda e, pst=pst, f4=f4, fc=fc, jb=jb: e.transpose(pst[:, f4 * 128:(f4 + 1) * 128], yt[:, fc, jb * 128:(jb + 1) * 128], ident_f[:]),
                             [yb, b_identf], [ptb])
                    copy_on(P, k.ev(), tk[:, half * 512:(half + 1) * 512], pst[:, :], [ptb], [tkb])
                P.dma(out_d[t0 + jb * 128:t0 + (jb + 1) * 128, :], tk[:], reads=[tkb], eng="pool")

        for ti, (t0, n) in enumerate(TILES):
            if t0 < L:
                final_tile(ti, t0, n)
        P.barrier()
    P.emit()
    st.close()
    return nc


DBG_WANT = []


def dbg_items(loc):
    return [f(loc) for f in DBG_WANT]


def finish_list(k, st, dbg_d, aps):
    P = k.P
    with ExitStack() as S:
        r = Ring(k, S, "fl", 2, [128, 4352], F32)
        row = 0
        for ap in aps:
            rows, cols = ap.shape
            for r0 in range(0, rows, 128):
                rr = min(128, rows - r0)
                t, b = r.next()
                P.dma(t[:rr, :cols], ap[r0:r0 + rr, :], writes=[b], eng="pool")
                P.dma(dbg_d[row:row + rr, :cols], t[:rr, :cols], reads=[b], eng="sp")
                row += rr
    P.emit()
    st.close()
    return k.nc


def finish(k, st, I, out_d, dbg_d, xT, b_xTt):
    P = k.P
    if dbg_d is not None:
        with ExitStack() as S:
            r = Ring(k, S, "fin", 2, [128, 8, 512], F32)
            for ti, (t0, n) in enumerate(TILES):
                t, b = r.next()
                P.dma(t[:, :, :n], xT[:, :, t0:t0 + n].rearrange("c p t -> p c t"), reads=[b_xTt[ti]], writes=[b])
                P.dma(dbg_d[:, :, t0:t0 + n].rearrange("c p t -> p c t"), t[:, :, :n], reads=[b], eng="pool")
    P.emit()
    st.close()
    return k.nc


def finish_dbg(k, st, I, out_d, dbg_d, items):
    P = k.P
    for (t, b, shape) in items:
        P.dma(dbg_d, t[:].rearrange("p a b -> p (a b)") if len(t.shape) == 3 else t[:], reads=[b], eng="pool")
    P.emit()
    st.close()
    return k.nc


def pcol(v, nchunk):
    return np.ascontiguousarray(np.asarray(v, np.float32).reshape(nchunk, 128).T)


def host_inputs(inp, b):
    f = np.float32
    m = {}
    m["x"] = np.ascontiguousarray(inp["x"][b], f)
    m["ctx"] = np.ascontiguousarray(inp["ctx"][b], f)
    cc = np.stack([np.asarray(inp["c"][b], f), np.asarray(inp["c_ctx"], f)], axis=-1)
    m["cc"] = np.ascontiguousarray(cc.reshape(8, 128, 2).transpose(1, 0, 2))
    m["ada_w"] = np.asarray(inp["ada_w"], f)
    m["ada_b"] = np.ascontiguousarray(np.asarray(inp["ada_b"], f).reshape(DEPTH, 72, 128).transpose(2, 0, 1))
    m["norm_g"] = np.ascontiguousarray(np.asarray(inp["norm_g"], f).reshape(DEPTH, 3, 8, 128).transpose(3, 0, 1, 2))
    for n in ("ffn_w_gate", "ffn_w_up", "ffn_w_down", "w_in", "mla_w_uq", "mla_w_ukv", "gla_w_gate", "gla_b_gate",
              "hy_filt_w1", "hy_filt_w2", "hy_filt_w3", "w_branch", "w_out"):
        m[n] = np.asarray(inp[n], f)
    m["mla_q_norm_g"] = np.ascontiguousarray(np.asarray(inp["mla_q_norm_g"], f).reshape(DEPTH, 2, 128).transpose(2, 0, 1))
    m["mla_kv_norm_g"] = np.ascontiguousarray(np.asarray(inp["mla_kv_norm_g"], f).T)
    m["gla_norm_g"] = np.ascontiguousarray(np.asarray(inp["gla_norm_g"], f).T)
    m["gqa_q_norm_g"] = np.ascontiguousarray(np.tile(np.asarray(inp["gqa_q_norm_g"], f), (1, 2)).T)
    m["gqa_k_norm_g"] = np.ascontiguousarray(np.tile(np.asarray(inp["gqa_k_norm_g"], f), (1, 2)).T)
    m["hy_sconv_w"] = np.ascontiguousarray(np.asarray(inp["hy_sconv_w"], f).reshape(DEPTH, 3, 12, 128).transpose(3, 0, 1, 2))
    m["hy_sconv_b"] = np.ascontiguousarray(np.asarray(inp["hy_sconv_b"], f).reshape(DEPTH, 12, 128).transpose(2, 0, 1))
    m["hy_filt_b1"] = np.ascontiguousarray(np.asarray(inp["hy_filt_b1"], f).T)
    m["hy_filt_b2"] = np.ascontiguousarray(np.asarray(inp["hy_filt_b2"], f).T)
    m["hy_filt_bias"] = np.ascontiguousarray(np.asarray(inp["hy_filt_bias"], f).reshape(DEPTH, 2, 4, 128).transpose(3, 0, 1, 2))
    m["final_g"] = pcol(inp["final_g"], 8)
    m.update(consts())
    return m


def kernel(**inputs):
    nc = bass.Bass("TRN2", target_bir_lowering=False)
    build(nc)
    in_maps = [host_inputs(inputs, b) for b in range(8)]
    res = run_bass_kernel_spmd(nc, in_maps, core_ids=list(range(8)))
    return np.stack([np.asarray(r["out"], np.float32) for r in res.results], axis=0)
```

```python
import math
from contextlib import ExitStack
import numpy as np
import ml_dtypes
import concourse.bass as bass
import concourse.mybir as mybir
from concourse.bass_utils import run_bass_kernel_spmd

F32 = mybir.dt.float32
BF16 = mybir.dt.bfloat16
AF = mybir.ActivationFunctionType
ALU = mybir.AluOpType
AX = mybir.AxisListType

D = 1024
L = 4096
CT = 256
NT = L + CT
DEPTH = 2
FF = 2816
NFC = FF // 128
P_IN = 8384
EPS = 1e-6
TILES = [(i * 512, 512) for i in range(8)] + [(L, CT)]
O_CQ, O_CKV, O_KR, O_BQ, O_BK, O_BV, O_LR, O_OG, O_CQQ, O_CK, O_CV, O_DP, O_GT = (
    0, 256, 384, 416, 672, 928, 1440, 1472, 1984, 2496, 2624, 2752, 4288)

DBG_STOP = None
PROJ_SECTIONS = {'mla', 'gqa', 'gla', 'hy'}
PROJ_TILES = 9
SKIP = ''
MLA_STEP = 99
NHEADS_DBG = 8
RING_F32, RING_BF, RING_OUT = 10, 16, 12


class Buf:
    __slots__ = ("name", "last_w", "readers")

    def __init__(self, name="b"):
        self.name = name
        self.last_w = None
        self.readers = []


class Op:
    __slots__ = ("eng", "fn", "deps", "signal", "sig_val", "sem", "is_dma", "bar", "snap")

    def __init__(self, eng, fn, is_dma):
        self.eng = eng
        self.fn = fn
        self.deps = []
        self.signal = False
        self.sig_val = None
        self.sem = None
        self.is_dma = is_dma
        self.bar = None
        self.snap = None


ENGS = ("pe", "act", "dve", "pool", "sp")
NWAIT = {}
SAME_ENGINE_SKIP = ("pe", "act", "dve")
SIGNAL_ALL = True
SKIP_ATTN = False
SKIP_GLA = False
ROPE_COPY_ENG = "act"
DMA_SLOTS = 8


class Prog:
    def __init__(self, nc):
        self.nc = nc
        self.ops = {e: [] for e in ENGS}
        self.nbar = 0

    def op(self, eng, fn, reads=(), writes=(), dma=False):
        o = Op(eng, fn, dma)
        if SIGNAL_ALL and not dma:
            o.signal = True
        deps = []
        for b in reads:
            if b.last_w is not None:
                deps.append(b.last_w)
        for b in writes:
            if b.last_w is not None:
                deps.append(b.last_w)
            deps.extend(b.readers)
        seen = set()
        for d in deps:
            if id(d) in seen or d is o:
                continue
            seen.add(id(d))
            if d.eng == eng and eng in SAME_ENGINE_SKIP and not d.is_dma:
                continue
            o.deps.append(d)
            d.signal = True
        for b in reads:
            b.readers.append(o)
        for b in writes:
            b.last_w = o
            b.readers = []
        self.ops[eng].append(o)
        return o

    def dma(self, out, in_, reads=(), writes=(), eng="sp", **kw):
        return self.op(eng, lambda e: e.dma_start(out=out, in_=in_, **kw), reads, writes, dma=True)

    def barrier(self):
        self.nbar += 1
        for e in ENGS:
            last = None
            for o in reversed(self.ops[e]):
                if o.bar is None:
                    last = o
                    break
            if last is not None and not last.is_dma:
                last.signal = True
            o = Op(e, None, False)
            o.bar = self.nbar
            o.snap = last
            self.ops[e].append(o)

    def emit(self):
        nc = self.nc
        with ExitStack() as st:
            csem = {e: st.enter_context(nc.semaphore("c_" + e)) for e in ENGS}
            bsem = {e: st.enter_context(nc.semaphore("b_" + e)) for e in ENGS}
            dsem = {e: [st.enter_context(nc.semaphore("d_%s_%d" % (e, i))) for i in range(DMA_SLOTS)] for e in ENGS}
            ccount = {e: 0 for e in ENGS}
            dcount = {e: [0] * DMA_SLOTS for e in ENGS}
            dnext = {e: 0 for e in ENGS}
            for e in ENGS:
                for o in self.ops[e]:
                    if o.bar is not None:
                        o.sig_val = list(dcount[e])
                        continue
                    if o.is_dma:
                        s = dnext[e]
                        dnext[e] = (s + 1) % DMA_SLOTS
                        o.sem = dsem[e][s]
                        prev = dcount[e][s]
                        dcount[e][s] = prev + 16
                        o.sig_val = prev + 16
                        o.signal = True
                        o.deps.append(("slot", o.sem, prev))
                    elif o.signal:
                        ccount[e] += 1
                        o.sem = csem[e]
                        o.sig_val = ccount[e]
            final_d = {e: list(dcount[e]) for e in ENGS}
            st.enter_context(nc.allow_non_contiguous_dma(reason="small strided vector loads"))
            blk = st.enter_context(nc.Block())

            def make(e):
                def body(eng):
                    waited = {}

                    def wait(sem, val):
                        if val <= 0:
                            return
                        k = id(sem)
                        if waited.get(k, 0) >= val:
                            return
                        waited[k] = val
                        NWAIT[e] = NWAIT.get(e, 0) + 1
                        eng.wait_ge(sem, val)

                    for o in self.ops[e]:
                        if o.bar is not None:
                            for i in range(DMA_SLOTS):
                                wait(dsem[e][i], o.sig_val[i])
                            if o.snap is not None and not o.snap.is_dma:
                                wait(o.snap.sem, o.snap.sig_val)
                            eng.nop().then_inc(bsem[e], 1)
                            for e2 in ENGS:
                                if e2 != e:
                                    wait(bsem[e2], o.bar)
                            continue
                        for d in o.deps:
                            if isinstance(d, tuple):
                                wait(d[1], d[2])
                            else:
                                wait(d.sem, d.sig_val)
                        ins = o.fn(eng)
                        if o.signal:
                            ins.then_inc(o.sem, 16 if o.is_dma else 1)
                    for i in range(DMA_SLOTS):
                        wait(dsem[e][i], final_d[e][i])
                return body

            blk.tensor(make("pe"))
            blk.scalar(make("act"))
            blk.vector(make("dve"))
            blk.gpsimd(make("pool"))
            blk.sync(make("sp"))


class Ring:
    def __init__(self, k, st, name, n, shape, dt):
        self.t = [st.enter_context(k.nc.sbuf_tensor("%s%d_%d" % (name, k.uid(), i), shape, dt)) for i in range(n)]
        self.b = [Buf(name + str(i)) for i in range(n)]
        self.i = 0

    def next(self):
        i = self.i
        self.i = (i + 1) % len(self.t)
        return self.t[i], self.b[i]


class KB:
    def __init__(self, nc, st):
        self.nc = nc
        self.st = st
        self.P = Prog(nc)
        self._uid = 0
        self.ps_t = [st.enter_context(nc.psum_tensor("psb%d" % i, [128, 512], F32)) for i in range(8)]
        self.ps_b = [Buf("ps%d" % i) for i in range(8)]
        self.acc_i = 0
        self.tmp_i = 0
        self.rr = 0

    def uid(self):
        self._uid += 1
        return self._uid

    def acc(self):
        i = self.acc_i
        self.acc_i = (i + 1) % 2
        return self.ps_t[i], self.ps_b[i]

    def tmp(self):
        i = 2 + self.tmp_i
        self.tmp_i = (self.tmp_i + 1) % 6
        return self.ps_t[i], self.ps_b[i]

    def sb(self, st, name, shape, dt):
        return st.enter_context(self.nc.sbuf_tensor("%s_%d" % (name, self.uid()), shape, dt)), Buf(name)

    def dram(self, name, shape, dt):
        return self.nc.dram_tensor(name, shape, dt).ap(), Buf(name)

    def ev(self):
        self.rr ^= 1
        return "dve" if self.rr else "act"


def copy_on(P, eng, out, in_, reads, writes):
    if eng == "act":
        return P.op("act", lambda e: e.copy(out=out, in_=in_), reads, writes)
    return P.op(eng, lambda e: e.tensor_copy(out=out, in_=in_), reads, writes)


def mm_group(P, ps, psb, pairs, reads, n=None):
    def fn(e):
        ins = None
        for i, (a, b) in enumerate(pairs):
            ins = e.matmul(ps, lhsT=a, rhs=b, start=(i == 0), stop=(i == len(pairs) - 1))
        return ins
    return P.op("pe", fn, reads, [psb])


def rope_tables(dim_layout):
    cos = np.ones((128, NT), np.float32)
    sin = np.zeros((128, NT), np.float32)
    RT = np.zeros((128, 128), np.float32)
    t = np.arange(L)
    rows = (t // 64).astype(np.float32)
    cols = (t % 64).astype(np.float32)
    for (r0, rd) in dim_layout:
        r = rd // 2
        half = r // 2
        freqs = (10000.0 ** (-np.arange(half, dtype=np.float32) / half)).astype(np.float32)
        for part, pos in ((0, rows), (1, cols)):
            base = r0 + part * r
            ang = pos[None, :] * freqs[:, None]
            c, s = np.cos(ang), np.sin(ang)
            cos[base:base + half, :L] = c
            cos[base + half:base + r, :L] = c
            sin[base:base + half, :L] = s
            sin[base + half:base + r, :L] = s
            for i in range(half):
                RT[base + half + i, base + i] = -1.0
                RT[base + i, base + half + i] = 1.0
    return cos, sin, RT


def dft_tables(n):
    N = 2 * n
    k = np.arange(n, dtype=np.float64)
    t = np.arange(n, dtype=np.float64)
    ang = 2.0 * np.pi * np.outer(t, k + 0.5) / N
    c, s = np.cos(ang), np.sin(ang)
    bf = ml_dtypes.bfloat16
    return (c.astype(np.float32).astype(bf), s.astype(np.float32).astype(bf),
            ((2.0 / N) * c.T).astype(np.float32).astype(bf), ((2.0 / N) * s.T).astype(np.float32).astype(bf))


def hyena_consts(n):
    t = np.linspace(0.0, 1.0, n, dtype=np.float32)
    w = (2.0 * math.pi * np.arange(n, dtype=np.float32) / n).astype(np.float32)
    f = np.linspace(1e-4, 15, 16, dtype=np.float32)
    fw = w[:, None] * f[None, :]
    feat = np.concatenate([t[:, None], np.cos(fw), -np.sin(fw)], axis=-1).astype(np.float32)
    deltas = np.abs(np.linspace(math.log(1e-2) / 1.5, math.log(1e-2) / 0.3, 512, dtype=np.float32))
    dec = np.exp(-t[:, None] * deltas[None, :]).astype(np.float32)
    decb = dec.copy()
    decb[0, :] = 0.0
    return np.ascontiguousarray(feat.T), dec, decb


def make_consts():
    c = {}
    c["k_ident"] = np.eye(128, dtype=np.float32)
    c["k_ones"] = np.ones((128, 128), np.float32)
    bo = np.zeros((128, 128), np.float32)
    bo[:64, :64] = 1.0
    bo[64:, 64:] = 1.0
    c["k_bones"] = bo
    cm, sm, rm = rope_tables([(64, 32)])
    c["k_cosM"], c["k_sinM"], c["k_rtM"] = cm, sm, rm
    cg, sg, rg = rope_tables([(0, 64), (64, 64)])
    c["k_cosG"], c["k_sinG"], c["k_rtG"] = cg, sg, rg
    for n, tag in ((L, "L"), (CT, "C")):
        a, b, c2, d2 = dft_tables(n)
        nb_ = n // 128
        a = np.ascontiguousarray(a.reshape(nb_, 128, nb_, 128).transpose(2, 1, 0, 3))
        b = np.ascontiguousarray(b.reshape(nb_, 128, nb_, 128).transpose(2, 1, 0, 3))
        c["k_dc1" + tag], c["k_ds1" + tag], c["k_dc2" + tag], c["k_ds2" + tag] = a, b, c2, d2
        ft, dec, decb = hyena_consts(n)
        c["k_feat" + tag], c["k_dec" + tag], c["k_decb" + tag] = ft, dec, decb
    j = np.arange(64)
    c["k_m_le"] = (j[:, None] <= j[None, :]).astype(np.float32)
    c["k_m_ge"] = (j[:, None] >= j[None, :]).astype(np.float32)
    c["k_m_gt"] = (j[:, None] > j[None, :]).astype(np.float32)
    c["k_m_lt"] = (j[:, None] < j[None, :]).astype(np.float32)
    return c


_CONSTS = None


def consts():
    global _CONSTS
    if _CONSTS is None:
        _CONSTS = make_consts()
    return _CONSTS


def build(nc, stop=None, dbg=None):
    st = ExitStack()
    k = KB(nc, st)
    P = k.P
    I = {}

    def din(name, shape, dt=F32):
        I[name] = nc.dram_tensor(name, list(shape), dt, kind="ExternalInput").ap()
        return I[name]

    din("x", [L, D]); din("ctx", [CT, D]); din("cc", [128, 8, 2])
    din("ada_w", [DEPTH, D, 9 * D]); din("ada_b", [128, DEPTH, 72]); din("norm_g", [128, DEPTH, 3, 8])
    din("ffn_w_gate", [DEPTH, 2, D, FF]); din("ffn_w_up", [DEPTH, 2, D, FF]); din("ffn_w_down", [DEPTH, 2, FF, D])
    din("w_in", [DEPTH, D, P_IN])
    din("mla_q_norm_g", [128, DEPTH, 2]); din("mla_w_uq", [DEPTH, 256, 768])
    din("mla_kv_norm_g", [128, DEPTH]); din("mla_w_ukv", [DEPTH, 128, 1024])
    din("gla_w_gate", [DEPTH, 2, 16, 256]); din("gla_b_gate", [DEPTH, 2, 256]); din("gla_norm_g", [128, DEPTH])
    din("gqa_q_norm_g", [128, DEPTH]); din("gqa_k_norm_g", [128, DEPTH])
    din("hy_sconv_w", [128, DEPTH, 3, 12]); din("hy_sconv_b", [128, DEPTH, 12])
    din("hy_filt_w1", [DEPTH, 33, 64]); din("hy_filt_b1", [64, DEPTH]); din("hy_filt_w2", [DEPTH, 64, 64])
    din("hy_filt_b2", [64, DEPTH]); din("hy_filt_w3", [DEPTH, 64, 2048]); din("hy_filt_bias", [128, DEPTH, 2, 4])
    din("w_branch", [DEPTH, 4, 512, D]); din("w_out", [DEPTH, D, D]); din("final_g", [128, 8])
    cs = consts()
    for name, arr in cs.items():
        din(name, arr.shape, BF16 if arr.dtype == ml_dtypes.bfloat16 else F32)
    out_d = nc.dram_tensor("out", [L, D], F32, kind="ExternalOutput").ap()
    dbg_d = None
    if dbg is not None:
        dbg_d = nc.dram_tensor("dbg", list(dbg), F32, kind="ExternalOutput").ap()

    G = ExitStack()
    st.enter_context(G)
    ident_f, b_identf = k.sb(G, "identf", [128, 128], F32)
    ident_b, b_identb = k.sb(G, "identb", [128, 128], BF16)
    ones_b, b_ones = k.sb(G, "onesb", [128, 128], BF16)
    bones_b, b_bones = k.sb(G, "bonesb", [128, 128], BF16)
    P.dma(ident_f[:], I["k_ident"], writes=[b_identf])
    P.dma(ident_b[:], I["k_ident"], writes=[b_identb], eng="pool")
    P.dma(ones_b[:], I["k_ones"], writes=[b_ones], eng="pool")
    P.dma(bones_b[:], I["k_bones"], writes=[b_bones], eng="pool")
    modv, b_modv = k.sb(G, "modv", [128, 72, 2], F32)
    gsv, b_gsv = k.sb(G, "gsv", [128, 3, 8, 2], F32)
    gtv, b_gtv = k.sb(G, "gtv", [128, 3, 8, 2], F32)
    smalls, b_smalls = k.sb(G, "smalls", [128, 64], F32)
    epsv, b_eps = k.sb(G, "epsv", [128, 1], F32)
    P.op("pool", lambda e: e.memset(epsv[:], EPS), [], [b_eps])

    xT, b_xT = k.dram("xT", [8, 128, NT], F32)
    b_xTt = [Buf("xT%d" % i) for i in range(len(TILES))]
    uT, _ = k.dram("uT", [8, 128, NT], BF16)
    b_uTt = [Buf("uT%d" % i) for i in range(len(TILES))]
    wg_b = [k.dram("wg_b%d" % i, [D, FF], BF16) for i in range(2)]
    wu_b = [k.dram("wu_b%d" % i, [D, FF], BF16) for i in range(2)]
    wd_b = [k.dram("wd_b%d" % i, [FF, D], BF16) for i in range(2)]
    win_b = k.dram("win_b", [D, P_IN], BF16)
    wbr_b = k.dram("wbr_b", [4, 512, D], BF16)
    wout_b = k.dram("wout_b", [D, D], BF16)

    def xT_tile_ap(t0, n):
        return xT[:, :, t0:t0 + n].rearrange("c p t -> p c t")

    def uT_tile_ap(t0, n):
        return uT[:, :, t0:t0 + n].rearrange("c p t -> p c t")

    with ExitStack() as S:
        xin = Ring(k, S, "xin", 8, [128, D], F32)
        xo = Ring(k, S, "xo", 2, [128, 8, 512], F32)
        for ti, (t0, n) in enumerate(TILES):
            ot, ob = xo.next()
            nb = n // 128
            blks = []
            for j in range(nb):
                tt, tb = xin.next()
                r0 = t0 + j * 128
                src = I["x"][r0:r0 + 128, :] if r0 < L else I["ctx"][r0 - L:r0 - L + 128, :]
                P.dma(tt[:], src, writes=[tb])
                blks.append((tt, tb))
                for fc in range(8):
                    pass
            for fc in range(8):
                ps, pb = k.tmp()
                for j in range(nb):
                    tt, tb = blks[j]
                    P.op("pe", lambda e, ps=ps, tt=tt, j=j, fc=fc: e.transpose(ps[:, j * 128:(j + 1) * 128], tt[:, fc * 128:(fc + 1) * 128], ident_f[:]),
                         [tb, b_identf], [pb])
                copy_on(P, k.ev(), ot[:, fc, :n], ps[:, :n], [pb], [ob])
            P.dma(xT_tile_ap(t0, n), ot[:, :, :n], reads=[ob], writes=[b_xTt[ti]], eng="pool")
        P.barrier()

    if stop == "s0":
        return finish(k, st, I, out_d, dbg_d, xT, b_xTt)

    def cast_dram(dst, src, rows, cols, bufs):
        step = 256
        for r in range(0, rows, step):
            rr = min(step, rows - r)
            P.dma(dst[r:r + rr, :], src[r:r + rr, :], writes=bufs, eng="pool")

    def norm_mod(S, rings, xt, xb, n, s, j, out, outb):
        sq_r, rstd_r, tmp_r = rings
        ps, pb = k.acc()
        sqs = []
        for fc in range(8):
            q, qb = sq_r.next()
            P.op("act", lambda e, q=q, fc=fc: e.activation(out=q[:, :n], in_=xt[:, fc, :n], func=AF.Square), [xb], [qb])
            sqs.append((q, qb))
            P.op("pe", lambda e, q=q, fc=fc, ps=ps: e.matmul(ps[:, :n], lhsT=ones_b[:], rhs=q[:, :n], start=(fc == 0), stop=(fc == 7)),
                 [qb, b_ones], [pb])
        rstd, rb = rstd_r.next()
        P.op("act", lambda e: e.activation(out=rstd[:, :n], in_=ps[:, :n], func=AF.Sqrt, bias=epsv[:, 0:1], scale=1.0 / D), [pb, b_eps], [rb])
        P.op("dve", lambda e: e.reciprocal(out=rstd[:, :n], in_=rstd[:, :n]), [rb], [rb])
        for fc in range(8):
            tp, tb = tmp_r.next()
            P.op("dve", lambda e, tp=tp, fc=fc: e.scalar_tensor_tensor(out=tp[:, :n], in0=xt[:, fc, :n], scalar=gsv[:, s, fc, j:j + 1],
                                                                         in1=rstd[:, :n], op0=ALU.mult, op1=ALU.mult),
                 [xb, rb, b_gsv], [tb])
            P.op("act", lambda e, tp=tp, fc=fc: e.activation(out=out[:, fc, :n], in_=tp[:, :n], func=AF.Identity,
                                                              bias=modv[:, 3 * s * 8 + fc, j:j + 1], scale=1.0),
                 [tb, b_modv], [outb])

    qA = k.dram("qA", [8, 128, NT], BF16)[0]; kA = k.dram("kA", [8, 128, NT], BF16)[0]
    vA = k.dram("vA", [34, 128, 8, 128], BF16)[0]
    qC = k.dram("qC", [4, 128, NT], BF16)[0]; kC = k.dram("kC", [4, 128, NT], BF16)[0]
    vC = k.dram("vC", [34, 128, 2, 128], BF16)[0]
    bqT = k.dram("bqT", [2, 128, NT], BF16)[0]; bkT = k.dram("bkT", [2, 128, NT], BF16)[0]
    bk_tok = k.dram("bk_tok", [NT, 256], BF16)[0]; bv_tok = k.dram("bv_tok", [NT, 512], BF16)[0]
    la_d = k.dram("la_d", [2, NT, 256], BF16)[0]
    ogT = k.dram("ogT", [4, 128, NT], BF16)[0]; dpT = k.dram("dpT", [12, 128, NT], BF16)[0]
    oT = k.dram("oT", [4, 4, 128, NT], BF16)[0]
    ofT = k.dram("ofT", [4, 128, NT], F32)[0]
    zT = k.dram("zT", [12, 128, NT], BF16)[0]
    hpm = k.dram("hpm", [2, 2, L, 512], BF16)[0]
    PQ = k.dram("PQ", [2, 2, L, 512], F32)[0]

    def do_layer(l):
        need_ctx = l < DEPTH - 1
        with ExitStack() as S:
            cct, b_cct = k.sb(S, "cct", [128, 8, 2], F32)
            P.dma(cct[:], I["cc"], writes=[b_cct])
            P.op("act", lambda e: e.activation(out=cct[:], in_=cct[:], func=AF.Silu), [b_cct], [b_cct])
            adab, b_adab = k.sb(S, "adab", [128, 72], F32)
            P.dma(adab[:], I["ada_b"][:, l, :], writes=[b_adab])
            ng, b_ng = k.sb(S, "ng", [128, 3, 8], F32)
            P.dma(ng[:], I["norm_g"][:, l, :, :], writes=[b_ng])
            awr = Ring(k, S, "adaw", 2, [128, 8, 1024], F32)
            ps, pb = k.acc()
            for i in range(9):
                wt, wb = awr.next()
                P.dma(wt[:], I["ada_w"][l, :, i * 1024:(i + 1) * 1024].rearrange("(kc p) n -> p kc n", p=128), writes=[wb])
                for fc in range(8):
                    ch = i * 8 + fc

                    def fn(e, wt=wt, fc=fc, ch=ch, ps=ps):
                        ins = None
                        for kc in range(8):
                            ins = e.matmul(ps[:, ch * 2:ch * 2 + 2], lhsT=wt[:, kc, fc * 128:(fc + 1) * 128], rhs=cct[:, kc, :],
                                           start=(kc == 0), stop=(kc == 7))
                        return ins
                    P.op("pe", fn, [wb, b_cct], [pb])
            for j in range(2):
                P.op("dve", lambda e, j=j, ps=ps: e.tensor_tensor(out=modv[:, :, j], in0=ps[:, j:144:2], in1=adab[:], op=ALU.add),
                     [pb, b_adab], [b_modv])
            for s in range(3):
                for j in range(2):
                    P.op("dve", lambda e, s=s, j=j: e.scalar_tensor_tensor(out=gsv[:, s, :, j], in0=modv[:, (3 * s + 1) * 8:(3 * s + 2) * 8, j], scalar=1.0,
                                                                             in1=ng[:, s, :], op0=ALU.add, op1=ALU.mult),
                         [b_modv, b_ng], [b_gsv])
                    P.op("dve", lambda e, s=s, j=j: e.tensor_scalar(out=gtv[:, s, :, j], in0=modv[:, (3 * s + 2) * 8:(3 * s + 3) * 8, j],
                                                                      scalar1=(1.0 if s == 1 else 0.5), scalar2=None, op0=ALU.mult),
                         [b_modv], [b_gtv])
            P.barrier()

        if stop == "ada" and l == 0:
            return finish_dbg(k, st, I, out_d, dbg_d, [(modv, b_modv, [128, 144])])

        for i in range(2):
            cast_dram(wg_b[i][0], I["ffn_w_gate"][l, i], D, FF, [wg_b[i][1]])
            cast_dram(wu_b[i][0], I["ffn_w_up"][l, i], D, FF, [wu_b[i][1]])
            cast_dram(wd_b[i][0], I["ffn_w_down"][l, i], FF, D, [wd_b[i][1]])
        cast_dram(win_b[0], I["w_in"][l], D, P_IN, [win_b[1]])
        cast_dram(wbr_b[0].rearrange("a k n -> (a k) n"), I["w_branch"][l].rearrange("a k n -> (a k) n"), 2048, D, [wbr_b[1]])
        cast_dram(wout_b[0], I["w_out"][l], D, D, [wout_b[1]])

        def ffn_multi(S, rg, ctxs, fi, s):
            wgr, wur, wdr, sgr, hTs = rg
            for fp in range(6):
                c0 = fp * 512
                pc = min(512, FF - c0)
                wgt, wgb = wgr.next()
                wut, wub = wur.next()
                P.dma(wgt[:, :, :pc], wg_b[fi][0].rearrange("(kc p) f -> p kc f", p=128)[:, :, c0:c0 + pc], reads=[wg_b[fi][1]], writes=[wgb])
                P.dma(wut[:, :, :pc], wu_b[fi][0].rearrange("(kc p) f -> p kc f", p=128)[:, :, c0:c0 + pc], reads=[wu_b[fi][1]], writes=[wub])
                for jj in range(pc // 128):
                    f = fp * 4 + jj
                    for ci_, (xt, xb, xn, xnb, n, j) in enumerate(ctxs):
                        hT, hb = hTs[ci_]
                        psg, pgb = k.tmp()
                        psu, pub = k.tmp()
                        mm_group(P, psg[:, :n], pgb, [(wgt[:, kc, jj * 128:(jj + 1) * 128], xn[:, kc, :n]) for kc in range(8)], [wgb, xnb])
                        mm_group(P, psu[:, :n], pub, [(wut[:, kc, jj * 128:(jj + 1) * 128], xn[:, kc, :n]) for kc in range(8)], [wub, xnb])
                        sg, sgb = sgr.next()
                        P.op("act", lambda e, sg=sg, psg=psg, n=n: e.activation(out=sg[:, :n], in_=psg[:, :n], func=AF.Silu), [pgb], [sgb])
                        P.op("dve", lambda e, sg=sg, psu=psu, f=f, hT=hT, n=n: e.tensor_tensor(out=hT[:, f, :n], in0=sg[:, :n], in1=psu[:, :n], op=ALU.mult),
                             [sgb, pub], [hb])
            for mp in range(4):
                wdt, wdb = wdr.next()
                P.dma(wdt[:], wd_b[fi][0].rearrange("(kc p) m -> p kc m", p=128)[:, :, mp * 256:(mp + 1) * 256], reads=[wd_b[fi][1]], writes=[wdb])
                for jj in range(2):
                    m = mp * 2 + jj
                    for ci_, (xt, xb, xn, xnb, n, j) in enumerate(ctxs):
                        hT, hb = hTs[ci_]
                        ps, pb = k.tmp()
                        mm_group(P, ps[:, :n], pb, [(wdt[:, kc, jj * 128:(jj + 1) * 128], hT[:, kc, :n]) for kc in range(NFC)], [wdb, hb])
                        P.op("dve", lambda e, ps=ps, m=m, xt=xt, n=n, j=j: e.scalar_tensor_tensor(out=xt[:, m, :n], in0=ps[:, :n], scalar=gtv[:, s, m, j:j + 1],
                                                                                                   in1=xt[:, m, :n], op0=ALU.mult, op1=ALU.add),
                             [pb, b_gtv, xb], [xb])

        def ffn_rings(S):
            wgr = Ring(k, S, "wg", 2, [128, 8, 512], BF16)
            wur = Ring(k, S, "wu", 2, [128, 8, 512], BF16)
            wdr = Ring(k, S, "wd", 2, [128, NFC, 256], BF16)
            sgr = Ring(k, S, "sg", 3, [128, 512], F32)
            hTs = [k.sb(S, "hT%d" % i, [128, NFC, 512], BF16) for i in range(2)]
            return (wgr, wur, wdr, sgr, hTs)

        BATCHES = [[0, 1], [2, 3], [4, 5], [6, 7], [8]]

        with ExitStack() as S:
            rg = ffn_rings(S)
            nmr = (Ring(k, S, "sq", 3, [128, 512], BF16), Ring(k, S, "rstd", 2, [128, 512], F32), Ring(k, S, "nmtmp", 2, [128, 512], F32))
            xr = Ring(k, S, "xt", 2, [128, 8, 512], F32)
            xnr = Ring(k, S, "xn", 3, [128, 8, 512], BF16)
            for batch in BATCHES:
                ctxs = []
                for ti in batch:
                    t0, n = TILES[ti]
                    j = 0 if t0 < L else 1
                    xt, xb = xr.next()
                    xn, xnb = xnr.next()
                    P.dma(xt[:, :, :n], xT_tile_ap(t0, n), reads=[b_xTt[ti]], writes=[xb])
                    norm_mod(S, nmr, xt, xb, n, 0, j, xn, xnb)
                    ctxs.append((xt, xb, xn, xnb, n, j))
                ffn_multi(S, rg, ctxs, 0, 0)
                for ti, (xt, xb, xn, xnb, n, j) in zip(batch, ctxs):
                    t0, n = TILES[ti]
                    P.dma(xT_tile_ap(t0, n), xt[:, :, :n], reads=[xb], writes=[b_xTt[ti]], eng="pool")
                    un, unb = xnr.next()
                    norm_mod(S, nmr, xt, xb, n, 1, j, un, unb)
                    P.dma(uT_tile_ap(t0, n), un[:, :, :n], reads=[unb], writes=[b_uTt[ti]], eng="pool")
            P.barrier()

        if stop == "ffn1" and l == 0:
            return finish(k, st, I, out_d, dbg_d, xT, b_xTt)

        winv = win_b[0].rearrange("(kc p) n -> p kc n", p=128)
        with ExitStack() as S:
            NB = lambda: Buf("x")
            WuqP, b_WuqP = k.sb(S, "WuqP", [128, 2, 8, 128], BF16)
            WukP, b_WukP = k.sb(S, "WukP", [128, 8, 128], BF16)
            WukV, b_WukV = k.sb(S, "WukV", [128, 8, 64], BF16)
            WkrP, b_WkrP = k.sb(S, "WkrP", [128, 8, 128], BF16)
            WckP, b_WckP = k.sb(S, "WckP", [128, 8, 4, 128], BF16)
            wgate, b_wgate = k.sb(S, "wgate", [16, 2, 256], BF16)
            biasb, b_biasb = k.sb(S, "biasb", [128, 512], F32)
            pv, b_pv = k.sb(S, "pv", [128, 8], F32)
            rtM, b_rtM = k.sb(S, "rtM", [128, 128], BF16)
            rtG, b_rtG = k.sb(S, "rtG", [128, 128], BF16)
            P.dma(rtM[:], I["k_rtM"], writes=[b_rtM], eng="pool")
            P.dma(rtG[:], I["k_rtG"], writes=[b_rtG], eng="pool")
            P.op("dve", lambda e: e.memset(WuqP[:], 0.0), [], [b_WuqP])
            P.op("dve", lambda e: e.memset(WukP[:], 0.0), [], [b_WukP])
            P.op("dve", lambda e: e.memset(WkrP[:], 0.0), [], [b_WkrP])
            P.op("dve", lambda e: e.memset(WckP[:], 0.0), [], [b_WckP])
            for kc in range(2):
                if 'B' not in SKIP:
                    P.dma(WuqP[:, kc, :, 0:96], I["mla_w_uq"][l, kc * 128:(kc + 1) * 128, :].rearrange("p (h c) -> p h c", c=96), writes=[b_WuqP], eng="pool")
            ukv = I["mla_w_ukv"][l].rearrange("p (h c) -> p h c", c=128)
            if 'B' not in SKIP:
                P.dma(WukP[:, :, 0:64], ukv[:, :, 0:64], writes=[b_WukP], eng="pool")
            if 'B' not in SKIP:
                P.dma(WukV[:], ukv[:, :, 64:128], writes=[b_WukV], eng="pool")
            if 'D' not in SKIP:
                P.dma(WkrP[:, :, 64:96], winv[:, :, O_KR:O_KR + 32], reads=[win_b[1]], writes=[b_WkrP])
            for g in range(2):
                for par in range(2):
                    if 'D' not in SKIP:
                        P.dma(WckP[:, :, g * 2 + par, par * 64:par * 64 + 64], winv[:, :, O_CK + g * 64:O_CK + g * 64 + 64], reads=[win_b[1]], writes=[b_WckP])
            if 'C' not in SKIP:
                P.dma(wgate[:], I["gla_w_gate"][l].rearrange("d r c -> r d c"), writes=[b_wgate], eng="pool")
            if 'A' not in SKIP:
                P.dma(biasb[:], I["gla_b_gate"][l].rearrange("d c -> (d c)").partition_broadcast(128), writes=[b_biasb])
            if 'E' not in SKIP:
                P.dma(pv[:, 0:2], I["mla_q_norm_g"][:, l, :], writes=[b_pv])
            if 'E' not in SKIP:
                P.dma(pv[:, 2:3], I["mla_kv_norm_g"][:, l:l + 1], writes=[b_pv])
            if 'E' not in SKIP:
                P.dma(pv[:, 3:4], I["gqa_q_norm_g"][:, l:l + 1], writes=[b_pv])
            if 'E' not in SKIP:
                P.dma(pv[:, 4:5], I["gqa_k_norm_g"][:, l:l + 1], writes=[b_pv])
            ur = Ring(k, S, "u", 2, [128, 8, 512], BF16)
            wr = Ring(k, S, "wp", 3, [128, 8, 512], BF16)
            tabr = Ring(k, S, "tab", 2, [128, 4, 512], F32)
            f32r = Ring(k, S, "f32r", RING_F32, [128, 512], F32)
            bfr = Ring(k, S, "bfr", RING_BF, [128, 512], BF16)
            cqr = Ring(k, S, "cq", 2, [128, 2, 512], F32)
            cqnr = Ring(k, S, "cqn", 2, [128, 2, 512], BF16)
            outr = Ring(k, S, "outr", RING_OUT, [128, 512], BF16)
            krr_r = Ring(k, S, "krr", 2, [128, 512], BF16)
            ckvn_r = Ring(k, S, "ckvn", 2, [128, 512], BF16)
            vaA = Ring(k, S, "vaA", 2, [128, 8, 128], BF16)
            vaC = Ring(k, S, "vaC", 2, [128, 2, 128], BF16)
            for (t_, b_) in zip(vaA.t + vaC.t, vaA.b + vaC.b):
                P.op("dve", lambda e, t_=t_: e.memset(t_[:], 1.0), [], [b_])
            lrr = Ring(k, S, "lrT", 4, [16, 512], BF16)

            def wcols(c0, width):
                wt, wb = wr.next()
                P.dma(wt[:, :, :width], winv[:, :, c0:c0 + width], reads=[win_b[1]], writes=[wb])
                return wt, wb

            def store(dst, src, sb, eng="pool"):
                P.dma(dst, src, reads=[sb], writes=[NB()], eng=eng)

            def do_tile(ti, t0, n):
                u, ub = ur.next()
                P.dma(u[:, :, :n], uT_tile_ap(t0, n), reads=[b_uTt[ti]], writes=[ub])
                tab, tabb = tabr.next()
                for i_, nm in enumerate(("k_cosM", "k_sinM", "k_cosG", "k_sinG")):
                    P.dma(tab[:, i_, :n], I[nm][:, t0:t0 + n], writes=[tabb])
                nchunk = n // 128

                def proj(wt, wb, c0, M):
                    ps, pb = k.tmp()
                    mm_group(P, ps[:M, :n], pb, [(wt[:, kc, c0:c0 + M], u[:, kc, :n]) for kc in range(8)], [wb, ub])
                    return ps, pb

                def rms_rstd(srcs, nfeat, lhsT, lb):
                    pss, pssb = k.acc()
                    for i_, (a, ab) in enumerate(srcs):
                        sq, sqb = bfr.next()
                        P.op("act", lambda e, sq=sq, a=a: e.activation(out=sq[:, :n], in_=a, func=AF.Square), [ab], [sqb])
                        P.op("pe", lambda e, sq=sq, i_=i_, pss=pss: e.matmul(pss[:, :n], lhsT=lhsT[:], rhs=sq[:, :n], start=(i_ == 0), stop=(i_ == len(srcs) - 1)),
                             [sqb, lb], [pssb])
                    r, rb = f32r.next()
                    P.op("act", lambda e: e.activation(out=r[:, :n], in_=pss[:, :n], func=AF.Sqrt, bias=epsv[:, 0:1], scale=1.0 / nfeat), [pssb, b_eps], [rb])
                    P.op("dve", lambda e: e.reciprocal(out=r[:, :n], in_=r[:, :n]), [rb], [rb])
                    return r, rb

                def rope(src, srcb, ci, rt, rtb, dst):
                    if dst is not None or True:
                        pc_, pcb_ = f32r.next()
                        copy_on(P, "dve", pc_[:, :n], src, [srcb], [pcb_])
                        src, srcb = pc_[:, :n], pcb_
                    qb_, qbb = bfr.next()
                    copy_on(P, ROPE_COPY_ENG, qb_[:, :n], src, [srcb], [qbb])
                    psr, psrb = k.tmp()
                    P.op("pe", lambda e: e.matmul(psr[:, :n], lhsT=rt[:], rhs=qb_[:, :n], start=True, stop=True), [qbb, rtb], [psrb])
                    t1, t1b = f32r.next()
                    t2, t2b = f32r.next()
                    P.op("dve", lambda e: e.tensor_tensor(out=t1[:, :n], in0=src, in1=tab[:, ci, :n], op=ALU.mult), [srcb, tabb], [t1b])
                    P.op("dve", lambda e: e.tensor_tensor(out=t2[:, :n], in0=psr[:, :n], in1=tab[:, ci + 1, :n], op=ALU.mult), [psrb, tabb], [t2b])
                    o_, ob_ = (dst or outr).next()
                    P.op("dve", lambda e: e.tensor_tensor(out=o_[:, :n], in0=t1[:, :n], in1=t2[:, :n], op=ALU.add), [t1b, t2b], [ob_])
                    return o_, ob_

                if 'mla' not in PROJ_SECTIONS:
                    return
                W, Wb = wcols(0, 416)
                cq, cqb = cqr.next()
                for c in range(2):
                    ps, pb = proj(W, Wb, c * 128, 128)
                    copy_on(P, "dve", cq[:, c, :n], ps[:, :n], [pb], [cqb])
                if MLA_STEP <= 1:
                    return
                r, rb = rms_rstd([(cq[:, 0, :n], cqb), (cq[:, 1, :n], cqb)], 256, ones_b, b_ones)
                cqn, cqnb = cqnr.next()
                for c in range(2):
                    P.op("dve", lambda e, c=c: e.scalar_tensor_tensor(out=cqn[:, c, :n], in0=cq[:, c, :n], scalar=pv[:, c:c + 1], in1=r[:, :n], op0=ALU.mult, op1=ALU.mult),
                         [cqb, rb, b_pv], [cqnb])
                if MLA_STEP <= 2:
                    return
                ps, pb = proj(W, Wb, 256, 128)
                ckv, ckvb = f32r.next()
                copy_on(P, "dve", ckv[:, :n], ps[:, :n], [pb], [ckvb])
                r2, r2b = rms_rstd([(ckv[:, :n], ckvb)], 128, ones_b, b_ones)
                ckvn, ckvnb = ckvn_r.next()
                P.op("dve", lambda e: e.scalar_tensor_tensor(out=ckvn[:, :n], in0=ckv[:, :n], scalar=pv[:, 2:3], in1=r2[:, :n], op0=ALU.mult, op1=ALU.mult),
                     [ckvb, r2b, b_pv], [ckvnb])
                if MLA_STEP <= 3:
                    return
                pskr, pskrb = k.tmp()
                mm_group(P, pskr[:, :n], pskrb, [(WkrP[:, kc, :], u[:, kc, :n]) for kc in range(8)], [b_WkrP, ub])
                krr, krrb = rope(pskr[:, :n], pskrb, 0, rtM, b_rtM, krr_r)
                if MLA_STEP <= 4:
                    return
                for h in range(NHEADS_DBG):
                    if 'Q' not in SKIP:
                        psq, psqb = k.tmp()
                        mm_group(P, psq[:, :n], psqb, [(WuqP[:, kc, h, :], cqn[:, kc, :n]) for kc in range(2)], [b_WuqP, cqnb])
                        if 'R' in SKIP:
                            qo, qob = outr.next()
                            copy_on(P, "act", qo[:, :n], psq[:, :n], [psqb], [qob])
                        else:
                            qo, qob = rope(psq[:, :n], psqb, 0, rtM, b_rtM, None)
                        if 'S' not in SKIP:
                            store(qA[h, :, t0:t0 + n], qo[:, :n], qob)
                    if 'K' not in SKIP:
                        psk, pskb = k.tmp()
                        mm_group(P, psk[:, :n], pskb, [(WukP[:, h, :], ckvn[:, :n]), (ident_b[:], krr[:, :n])], [b_WukP, ckvnb, b_identb, krrb])
                        ko, kob = outr.next()
                        copy_on(P, "act", ko[:, :n], psk[:, :n], [pskb], [kob])
                        if 'S' not in SKIP:
                            store(kA[h, :, t0:t0 + n], ko[:, :n], kob)
                if MLA_STEP <= 5:
                    return
                for jc in range(nchunk):
                    psv, psvb = k.tmp()
                    P.op("pe", lambda e, jc=jc, psv=psv: e.matmul(psv[:, :], lhsT=ckvn[:, jc * 128:(jc + 1) * 128], rhs=WukV[:].rearrange("p h d -> p (h d)"), start=True, stop=True),
                         [ckvnb, b_WukV], [psvb])
                    va, vab = vaA.next()
                    copy_on(P, "dve", va[:, :, 0:64], psv[:, :].rearrange("p (h d) -> p h d", d=64), [psvb], [vab])
                    store(vA[t0 // 128 + jc], va[:], vab)

                if 'gqa' not in PROJ_SECTIONS:
                    return
                def normrope(ps, pb, gcol):
                    pc, pcb = f32r.next()
                    copy_on(P, "dve", pc[:, :n], ps[:, :n], [pb], [pcb])
                    r_, rb_ = rms_rstd([(pc[:, :n], pcb)], 64, bones_b, b_bones)
                    P.op("dve", lambda e: e.scalar_tensor_tensor(out=pc[:, :n], in0=pc[:, :n], scalar=pv[:, gcol:gcol + 1], in1=r_[:, :n], op0=ALU.mult, op1=ALU.mult),
                         [pcb, rb_, b_pv], [pcb])
                    return rope(pc[:, :n], pcb, 2, rtG, b_rtG, None)

                W, Wb = wcols(O_CQQ, 512)
                for c in range(4):
                    ps, pb = proj(W, Wb, c * 128, 128)
                    qo, qob = normrope(ps, pb, 3)
                    store(qC[c, :, t0:t0 + n], qo[:, :n], qob)
                for kt in range(4):
                    ps, pb = k.tmp()
                    mm_group(P, ps[:, :n], pb, [(WckP[:, kc, kt, :], u[:, kc, :n]) for kc in range(8)], [b_WckP, ub])
                    ko, kob = normrope(ps, pb, 4)
                    store(kC[kt, :, t0:t0 + n], ko[:, :n], kob)
                W, Wb = wcols(O_CV, 128)
                for jc in range(nchunk):
                    psv, psvb = k.tmp()
                    mm_group(P, psv[:, :128], psvb, [(u[:, kc, jc * 128:(jc + 1) * 128], W[:, kc, 0:128]) for kc in range(8)], [ub, Wb])
                    va, vab = vaC.next()
                    copy_on(P, "dve", va[:, :, 0:64], psv[:, 0:128].rearrange("p (h d) -> p h d", d=64), [psvb], [vab])
                    store(vC[t0 // 128 + jc], va[:], vab)

                if 'gla' not in PROJ_SECTIONS:
                    return
                W, Wb = wcols(O_BQ, 512)
                for c in range(2):
                    ps, pb = proj(W, Wb, c * 128, 128)
                    o_, ob_ = outr.next()
                    P.op("act", lambda e, o_=o_, ps=ps: e.activation(out=o_[:, :n], in_=ps[:, :n], func=AF.Copy, scale=0.125), [pb], [ob_])
                    store(bqT[c, :, t0:t0 + n], o_[:, :n], ob_)
                    ps, pb = proj(W, Wb, 256 + c * 128, 128)
                    o_, ob_ = outr.next()
                    copy_on(P, "dve", o_[:, :n], ps[:, :n], [pb], [ob_])
                    store(bkT[c, :, t0:t0 + n], o_[:, :n], ob_)
                for jc in range(nchunk):
                    ps, pb = k.tmp()
                    mm_group(P, ps[:, :256], pb, [(u[:, kc, jc * 128:(jc + 1) * 128], W[:, kc, 256:512]) for kc in range(8)], [ub, Wb])
                    o_, ob_ = outr.next()
                    copy_on(P, "act", o_[:, :256], ps[:, :256], [pb], [ob_])
                    store(bk_tok[t0 + jc * 128:t0 + (jc + 1) * 128, :], o_[:, :256], ob_)
                W, Wb = wcols(O_BV, 512)
                for jc in range(nchunk):
                    ps, pb = k.tmp()
                    mm_group(P, ps[:, :], pb, [(u[:, kc, jc * 128:(jc + 1) * 128], W[:, kc, 0:512]) for kc in range(8)], [ub, Wb])
                    o_, ob_ = outr.next()
                    copy_on(P, "dve", o_[:, :], ps[:, :], [pb], [ob_])
                    store(bv_tok[t0 + jc * 128:t0 + (jc + 1) * 128, :], o_[:, :], ob_)
                W, Wb = wcols(O_LR, 32)
                for d in range(2):
                    ps, pb = k.tmp()
                    mm_group(P, ps[:16, :n], pb, [(W[:, kc, d * 16:(d + 1) * 16], u[:, kc, :n]) for kc in range(8)], [Wb, ub])
                    lr, lrb = lrr.next()
                    copy_on(P, "act", lr[:, :n], ps[:16, :n], [pb], [lrb])
                    for jc in range(nchunk):
                        ps2, pb2 = k.tmp()
                        P.op("pe", lambda e, ps2=ps2, lr=lr, jc=jc, d=d: e.matmul(ps2[:, :256], lhsT=lr[:, jc * 128:(jc + 1) * 128], rhs=wgate[:, d, :], start=True, stop=True),
                             [lrb, b_wgate], [pb2])
                        t1, t1b = f32r.next()
                        P.op("dve", lambda e, t1=t1, ps2=ps2, d=d: e.tensor_tensor(out=t1[:, :256], in0=ps2[:, :256], in1=biasb[:, d * 256:(d + 1) * 256], op=ALU.add), [pb2, b_biasb], [t1b])
                        P.op("act", lambda e, t1=t1: e.activation(out=t1[:, :256], in_=t1[:, :256], func=AF.Exp, scale=-1.0), [t1b], [t1b])
                        P.op("act", lambda e, t1=t1: e.activation(out=t1[:, :256], in_=t1[:, :256], func=AF.Ln, bias=1.0, scale=1.0), [t1b], [t1b])
                        o_, ob_ = outr.next()
                        P.op("dve", lambda e, t1=t1, o_=o_: e.tensor_scalar(out=o_[:, :256], in0=t1[:, :256], scalar1=-1.0 / 16.0, scalar2=None, op0=ALU.mult), [t1b], [ob_])
                        store(la_d[d, t0 + jc * 128:t0 + (jc + 1) * 128, :], o_[:, :256], ob_)
                W, Wb = wcols(O_OG, 512)
                for c in range(4):
                    ps, pb = proj(W, Wb, c * 128, 128)
                    o_, ob_ = outr.next()
                    P.op("act", lambda e, o_=o_, ps=ps: e.activation(out=o_[:, :n], in_=ps[:, :n], func=AF.Silu), [pb], [ob_])
                    store(ogT[c, :, t0:t0 + n], o_[:, :n], ob_)
                if 'hy' not in PROJ_SECTIONS:
                    return
                for pp in range(3):
                    W, Wb = wcols(O_DP + pp * 512, 512)
                    for c in range(4):
                        ps, pb = proj(W, Wb, c * 128, 128)
                        o_, ob_ = outr.next()
                        copy_on(P, k.ev(), o_[:, :n], ps[:, :n], [pb], [ob_])
                        store(dpT[pp * 4 + c, :, t0:t0 + n], o_[:, :n], ob_)

            for ti, (t0, n) in enumerate(TILES[:PROJ_TILES]):
                do_tile(ti, t0, n)
                P.barrier()
            P.barrier()

        if stop == "proj" and l == 0:
            return finish_list(k, st, dbg_d, dbg_items(locals()))

        with ExitStack() as S:
            NB = lambda: Buf("x")
            Kr = Ring(k, S, "Kr", 2, [128, NT], BF16)
            Vr = Ring(k, S, "Vr", 2, [128, 34, 128], BF16)
            qr = Ring(k, S, "qr", 3, [128, 512], BF16)
            pr = Ring(k, S, "pr", 7, [128, 512], BF16)
            rcr = Ring(k, S, "rcr", 2, [128, 512], F32)
            orr = Ring(k, S, "orr", 3, [128, 512], BF16)
            for br in (() if SKIP_ATTN else (0, 2)):
                scale = (96.0 ** -0.5) if br == 0 else 0.125
                for h in range(8):
                    Kt, Kb = Kr.next()
                    Vt, Vb = Vr.next()
                    if br == 0:
                        ksrc = kA[h]; vsrc = vA[:, :, h, :]; qsrc = qA[h]
                    else:
                        g = h // 4
                        ksrc = kC[g * 2 + (h % 2)]; vsrc = vC[:, :, g, :]; qsrc = qC[h // 2]
                    P.dma(Kt[:], ksrc, writes=[Kb])
                    P.dma(Vt[:], vsrc.rearrange("c p d -> p c d"), writes=[Vb])
                    for ti, (t0, n) in enumerate(TILES):
                        if t0 >= L:
                            if not need_ctx:
                                continue
                            kcs = list(range(32, 34))
                        else:
                            kcs = list(range(34))
                        qt, qb_ = qr.next()
                        P.dma(qt[:, :n], qsrc[:, t0:t0 + n], writes=[qb_])
                        psO, psOb = k.acc()
                        pend = []

                        def emit_pv(item, psO=psO, psOb=psOb, Vt=Vt, Vb=Vb, n=n, kcs=kcs):
                            i_, kc, pt, ptb = item
                            P.op("pe", lambda e: e.matmul(psO[:, :n], lhsT=Vt[:, kc, :], rhs=pt[:, :n], start=(i_ == 0), stop=(i_ == len(kcs) - 1)),
                                 [Vb, ptb], [psOb])
                        for i_, kc in enumerate(kcs):
                            psS, psSb = k.tmp()
                            P.op("pe", lambda e, psS=psS, Kt=Kt, kc=kc, qt=qt, n=n: e.matmul(psS[:, :n], lhsT=Kt[:, kc * 128:(kc + 1) * 128], rhs=qt[:, :n], start=True, stop=True),
                                 [Kb, qb_], [psSb])
                            pt, ptb = pr.next()
                            P.op("act", lambda e, pt=pt, psS=psS, n=n, scale=scale: e.activation(out=pt[:, :n], in_=psS[:, :n], func=AF.Exp, scale=scale), [psSb], [ptb])
                            pend.append((i_, kc, pt, ptb))
                            if len(pend) > 3:
                                emit_pv(pend.pop(0))
                        while pend:
                            emit_pv(pend.pop(0))
                        rc, rcb = rcr.next()
                        P.op("dve", lambda e, rc=rc, psO=psO, n=n: e.reciprocal(out=rc[64:128, :n], in_=psO[64:128, :n]), [psOb], [rcb])
                        ot, otb = orr.next()
                        P.op("dve", lambda e, ot=ot, psO=psO, rc=rc, n=n: e.tensor_tensor(out=ot[0:64, :n], in0=psO[0:64, :n], in1=rc[64:128, :n], op=ALU.mult), [psOb, rcb], [otb])
                        P.dma(oT[br, h // 2, (h % 2) * 64:(h % 2) * 64 + 64, t0:t0 + n], ot[0:64, :n], reads=[otb], writes=[NB()], eng="pool")
            P.barrier()

        if stop == "attn" and l == 0:
            return finish_list(k, st, dbg_d, dbg_items(locals()))

        with ExitStack() as S:
            NB = lambda: Buf("x")
            msk, b_msk = k.sb(S, "msk", [64, 4, 64], F32)
            msk4, b_msk4 = k.sb(S, "msk4", [64, 2, 4, 64], F32)
            hmask, b_hmask = k.sb(S, "hmask", [128, 2], F32)
            P.op("dve", lambda e: e.memset(hmask[:], 0.0), [], [b_hmask])
            P.op("dve", lambda e: e.memset(hmask[0:64, 0:1], 1.0), [b_hmask], [b_hmask])
            P.op("dve", lambda e: e.memset(hmask[64:128, 1:2], 1.0), [b_hmask], [b_hmask])
            for m_ in range(2):
                for h_ in range(4):
                    P.dma(msk4[:, m_, h_, :], I["k_m_le" if m_ == 0 else "k_m_ge"], writes=[b_msk4])
            mskb, b_mskb = k.sb(S, "mskb", [64, 4, 64], BF16)
            for i_, nm in enumerate(("k_m_le", "k_m_ge", "k_m_gt", "k_m_lt")):
                P.dma(msk[:, i_, :], I[nm], writes=[b_msk])
                P.dma(mskb[:, i_, :], I[nm], writes=[b_mskb], eng="pool")
            gng, b_gng = k.sb(S, "gng", [128, 1], F32)
            P.dma(gng[:], I["gla_norm_g"][:, l:l + 1], writes=[b_gng])
            Sst = [k.sb(S, "Sst%d" % i, [128, 128], F32) for i in range(2)]
            Sbf = [k.sb(S, "Sbf%d" % i, [128, 128], BF16) for i in range(2)]
            lar = Ring(k, S, "la", 2, [64, 8, 256], BF16)
            bktr = Ring(k, S, "bkt", 2, [64, 8, 256], BF16)
            bvr = Ring(k, S, "bv", 2, [64, 8, 512], BF16)
            bqr = Ring(k, S, "bq", 4, [128, 512], BF16)
            bkr = Ring(k, S, "bk", 4, [128, 512], BF16)
            Er = Ring(k, S, "E", 8, [128, 64], F32)
            qdr = Ring(k, S, "qd", 12, [128, 64], BF16)
            Abr = Ring(k, S, "Ab", 3, [64, 256], BF16)
            ker = Ring(k, S, "ke", 3, [64, 256], F32)
            kebr = Ring(k, S, "keb", 3, [64, 256], BF16)
            ogr = Ring(k, S, "og", 2, [128, 4, 512], F32)
            sqr = Ring(k, S, "gsq", 3, [128, 512], BF16)
            rsr = Ring(k, S, "grs", 2, [128, 512], F32)
            ogtr = Ring(k, S, "ogt", 2, [128, 512], BF16)
            outr2 = Ring(k, S, "gout", 3, [128, 512], BF16)
            tmpf = Ring(k, S, "gtmp", 2, [128, 512], F32)

            def gla_chunk(dirn, la, lab, bkt, bktb, bv, bvb, bq, bk, ci, og, ogb):
                mi = 0 if dirn == 0 else 1
                mk = 2 if dirn == 0 else 3
                last = 63 if dirn == 0 else 0
                qd = []
                decs = []
                for pr in range(2):
                    psb, psbb = k.tmp()
                    P.op("pe", lambda e, psb=psb, pr=pr: e.matmul(psb[:, :64], lhsT=la[:, ci, pr * 128:(pr + 1) * 128], rhs=mskb[:, mi, :], start=True, stop=True),
                         [lab, b_mskb], [psbb])
                    E, Eb = Er.next()
                    Ei, Eib = Er.next()
                    P.op("act", lambda e, E=E, psb=psb: e.activation(out=E[:, :], in_=psb[:, :64], func=AF.Exp), [psbb], [Eb])
                    P.op("act", lambda e, Ei=Ei, psb=psb: e.activation(out=Ei[:, :], in_=psb[:, :64], func=AF.Exp, scale=-1.0), [psbb], [Eib])
                    qt0, qt0b = qdr.next()
                    qt1, qt1b = qdr.next()
                    kt, ktb = qdr.next()
                    bqt, bqb = bq[pr]
                    bkt_, bkb_ = bk[pr]
                    for hh_, (qt_, qtb_) in enumerate(((qt0, qt0b), (qt1, qt1b))):
                        P.op("dve", lambda e, qt_=qt_, bqt=bqt, E=E, hh_=hh_: e.scalar_tensor_tensor(out=qt_[:, :], in0=bqt[:, ci * 64:(ci + 1) * 64], scalar=hmask[:, hh_:hh_ + 1], in1=E[:, :],
                                                                                                     op0=ALU.mult, op1=ALU.mult), [bqb, Eb, b_hmask], [qtb_])
                    P.op("dve", lambda e, kt=kt, bkt_=bkt_, Ei=Ei: e.tensor_tensor(out=kt[:, :], in0=bkt_[:, ci * 64:(ci + 1) * 64], in1=Ei[:, :], op=ALU.mult), [bkb_, Eib], [ktb])
                    qd.append(((qt0, qt0b), (qt1, qt1b), kt, ktb))
                    decs.append((E, Eb))
                psA, psAb = k.tmp()
                for h in range(4):
                    q0_, q1_, kt, ktb = qd[h // 2]
                    qt, qtb = (q0_, q1_)[h % 2]
                    P.op("pe", lambda e, h=h, qt=qt, kt=kt: e.matmul(psA[:64, h * 64:(h + 1) * 64], lhsT=kt[:, :], rhs=qt[:, :], start=True, stop=True),
                         [qtb, ktb], [psAb])
                Ab, Abb = Abr.next()
                P.op("dve", lambda e: e.tensor_tensor(out=Ab[:, :], in0=psA[:64, :256], in1=msk4[:, mi, :, :].rearrange("p h i -> p (h i)"), op=ALU.mult), [psAb, b_msk4], [Abb])
                pso, psob = k.tmp()
                for h in range(4):
                    q0_, q1_, kt, ktb = qd[h // 2]
                    qt, qtb = (q0_, q1_)[h % 2]
                    Sb_, Sbb_ = Sbf[h // 2]

                    def fn(e, h=h, qt=qt, Sb_=Sb_):
                        e.matmul(pso[:, h * 64:(h + 1) * 64], lhsT=bv[:, ci, h * 128:(h + 1) * 128], rhs=Ab[:, h * 64:(h + 1) * 64], start=True, stop=False)
                        return e.matmul(pso[:, h * 64:(h + 1) * 64], lhsT=Sb_[:, :], rhs=qt[:, :], start=False, stop=True)
                    P.op("pe", fn, [bvb, Abb, Sbb_, qtb], [psob])
                ogv = og[:, :, ci * 64:(ci + 1) * 64]
                if dirn == 0:
                    P.op("act", lambda e: e.copy(out=ogv, in_=pso[:, :256].rearrange("p (h i) -> p h i", h=4)), [psob], [ogb])
                else:
                    P.op("dve", lambda e: e.tensor_tensor(out=ogv, in0=ogv, in1=pso[:, :256].rearrange("p (h i) -> p h i", h=4), op=ALU.add), [psob, ogb], [ogb])
                psR, psRb = k.tmp()
                P.op("pe", lambda e: e.matmul(psR[:64, :256], lhsT=mskb[:, mk, :], rhs=la[:, ci, :], start=True, stop=True), [lab, b_mskb], [psRb])
                ke, keb = ker.next()
                P.op("act", lambda e: e.activation(out=ke[:, :], in_=psR[:64, :256], func=AF.Exp), [psRb], [keb])
                kb_, kbb_ = kebr.next()
                P.op("dve", lambda e: e.tensor_tensor(out=kb_[:, :], in0=ke[:, :], in1=bkt[:, ci, :], op=ALU.mult), [keb, bktb], [kbb_])
                for pr in range(2):
                    E, Eb = decs[pr]
                    St, Stb = Sst[pr]
                    Sb_, Sbb_ = Sbf[pr]
                    for hh in range(2):
                        h = pr * 2 + hh
                        psU, psUb = k.tmp()
                        P.op("pe", lambda e, psU=psU, pr=pr, h=h: e.matmul(psU[:, :128], lhsT=kb_[:, pr * 128:(pr + 1) * 128], rhs=bv[:, ci, h * 128:(h + 1) * 128], start=True, stop=True),
                             [kbb_, bvb], [psUb])
                        r0 = hh * 64
                        P.op("dve", lambda e, psU=psU, r0=r0, St=St, E=E: e.scalar_tensor_tensor(out=St[r0:r0 + 64, :], in0=St[r0:r0 + 64, :], scalar=E[r0:r0 + 64, last:last + 1],
                                                                                                   in1=psU[r0:r0 + 64, :128], op0=ALU.mult, op1=ALU.add),
                             [psUb, Stb, Eb], [Stb])
                    P.op("act", lambda e, St=St, Sb_=Sb_: e.copy(out=Sb_[:, :], in_=St[:, :]), [Stb], [Sbb_])

            def finalize(og, ogb, t0, n):
                for h in range(4):
                    sq, sqb = sqr.next()
                    P.op("act", lambda e, sq=sq, h=h: e.activation(out=sq[:, :n], in_=og[:, h, :n], func=AF.Square), [ogb], [sqb])
                    pss, pssb = k.acc()
                    P.op("pe", lambda e, sq=sq, pss=pss: e.matmul(pss[:, :n], lhsT=ones_b[:], rhs=sq[:, :n], start=True, stop=True), [sqb, b_ones], [pssb])
                    rs, rsb = rsr.next()
                    P.op("act", lambda e, rs=rs, pss=pss: e.activation(out=rs[:, :n], in_=pss[:, :n], func=AF.Sqrt, bias=epsv[:, 0:1], scale=1.0 / 128), [pssb, b_eps], [rsb])
                    P.op("dve", lambda e, rs=rs: e.reciprocal(out=rs[:, :n], in_=rs[:, :n]), [rsb], [rsb])
                    ogt, ogtb = ogtr.next()
                    P.dma(ogt[:, :n], ogT[h, :, t0:t0 + n], writes=[ogtb])
                    tf, tfb = tmpf.next()
                    P.op("dve", lambda e, tf=tf, h=h, rs=rs: e.scalar_tensor_tensor(out=tf[:, :n], in0=og[:, h, :n], scalar=gng[:, 0:1], in1=rs[:, :n], op0=ALU.mult, op1=ALU.mult),
                         [ogb, rsb, b_gng], [tfb])
                    ot_, otb_ = outr2.next()
                    P.op("dve", lambda e, tf=tf, ot_=ot_, ogt=ogt: e.tensor_tensor(out=ot_[:, :n], in0=tf[:, :n], in1=ogt[:, :n], op=ALU.mult), [tfb, ogtb], [otb_])
                    P.dma(oT[1, h, :, t0:t0 + n], ot_[:, :n], reads=[otb_], writes=[NB()], eng="pool")

            def gla_group(dirn, t0, n, want_out):
                nch = n // 64
                la, lab = lar.next()
                bkt, bktb = bktr.next()
                bv, bvb = bvr.next()
                P.dma(la[:, :nch, :], la_d[dirn, t0:t0 + n, :].rearrange("(c j) d -> j c d", j=64), writes=[lab])
                P.dma(bkt[:, :nch, :], bk_tok[t0:t0 + n, :].rearrange("(c j) d -> j c d", j=64), writes=[bktb])
                P.dma(bv[:, :nch, :], bv_tok[t0:t0 + n, :].rearrange("(c j) d -> j c d", j=64), writes=[bvb])
                bq, bk = [], []
                for pr in range(2):
                    t_, b_ = bqr.next()
                    P.dma(t_[:, :n], bqT[pr, :, t0:t0 + n], writes=[b_])
                    bq.append((t_, b_))
                    t_, b_ = bkr.next()
                    P.dma(t_[:, :n], bkT[pr, :, t0:t0 + n], writes=[b_])
                    bk.append((t_, b_))
                og, ogb = ogr.next()
                if dirn == 1 and want_out:
                    P.dma(og[:, :, :n], ofT[:, :, t0:t0 + n].rearrange("h p t -> p h t"), writes=[ogb])
                order = range(nch) if dirn == 0 else range(nch - 1, -1, -1)
                for ci in order:
                    gla_chunk(dirn, la, lab, bkt, bktb, bv, bvb, bq, bk, ci, og, ogb)
                if want_out:
                    if dirn == 0:
                        P.dma(ofT[:, :, t0:t0 + n].rearrange("h p t -> p h t"), og[:, :, :n], reads=[ogb], writes=[NB()], eng="pool")
                    else:
                        finalize(og, ogb, t0, n)

            for dirn in (() if SKIP_GLA else range(2)):
                for (St, Stb), (Sb_, Sbb_) in zip(Sst, Sbf):
                    P.op("dve", lambda e, St=St: e.memset(St[:], 0.0), [], [Stb])
                    P.op("dve", lambda e, Sb_=Sb_: e.memset(Sb_[:], 0.0), [], [Sbb_])
                gla_group(dirn, L, CT, need_ctx)
                groups = [(i * 512, 512) for i in range(8)]
                if dirn == 1:
                    groups = groups[::-1]
                if dirn == 1:
                    P.barrier()
                for (t0, n) in groups:
                    gla_group(dirn, t0, n, True)
            P.barrier()

        if stop == "gla" and l == 0:
            return finish_list(k, st, dbg_d, dbg_items(locals()))

        TWO_PI = 2.0 * math.pi
        segs = [(0, L, "L")] + ([(L, CT, "C")] if need_ctx else [])
        def hy_segment(toff, ns, tag):
            nblk = ns // 128
            tw = min(512, ns)
            ntt = ns // tw
            with ExitStack() as S:
                w1s, b_w1s = k.sb(S, "hw1", [33, 64], F32)
                w2s, b_w2s = k.sb(S, "hw2", [64, 64], F32)
                w3s, b_w3s = k.sb(S, "hw3", [64, 2048], F32)
                bb, b_bb = k.sb(S, "hbb", [64, 4], F32)
                P.dma(w1s[:], I["hy_filt_w1"][l], writes=[b_w1s])
                P.dma(w2s[:], I["hy_filt_w2"][l], writes=[b_w2s])
                P.dma(w3s[:], I["hy_filt_w3"][l], writes=[b_w3s])
                P.dma(bb[:, 0:1], I["hy_filt_b1"][:, l:l + 1], writes=[b_bb])
                P.dma(bb[:, 1:2], I["hy_filt_b2"][:, l:l + 1], writes=[b_bb])
                P.op("dve", lambda e: e.memset(bb[:, 2:3], -math.pi), [b_bb], [b_bb])
                ftr = Ring(k, S, "feat", 2, [33, 512], F32)
                h1r = Ring(k, S, "h1", 2, [64, 512], F32)
                h2r = Ring(k, S, "h2", 2, [64, 512], F32)
                decr = Ring(k, S, "dec", 2, [128, 2, 512], F32)
                hfr = Ring(k, S, "hf", 4, [128, 512], F32)
                hor = Ring(k, S, "ho", 4, [128, 512], BF16)

                kir = Ring(k, S, "ki", 2, [64, 512], mybir.dt.int32)
                kfr = Ring(k, S, "kf", 4, [64, 512], F32)

                def sin_layer(ps, pb, bcol, out, outb, w):
                    u_, ub_ = kfr.next()
                    P.op("dve", lambda e: e.tensor_scalar(out=u_[:, :w], in0=ps[:64, :w], scalar1=bb[:, bcol:bcol + 1], scalar2=1.0 / TWO_PI, op0=ALU.add, op1=ALU.mult), [pb, b_bb], [ub_])
                    ki, kib = kir.next()
                    P.op("dve", lambda e: e.tensor_copy(out=ki[:, :w], in_=u_[:, :w]), [ub_], [kib])
                    kf, kfb = kfr.next()
                    P.op("dve", lambda e: e.tensor_copy(out=kf[:, :w], in_=ki[:, :w]), [kib], [kfb])
                    P.op("dve", lambda e: e.tensor_tensor(out=u_[:, :w], in0=u_[:, :w], in1=kf[:, :w], op=ALU.subtract), [ub_, kfb], [ub_])
                    P.op("dve", lambda e: e.tensor_scalar(out=kf[:, :w], in0=u_[:, :w], scalar1=0.5, scalar2=None, op0=ALU.is_gt), [ub_], [kfb])
                    P.op("dve", lambda e: e.tensor_tensor(out=u_[:, :w], in0=u_[:, :w], in1=kf[:, :w], op=ALU.subtract), [ub_, kfb], [ub_])
                    P.op("dve", lambda e: e.tensor_scalar(out=kf[:, :w], in0=u_[:, :w], scalar1=-0.5, scalar2=None, op0=ALU.is_lt), [ub_], [kfb])
                    P.op("dve", lambda e: e.tensor_tensor(out=u_[:, :w], in0=u_[:, :w], in1=kf[:, :w], op=ALU.add), [ub_, kfb], [ub_])
                    P.op("act", lambda e: e.activation(out=out[:, :w], in_=u_[:, :w], func=AF.Sin, scale=TWO_PI), [ub_], [outb])

                def filt_tile(t0):
                    ft, ftb = ftr.next()
                    P.dma(ft[:, :tw], I["k_feat" + tag][:, t0:t0 + tw], writes=[ftb])
                    ps, pb = k.tmp()
                    P.op("pe", lambda e: e.matmul(ps[:64, :tw], lhsT=w1s[:], rhs=ft[:, :tw], start=True, stop=True), [b_w1s, ftb], [pb])
                    h1, h1b = h1r.next()
                    sin_layer(ps, pb, 0, h1, h1b, tw)
                    ps2, pb2 = k.tmp()
                    P.op("pe", lambda e: e.matmul(ps2[:64, :tw], lhsT=w2s[:], rhs=h1[:, :tw], start=True, stop=True), [b_w2s, h1b], [pb2])
                    h2, h2b = h2r.next()
                    sin_layer(ps2, pb2, 1, h2, h2b, tw)
                    for jb in range(tw // 128):
                        r0 = t0 + jb * 128
                        dc, dcb = decr.next()
                        P.dma(dc[:, 0, :], I["k_dec" + tag][r0:r0 + 128, :], writes=[dcb])
                        P.dma(dc[:, 1, :], I["k_decb" + tag][r0:r0 + 128, :], writes=[dcb])
                        for o in range(2):
                            hf_, hb_ = [], []
                            for dr in range(2):
                                ps3, pb3 = k.tmp()
                                c0 = o * 1024 + dr * 512
                                P.op("pe", lambda e, ps3=ps3, c0=c0, jb=jb: e.matmul(ps3[:, :], lhsT=h2[:, jb * 128:(jb + 1) * 128], rhs=w3s[:, c0:c0 + 512], start=True, stop=True),
                                     [h2b, b_w3s], [pb3])
                                hf, hfb = hfr.next()
                                P.op("dve", lambda e, hf=hf, ps3=ps3, dr=dr, dc=dc: e.tensor_tensor(out=hf[:, :], in0=ps3[:, :], in1=dc[:, dr, :], op=ALU.mult), [pb3, dcb], [hfb])
                                hf_.append((hf, hfb))
                            for pm, op_ in ((0, ALU.add), (1, ALU.subtract)):
                                ho, hob = hor.next()
                                P.op("pool", lambda e, ho=ho, op_=op_, a=hf_[0][0], b=hf_[1][0]: e.tensor_tensor(out=ho[:, :], in0=a[:, :], in1=b[:, :], op=op_),
                                     [hf_[0][1], hf_[1][1]], [hob])
                                P.dma(hpm[o, pm, r0:r0 + 128, :], ho[:, :], reads=[hob], writes=[Buf("x")], eng="pool")

                for tt in range(ntt):
                    filt_tile(tt * tw)
                P.barrier()
            with ExitStack() as S:
                hres, b_hres = k.sb(S, "hres", [128, 2, 32, 512], BF16)
                tbr = Ring(k, S, "tb1", 2, [128, 2, 32, 128], BF16)
                pqr = Ring(k, S, "pq", 4, [128, 512], F32)
                for o in range(2):
                    for pm in range(2):
                        P.dma(hres[:, pm, :nblk, :], hpm[o, pm, 0:ns, :].rearrange("(c p) d -> p c d", p=128), writes=[b_hres])
                    for fc in range(nblk):
                        tb, tbb = tbr.next()
                        P.dma(tb[:, 0, :nblk, :], I["k_dc1" + tag][fc], writes=[tbb])
                        P.dma(tb[:, 1, :nblk, :], I["k_ds1" + tag][fc], writes=[tbb])
                        for pm in range(2):
                            ps, pb = k.tmp()
                            mm_group(P, ps[:, :], pb, [(tb[:, pm, tc, :], hres[:, pm, tc, :]) for tc in range(nblk)], [tbb, b_hres])
                            pq, pqb = pqr.next()
                            copy_on(P, k.ev(), pq[:, :], ps[:, :], [pb], [pqb])
                            P.dma(PQ[o, pm, fc * 128:(fc + 1) * 128, :], pq[:, :], reads=[pqb], writes=[Buf("x")], eng="pool")
                P.barrier()
            with ExitStack() as S:
                scw, b_scw = k.sb(S, "scw", [128, 3, 12], F32)
                scb, b_scb = k.sb(S, "scb", [128, 12], F32)
                fbs, b_fbs = k.sb(S, "fbs", [128, 2, 4], F32)
                P.dma(scw[:], I["hy_sconv_w"][:, l, :, :], writes=[b_scw])
                P.dma(scb[:], I["hy_sconv_b"][:, l, :], writes=[b_scb])
                P.dma(fbs[:], I["hy_filt_bias"][:, l, :, :], writes=[b_fbs])
                vtok, b_vtok = k.sb(S, "vtok", [128, 32, 512], BF16)
                Yt, b_Yt = k.sb(S, "Yt", [128, 2, 32, 512], BF16)
                pr_ = Ring(k, S, "pin", 2, [128, 514], BF16)
                zr = Ring(k, S, "zf", 3, [128, 512], F32)
                zbr = Ring(k, S, "zb", 3, [128, 512], BF16)
                tb1 = Ring(k, S, "tb1b", 2, [128, 2, 32, 128], BF16)
                tb2 = Ring(k, S, "tb2", 2, [128, 2, 4, 512], BF16)
                pqr = Ring(k, S, "pq2", 2, [128, 2, 512], F32)
                ewr = Ring(k, S, "ew", 4, [128, 512], F32)
                xr_ = Ring(k, S, "x1t", 2, [128, 512], BF16)
                vr_ = Ring(k, S, "vtt", 2, [128, 512], BF16)

                def to_vtok(zf, zfb, cc, t0, w):
                    pst, ptb = k.tmp()
                    for jb in range(w // 128):
                        P.op("pe", lambda e, jb=jb: e.transpose(pst[:, jb * 128:(jb + 1) * 128], zf[:, jb * 128:(jb + 1) * 128], ident_f[:]), [zfb, b_identf], [ptb])
                    b0 = t0 // 128
                    nb_ = w // 128
                    copy_on(P, k.ev(), vtok[:, b0:b0 + nb_, cc * 128:(cc + 1) * 128], pst[:, :w].rearrange("p (b c) -> p b c", c=128), [ptb], [b_vtok])

                def sconv_tile(c, t0):
                    pin, pinb = pr_.next()
                    lo = max(t0 - 1, 0)
                    hi = min(t0 + tw + 1, ns)
                    off = lo - (t0 - 1)
                    P.dma(pin[:, off:off + (hi - lo)], dpT[c, :, toff + lo:toff + hi], writes=[pinb])
                    zf, zfb = zr.next()
                    P.op("act", lambda e: e.activation(out=zf[:, :tw], in_=pin[:, 1:tw + 1], func=AF.Identity, scale=scw[:, 1, c:c + 1], bias=scb[:, c:c + 1]), [pinb, b_scw, b_scb], [zfb])
                    a0 = 1 if t0 == 0 else 0
                    P.op("dve", lambda e: e.scalar_tensor_tensor(out=zf[:, a0:tw], in0=pin[:, a0:tw], scalar=scw[:, 0, c:c + 1], in1=zf[:, a0:tw], op0=ALU.mult, op1=ALU.add), [pinb, b_scw, zfb], [zfb])
                    a1 = tw - 1 if t0 + tw >= ns else tw
                    P.op("dve", lambda e: e.scalar_tensor_tensor(out=zf[:, 0:a1], in0=pin[:, 2:a1 + 2], scalar=scw[:, 2, c:c + 1], in1=zf[:, 0:a1], op0=ALU.mult, op1=ALU.add), [pinb, b_scw, zfb], [zfb])
                    zb, zbb = zbr.next()
                    copy_on(P, "act", zb[:, :tw], zf[:, :tw], [zfb], [zbb])
                    P.dma(zT[c, :, toff + t0:toff + t0 + tw], zb[:, :tw], reads=[zbb], writes=[Buf("x")], eng="pool")
                    if c >= 8:
                        to_vtok(zf, zfb, c - 8, t0, tw)

                for c in range(12):
                    for tt in range(ntt):
                        sconv_tile(c, tt * tw)
                P.barrier()

                bank = [(k.ps_t[i], k.ps_b[i]) for i in range(8)]
                for o in range(2):
                    for fc in range(nblk):
                        tb, tbb = tb1.next()
                        P.dma(tb[:, 0, :nblk, :], I["k_dc1" + tag][fc], writes=[tbb])
                        P.dma(tb[:, 1, :nblk, :], I["k_ds1" + tag][fc], writes=[tbb])
                        pq, pqb = pqr.next()
                        P.dma(pq[:, 0, :], PQ[o, 0, fc * 128:(fc + 1) * 128, :], writes=[pqb])
                        P.dma(pq[:, 1, :], PQ[o, 1, fc * 128:(fc + 1) * 128, :], writes=[pqb])
                        psA, pAb = k.tmp()
                        psB, pBb = k.tmp()
                        mm_group(P, psA[:, :], pAb, [(tb[:, 0, tc, :], vtok[:, tc, :]) for tc in range(nblk)], [tbb, b_vtok])
                        mm_group(P, psB[:, :], pBb, [(tb[:, 1, tc, :], vtok[:, tc, :]) for tc in range(nblk)], [tbb, b_vtok])
                        e1, e1b = ewr.next(); e2, e2b = ewr.next(); e3, e3b = ewr.next(); e4, e4b = ewr.next()
                        P.op("dve", lambda e, e1=e1, pq=pq, psA=psA: e.tensor_tensor(out=e1[:, :], in0=psA[:, :], in1=pq[:, 0, :], op=ALU.mult), [pAb, pqb], [e1b])
                        P.op("dve", lambda e, e2=e2, pq=pq, psB=psB: e.tensor_tensor(out=e2[:, :], in0=psB[:, :], in1=pq[:, 1, :], op=ALU.mult), [pBb, pqb], [e2b])
                        P.op("dve", lambda e, e3=e3, pq=pq, psB=psB: e.tensor_tensor(out=e3[:, :], in0=psB[:, :], in1=pq[:, 0, :], op=ALU.mult), [pBb, pqb], [e3b])
                        P.op("dve", lambda e, e4=e4, pq=pq, psA=psA: e.tensor_tensor(out=e4[:, :], in0=psA[:, :], in1=pq[:, 1, :], op=ALU.mult), [pAb, pqb], [e4b])
                        P.op("pool", lambda e, e1=e1, e2=e2, fc=fc: e.tensor_tensor(out=Yt[:, 0, fc, :], in0=e1[:, :], in1=e2[:, :], op=ALU.subtract), [e1b, e2b], [b_Yt])
                        P.op("pool", lambda e, e3=e3, e4=e4, fc=fc: e.tensor_tensor(out=Yt[:, 1, fc, :], in0=e3[:, :], in1=e4[:, :], op=ALU.add), [e3b, e4b], [b_Yt])
                    for tt in range(ntt):
                        t0 = tt * tw
                        grp = bank[0:4] if tt % 2 == 0 else bank[4:8]
                        for f4 in range(0, nblk, 4):
                            nf = min(4, nblk - f4)
                            t2, t2b = tb2.next()
                            P.dma(t2[:, 0, :nf, :tw], I["k_dc2" + tag][f4 * 128:(f4 + nf) * 128, t0:t0 + tw].rearrange("(c p) t -> p c t", p=128), writes=[t2b])
                            P.dma(t2[:, 1, :nf, :tw], I["k_ds2" + tag][f4 * 128:(f4 + nf) * 128, t0:t0 + tw].rearrange("(c p) t -> p c t", p=128), writes=[t2b])
                            for cc in range(4):
                                ps, pb = grp[cc]

                                def fn(e, ps=ps, cc=cc, f4=f4, nf=nf, t2=t2):
                                    ins = None
                                    for fi in range(nf):
                                        fc = f4 + fi
                                        for cs in range(2):
                                            ins = e.matmul(ps[:, :tw], lhsT=Yt[:, cs, fc, cc * 128:(cc + 1) * 128], rhs=t2[:, cs, fi, :tw],
                                                           start=(fc == 0 and cs == 0), stop=(fc == nblk - 1 and cs == 1))
                                    return ins
                                P.op("pe", fn, [b_Yt, t2b], [pb])
                        for cc in range(4):
                            ps, pb = grp[cc]
                            xsrc = zT[(0 if o == 0 else 4) + cc, :, toff + t0:toff + t0 + tw]
                            vsrc = zT[8 + cc, :, toff + t0:toff + t0 + tw]
                            x1, x1b = xr_.next()
                            vv, vvb = vr_.next()
                            P.dma(x1[:, :tw], xsrc, writes=[x1b])
                            P.dma(vv[:, :tw], vsrc, writes=[vvb])
                            zf, zfb = zr.next()
                            P.op("dve", lambda e, zf=zf, vv=vv, ps=ps, cc=cc, o=o: e.scalar_tensor_tensor(out=zf[:, :tw], in0=vv[:, :tw], scalar=fbs[:, o, cc:cc + 1], in1=ps[:, :tw], op0=ALU.mult, op1=ALU.add),
                                 [vvb, b_fbs, pb], [zfb])
                            P.op("dve", lambda e, zf=zf, x1=x1: e.tensor_tensor(out=zf[:, :tw], in0=zf[:, :tw], in1=x1[:, :tw], op=ALU.mult), [zfb, x1b], [zfb])
                            zb, zbb = zbr.next()
                            copy_on(P, "act", zb[:, :tw], zf[:, :tw], [zfb], [zbb])
                            if o == 0:
                                P.dma(zT[8 + cc, :, toff + t0:toff + t0 + tw], zb[:, :tw], reads=[zbb, vvb], writes=[Buf("x")], eng="pool")
                            else:
                                P.dma(oT[3, cc, :, toff + t0:toff + t0 + tw], zb[:, :tw], reads=[zbb], writes=[Buf("x")], eng="pool")
                            if o == 0:
                                pass
                        if o == 0:
                            pass
                    if o == 0:
                        P.barrier()
                        for cc in range(4):
                            for tt in range(ntt):
                                t0 = tt * tw
                                vv, vvb = vr_.next()
                                P.dma(vv[:, :tw], zT[8 + cc, :, toff + t0:toff + t0 + tw], writes=[vvb])
                                zf, zfb = zr.next()
                                copy_on(P, "dve", zf[:, :tw], vv[:, :tw], [vvb], [zfb])
                                to_vtok(zf, zfb, cc, t0, tw)
                P.barrier()

        for (toff_, ns_, tag_) in segs:
            hy_segment(toff_, ns_, tag_)

        if stop == "hyena" and l == 0:
            return finish_list(k, st, dbg_d, dbg_items(locals()))

        with ExitStack() as S:
            wbr_sb, b_wbr = k.sb(S, "wbr", [128, 16, 1024], BF16)
            wout_sb, b_wout = k.sb(S, "wout", [128, 8, 1024], BF16)
            P.dma(wbr_sb[:], wbr_b[0].rearrange("a (kc p) n -> p (a kc) n", p=128), reads=[wbr_b[1]], writes=[b_wbr])
            P.dma(wout_sb[:], wout_b[0].rearrange("(kc p) n -> p kc n", p=128), reads=[wout_b[1]], writes=[b_wout])
            xr = Ring(k, S, "xt", 1, [128, 8, 512], F32)
            xnr = Ring(k, S, "xn", 1, [128, 8, 512], BF16)
            ur = Ring(k, S, "u", 1, [128, 8, 512], BF16)
            otr = Ring(k, S, "ot", 1, [128, 16, 512], BF16)
            wgr_ = Ring(k, S, "wgt", 2, [128, 8, 1024], BF16)
            maccr = Ring(k, S, "macc", 1, [128, 8, 512], F32)
            sigr = Ring(k, S, "sig", 3, [128, 512], F32)

            def merge_tile(ti, t0, n):
                j = 0 if t0 < L else 1
                xt, xb = xr.next()
                u, ub = ur.next()
                ot, otb = otr.next()
                P.dma(xt[:, :, :n], xT_tile_ap(t0, n), reads=[b_xTt[ti]], writes=[xb])
                P.dma(u[:, :, :n], uT_tile_ap(t0, n), reads=[b_uTt[ti]], writes=[ub])
                P.dma(ot[:, :, :n], oT[:, :, :, t0:t0 + n].rearrange("a c p t -> p (a c) t"), writes=[otb])
                macc, mb = maccr.next()
                for br in range(4):
                    wg_, wgb_ = wgr_.next()
                    P.dma(wg_[:], winv[:, :, O_GT + br * 1024:O_GT + (br + 1) * 1024], reads=[win_b[1]], writes=[wgb_])
                    for m in range(8):
                        psg, pgb = k.tmp()
                        mm_group(P, psg[:, :n], pgb, [(wg_[:, kc, m * 128:(m + 1) * 128], u[:, kc, :n]) for kc in range(8)], [wgb_, ub])
                        psz, pzb = k.tmp()
                        mm_group(P, psz[:, :n], pzb, [(wbr_sb[:, br * 4 + kc, m * 128:(m + 1) * 128], ot[:, br * 4 + kc, :n]) for kc in range(4)], [b_wbr, otb])
                        sg, sgb = sigr.next()
                        P.op("act", lambda e, sg=sg, psg=psg: e.activation(out=sg[:, :n], in_=psg[:, :n], func=AF.Sigmoid), [pgb], [sgb])
                        if br == 0:
                            P.op("dve", lambda e, sg=sg, psz=psz, m=m: e.tensor_tensor(out=macc[:, m, :n], in0=sg[:, :n], in1=psz[:, :n], op=ALU.mult), [sgb, pzb], [mb])
                        else:
                            P.op("dve", lambda e, sg=sg, psz=psz: e.tensor_tensor(out=sg[:, :n], in0=sg[:, :n], in1=psz[:, :n], op=ALU.mult), [sgb, pzb], [sgb])
                            P.op("dve", lambda e, sg=sg, m=m: e.tensor_tensor(out=macc[:, m, :n], in0=macc[:, m, :n], in1=sg[:, :n], op=ALU.add), [sgb, mb], [mb])
                mT, mTb = xnr.next()
                for m in range(8):
                    copy_on(P, "act", mT[:, m, :n], macc[:, m, :n], [mb], [mTb])
                for mo in range(8):
                    psy, pyb = k.tmp()
                    mm_group(P, psy[:, :n], pyb, [(wout_sb[:, kc, mo * 128:(mo + 1) * 128], mT[:, kc, :n]) for kc in range(8)], [b_wout, mTb])
                    P.op("dve", lambda e, psy=psy, mo=mo: e.scalar_tensor_tensor(out=xt[:, mo, :n], in0=psy[:, :n], scalar=gtv[:, 1, mo, j:j + 1],
                                                                                   in1=xt[:, mo, :n], op0=ALU.mult, op1=ALU.add), [pyb, b_gtv, xb], [xb])
                P.dma(xT_tile_ap(t0, n), xt[:, :, :n], reads=[xb], writes=[b_xTt[ti]], eng="pool")

            for ti, (t0, n) in enumerate(TILES):
                if t0 >= L and not need_ctx:
                    continue
                merge_tile(ti, t0, n)
            P.barrier()

        with ExitStack() as S:
            rg = ffn_rings(S)
            nmr = (Ring(k, S, "sq", 3, [128, 512], BF16), Ring(k, S, "rstd", 2, [128, 512], F32), Ring(k, S, "nmtmp", 2, [128, 512], F32))
            xr = Ring(k, S, "xt", 2, [128, 8, 512], F32)
            xnr = Ring(k, S, "xn", 2, [128, 8, 512], BF16)

            for batch in BATCHES:
                ctxs = []
                tis = []
                for ti in batch:
                    t0, n = TILES[ti]
                    if t0 >= L and not need_ctx:
                        continue
                    j = 0 if t0 < L else 1
                    xt, xb = xr.next()
                    xn, xnb = xnr.next()
                    P.dma(xt[:, :, :n], xT_tile_ap(t0, n), reads=[b_xTt[ti]], writes=[xb])
                    norm_mod(S, nmr, xt, xb, n, 2, j, xn, xnb)
                    ctxs.append((xt, xb, xn, xnb, n, j))
                    tis.append(ti)
                if not ctxs:
                    continue
                ffn_multi(S, rg, ctxs, 1, 2)
                for ti, (xt, xb, xn, xnb, n, j) in zip(tis, ctxs):
                    t0, n = TILES[ti]
                    P.dma(xT_tile_ap(t0, n), xt[:, :, :n], reads=[xb], writes=[b_xTt[ti]], eng="pool")
            P.barrier()

        if stop == "merge" and l == 0:
            return finish(k, st, I, out_d, dbg_d, xT, b_xTt)

    for l_ in range(DEPTH):
        r_ = do_layer(l_)
        if r_ is not None:
            return r_

    with ExitStack() as S:
        fg, b_fg = k.sb(S, "fg", [128, 8], F32)
        P.dma(fg[:], I["final_g"], writes=[b_fg])
        xr = Ring(k, S, "xt", 2, [128, 8, 512], F32)
        sq_r = Ring(k, S, "sq", 3, [128, 512], BF16)
        rstd_r = Ring(k, S, "rstd", 2, [128, 512], F32)
        yr = Ring(k, S, "yt", 2, [128, 8, 512], F32)
        tokr = Ring(k, S, "tok", 3, [128, 1024], F32)

        def final_tile(ti, t0, n):
            xt, xb = xr.next()
            P.dma(xt[:, :, :n], xT_tile_ap(t0, n), reads=[b_xTt[ti]], writes=[xb])
            ps, pb = k.acc()
            for fc in range(8):
                q, qb = sq_r.next()
                P.op("act", lambda e, q=q, fc=fc: e.activation(out=q[:, :n], in_=xt[:, fc, :n], func=AF.Square), [xb], [qb])
                P.op("pe", lambda e, q=q, fc=fc: e.matmul(ps[:, :n], lhsT=ones_b[:], rhs=q[:, :n], start=(fc == 0), stop=(fc == 7)), [qb, b_ones], [pb])
            rstd, rb = rstd_r.next()
            P.op("act", lambda e: e.activation(out=rstd[:, :n], in_=ps[:, :n], func=AF.Sqrt, bias=epsv[:, 0:1], scale=1.0 / D), [pb, b_eps], [rb])
            P.op("dve", lambda e: e.reciprocal(out=rstd[:, :n], in_=rstd[:, :n]), [rb], [rb])
            yt, yb = yr.next()
            for fc in range(8):
                P.op("dve", lambda e, fc=fc: e.scalar_tensor_tensor(out=yt[:, fc, :n], in0=xt[:, fc, :n], scalar=fg[:, fc:fc + 1], in1=rstd[:, :n],
                                                                     op0=ALU.mult, op1=ALU.mult), [xb, rb, b_fg], [yb])
            for jb in range(n // 128):
                tk, tkb = tokr.next()
                for half in range(2):
                    pst, ptb = k.tmp()
                    for f4 in range(4):
                        fc = half * 4 + f4
                        P.op("pe", lambda e, pst=pst, f4=f4, fc=fc, jb=jb: e.transpose(pst[:, f4 * 128:(f4 + 1) * 128], yt[:, fc, jb * 128:(jb + 1) * 128], ident_f[:]),
                             [yb, b_identf], [ptb])
                    copy_on(P, k.ev(), tk[:, half * 512:(half + 1) * 512], pst[:, :], [ptb], [tkb])
                P.dma(out_d[t0 + jb * 128:t0 + (jb + 1) * 128, :], tk[:], reads=[tkb], eng="pool")

        for ti, (t0, n) in enumerate(TILES):
            if t0 < L:
                final_tile(ti, t0, n)
        P.barrier()
    P.emit()
    st.close()
    return nc


DBG_WANT = []


def dbg_items(loc):
    return [f(loc) for f in DBG_WANT]


def finish_list(k, st, dbg_d, aps):
    P = k.P
    with ExitStack() as S:
        r = Ring(k, S, "fl", 2, [128, 4352], F32)
        row = 0
        for ap in aps:
            rows, cols = ap.shape
            for r0 in range(0, rows, 128):
                rr = min(128, rows - r0)
                t, b = r.next()
                P.dma(t[:rr, :cols], ap[r0:r0 + rr, :], writes=[b], eng="pool")
                P.dma(dbg_d[row:row + rr, :cols], t[:rr, :cols], reads=[b], eng="sp")
                row += rr
    P.emit()
    st.close()
    return k.nc


def finish(k, st, I, out_d, dbg_d, xT, b_xTt):
    P = k.P
    if dbg_d is not None:
        with ExitStack() as S:
            r = Ring(k, S, "fin", 2, [128, 8, 512], F32)
            for ti, (t0, n) in enumerate(TILES):
                t, b = r.next()
                P.dma(t[:, :, :n], xT[:, :, t0:t0 + n].rearrange("c p t -> p c t"), reads=[b_xTt[ti]], writes=[b])
                P.dma(dbg_d[:, :, t0:t0 + n].rearrange("c p t -> p c t"), t[:, :, :n], reads=[b], eng="pool")
    P.emit()
    st.close()
    return k.nc


def finish_dbg(k, st, I, out_d, dbg_d, items):
    P = k.P
    for (t, b, shape) in items:
        P.dma(dbg_d, t[:].rearrange("p a b -> p (a b)") if len(t.shape) == 3 else t[:], reads=[b], eng="pool")
    P.emit()
    st.close()
    return k.nc


def pcol(v, nchunk):
    return np.ascontiguousarray(np.asarray(v, np.float32).reshape(nchunk, 128).T)


def host_inputs(inp, b):
    f = np.float32
    m = {}
    m["x"] = np.ascontiguousarray(inp["x"][b], f)
    m["ctx"] = np.ascontiguousarray(inp["ctx"][b], f)
    cc = np.stack([np.asarray(inp["c"][b], f), np.asarray(inp["c_ctx"], f)], axis=-1)
    m["cc"] = np.ascontiguousarray(cc.reshape(8, 128, 2).transpose(1, 0, 2))
    m["ada_w"] = np.asarray(inp["ada_w"], f)
    m["ada_b"] = np.ascontiguousarray(np.asarray(inp["ada_b"], f).reshape(DEPTH, 72, 128).transpose(2, 0, 1))
    m["norm_g"] = np.ascontiguousarray(np.asarray(inp["norm_g"], f).reshape(DEPTH, 3, 8, 128).transpose(3, 0, 1, 2))
    for n in ("ffn_w_gate", "ffn_w_up", "ffn_w_down", "w_in", "mla_w_uq", "mla_w_ukv", "gla_w_gate", "gla_b_gate",
              "hy_filt_w1", "hy_filt_w2", "hy_filt_w3", "w_branch", "w_out"):
        m[n] = np.asarray(inp[n], f)
    m["mla_q_norm_g"] = np.ascontiguousarray(np.asarray(inp["mla_q_norm_g"], f).reshape(DEPTH, 2, 128).transpose(2, 0, 1))
    m["mla_kv_norm_g"] = np.ascontiguousarray(np.asarray(inp["mla_kv_norm_g"], f).T)
    m["gla_norm_g"] = np.ascontiguousarray(np.asarray(inp["gla_norm_g"], f).T)
    m["gqa_q_norm_g"] = np.ascontiguousarray(np.tile(np.asarray(inp["gqa_q_norm_g"], f), (1, 2)).T)
    m["gqa_k_norm_g"] = np.ascontiguousarray(np.tile(np.asarray(inp["gqa_k_norm_g"], f), (1, 2)).T)
    m["hy_sconv_w"] = np.ascontiguousarray(np.asarray(inp["hy_sconv_w"], f).reshape(DEPTH, 3, 12, 128).transpose(3, 0, 1, 2))
    m["hy_sconv_b"] = np.ascontiguousarray(np.asarray(inp["hy_sconv_b"], f).reshape(DEPTH, 12, 128).transpose(2, 0, 1))
    m["hy_filt_b1"] = np.ascontiguousarray(np.asarray(inp["hy_filt_b1"], f).T)
    m["hy_filt_b2"] = np.ascontiguousarray(np.asarray(inp["hy_filt_b2"], f).T)
    m["hy_filt_bias"] = np.ascontiguousarray(np.asarray(inp["hy_filt_bias"], f).reshape(DEPTH, 2, 4, 128).transpose(3, 0, 1, 2))
    m["final_g"] = pcol(inp["final_g"], 8)
    m.update(consts())
    return m


def kernel(**inputs):
    nc = bass.Bass("TRN2", target_bir_lowering=False)
    build(nc)
    in_maps = [host_inputs(inputs, b) for b in range(8)]
    res = run_bass_kernel_spmd(nc, in_maps, core_ids=list(range(8)))
    return np.stack([np.asarray(r["out"], np.float32) for r in res.results], axis=0)
```
